# Optimizing a Trainium2 kernel written in Bass

```python
import jax, jax.numpy as jnp
from jax import lax
import numpy as np

D_MODEL = 1024
BATCH = 4
SEQ = 8192
DEPTH = 2

GRID_W = 64
CTX_LEN = 256
HEAD_DIM = 64
D_A = D_MODEL // 2
D_B = D_MODEL // 4
D_C = D_MODEL // 4
D_MIX = D_A + D_B + D_C
H_A = D_A // HEAD_DIM
H_B = D_B // HEAD_DIM
H_C = D_C // HEAD_DIM
D_IN = 2 * D_A + 2 * D_B + D_C
CHUNK = 128
CONV_W = 4
CONV_LEFT = 2
RG_C = 8.0
D_FF = -(-(8 * D_MODEL) // (3 * 256)) * 256
N_MOD = 6
DEEPNORM_ALPHA = (2 * DEPTH) ** 0.25
DEEPNORM_BETA = (8 * DEPTH) ** -0.25
LN_EPS = 1e-6
POS_BASE = 10000.0

kernel_name = "hybrid_rglru_chunkmlp_fourier_dit"


def layer_norm(x, g=None, b=None):
    xf = x.astype(jnp.float32)
    mu = jnp.mean(xf, -1, keepdims=True)
    var = jnp.mean(jnp.square(xf - mu), -1, keepdims=True)
    y = (xf - mu) * lax.rsqrt(var + LN_EPS)
    if g is not None:
        y = y * g.astype(jnp.float32) + b.astype(jnp.float32)
    return y.astype(x.dtype)


def rms_norm(x, g):
    xf = x.astype(jnp.float32)
    y = xf * lax.rsqrt(jnp.mean(jnp.square(xf), -1, keepdims=True) + LN_EPS)
    return (y * g.astype(jnp.float32)).astype(x.dtype)


def pos_embed_2d(rows, dim):
    quarter = dim // 4
    freqs = POS_BASE ** (-jnp.arange(quarter, dtype=jnp.float32) / quarter)
    r = jnp.repeat(jnp.arange(rows, dtype=jnp.float32), GRID_W)
    col = jnp.tile(jnp.arange(GRID_W, dtype=jnp.float32), rows)

    def enc(p):
        ang = p[:, None] * freqs[None, :]
        return jnp.concatenate([jnp.sin(ang), jnp.cos(ang)], -1)

    return jnp.concatenate([enc(r), enc(col)], -1)


def dwconv_centred(x, w, b):
    L = x.shape[1]
    xp = jnp.pad(x, ((0, 0), (CONV_LEFT, CONV_W - 1 - CONV_LEFT), (0, 0)))
    y = b
    for k in range(CONV_W):
        y = y + w[k] * xp[:, k:k + L]
    return y


def _combine(left, right):
    a_l, b_l = left
    a_r, b_r = right
    return a_l * a_r, a_r * b_l + b_r


def linear_scan(a, b, reverse):
    return lax.associative_scan(_combine, (a, b), reverse=reverse, axis=1)


def rglru_bidir(xa, conv_w, conv_b, wa, ba, wx, bx, lam, h0=None):
    Bsz, L, _ = xa.shape
    xc = dwconv_centred(xa, conv_w, conv_b).astype(jnp.float32)
    xh = xc.reshape(Bsz, L, H_A, HEAD_DIM)
    r = jax.nn.sigmoid(jnp.einsum("blhc,dhce->dblhe", xh, wa.astype(jnp.float32)).reshape(2, Bsz, L, D_A)
                       + ba.astype(jnp.float32)[:, None, None])
    i = jax.nn.sigmoid(jnp.einsum("blhc,dhce->dblhe", xh, wx.astype(jnp.float32)).reshape(2, Bsz, L, D_A)
                       + bx.astype(jnp.float32)[:, None, None])
    log_a = -RG_C * r * jax.nn.softplus(-lam.astype(jnp.float32))[:, None, None]
    a = jnp.exp(log_a)
    u = jnp.sqrt(-jnp.expm1(2.0 * log_a)) * (i * xc[None])
    acum_f, h_f = linear_scan(a[0], u[0], False)
    acum_b, h_b = linear_scan(a[1], u[1], True)
    if h0 is not None:
        h_f = h_f + acum_f * h0[0][:, None]
        h_b = h_b + acum_b * h0[1][:, None]
    return h_f, h_b


def spatial_gating(uv, ws, bs):
    Bsz, L, _ = uv.shape
    z = jax.nn.gelu(uv, approximate=False)
    u, v = z[..., :D_B], z[..., D_B:]
    v = layer_norm(v.reshape(Bsz, L // CHUNK, CHUNK, H_B, HEAD_DIM))
    s = jnp.einsum("hpq,bnqhc->bnphc", ws, v) + jnp.transpose(bs)[:, :, None]
    return u * s.reshape(Bsz, L, D_B)


def fourier_mix(xf, wf):
    Bsz, L, _ = xf.shape
    z = xf.reshape(Bsz, L, H_C, HEAD_DIM).astype(jnp.float32)
    f = jnp.fft.fft2(z, axes=(1, 3), norm="ortho").real
    return jnp.einsum("blhc,hce->blhe", f, wf.astype(jnp.float32)).reshape(Bsz, L, D_C).astype(xf.dtype)


def mix_outputs(p, h_lru, sg_ws, sg_b, fourier_w, g_mix, w_out):
    y_a = jax.nn.gelu(p[..., D_A:2 * D_A], approximate=False) * h_lru.astype(p.dtype)
    y_b = spatial_gating(p[..., 2 * D_A:2 * D_A + 2 * D_B], sg_ws, sg_b)
    y_c = fourier_mix(p[..., 2 * D_A + 2 * D_B:], fourier_w)
    y = jnp.concatenate([rms_norm(y_a, g_mix[:D_A]),
                         rms_norm(y_b, g_mix[D_A:D_A + D_B]),
                         rms_norm(y_c, g_mix[D_A + D_B:])], -1)
    return y @ w_out


def swiglu(h, w_up, w_down):
    gu = h @ w_up
    return (jax.nn.silu(gu[..., :D_FF]) * gu[..., D_FF:]) @ w_down


def setup_inputs(seed: int = 0) -> dict:
    key = jax.random.key(seed)
    ks = jax.random.split(key, 26)
    f32 = jnp.float32

    def nrm(k, shape, s):
        return jax.random.normal(k, shape, f32) * s

    a0 = jax.random.uniform(ks[13], (DEPTH, 2, D_A), f32, 0.9, 0.999)
    return {
        "x": nrm(ks[0], (BATCH, SEQ, D_MODEL), 1.0),
        "c": nrm(ks[1], (BATCH, D_MODEL), 1.0),
        "ctx": nrm(ks[2], (BATCH, CTX_LEN, D_MODEL), 1.0),
        "c_ctx": nrm(ks[3], (D_MODEL,), 1.0),
        "w_mod": nrm(ks[4], (DEPTH, D_MODEL, N_MOD * D_MODEL), 0.5 * D_MODEL ** -0.5),
        "b_mod": nrm(ks[5], (DEPTH, N_MOD * D_MODEL), 0.02),
        "w_in": nrm(ks[6], (DEPTH, D_MODEL, D_IN), D_MODEL ** -0.5),
        "conv_w": nrm(ks[7], (DEPTH, CONV_W, D_A), CONV_W ** -0.5),
        "conv_b": nrm(ks[8], (DEPTH, D_A), 0.02),
        "lru_wa": nrm(ks[9], (DEPTH, 2, H_A, HEAD_DIM, HEAD_DIM), HEAD_DIM ** -0.5),
        "lru_ba": nrm(ks[10], (DEPTH, 2, D_A), 0.02),
        "lru_wx": nrm(ks[11], (DEPTH, 2, H_A, HEAD_DIM, HEAD_DIM), HEAD_DIM ** -0.5),
        "lru_bx": nrm(ks[12], (DEPTH, 2, D_A), 0.02),
        "lru_lam": jnp.log(a0) - jnp.log1p(-a0),
        "sg_ws": nrm(ks[14], (DEPTH, H_B, CHUNK, CHUNK), CHUNK ** -0.5),
        "sg_b": 1.0 + nrm(ks[15], (DEPTH, H_B, CHUNK), 0.02),
        "fourier_w": nrm(ks[16], (DEPTH, H_C, HEAD_DIM, HEAD_DIM), HEAD_DIM ** -0.5),
        "g_mix": 1.0 + nrm(ks[17], (DEPTH, D_MIX), 0.02),
        "w_out": nrm(ks[18], (DEPTH, D_MIX, D_MODEL), DEEPNORM_BETA * D_MIX ** -0.5),
        "ln1_g": 1.0 + nrm(ks[19], (DEPTH, D_MODEL), 0.02),
        "ln1_b": nrm(ks[20], (DEPTH, D_MODEL), 0.02),
        "w_up": nrm(ks[21], (DEPTH, D_MODEL, 2 * D_FF), D_MODEL ** -0.5),
        "w_down": nrm(ks[22], (DEPTH, D_FF, D_MODEL), DEEPNORM_BETA * D_FF ** -0.5),
        "ln2_g": 1.0 + nrm(ks[23], (DEPTH, D_MODEL), 0.02),
        "ln2_b": nrm(ks[24], (DEPTH, D_MODEL), 0.02),
    }


def reference(x, c, ctx, c_ctx, w_mod, b_mod, w_in, conv_w, conv_b, lru_wa, lru_ba, lru_wx, lru_bx,
              lru_lam, sg_ws, sg_b, fourier_w, g_mix, w_out, ln1_g, ln1_b, w_up, w_down, ln2_g, ln2_b):
    Bsz, L, D = x.shape
    rows = L // GRID_W
    xl = x + pos_embed_2d(rows, D).astype(x.dtype)[None]
    xc = ctx
    sc = jax.nn.silu(c)
    sc_ctx = jax.nn.silu(c_ctx)
    for l in range(DEPTH):
        last = l == DEPTH - 1
        mod_l = (sc @ w_mod[l] + b_mod[l]).reshape(Bsz, N_MOD, 1, D)
        mod_c = (sc_ctx @ w_mod[l] + b_mod[l]).reshape(N_MOD, 1, 1, D)
        lru = (conv_w[l], conv_b[l], lru_wa[l], lru_ba[l], lru_wx[l], lru_bx[l], lru_lam[l])
        mix = (sg_ws[l], sg_b[l], fourier_w[l], g_mix[l], w_out[l])

        hc = layer_norm(xc) * (1.0 + mod_c[1]) + mod_c[0]
        hl = layer_norm(xl) * (1.0 + mod_l[:, 1]) + mod_l[:, 0]
        if last:
            hf_c, hb_c = rglru_bidir(hc @ w_in[l][:, :D_A], *lru)
        else:
            pc = hc @ w_in[l]
            hf_c, hb_c = rglru_bidir(pc[..., :D_A], *lru)
            yc = mix_outputs(pc, hf_c + hb_c, *mix)
        pl = hl @ w_in[l]
        hf_l, hb_l = rglru_bidir(pl[..., :D_A], *lru, h0=(hf_c[:, -1], hb_c[:, 0]))
        yl = mix_outputs(pl, hf_l + hb_l, *mix)
        xl = layer_norm(DEEPNORM_ALPHA * xl + mod_l[:, 2] * yl, ln1_g[l], ln1_b[l])

        hl = layer_norm(xl) * (1.0 + mod_l[:, 4]) + mod_l[:, 3]
        xl = layer_norm(DEEPNORM_ALPHA * xl + mod_l[:, 5] * swiglu(hl, w_up[l], w_down[l]), ln2_g[l], ln2_b[l])

        if not last:
            xc = layer_norm(DEEPNORM_ALPHA * xc + mod_c[2] * yc, ln1_g[l], ln1_b[l])
            hc = layer_norm(xc) * (1.0 + mod_c[4]) + mod_c[3]
            xc = layer_norm(DEEPNORM_ALPHA * xc + mod_c[5] * swiglu(hc, w_up[l], w_down[l]), ln2_g[l], ln2_b[l])
    return xl
```

```python
import math
from contextlib import ExitStack

import numpy as np
import ml_dtypes
import concourse.bass as bass
import concourse.mybir as mybir
from concourse.bass_utils import run_bass_kernel_spmd

F32 = mybir.dt.float32
BF16 = mybir.dt.bfloat16
I32 = mybir.dt.int32
ALU = mybir.AluOpType
AF = mybir.ActivationFunctionType

D = 1024
T = 8192
TC = 256
TT = T + TC
TL = 256
NT = T // TL
DEPTH = 2
DA, DB, DC = 512, 256, 256
DIN = 1792
DFF = 2816
NF = DFF // 128
EPS = 1e-6
ALPHA = (2 * DEPTH) ** 0.25
EPS_POST = EPS / (ALPHA * ALPHA)
N_CORES = 8
TLOC = T // 2
NTL = TLOC // TL

SAME_ENG_SYNC = True
N_DMA_SEMS = 40


class FW:
    def __init__(self, nc, es):
        self.nc = nc
        self.es = es
        self.engs = {"pe": nc.tensor, "act": nc.scalar, "dve": nc.vector, "pool": nc.gpsimd, "sp": nc.sync}
        self.sems = {}
        self.cnt = {}
        for e in self.engs:
            self.sems[e] = es.enter_context(nc.semaphore("s_" + e))
            self.cnt[e] = 0
        self.dsem = {}
        for q in ("sp", "pool"):
            lst = []
            for i in range(N_DMA_SEMS):
                key = "d_%s_%d" % (q, i)
                self.sems[key] = es.enter_context(nc.semaphore(key))
                self.cnt[key] = 0
                lst.append(key)
            self.dsem[q] = [lst, 0]
        self.seen = {e: {} for e in self.engs}
        self.lastw = {}
        self.readers = {}
        self.ninst = 0

    def _wait(self, eng, ev):
        sk, v, prod = ev
        if prod == "pe" and eng == "pe":
            return
        if prod == eng and not SAME_ENG_SYNC:
            return
        if self.seen[eng].get(sk, 0) >= v:
            return
        self.engs[eng].wait_ge(self.sems[sk], v)
        self.seen[eng][sk] = v

    def _deps(self, eng, reads, writes):
        for k in reads:
            ev = self.lastw.get(k)
            if ev is not None:
                self._wait(eng, ev)
        for k in writes:
            ev = self.lastw.get(k)
            if ev is not None:
                self._wait(eng, ev)
            for ev in list(self.readers.get(k, {}).values()):
                self._wait(eng, ev)

    def _record(self, ev, reads, writes):
        for k in writes:
            self.lastw[k] = ev
            self.readers[k] = {}
        for k in reads:
            if k in writes:
                continue
            self.readers.setdefault(k, {})[ev[0]] = ev

    def op(self, eng, fn, reads=(), writes=()):
        self._deps(eng, reads, writes)
        inst = fn(self.engs[eng])
        self.cnt[eng] += 1
        inst.then_inc(self.sems[eng], 1)
        ev = (eng, self.cnt[eng], eng)
        self._record(ev, reads, writes)
        self.ninst += 1
        return ev

    def dma(self, q, out, in_, reads=(), writes=(), **kw):
        self._deps(q, reads, writes)
        lst, idx = self.dsem[q]
        sk = lst[idx % len(lst)]
        self.dsem[q][1] = idx + 1
        if self.cnt[sk] > 0:
            self._wait(q, (sk, self.cnt[sk], "dma"))
        inst = self.engs[q].dma_start(out=out, in_=in_, **kw)
        self.cnt[sk] += 16
        inst.then_inc(self.sems[sk], 16)
        ev = (sk, self.cnt[sk], "dma")
        self._record(ev, reads, writes)
        self.ninst += 1
        return ev

    def barrier(self):
        for e in self.engs:
            for e2 in ("pe", "act", "dve", "pool"):
                if e2 != e and self.cnt[e2] > 0:
                    self._wait(e, (e2, self.cnt[e2], e2))
            if self.cnt.get("cc", 0) > 0:
                self._wait(e, ("cc", self.cnt["cc"], "dma"))
            for q in self.dsem:
                for sk in self.dsem[q][0]:
                    if self.cnt[sk] > 0:
                        self._wait(e, (sk, self.cnt[sk], "dma"))
        self.lastw.clear()
        self.readers.clear()

    def collective(self, kind, in_ap, out_ap, groups, reads, writes):
        self._deps("pool", reads, writes)
        if "cc" not in self.sems:
            self.sems["cc"] = self.es.enter_context(self.nc.semaphore("s_cc"))
            self.cnt["cc"] = 0
        inst = self.nc.gpsimd.collective_compute(kind, ALU.bypass, replica_groups=groups, ins=[in_ap], outs=[out_ap])
        self.cnt["cc"] += 1
        inst.then_inc(self.sems["cc"], 1)
        ev = ("cc", self.cnt["cc"], "dma")
        self._record(ev, reads, writes)
        return ev

    def wait_all(self, eng="sp"):
        for ev in list(self.lastw.values()):
            self._wait(eng, ev)
        for d in list(self.readers.values()):
            for ev in list(d.values()):
                self._wait(eng, ev)
        for q in self.dsem:
            for sk in self.dsem[q][0]:
                if self.cnt[sk] > 0:
                    self._wait(eng, (sk, self.cnt[sk], "dma"))
        for e in ("pe", "act", "dve", "pool"):
            if self.cnt[e] > 0:
                self._wait(eng, (e, self.cnt[e], e))


def build_program(stop_after=None):
    nc = bass.Bass("TRN2", target_bir_lowering=False, num_devices=8)

    def din(name, shape, dt=F32):
        return nc.dram_tensor(name, list(shape), dt, kind="ExternalInput").ap()

    dbg = stop_after is not None

    CC_BUFS = ("XAL", "GGL", "ZL", "XAF", "GGF", "ZF")

    def dscr(name, shape, dt=F32):
        ext = dbg and name not in CC_BUFS
        return nc.dram_tensor(name, list(shape), dt, kind="ExternalOutput" if ext else "Internal").ap()

    x_d = din("x", [TLOC, D])
    ctx_d = din("ctx", [TC, D])
    cc_d = din("cc", [2, D])
    pos_d = din("pos", [TLOC, D])
    rmask_d = din("rmask", [128, 2])
    wmod_d = din("w_mod", [DEPTH, D, 6 * D])
    bmod_d = din("b_mod", [DEPTH, 6 * D])
    win_d = din("w_in", [DEPTH, D, DIN])
    convw_d = din("conv_w", [DEPTH, 4, DA])
    convb_d = din("conv_b", [DEPTH, DA])
    wa_d = din("lru_wa", [DEPTH, 2, 8, 64, 64])
    ba_d = din("lru_ba", [DEPTH, 2, DA])
    wx_d = din("lru_wx", [DEPTH, 2, 8, 64, 64])
    bx_d = din("lru_bx", [DEPTH, 2, DA])
    lam_d = din("lru_lam", [DEPTH, 2, DA])
    ws_d = din("sg_ws", [DEPTH, 4, 128, 128])
    sgb_d = din("sg_b", [DEPTH, 4, 128])
    wf_d = din("fourier_w", [DEPTH, 4, 64, 64])
    gmix_d = din("g_mix", [DEPTH, D])
    wout_d = din("w_out", [DEPTH, D, D])
    ln1g_d = din("ln1_g", [DEPTH, D])
    ln1b_d = din("ln1_b", [DEPTH, D])
    wup_d = din("w_up", [DEPTH, D, 2 * DFF])
    wdn_d = din("w_down", [DEPTH, DFF, D])
    ln2g_d = din("ln2_g", [DEPTH, D])
    ln2b_d = din("ln2_b", [DEPTH, D])
    ident_d = din("ident", [128, 128])
    t1_d = din("t1", [128, 3, 128], BF16)
    tw_d = din("tw", [128, T], BF16)
    c256_d = din("c256", [128, 2, 2, 256], BF16)
    cspad_d = din("cspad", [64, 2, 2, 128])
    out_d = nc.dram_tensor("out", [TLOC, D], F32, kind="ExternalOutput").ap()

    XB = dscr("XB", [TLOC + TC, D])
    XAL = dscr("XAL", [DA, TLOC])
    GGL = dscr("GGL", [DA, TLOC])
    ZL = dscr("ZL", [DC, TLOC], BF16)
    XAF = dscr("XAF", [2 * DA, TLOC])
    GGF = dscr("GGF", [2 * DA, TLOC])
    ZF = dscr("ZF", [2 * DC, TLOC], BF16)
    XAC = dscr("XAC", [DA, TC])
    GGC = dscr("GGC", [DA, TC])
    YBL = dscr("YBL", [DB, TLOC + TC], BF16)
    XC = dscr("XC", [DA, TT])
    HF = dscr("HF", [DA, TT])
    YT = dscr("YT", [D, TT], BF16)
    GD = dscr("GD", [128, 2, 64, 256], BF16)
    MODS = dscr("MODS", [2, 6 * D])

    XCv = XC.rearrange("(c p) t -> p c t", p=128)
    XALv = XAL.rearrange("(c p) t -> p c t", p=128)
    GGLv = GGL.rearrange("(c p) t -> p c t", p=128)
    ZLv = ZL.rearrange("(c p) t -> p c t", p=128)
    XACv = XAC.rearrange("(c p) t -> p c t", p=128)
    GGCv = GGC.rearrange("(c p) t -> p c t", p=128)
    YBLv = YBL.rearrange("(c p) t -> p c t", p=128)
    XAFv = XAF.rearrange("(c r p) t -> r p c t", r=2, p=128)
    GGFv = GGF.rearrange("(c r p) t -> r p c t", r=2, p=128)
    ZFv = ZF.rearrange("(r c p) t -> r p c t", r=2, p=128)
    PAIRS = [[0, 1], [2, 3], [4, 5], [6, 7]]
    HFv = HF.rearrange("(c p) t -> p c t", p=128)
    YTv = YT.rearrange("(c p) t -> p c t", p=128)

    TILES = [(T, True)] + [(TL * i, False) for i in range(NT)]
    LTILES = [(TLOC, True)] + [(TL * i, False) for i in range(NTL)]
    import os as _os
    if _os.environ.get("DBG_NT"):
        LTILES = LTILES[:int(_os.environ["DBG_NT"])]

    with ExitStack() as es:
        fw = FW(nc, es)
        op = fw.op
        dma = fw.dma

        uid = [0]

        def sb(es_, name, shape, dt=F32):
            uid[0] += 1
            return es_.enter_context(nc.sbuf_tensor("%s_s%d" % (name, uid[0]), list(shape), dt))

        def ps(es_, name, shape, dt=F32):
            uid[0] += 1
            return es_.enter_context(nc.psum_tensor("%s_p%d" % (name, uid[0]), list(shape), dt))

        es.enter_context(nc.allow_non_contiguous_dma(reason="small strided parameter loads"))

        ident = sb(es, "ident", [128, 128])
        ones_f = sb(es, "ones_f", [128, 128])
        dma("sp", ident[:], ident_d[:, :], writes=["ident"])
        op("pool", lambda e: e.memset(ones_f[:], 1.0), writes=["ones_f"])

        def rsqrt(out, x, tmp, kout, kx, ktmp, eng="pool", iters=3):
            xi = x.bitcast(I32)
            oi = out.bitcast(I32)
            op("dve", lambda e: e.tensor_scalar(out=oi, in0=xi, scalar1=1, scalar2=None,
                                                op0=ALU.arith_shift_right), reads=[kx], writes=[kout])
            op("dve", lambda e: e.tensor_scalar(out=oi, in0=oi, scalar1=-1.0, scalar2=float(0x5F3759DF),
                                                op0=ALU.mult, op1=ALU.add), writes=[kout])
            for _ in range(iters):
                op(eng, lambda e: e.tensor_tensor(out=tmp, in0=x, in1=out, op=ALU.mult), reads=[kx, kout], writes=[ktmp])
                op(eng, lambda e: e.tensor_tensor(out=tmp, in0=tmp, in1=out, op=ALU.mult), reads=[kout], writes=[ktmp])
                op(eng, lambda e: e.tensor_scalar(out=tmp, in0=tmp, scalar1=-0.5, scalar2=1.5,
                                                  op0=ALU.mult, op1=ALU.add), writes=[ktmp])
                op(eng, lambda e: e.tensor_tensor(out=out, in0=out, in1=tmp, op=ALU.mult), reads=[ktmp], writes=[kout])

        def resid_rows(l, col0, is_ctx):
            if l == 0:
                return (ctx_d[0:TL, :] if is_ctx else x_d[col0:col0 + TL, :])
            return XB[col0:col0 + TL, :]

        def load_resid(l, col0, is_ctx, xt, kx, pt=None, kp=None):
            src = resid_rows(l, col0, is_ctx)
            dma("sp", xt[:], src.rearrange("(s p) d -> p s d", p=128), writes=[kx])
            if l == 0 and not is_ctx:
                dma("sp", pt[:], pos_d[col0:col0 + TL, :].rearrange("(s p) d -> p s d", p=128), writes=[kp])
                op("pool", lambda e: e.tensor_tensor(out=xt[:], in0=xt[:], in1=pt[:], op=ALU.add),
                   reads=[kp], writes=[kx])

        def ln_tile(xt, kx, xh, kxh, scr, eps, eng="pool", iters=3):
            st, mv, ve, rs, tmp = scr["st"], scr["mv"], scr["ve"], scr["rs"], scr["tmp"]
            kk = scr["k"]
            for s in range(2):
                for h in range(2):
                    op("dve", lambda e: e.bn_stats(out=st[:, s, h, :], in_=xt[:, s, h * 512:(h + 1) * 512]),
                       reads=[kx], writes=[kk + "st"])
                op("dve", lambda e: e.bn_aggr(out=mv[:, s, :], in_=st[:, s, :, :].rearrange("p a b -> p (a b)")),
                   reads=[kk + "st"], writes=[kk + "mv"])
            op("dve", lambda e: e.tensor_scalar(out=ve[:], in0=mv[:, :, 1], scalar1=float(eps), scalar2=None,
                                                op0=ALU.add), reads=[kk + "mv"], writes=[kk + "ve"])
            rsqrt(rs[:], ve[:], tmp[:], kk + "rs", kk + "ve", kk + "tmp", eng=eng, iters=iters)
            for s in range(2):
                op("dve", lambda e: e.tensor_scalar(out=xh[:, s, :], in0=xt[:, s, :], scalar1=mv[:, s, 0:1],
                                                    scalar2=rs[:, s:s + 1], op0=ALU.subtract, op1=ALU.mult),
                   reads=[kx, kk + "mv", kk + "rs"], writes=[kxh])

        def ln_stats(xt, kx, scr, eps, iters=3):
            st, mv, ve, rs, tmp = scr["st"], scr["mv"], scr["ve"], scr["rs"], scr["tmp"]
            kk = scr["k"]
            for s in range(2):
                for h in range(2):
                    op("dve", lambda e: e.bn_stats(out=st[:, s, h, :], in_=xt[:, s, h * 512:(h + 1) * 512]),
                       reads=(kx if isinstance(kx, list) else [kx]), writes=[kk + "st%d%d" % (s, h)])
                op("dve", lambda e: e.bn_aggr(out=mv[:, s, :], in_=st[:, s, :, :].rearrange("p a b -> p (a b)")),
                   reads=[kk + "st%d0" % s, kk + "st%d1" % s], writes=[kk + "mv"])
            op("dve", lambda e: e.tensor_scalar(out=ve[:], in0=mv[:, :, 1], scalar1=float(eps), scalar2=None,
                                                op0=ALU.add), reads=[kk + "mv"], writes=[kk + "ve"])
            rsqrt(rs[:], ve[:], tmp[:], kk + "rs", kk + "ve", kk + "tmp", iters=iters)

        def ln_apply(xt, kx, xh, kxh, scr):
            mv, rs = scr["mv"], scr["rs"]
            kk = scr["k"]
            for s in range(2):
                op("dve", lambda e: e.tensor_scalar(out=xh[:, s, :], in0=xt[:, s, :], scalar1=mv[:, s, 0:1],
                                                    scalar2=rs[:, s:s + 1], op0=ALU.subtract, op1=ALU.mult),
                   reads=[kx, kk + "mv", kk + "rs"], writes=[kxh])

        def ln_scr(es_, name):
            return dict(st=sb(es_, name + "st", [128, 2, 2, 6]), mv=sb(es_, name + "mv", [128, 2, 2]),
                        ve=sb(es_, name + "ve", [128, 2]), rs=sb(es_, name + "rs", [128, 2]),
                        tmp=sb(es_, name + "tmp", [128, 2]), k=name)

        def transpose_mod(xh, kxh, hT, khT, tps, modc, strm, jsc, jsh):
            for kp in range(4):
                tp, ktp = tps[kp % 2]
                for j in range(2):
                    k = 2 * kp + j
                    for s in range(2):
                        op("pe", lambda e: e.transpose(out=tp[:, j, s * 128:(s + 1) * 128],
                                                       in_=xh[:, s, k * 128:(k + 1) * 128], identity=ident[:]),
                           reads=[kxh, "ident"], writes=[ktp])
                for j in range(2):
                    k = 2 * kp + j
                    op("act", lambda e: e.activation(out=hT[:, k, :], in_=tp[:, j, :], func=AF.Identity,
                                                     scale=modc[:, strm, jsc, k:k + 1], bias=modc[:, strm, jsh, k:k + 1]),
                       reads=["modc"], writes=[ktp, khT])

        for l in range(DEPTH):
            last = (l == DEPTH - 1)
            with ExitStack() as el:
                with ExitStack() as ep:
                    ep.enter_context(nc.named_scope("P0_l%d" % l))
                    cct = sb(ep, "cct", [128, 8, 2])
                    sct = sb(ep, "sct", [128, 8, 2])
                    scbc = sb(ep, "scbc", [128, 8, 128])
                    bmbc = sb(ep, "bmbc", [128, 6 * D])
                    modbc = sb(ep, "modbc", [128, 6 * D])
                    wm = [sb(ep, "wm%d" % i, [128, 3072]) for i in range(3)]
                    pmod = [ps(ep, "pmod%d" % i, [128, 512]) for i in range(6)]
                    for s in range(2):
                        dma("sp", cct[:, :, s], cc_d[s, :].rearrange("(k p) -> p k", p=128), writes=["cct"])
                    dma("sp", bmbc[:], bmod_d[l:l + 1, :].partition_broadcast(128), writes=["bmbc"])
                    op("act", lambda e: e.activation(out=sct[:], in_=cct[:], func=AF.Tanh, scale=0.5), reads=["cct"], writes=["sct"])
                    op("dve", lambda e: e.tensor_scalar(out=sct[:], in0=sct[:], scalar1=0.5, scalar2=0.5, op0=ALU.mult, op1=ALU.add), writes=["sct"])
                    op("dve", lambda e: e.tensor_tensor(out=sct[:], in0=sct[:], in1=cct[:], op=ALU.mult), reads=["cct"], writes=["sct"])
                    for k in range(8):
                        for s in range(2):
                            op("dve", lambda e: e.tensor_scalar(out=scbc[:, k, 64 * s:64 * s + 64], in0=ones_f[:, 0:64],
                                                                scalar1=sct[:, k, s:s + 1], scalar2=None, op0=ALU.mult),
                               reads=["sct", "ones_f"], writes=["scbc"])
                    ld = 0
                    for half in range(2):
                        for k in range(8):
                            w = wm[ld % 3]
                            kw = "wm%d" % (ld % 3)
                            ld += 1
                            dma("sp", w[:], wmod_d[l, k * 128:(k + 1) * 128, half * 3072:(half + 1) * 3072], writes=[kw])
                            for n in range(6):
                                op("pe", lambda e: e.matmul(pmod[n][:], lhsT=scbc[:, k, :], rhs=w[:, n * 512:(n + 1) * 512],
                                                            start=(k == 0), stop=(k == 7)),
                                   reads=["scbc", kw], writes=["pmod%d" % n])
                        for n in range(6):
                            c0 = half * 3072 + n * 512
                            op("dve", lambda e: e.tensor_tensor(out=modbc[:, c0:c0 + 512], in0=pmod[n][:], in1=bmbc[:, c0:c0 + 512],
                                                                op=ALU.add), reads=["bmbc"], writes=["pmod%d" % n, "modbc"])
                    dma("sp", MODS[0:1, :], modbc[0:1, :], reads=["modbc"], writes=["MODS"])
                    dma("sp", MODS[1:2, :], modbc[64:65, :], reads=["modbc"], writes=["MODS"])
                    fw.barrier()

                modc = sb(el, "modc", [128, 2, 6, 8])
                gmixc = sb(el, "gmixc", [128, 8])
                for s in range(2):
                    for j in range(6):
                        dma("sp", modc[:, s, j, :], MODS[s, j * D:(j + 1) * D].rearrange("(k p) -> p k", p=128), reads=["MODS"], writes=["modc"])
                dma("sp", gmixc[:], gmix_d[l, :].rearrange("(k p) -> p k", p=128), writes=["gmixc"])
                for j in (1, 4):
                    op("dve", lambda e: e.tensor_scalar(out=modc[:, :, j, :], in0=modc[:, :, j, :], scalar1=1.0, scalar2=None,
                                                        op0=ALU.add), writes=["modc"])

                with ExitStack() as ez:
                    zT = sb(ez, "zT", [128, 2, T], BF16)
                    zTc = sb(ez, "zTc", [128, 2, TC], BF16)
                    with ExitStack() as e1:
                        e1.enter_context(nc.named_scope("M1_l%d" % l))
                        win = sb(e1, "win", [128, 8, DIN], BF16)
                        for k in range(8):
                            dma("pool", win[:, k, :], win_d[l, k * 128:(k + 1) * 128, :], writes=["win"])
                        wsT = sb(e1, "wsT", [128, 4, 128], BF16)
                        wsr = sb(e1, "wsr", [128, 4, 128])
                        bsT = sb(e1, "bsT", [128, 4])
                        dma("sp", wsr[:], ws_d[l].rearrange("h p q -> p h q"), writes=["wsr"])
                        dma("sp", bsT[:], sgb_d[l].rearrange("h p -> p h"), writes=["bsT"])
                        xt = [sb(e1, "xt%d" % i, [128, 2, D]) for i in range(2)]
                        pt = [sb(e1, "pt%d" % i, [128, 2, D]) for i in range(2)] if l == 0 else [None, None]
                        xh = sb(e1, "xh", [128, 2, D])
                        hT = [sb(e1, "hT%d" % i, [128, 8, TL], BF16) for i in range(2)]
                        lns = ln_scr(e1, "l1")
                        xaS = [sb(e1, "xaS%d" % i, [128, 4, TL]) for i in range(2)]
                        ggS = [sb(e1, "ggS%d" % i, [128, 4, TL]) for i in range(2)]
                        yBT = [sb(e1, "yBT%d" % i, [128, 2, TL], BF16) for i in range(2)]
                        zS = [sb(e1, "zS%d" % i, [128, 2, TL], BF16) for i in range(2)]
                        zB = [sb(e1, "zB%d" % i, [128, 512]) for i in range(2)]
                        yb = [sb(e1, "yb%d" % i, [128, 256]) for i in range(2)]
                        vh = sb(e1, "vh", [128, 256], BF16)
                        bst = sb(e1, "bst", [128, 4, 6])
                        bmv = sb(e1, "bmv", [128, 4, 2])
                        bve = sb(e1, "bve", [128, 4])
                        brs = sb(e1, "brs", [128, 4])
                        btmp = sb(e1, "btmp", [128, 4])
                        ssB = sb(e1, "ssB", [128, 2])
                        rB = sb(e1, "rB", [128, 2])
                        rsB = sb(e1, "rsB", [128, 2])
                        rtmp = sb(e1, "rtmp", [128, 2])
                        junk = sb(e1, "junk", [128, 256])
                        tp0 = ps(e1, "tp0", [128, 2, TL]); tp1 = ps(e1, "tp1", [128, 2, TL])
                        tps = [(tp0, "tp0"), (tp1, "tp1")]
                        pa = [ps(e1, "pa%d" % i, [128, 2, TL]) for i in range(2)]
                        pb = [ps(e1, "pb%d" % i, [128, 512]) for i in range(2)]
                        pss = ps(e1, "pss", [128, 512])
                        pyt = ps(e1, "pyt", [128, 2, TL])
                        for h in range(4):
                            op("pe", lambda e: e.transpose(out=pb[0][:, h * 128:(h + 1) * 128], in_=wsr[:, h, :], identity=ident[:]),
                               reads=["wsr", "ident"], writes=["pb0"])
                        op("dve", lambda e: e.tensor_copy(out=wsT[:].rearrange("p h q -> p (h q)"), in_=pb[0][:]), writes=["pb0", "wsT"])

                        tl = LTILES
                        nM = len(tl)
                        zBt = [[sb(e1, "zBt%d_%d" % (i, s_), [128, 512]) for s_ in range(2)] for i in range(2)]

                        def m1_load(ti):
                            nb = ti % 2
                            load_resid(l, tl[ti][0], tl[ti][1], xt[nb], "xt%d" % nb, pt[nb], "pt%d" % nb)

                        def m1_A(ti):
                            b = ti % 2
                            ln_tile(xt[b], "xt%d" % b, xh, "xh", lns, EPS, eng="dve", iters=2)

                        def m1_T(ti):
                            col0, is_ctx = tl[ti]
                            b = ti % 2
                            transpose_mod(xh, "xh", hT[b], "hT%d" % b, tps, modc, 1 if is_ctx else 0, 1, 0)

                        def m1_P(ti):
                            col0, is_ctx = tl[ti]
                            b = ti % 2
                            khT = "hT%d" % b
                            pai = 0
                            only_xa = last and is_ctx
                            for grp in range(2 if only_xa else 4):
                                p_, kp_ = pa[pai % 2], "pa%d" % (pai % 2)
                                pai += 1
                                for j in range(2):
                                    oc = grp * 2 + j
                                    for k in range(8):
                                        op("pe", lambda e: e.matmul(p_[:, j, :], lhsT=win[:, k, oc * 128:(oc + 1) * 128], rhs=hT[b][:, k, :],
                                                                    start=(k == 0), stop=(k == 7)), reads=["win", khT], writes=[kp_])
                                if grp < 2:
                                    op("act", lambda e: e.activation(out=xaS[b][:, 2 * grp:2 * grp + 2, :], in_=p_[:], func=AF.Identity),
                                       writes=[kp_, "xaS%d" % b])
                                else:
                                    g2 = grp - 2
                                    op("act", lambda e: e.activation(out=ggS[b][:, 2 * g2:2 * g2 + 2, :], in_=p_[:], func=AF.Gelu),
                                       writes=[kp_, "ggS%d" % b])
                            dma("sp", XACv[:, :, :] if is_ctx else XALv[:, :, col0:col0 + TL], xaS[b][:], reads=["xaS%d" % b], writes=["XA"])
                            if only_xa:
                                return
                            dma("sp", GGCv[:, :, :] if is_ctx else GGLv[:, :, col0:col0 + TL], ggS[b][:], reads=["ggS%d" % b], writes=["GG"])
                            p_, kp_ = pa[pai % 2], "pa%d" % (pai % 2)
                            for j in range(2):
                                for k in range(8):
                                    op("pe", lambda e: e.matmul(p_[:, j, :], lhsT=win[:, k, 1536 + j * 128:1536 + (j + 1) * 128], rhs=hT[b][:, k, :],
                                                                start=(k == 0), stop=(k == 7)), reads=["win", khT], writes=[kp_])
                            if is_ctx:
                                op("act", lambda e: e.activation(out=zTc[:, :, :], in_=p_[:], func=AF.Identity), writes=[kp_, "zTc"])
                            else:
                                op("act", lambda e: e.activation(out=zS[b][:], in_=p_[:], func=AF.Identity), writes=[kp_, "zS%d" % b])
                                dma("sp", ZLv[:, :, col0:col0 + TL], zS[b][:], reads=["zS%d" % b], writes=["ZL"])

                        def m1_Bproj(ti):
                            col0, is_ctx = tl[ti]
                            if last and is_ctx:
                                return
                            b = ti % 2
                            for s in range(2):
                                p_, kp_ = pb[s], "pb%d" % s
                                for k in range(8):
                                    op("pe", lambda e: e.matmul(p_[:], lhsT=hT[b][:, k, s * 128:(s + 1) * 128], rhs=win[:, k, 1024:1536],
                                                                start=(k == 0), stop=(k == 7)), reads=["win", "hT%d" % b], writes=[kp_])
                                op("act", lambda e: e.activation(out=zBt[b][s][:], in_=p_[:], func=AF.Gelu), writes=[kp_, "zBt%d_%d" % (b, s)])

                        bmv8 = sb(e1, "bmv8", [128, 8, 2])
                        bve8 = sb(e1, "bve8", [128, 8])
                        brs8 = sb(e1, "brs8", [128, 8])
                        btmp8 = sb(e1, "btmp8", [128, 8])
                        bst8 = sb(e1, "bst8", [128, 8, 6])
                        vh2 = sb(e1, "vh2", [128, 2, 256], BF16)
                        ybp = [[sb(e1, "ybp%d_%d" % (i, s_), [128, 256]) for s_ in range(2)] for i in range(2)]
                        ssBp = [sb(e1, "ssBp%d" % i, [128, 2]) for i in range(2)]

                        def skipB(ti):
                            return last and tl[ti][1]

                        def m1_B1a(ti):
                            if skipB(ti):
                                return
                            b = ti % 2
                            for s in range(2):
                                zb, kzb = zBt[b][s], "zBt%d_%d" % (b, s)
                                for h in range(4):
                                    op("dve", lambda e: e.bn_stats(out=bst8[:, 4 * s + h, :], in_=zb[:, 256 + 64 * h:256 + 64 * h + 64]),
                                       reads=[kzb], writes=["bst8_%d" % (4 * s + h)])
                                for h in range(4):
                                    op("dve", lambda e: e.bn_aggr(out=bmv8[:, 4 * s + h, :], in_=bst8[:, 4 * s + h, :]),
                                       reads=["bst8_%d" % (4 * s + h)], writes=["bmv8"])
                            op("dve", lambda e: e.tensor_scalar(out=bve8[:], in0=bmv8[:, :, 1], scalar1=EPS, scalar2=None, op0=ALU.add),
                               reads=["bmv8"], writes=["bve8"])
                            rsqrt(brs8[:], bve8[:], btmp8[:], "brs8", "bve8", "btmp8", iters=2)
                            for s in range(2):
                                zb, kzb = zBt[b][s], "zBt%d_%d" % (b, s)
                                for h in range(4):
                                    op("dve", lambda e: e.tensor_scalar(out=vh2[:, s, 64 * h:64 * h + 64], in0=zb[:, 256 + 64 * h:256 + 64 * h + 64],
                                                                        scalar1=bmv8[:, 4 * s + h, 0:1], scalar2=brs8[:, 4 * s + h:4 * s + h + 1],
                                                                        op0=ALU.subtract, op1=ALU.mult),
                                       reads=[kzb, "bmv8", "brs8"], writes=["vh2_%d" % (4 * s + h)])

                        def m1_smm(ti):
                            if skipB(ti):
                                return
                            for s in range(2):
                                for h in range(4):
                                    op("pe", lambda e: e.matmul(pss[:, 256 * s + 64 * h:256 * s + 64 * h + 64], lhsT=wsT[:, h, :], rhs=vh2[:, s, 64 * h:64 * h + 64],
                                                                start=True, stop=True), reads=["wsT", "vh2_%d" % (4 * s + h)], writes=["pss"])

                        def m1_B1b(ti):
                            if skipB(ti):
                                return
                            b = ti % 2
                            for s in range(2):
                                zb, kzb = zBt[b][s], "zBt%d_%d" % (b, s)
                                for h in range(4):
                                    op("dve", lambda e: e.scalar_tensor_tensor(out=ybp[b][s][:, 64 * h:64 * h + 64], in0=pss[:, 256 * s + 64 * h:256 * s + 64 * h + 64],
                                                                               scalar=bsT[:, h:h + 1], in1=zb[:, 64 * h:64 * h + 64],
                                                                               op0=ALU.add, op1=ALU.mult),
                                       reads=["bsT", kzb], writes=["pss", "ybp%d_%d_%d" % (b, s, h)])
                            for s in range(2):
                                op("act", lambda e: e.activation(out=junk[:], in_=ybp[b][s][:], func=AF.Square, accum_out=ssBp[b][:, s:s + 1]),
                                   reads=["ybp%d_%d_%d" % (b, s, h) for h in range(4)], writes=["junk", "ssBp%d" % b])

                        def m1_B2(ti):
                            if skipB(ti):
                                return
                            col0, is_ctx = tl[ti]
                            b = ti % 2
                            op("dve", lambda e: e.tensor_scalar(out=rB[:], in0=ssBp[b][:], scalar1=1.0 / DB, scalar2=EPS, op0=ALU.mult, op1=ALU.add),
                               reads=["ssBp%d" % b], writes=["rB"])
                            rsqrt(rsB[:], rB[:], rtmp[:], "rsB", "rB", "rtmp", iters=2)
                            for s in range(2):
                                kyb = ["ybp%d_%d_%d" % (b, s, h) for h in range(4)]
                                op("dve", lambda e: e.tensor_scalar(out=ybp[b][s][:], in0=ybp[b][s][:], scalar1=rsB[:, s:s + 1], scalar2=None, op0=ALU.mult),
                                   reads=["rsB"], writes=kyb)
                                for c in range(2):
                                    op("pe", lambda e: e.transpose(out=pyt[:, c, s * 128:(s + 1) * 128], in_=ybp[b][s][:, c * 128:(c + 1) * 128],
                                                                   identity=ident[:]), reads=kyb + ["ident"], writes=["pyt"])
                            for c in range(2):
                                op("act", lambda e: e.activation(out=yBT[b][:, c, :], in_=pyt[:, c, :], func=AF.Identity, scale=gmixc[:, 4 + c:5 + c]),
                                   reads=["gmixc"], writes=["pyt", "yBT%d" % b])
                            dma("sp", YBLv[:, :, col0:col0 + TL], yBT[b][:], reads=["yBT%d" % b], writes=["YBL"])

                        m1_load(0)
                        if nM > 1:
                            m1_load(1)
                        m1_A(0)
                        m1_T(0)
                        for ti in range(nM + 2):
                            if ti < nM:
                                m1_P(ti)
                            if ti + 1 < nM:
                                m1_A(ti + 1)
                            if 2 <= ti:
                                m1_B2(ti - 2)
                            if ti + 1 < nM:
                                m1_T(ti + 1)
                            if 1 <= ti <= nM:
                                m1_B1a(ti - 1)
                            if ti < nM:
                                m1_Bproj(ti)
                            if 1 <= ti <= nM:
                                m1_smm(ti - 1)
                                m1_B1b(ti - 1)
                            if ti + 2 < nM:
                                m1_load(ti + 2)
                        fw.barrier()
                    fw.collective("AllGather", ZL[:, :], ZF[:, :], PAIRS, ["ZL"], ["ZF"])
                    fw.barrier()
                    for c_ in range(4):
                        fw.collective("AllGather", XAL[c_ * 128:(c_ + 1) * 128, :], XAF[c_ * 256:(c_ + 1) * 256, :], PAIRS, ["XA"], ["XAF"])
                    for c_ in range(4):
                        fw.collective("AllGather", GGL[c_ * 128:(c_ + 1) * 128, :], GGF[c_ * 256:(c_ + 1) * 256, :], PAIRS, ["GG"], ["GGF"])
                    if stop_after == "M1" and l == 0:
                        xafd = nc.dram_tensor("XAFd", [2 * DA, TLOC], F32, kind="ExternalOutput").ap()
                        zfd = nc.dram_tensor("ZFd", [2 * DC, TLOC], BF16, kind="ExternalOutput").ap()
                        dma("sp", xafd[:, :], XAF[:, :], writes=["xafd"])
                        dma("sp", zfd[:, :], ZF[:, :], writes=["zfd"])
                        fw.wait_all("sp")
                        return nc

                    if True:
                        with ExitStack() as e2:
                            e2.enter_context(nc.named_scope("DFT_l%d" % l))
                            t1 = sb(e2, "t1", [128, 3, 128], BF16)
                            tw = sb(e2, "tw", [128, 128, 64], BF16)
                            c256 = sb(e2, "c256", [128, 2, 2, 256], BF16)
                            cspad = sb(e2, "cspad", [64, 2, 2, 128])
                            wft = sb(e2, "wft", [64, 4, 64])
                            abd = sb(e2, "abd", [128, 2, 256], BF16)
                            abdc = sb(e2, "abdc", [128, 2, 256], BF16)
                            XTs = sb(e2, "XTs", [128, 2, T])
                            XTc = sb(e2, "XTc", [128, 2, TC])
                            Yp = [sb(e2, "Yp%d" % i, [128, 512], BF16) for i in range(2)]
                            Gs = [sb(e2, "Gs%d" % i, [128, 2, 8, 256], BF16) for i in range(2)]
                            Rb = [sb(e2, "Rb%d" % i, [128, 8, 256], BF16) for i in range(2)]
                            sq = sb(e2, "sq", [128, 2, TL])
                            rr = sb(e2, "rr", [128, TL])
                            rrs = sb(e2, "rrs", [128, TL])
                            rrt = sb(e2, "rrt", [128, TL])
                            yCT = [sb(e2, "yCT%d" % i, [128, 2, TL], BF16) for i in range(2)]
                            pY = [ps(e2, "pY%d" % i, [128, 512]) for i in range(2)]
                            pG = [ps(e2, "pG%d" % i, [128, 2, 256]) for i in range(2)]
                            pX = [ps(e2, "pX%d" % i, [128, 8, 64]) for i in range(2)]
                            pS = ps(e2, "pS", [128, 512])
                            pS1 = ps(e2, "pS1", [128, 512])
                            for r_ in range(2):
                                dma("sp", zT[:, :, r_ * TLOC:(r_ + 1) * TLOC], ZFv[r_], writes=["zT"])
                            dma("sp", t1[:], t1_d[:, :, :], writes=["t1"])
                            dma("sp", tw[:].rearrange("p a b -> p (a b)"), tw_d[:, :], writes=["tw"])
                            dma("sp", c256[:], c256_d[:, :, :, :], writes=["c256"])
                            dma("sp", cspad[:], cspad_d[:, :, :, :], writes=["cspad"])
                            dma("sp", wft[:], wf_d[l].rearrange("g j e -> j g e"), writes=["wft"])
                            for cc in range(2):
                                for pq in range(2):
                                    for gl in range(2):
                                        op("pe", lambda e: e.matmul(pY[0][:, pq * 128 + gl * 64:pq * 128 + gl * 64 + 64],
                                                                    lhsT=cspad[:, pq, gl, :], rhs=wft[:, 2 * cc + gl, :], start=True, stop=True),
                                           reads=["cspad", "wft"], writes=["pY0"])
                                op("act", lambda e: e.activation(out=abd[:, cc, :], in_=pY[0][:, 0:256], func=AF.Identity,
                                                                 scale=1.0 / math.sqrt(T * 64.0)), writes=["pY0", "abd"])
                                op("act", lambda e: e.activation(out=abdc[:, cc, :], in_=pY[0][:, 0:256], func=AF.Identity,
                                                                 scale=1.0 / math.sqrt(TC * 64.0)), writes=["pY0", "abdc"])

                            rr2 = [sb(e2, "rr2_%d" % i, [128, TL]) for i in range(2)]
                            rrs2 = [sb(e2, "rrs2_%d" % i, [128, TL]) for i in range(2)]
                            rrt2 = [sb(e2, "rrt2_%d" % i, [128, TL]) for i in range(2)]

                            def rmsc1(src, b):
                                op("act", lambda e: e.activation(out=sq[:], in_=src, func=AF.Square), reads=["XT"], writes=["sq"])
                                pS_ = pS if b == 0 else pS1
                                for c in range(2):
                                    op("pe", lambda e: e.matmul(pS_[:, 0:TL], lhsT=ones_f[:], rhs=sq[:, c, :], start=(c == 0), stop=(c == 1)),
                                       reads=["ones_f", "sq"], writes=["pS%d" % b])

                            def rmsc2(b):
                                pS_ = pS if b == 0 else pS1
                                op("act", lambda e: e.activation(out=rr2[b][:], in_=pS_[:, 0:TL], func=AF.Ln, scale=1.0 / DC, bias=EPS),
                                   writes=["pS%d" % b, "rr2_%d" % b])
                                op("act", lambda e: e.activation(out=rrs2[b][:], in_=rr2[b][:], func=AF.Exp, scale=-0.5),
                                   reads=["rr2_%d" % b], writes=["rrs2_%d" % b])

                            def rmsc3(src, col0, b):
                                for c in range(2):
                                    op("dve", lambda e: e.scalar_tensor_tensor(out=yCT[b][:, c, :], in0=src[:, c, :], scalar=gmixc[:, 6 + c:7 + c],
                                                                               in1=rrs2[b][:], op0=ALU.mult, op1=ALU.mult),
                                       reads=["XT", "gmixc", "rrs2_%d" % b], writes=["yCT%d_%d" % (b, c)])
                                dma("sp", YTv[:, 6:8, col0:col0 + TL], yCT[b][:], reads=["yCT%d_0" % b, "yCT%d_1" % b], writes=["YT"])

                            def rms_store_c(src, col0, b):
                                rmsc1(src, b)
                                rmsc2(b)
                                rmsc3(src, col0, b)

                            if not last:
                                Ypc = [sb(e2, "Ypc%d" % i, [128, 512], BF16) for i in range(2)]
                                for t in range(2):
                                    for cc in range(2):
                                        op("pe", lambda e: e.matmul(pY[t][:, cc * 256:(cc + 1) * 256], lhsT=zTc[:, cc, t * 128:(t + 1) * 128],
                                                                    rhs=abdc[:, cc, :], start=True, stop=True), reads=["zTc", "abdc"], writes=["pY%d" % t])
                                    op("act", lambda e: e.activation(out=Ypc[t][:], in_=pY[t][:], func=AF.Identity), writes=["pY%d" % t, "Ypc%d" % t])
                                for cc in range(2):
                                    n = 0
                                    for t in range(2):
                                        for pq in range(2):
                                            op("pe", lambda e: e.matmul(pG[cc][:].rearrange("p a b -> p (a b)")[:, 0:256],
                                                                        lhsT=Ypc[t][:, cc * 256 + pq * 128:cc * 256 + (pq + 1) * 128],
                                                                        rhs=c256[:, t, pq, :], start=(n == 0), stop=(n == 3)),
                                               reads=["Ypc%d" % t, "c256"], writes=["pG%d" % cc])
                                            n += 1
                                    op("dve", lambda e: e.tensor_copy(out=XTc[:, cc, :], in_=pG[cc][:].rearrange("p a b -> p (a b)")[:, 0:256]),
                                       writes=["pG%d" % cc, "XT"])
                                rms_store_c(XTc[:, :, :], T, 0)

                            zTv = zT[:].rearrange("p c (a b) -> p c b a", b=64)
                            GDv = GD
                            for l2 in range(64):
                                b = l2 % 2
                                for cc in range(2):
                                    op("pe", lambda e: e.matmul(pY[b][:, cc * 256:(cc + 1) * 256], lhsT=zTv[:, cc, l2, :], rhs=abd[:, cc, :],
                                                                start=True, stop=True), reads=["zT", "abd"], writes=["pY%d" % b])
                                op("act", lambda e: e.activation(out=Yp[b][:], in_=pY[b][:], func=AF.Identity), writes=["pY%d" % b, "Yp%d" % b])
                                Ypv = Yp[b][:].rearrange("p (c q j) -> p q c j", c=2, q=2)
                                combos = [(0, 0, 0), (0, 1, 1), (1, 0, 1), (1, 1, 2)]
                                for (ri, pq, ti_) in combos:
                                    op("pe", lambda e: e.matmul(pG[b][:, ri, :].rearrange("p (c j) -> p c j", c=2), lhsT=t1[:, ti_, :], rhs=Ypv[:, pq, :, :],
                                                                start=(pq == 0), stop=(pq == 1)), reads=["t1", "Yp%d" % b], writes=["pG%d" % b])
                                gb = (l2 // 8) % 2
                                op("dve", lambda e: e.tensor_copy(out=Gs[gb][:, :, l2 % 8, :], in_=pG[b][:]), writes=["pG%d" % b, "Gs%d" % gb])
                                if l2 % 8 == 7:
                                    l0 = l2 - 7
                                    dma("sp", GDv[:, :, l0:l0 + 8, :], Gs[gb][:], reads=["Gs%d" % gb], writes=["GD"])
                            XTv = XTs[:].rearrange("p c (k2 k1) -> p c k1 k2", k1=128)
                            for kb in range(16):
                                b = kb % 2
                                dma("sp", Rb[b][:], GD[kb * 8:(kb + 1) * 8, :, :, :].rearrange("k r l c -> (r l) k c"), reads=["GD"], writes=["Rb%d" % b])
                                for cc in range(2):
                                    px, kpx = pX[cc], "pX%d" % cc
                                    for r in range(8):
                                        op("pe", lambda e: e.matmul(px[:, r, :], lhsT=Rb[b][:, r, cc * 128:(cc + 1) * 128], rhs=tw[:, kb * 8 + r, :],
                                                                    start=True, stop=True), reads=["Rb%d" % b, "tw"], writes=[kpx])
                                    op("act" if cc == 0 else "dve",
                                       (lambda e: e.activation(out=XTv[:, cc, kb * 8:(kb + 1) * 8, :], in_=px[:], func=AF.Identity)) if cc == 0 else
                                       (lambda e: e.tensor_copy(out=XTv[:, cc, kb * 8:(kb + 1) * 8, :], in_=px[:])),
                                       writes=[kpx, "XT"])
                            for ti in range(NT + 2):
                                if ti < NT:
                                    rmsc1(XTs[:, :, ti * TL:(ti + 1) * TL], ti % 2)
                                if 0 <= ti - 1 < NT:
                                    rmsc2((ti - 1) % 2)
                                if 0 <= ti - 2 < NT:
                                    t2 = ti - 2
                                    rmsc3(XTs[:, :, t2 * TL:(t2 + 1) * TL], t2 * TL, t2 % 2)
                            fw.barrier()
                if stop_after == "DFT" and l == 0:
                    return nc

                with ExitStack() as e3:
                    wg = sb(e3, "wg", [128, 2, 2, 4, 128], BF16)
                    cw = sb(e3, "cw", [128, 4, 4])
                    cb = sb(e3, "cb", [128, 4])
                    gb_ = sb(e3, "gbias", [128, 2, 2, 4])
                    lam = sb(e3, "lam", [128, 2, 4])
                    hnsp = sb(e3, "hnsp", [128, 2, 4])
                    nsp = sb(e3, "nsp", [128, 2, 4])
                    op("pool", lambda e: e.memset(wg[:], 0.0), writes=["wg"])
                    for ax, wd in enumerate((wa_d, wx_d)):
                        for h2 in range(2):
                            for d_ in range(2):
                                src = wd[l, d_].rearrange("(c h) i e -> h i c e", h=2)[h2]
                                dma("pool", wg[64 * h2:64 * h2 + 64, d_, ax, :, 64 * h2:64 * h2 + 64], src, writes=["wg"])
                    for k in range(4):
                        dma("sp", cw[:, :, k], convw_d[l, k, :].rearrange("(c p) -> p c", p=128), writes=["cw"])
                    dma("sp", cb[:], convb_d[l].rearrange("(c p) -> p c", p=128), writes=["cb"])
                    for d_ in range(2):
                        dma("sp", gb_[:, 0, d_, :], ba_d[l, d_, :].rearrange("(c p) -> p c", p=128), writes=["gbias"])
                        dma("sp", gb_[:, 1, d_, :], bx_d[l, d_, :].rearrange("(c p) -> p c", p=128), writes=["gbias"])
                        dma("sp", lam[:, d_, :], lam_d[l, d_, :].rearrange("(c p) -> p c", p=128), writes=["lam"])
                    op("dve", lambda e: e.tensor_scalar(out=gb_[:], in0=gb_[:], scalar1=0.5, scalar2=None, op0=ALU.mult), writes=["gbias"])
                    op("act", lambda e: e.activation(out=lam[:], in_=lam[:], func=AF.Exp, scale=-1.0), writes=["lam"])
                    op("act", lambda e: e.activation(out=lam[:], in_=lam[:], func=AF.Ln, bias=1.0), writes=["lam"])
                    op("dve", lambda e: e.tensor_scalar(out=hnsp[:], in0=lam[:], scalar1=-4.0, scalar2=None, op0=ALU.mult), reads=["lam"], writes=["hnsp"])
                    op("dve", lambda e: e.tensor_scalar(out=nsp[:], in0=lam[:], scalar1=-8.0, scalar2=None, op0=ALU.mult), reads=["lam"], writes=["nsp"])

                    xaH = [sb(e3, "xaH%d" % i, [128, 4, TL + 3]) for i in range(2)]
                    xc = [sb(e3, "xc%d" % i, [128, 4, TL]) for i in range(2)]
                    xcb = [sb(e3, "xcb%d" % i, [128, 4, TL], BF16) for i in range(2)]
                    hS = [sb(e3, "hS%d" % i, [128, 4, TL]) for i in range(2)]
                    hfL = [sb(e3, "hfL%d" % i, [128, 4, TL]) for i in range(2)]
                    ggL = [sb(e3, "ggL%d" % i, [128, 4, TL]) for i in range(2)]
                    tr = sb(e3, "tr", [128, 4, TL])
                    tiS = [sb(e3, "tiS%d" % i, [128, 4, TL]) for i in range(2)]
                    aS = [sb(e3, "aS%d" % i, [128, 4, TL]) for i in range(2)]
                    mS = [sb(e3, "mS%d" % i, [128, 4, TL]) for i in range(2)]
                    uS = sb(e3, "uS", [128, 4, TL])
                    ya = [sb(e3, "ya%d" % i, [128, 4, TL]) for i in range(3)]
                    sq = sb(e3, "sqa", [128, 4, TL])
                    rr = [sb(e3, "rra%d" % i, [128, TL]) for i in range(2)]
                    rrs = [sb(e3, "rrsa%d" % i, [128, TL]) for i in range(2)]
                    rrt = [sb(e3, "rrta%d" % i, [128, TL]) for i in range(2)]
                    yAT = [sb(e3, "yAT%d" % i, [128, 4, TL], BF16) for i in range(2)]
                    pg = [ps(e3, "pg%d" % i, [128, 2, TL]) for i in range(4)]
                    pSs = [ps(e3, "pSa%d" % i, [128, 512]) for i in range(2)]

                    def xck(b):
                        return ["xc%d_%d" % (b, c) for c in range(4)]

                    def gates(d, b):
                        for c in range(4):
                            for ax in range(2):
                                op("pe", lambda e: e.matmul(pg[c][:, ax, :], lhsT=wg[:, d, ax, c, :], rhs=xcb[b][:, c, :], start=True, stop=True),
                                   reads=["wg", "xcb%d" % b], writes=["pg%d" % c])
                        for c in range(4):
                            op("act", lambda e: e.activation(out=tr[:, c, :], in_=pg[c][:, 0, :], func=AF.Tanh, scale=0.5, bias=gb_[:, 0, d, c:c + 1]),
                               reads=["gbias"], writes=["pg%d" % c, "tr"])
                            op("act", lambda e: e.activation(out=tiS[b][:, c, :], in_=pg[c][:, 1, :], func=AF.Tanh, scale=0.5, bias=gb_[:, 1, d, c:c + 1]),
                               reads=["gbias"], writes=["pg%d" % c, "tiS%d" % b])
                        for c in range(4):
                            op("act", lambda e: e.activation(out=aS[b][:, c, :], in_=tr[:, c, :], func=AF.Exp, scale=hnsp[:, d, c:c + 1], bias=hnsp[:, d, c:c + 1]),
                               reads=["tr", "hnsp"], writes=["aS%d" % b])
                            if d == 1:
                                op("act", lambda e: e.activation(out=mS[b][:, c, :], in_=tr[:, c, :], func=AF.Exp, scale=nsp[:, d, c:c + 1], bias=nsp[:, d, c:c + 1]),
                                   reads=["tr", "nsp"], writes=["mS%d" % b])
                        if d == 0:
                            op("pool", lambda e: e.tensor_tensor(out=mS[b][:], in0=aS[b][:], in1=aS[b][:], op=ALU.mult), reads=["aS%d" % b], writes=["mS%d" % b])
                        op("act", lambda e: e.activation(out=mS[b][:], in_=mS[b][:], func=AF.Sqrt, scale=-0.25, bias=0.25), writes=["mS%d" % b])

                    def make_u(b):
                        op("dve", lambda e: e.scalar_tensor_tensor(out=uS[:], in0=tiS[b][:], scalar=1.0, in1=xc[b][:], op0=ALU.add, op1=ALU.mult),
                           reads=["tiS%d" % b] + xck(b), writes=["uS"])
                        op("dve", lambda e: e.tensor_tensor(out=uS[:], in0=uS[:], in1=mS[b][:], op=ALU.mult), reads=["mS%d" % b], writes=["uS"])

                    scope_ = nc.named_scope("M2a_l%d" % l)
                    scope_.__enter__()

                    def load_xa(ti, b):
                        col0, is_ctx = TILES[ti]
                        first = is_ctx or col0 == 0
                        lastt = is_ctx or col0 == T - TL
                        lo = 0 if first else 2
                        hi = 0 if lastt else 1
                        if first:
                            op("pool", lambda e: e.memset(xaH[b][:, :, 0:2], 0.0), writes=["xaH%d" % b])
                        if lastt:
                            op("pool", lambda e: e.memset(xaH[b][:, :, TL + 2:TL + 3], 0.0), writes=["xaH%d" % b])
                        if is_ctx:
                            dma("sp", xaH[b][:, :, 2:TL + 2], XACv[:, :, :], writes=["xaH%d" % b])
                        else:
                            g0, g1 = col0 - lo, col0 + TL + hi
                            d0 = 2 - lo
                            for r_ in range(2):
                                a0, a1 = max(g0, r_ * TLOC), min(g1, (r_ + 1) * TLOC)
                                if a1 > a0:
                                    dma("sp", xaH[b][:, :, d0 + a0 - g0:d0 + a1 - g0], XAFv[r_][:, :, a0 - r_ * TLOC:a1 - r_ * TLOC],
                                        reads=["XAF"], writes=["xaH%d" % b])

                    def f0(ti):
                        col0, is_ctx = TILES[ti]
                        b = ti % 2
                        for c in range(4):
                            op("dve", lambda e: e.tensor_scalar(out=xc[b][:, c, :], in0=xaH[b][:, c, 0:TL], scalar1=cw[:, c, 0:1], scalar2=cb[:, c:c + 1],
                                                                op0=ALU.mult, op1=ALU.add), reads=["xaH%d" % b, "cw", "cb"], writes=["xc%d_%d" % (b, c)])
                        for k in range(1, 4):
                            for c in range(4):
                                op("dve", lambda e: e.scalar_tensor_tensor(out=xc[b][:, c, :], in0=xaH[b][:, c, k:k + TL], scalar=cw[:, c, k:k + 1],
                                                                           in1=xc[b][:, c, :], op0=ALU.mult, op1=ALU.add),
                                   reads=["xaH%d" % b, "cw"], writes=["xc%d_%d" % (b, c)])
                        op("pool", lambda e: e.tensor_copy(out=xcb[b][:], in_=xc[b][:]), reads=xck(b), writes=["xcb%d" % b])
                        dma("sp", XCv[:, :, col0:col0 + TL], xc[b][:], reads=xck(b), writes=["XC"])
                        gates(0, b)

                    def f1(ti):
                        col0, is_ctx = TILES[ti]
                        b = ti % 2
                        pb_ = (ti - 1) % 2
                        make_u(b)
                        for c in range(4):
                            init = 0.0 if ti == 0 else hS[pb_][:, c, TL - 1:TL]
                            op("dve", lambda e: e.tensor_tensor_scan(out=hS[b][:, c, :], data0=aS[b][:, c, :], data1=uS[:, c, :], initial=init,
                                                                     op0=ALU.mult, op1=ALU.add),
                               reads=["aS%d" % b, "uS"] + ([] if ti == 0 else ["hS%d" % pb_]), writes=["hS%d" % b])
                        if not (last and is_ctx):
                            dma("sp", HFv[:, :, col0:col0 + TL], hS[b][:], reads=["hS%d" % b], writes=["HF"])

                    nT = len(TILES)
                    load_xa(0, 0)
                    load_xa(1, 1)
                    f0(0)
                    for ti in range(nT):
                        if ti + 2 < nT:
                            load_xa(ti + 2, ti % 2)
                        if ti + 1 < nT:
                            f0(ti + 1)
                        f1(ti)
                    fw.barrier()
                    scope_.__exit__(None, None, None)
                    if stop_after == "M2a" and l == 0:
                        return nc
                    scope_ = nc.named_scope("M2b_l%d" % l)
                    scope_.__enter__()

                    order = [0] + list(range(NT, 0, -1))
                    nO = len(order)

                    def load_x(oi):
                        col0, is_ctx = TILES[order[oi]]
                        b = oi % 2
                        dma("sp", xc[b][:], XCv[:, :, col0:col0 + TL], reads=["XC"], writes=xck(b))

                    def load_hg(oi):
                        col0, is_ctx = TILES[order[oi]]
                        b = oi % 2
                        if not (last and is_ctx):
                            dma("sp", hfL[b][:], HFv[:, :, col0:col0 + TL], reads=["HF"], writes=["hfL%d" % b])
                            ggsrc = GGCv[:, :, :] if is_ctx else GGFv[col0 // TLOC][:, :, col0 % TLOC:col0 % TLOC + TL]
                            dma("sp", ggL[b][:], ggsrc, reads=["GGF"], writes=["ggL%d" % b])

                    def b0(oi):
                        b = oi % 2
                        op("act", lambda e: e.activation(out=xcb[b][:], in_=xc[b][:], func=AF.Identity), reads=xck(b), writes=["xcb%d" % b])
                        gates(1, b)

                    def b1(oi):
                        col0, is_ctx = TILES[order[oi]]
                        b = oi % 2
                        pb_ = (oi - 1) % 2
                        y3 = oi % 3
                        make_u(b)
                        for c in range(4):
                            init = 0.0 if oi == 0 else hS[pb_][:, c, 0:1]
                            op("dve", lambda e: e.tensor_tensor_scan(out=hS[b][:, c, ::-1], data0=aS[b][:, c, ::-1], data1=uS[:, c, ::-1], initial=init,
                                                                     op0=ALU.mult, op1=ALU.add),
                               reads=["aS%d" % b, "uS"] + ([] if oi == 0 else ["hS%d" % pb_]), writes=["hS%d" % b])
                        if last and is_ctx:
                            return
                        op("pool", lambda e: e.tensor_tensor(out=ya[y3][:], in0=hS[b][:], in1=hfL[b][:], op=ALU.add), reads=["hS%d" % b, "hfL%d" % b], writes=["ya%d" % y3])
                        op("pool", lambda e: e.tensor_tensor(out=ya[y3][:], in0=ya[y3][:], in1=ggL[b][:], op=ALU.mult), reads=["ggL%d" % b], writes=["ya%d" % y3])
                        op("act", lambda e: e.activation(out=sq[:], in_=ya[y3][:], func=AF.Square), reads=["ya%d" % y3], writes=["sqa"])
                        for c in range(4):
                            op("pe", lambda e: e.matmul(pSs[b][:, 0:TL], lhsT=ones_f[:], rhs=sq[:, c, :], start=(c == 0), stop=(c == 3)),
                               reads=["ones_f", "sqa"], writes=["pSa%d" % b])

                    def b2a(oi):
                        col0, is_ctx = TILES[order[oi]]
                        b = oi % 2
                        if last and is_ctx:
                            return
                        op("dve", lambda e: e.tensor_scalar(out=rr[b][:], in0=pSs[b][:, 0:TL], scalar1=1.0 / DA, scalar2=EPS, op0=ALU.mult, op1=ALU.add),
                           writes=["pSa%d" % b, "rra%d" % b])
                        rsqrt(rrs[b][:], rr[b][:], rrt[b][:], "rrsa%d" % b, "rra%d" % b, "rrta%d" % b, iters=2)

                    def b2b(oi):
                        col0, is_ctx = TILES[order[oi]]
                        b = oi % 2
                        y3 = oi % 3
                        if last and is_ctx:
                            return
                        for c in range(4):
                            op("dve", lambda e: e.scalar_tensor_tensor(out=yAT[b][:, c, :], in0=ya[y3][:, c, :], scalar=gmixc[:, c:c + 1], in1=rrs[b][:],
                                                                       op0=ALU.mult, op1=ALU.mult), reads=["ya%d" % y3, "gmixc", "rrsa%d" % b], writes=["yAT%d_%d" % (b, c)])
                        dma("sp", YTv[:, 0:4, col0:col0 + TL], yAT[b][:], reads=["yAT%d_%d" % (b, c) for c in range(4)], writes=["YT"])

                    load_x(0)
                    load_hg(0)
                    load_x(1)
                    b0(0)
                    for oi in range(nO + 2):
                        if oi + 1 < nO:
                            load_hg(oi + 1)
                            b0(oi + 1)
                        if oi < nO:
                            b1(oi)
                        if oi + 2 < nO:
                            load_x(oi + 2)
                        if 0 <= oi - 1 < nO:
                            b2a(oi - 1)
                        if 0 <= oi - 2 < nO:
                            b2b(oi - 2)
                    fw.barrier()
                    scope_.__exit__(None, None, None)
                if stop_after == "M2b" and l == 0:
                    return nc

                m3_tiles = LTILES[1:] if last else LTILES
                wdn = sb(el, "wdn", [128, NF, D], BF16)
                with ExitStack() as e4:
                    e4.enter_context(nc.named_scope("M3a_l%d" % l))
                    wout = sb(e4, "wout", [128, 8, D], BF16)
                    for k in range(8):
                        dma("pool", wout[:, k, :], wout_d[l, k * 128:(k + 1) * 128, :], writes=["wout"])
                    for f in range(NF):
                        dma("pool", wdn[:, f, :], wdn_d[l, f * 128:(f + 1) * 128, :], writes=["wdn"])
                    gate = sb(e4, "gate1", [128, 2, D])
                    lng = sb(e4, "ln1g", [128, D])
                    lnb = sb(e4, "ln1b", [128, D])
                    for s in range(2):
                        dma("sp", gate[:, s, :], MODS[s:s + 1, 2 * D:3 * D].partition_broadcast(128), reads=["MODS"], writes=["gate1"])
                    dma("sp", lng[:], ln1g_d[l:l + 1, :].partition_broadcast(128), writes=["ln1g"])
                    dma("sp", lnb[:], ln1b_d[l:l + 1, :].partition_broadcast(128), writes=["ln1b"])
                    op("dve", lambda e: e.tensor_scalar(out=gate[:], in0=gate[:], scalar1=1.0 / ALPHA, scalar2=None, op0=ALU.mult), writes=["gate1"])
                    rmask = sb(e4, "rmask", [128, 2])
                    dma("sp", rmask[:], rmask_d[:, :], writes=["rmask"])
                    woutL = sb(e4, "woutL", [128, 8, D], BF16)
                    wsel = [sb(e4, "wsel%d" % r_, [128, 6, D], BF16) for r_ in range(2)]
                    woutC = sb(e4, "woutC", [128, 8, D], BF16) if not last else None
                    AC = (0, 1, 2, 3, 6, 7)
                    for k in range(8):
                        op("dve", lambda e: e.tensor_tensor(out=woutL[:, k, :], in0=wout[:, k, :], in1=gate[:, 0, :], op=ALU.mult),
                           reads=["wout", "gate1"], writes=["woutL"])
                        if not last:
                            op("pool", lambda e: e.tensor_tensor(out=woutC[:, k, :], in0=wout[:, k, :], in1=gate[:, 1, :], op=ALU.mult),
                               reads=["wout", "gate1"], writes=["woutC"])
                    for r_ in range(2):
                        for j, k in enumerate(AC):
                            if r_ == 0:
                                op("dve", lambda e: e.tensor_scalar(out=wsel[r_][:, j, :], in0=woutL[:, k, :], scalar1=rmask[:, r_:r_ + 1], scalar2=None, op0=ALU.mult),
                                   reads=["woutL", "rmask"], writes=["wsel%d_%d" % (r_, j)])
                            else:
                                op("act", lambda e: e.activation(out=wsel[r_][:, j, :], in_=woutL[:, k, :], func=AF.Identity, scale=rmask[:, r_:r_ + 1]),
                                   reads=["woutL", "rmask"], writes=["wsel%d_%d" % (r_, j)])
                    yTb = [sb(e4, "yTb%d" % i, [128, 2, TL], BF16) for i in range(3)]
                    yC = [[sb(e4, "yC%d_%d" % (i, r_), [128, 6, TL], BF16) for r_ in range(2)] for i in range(3)]
                    xt = [sb(e4, "xt%d" % i, [128, 2, D]) for i in range(2)]
                    pt = [sb(e4, "pt%d" % i, [128, 2, D]) for i in range(2)] if l == 0 else [None, None]
                    rt = [sb(e4, "rt%d" % i, [128, 2, D]) for i in range(2)]
                    lns3 = [ln_scr(e4, "l3a"), ln_scr(e4, "l3b")]
                    po = [ps(e4, "po%d" % i, [128, 512]) for i in range(8)]
                    n3 = len(m3_tiles)

                    def loadY(ti):
                        col0, is_ctx = m3_tiles[ti]
                        y = ti % 3
                        dma("sp", yTb[y][:], YBLv[:, :, col0:col0 + TL], reads=["YBL"], writes=["yTb%d" % y])
                        if is_ctx:
                            dma("sp", yC[y][0][:, 0:4, :], YTv[:, 0:4, T:T + TL], reads=["YT"], writes=["yC%d_0" % y])
                            dma("sp", yC[y][0][:, 4:6, :], YTv[:, 6:8, T:T + TL], reads=["YT"], writes=["yC%d_0" % y])
                        else:
                            for r_ in range(2):
                                g0 = r_ * TLOC + col0
                                dma("sp", yC[y][r_][:, 0:4, :], YTv[:, 0:4, g0:g0 + TL], reads=["YT"], writes=["yC%d_%d" % (y, r_)])
                                dma("sp", yC[y][r_][:, 4:6, :], YTv[:, 6:8, g0:g0 + TL], reads=["YT"], writes=["yC%d_%d" % (y, r_)])

                    def loadX(ti):
                        col0, is_ctx = m3_tiles[ti]
                        b = ti % 2
                        load_resid(l, col0, is_ctx, xt[b], "xt%d" % b, pt[b], "pt%d" % b)

                    def mm3(ti):
                        col0, is_ctx = m3_tiles[ti]
                        b = ti % 2
                        y = ti % 3
                        for s in range(2):
                            for hf_ in range(2):
                                p_, kp_ = po[4 * b + 2 * s + hf_], "po%d" % (4 * b + 2 * s + hf_)
                                cs = slice(hf_ * 512, (hf_ + 1) * 512)
                                ts_ = slice(s * 128, (s + 1) * 128)
                                steps = []
                                wB = woutC if is_ctx else woutL
                                for j in range(2):
                                    steps.append((yTb[y][:, j, ts_], wB[:, 4 + j, cs], ["yTb%d" % y, "woutC" if is_ctx else "woutL"]))
                                if is_ctx:
                                    for j, k in enumerate(AC):
                                        steps.append((yC[y][0][:, j, ts_], woutC[:, k, cs], ["yC%d_0" % y, "woutC"]))
                                else:
                                    for r_ in range(2):
                                        for j in range(6):
                                            steps.append((yC[y][r_][:, j, ts_], wsel[r_][:, j, cs], ["yC%d_%d" % (y, r_), "wsel%d_%d" % (r_, j)]))
                                for n_, (lh, rh, rd) in enumerate(steps):
                                    op("pe", lambda e: e.matmul(p_[:], lhsT=lh, rhs=rh, start=(n_ == 0), stop=(n_ == len(steps) - 1)), reads=rd, writes=[kp_])

                    def res3(ti):
                        b = ti % 2
                        for s in range(2):
                            for hf_ in range(2):
                                p_, kp_ = po[4 * b + 2 * s + hf_], "po%d" % (4 * b + 2 * s + hf_)
                                op("dve", lambda e: e.tensor_tensor(out=rt[b][:, s, hf_ * 512:(hf_ + 1) * 512], in0=p_[:],
                                                                    in1=xt[b][:, s, hf_ * 512:(hf_ + 1) * 512], op=ALU.add),
                                   reads=["xt%d" % b], writes=[kp_, "rt%d_%d" % (b, s)])

                    def stats3(ti):
                        b = ti % 2
                        ln_stats(rt[b], ["rt%d_0" % b, "rt%d_1" % b], lns3[b], EPS_POST, iters=2)

                    def norm3(ti):
                        col0, is_ctx = m3_tiles[ti]
                        b = ti % 2
                        mv, rs, kk = lns3[b]["mv"], lns3[b]["rs"], lns3[b]["k"]
                        for s in range(2):
                            kr = "rt%d_%d" % (b, s)
                            op("dve", lambda e: e.tensor_scalar(out=rt[b][:, s, :], in0=rt[b][:, s, :], scalar1=mv[:, s, 0:1],
                                                                scalar2=rs[:, s:s + 1], op0=ALU.subtract, op1=ALU.mult),
                               reads=[kk + "mv", kk + "rs"], writes=[kr])
                            op("dve", lambda e: e.tensor_tensor(out=rt[b][:, s, :], in0=rt[b][:, s, :], in1=lng[:], op=ALU.mult), reads=["ln1g"], writes=[kr])
                            op("pool", lambda e: e.tensor_tensor(out=rt[b][:, s, :], in0=rt[b][:, s, :], in1=lnb[:], op=ALU.add), reads=["ln1b"], writes=[kr])

                    def store3(ti):
                        col0, is_ctx = m3_tiles[ti]
                        b = ti % 2
                        dma("sp", XB[col0:col0 + TL, :].rearrange("(s p) d -> p s d", p=128), rt[b][:], reads=["rt%d_0" % b, "rt%d_1" % b], writes=["XB"])

                    for i_ in range(min(3, n3)):
                        loadY(i_)
                    for i_ in range(min(2, n3)):
                        loadX(i_)
                    mm3(0)
                    res3(0)
                    stats3(0)
                    for ti in range(n3):
                        if ti + 3 < n3:
                            loadY(ti + 3)
                        if ti + 2 < n3:
                            loadX(ti + 2)
                        if ti >= 1:
                            store3(ti - 1)
                        if ti + 1 < n3:
                            mm3(ti + 1)
                            res3(ti + 1)
                            stats3(ti + 1)
                        norm3(ti)
                    store3(n3 - 1)
                    fw.barrier()
                if stop_after == "M3a" and l == 0:
                    return nc

                with ExitStack() as e5:
                    e5.enter_context(nc.named_scope("M3b_l%d" % l))
                    wup = sb(e5, "wup", [128, 8, 2 * DFF], BF16)
                    for k in range(8):
                        for c0 in range(0, 2 * DFF, 2048):
                            c1 = min(c0 + 2048, 2 * DFF)
                            dma("pool", wup[:, k, c0:c1], wup_d[l, k * 128:(k + 1) * 128, c0:c1], writes=["wup"])
                    gate = sb(e5, "gate2", [128, 2, D])
                    lng = sb(e5, "ln2g", [128, D])
                    lnb = sb(e5, "ln2b", [128, D])
                    for s in range(2):
                        dma("sp", gate[:, s, :], MODS[s:s + 1, 5 * D:6 * D].partition_broadcast(128), reads=["MODS"], writes=["gate2"])
                    dma("sp", lng[:], ln2g_d[l:l + 1, :].partition_broadcast(128), writes=["ln2g"])
                    dma("sp", lnb[:], ln2b_d[l:l + 1, :].partition_broadcast(128), writes=["ln2b"])
                    op("dve", lambda e: e.tensor_scalar(out=gate[:], in0=gate[:], scalar1=1.0 / ALPHA, scalar2=None, op0=ALU.mult), writes=["gate2"])
                    xt = [sb(e5, "xu%d" % i, [128, 2, D]) for i in range(2)]
                    xh = sb(e5, "xhu", [128, 2, D])
                    h2T = [sb(e5, "h2T%d" % i, [128, 8, TL], BF16) for i in range(2)]
                    actT = sb(e5, "actT", [128, NF, TL], BF16)
                    sg = [sb(e5, "sg%d" % i, [128, TL]) for i in range(2)]
                    lns = ln_scr(e5, "l5")
                    lns2 = ln_scr(e5, "l6")
                    tp0 = ps(e5, "tq0", [128, 2, TL]); tp1 = ps(e5, "tq1", [128, 2, TL])
                    tps = [(tp0, "tq0"), (tp1, "tq1")]
                    pu = [ps(e5, "pu%d" % i, [128, 2, TL]) for i in range(2)]
                    pd = [ps(e5, "pd%d" % i, [128, 512]) for i in range(4)]

                    def load5(ti, b):
                        col0, is_ctx = m3_tiles[ti]
                        dma("sp", xt[b][:], XB[col0:col0 + TL, :].rearrange("(s p) d -> p s d", p=128), reads=["XB"], writes=["xu%d" % b])

                    su = [sb(e5, "su%d" % i, [128, TL]) for i in range(2)]
                    n5 = len(m3_tiles)
                    folded = [False]

                    def fold_gate():
                        for f in range(NF):
                            op("dve" if f % 2 == 0 else "pool",
                               lambda e: e.tensor_tensor(out=wdn[:, f, :], in0=wdn[:, f, :], in1=gate[:, 0, :], op=ALU.mult), reads=["gate2"], writes=["wdn"])
                        folded[0] = True

                    def stA(ti):
                        b = ti % 2
                        ln_tile(xt[b], "xu%d" % b, xh, "xhu", lns, EPS, eng="dve", iters=2)

                    def stT(ti):
                        col0, is_ctx = m3_tiles[ti]
                        b = ti % 2
                        transpose_mod(xh, "xhu", h2T[b], "h2T%d" % b, tps, modc, 1 if is_ctx else 0, 4, 3)

                    def stU(ti):
                        b = ti % 2
                        for f in range(NF):
                            p_, kp_ = pu[f % 2], "pu%d" % (f % 2)
                            for j in range(2):
                                c0 = j * DFF + f * 128
                                for k in range(8):
                                    op("pe", lambda e: e.matmul(p_[:, j, :], lhsT=wup[:, k, c0:c0 + 128], rhs=h2T[b][:, k, :], start=(k == 0), stop=(k == 7)),
                                       reads=["wup", "h2T%d" % b], writes=[kp_])
                            op("act", lambda e: e.activation(out=sg[f % 2][:], in_=p_[:, 0, :], func=AF.Silu), writes=[kp_, "sg%d" % (f % 2)])
                            op("act", lambda e: e.activation(out=su[f % 2][:], in_=p_[:, 1, :], func=AF.Identity), writes=[kp_, "su%d" % (f % 2)])
                            op("pool", lambda e: e.tensor_tensor(out=actT[:, f, :], in0=su[f % 2][:], in1=sg[f % 2][:], op=ALU.mult),
                               reads=["sg%d" % (f % 2), "su%d" % (f % 2)], writes=["actT"])

                    def stD(ti):
                        for s in range(2):
                            for hf_ in range(2):
                                p_, kp_ = pd[2 * s + hf_], "pd%d" % (2 * s + hf_)
                                for f in range(NF):
                                    op("pe", lambda e: e.matmul(p_[:], lhsT=actT[:, f, s * 128:(s + 1) * 128], rhs=wdn[:, f, hf_ * 512:(hf_ + 1) * 512],
                                                                start=(f == 0), stop=(f == NF - 1)), reads=["wdn", "actT"], writes=[kp_])

                    def stE(ti):
                        col0, is_ctx = m3_tiles[ti]
                        b = ti % 2
                        kx = "xu%d" % b
                        for s in range(2):
                            for hf_ in range(2):
                                p_, kp_ = pd[2 * s + hf_], "pd%d" % (2 * s + hf_)
                                cs = slice(hf_ * 512, (hf_ + 1) * 512)
                                if folded[0]:
                                    op("dve", lambda e: e.tensor_tensor(out=xt[b][:, s, cs], in0=p_[:], in1=xt[b][:, s, cs], op=ALU.add), writes=[kp_, kx])
                                else:
                                    op("dve", lambda e: e.tensor_tensor(out=xh[:, s, cs], in0=p_[:], in1=gate[:, 1 if is_ctx else 0, cs], op=ALU.mult),
                                       reads=["gate2"], writes=[kp_, "xhu"])
                        if not folded[0]:
                            op("pool", lambda e: e.tensor_tensor(out=xt[b][:], in0=xt[b][:], in1=xh[:], op=ALU.add), reads=["xhu"], writes=[kx])
                        ln_tile(xt[b], kx, xt[b], kx, lns2, EPS_POST, eng="dve", iters=3)
                        for s in range(2):
                            op("dve", lambda e: e.tensor_tensor(out=xt[b][:, s, :], in0=xt[b][:, s, :], in1=lng[:], op=ALU.mult), reads=["ln2g"], writes=[kx])
                            op("dve", lambda e: e.tensor_tensor(out=xt[b][:, s, :], in0=xt[b][:, s, :], in1=lnb[:], op=ALU.add), reads=["ln2b"], writes=[kx])
                        dst = out_d[col0:col0 + TL, :] if last else XB[col0:col0 + TL, :]
                        dma("sp", dst.rearrange("(s p) d -> p s d", p=128), xt[b][:], reads=[kx], writes=["XB"])

                    load5(0, 0)
                    if n5 > 1:
                        load5(1, 1)
                    if not m3_tiles[0][1]:
                        fold_gate()
                    stA(0)
                    stT(0)
                    for ti in range(n5):
                        stU(ti)
                        if ti + 1 < n5:
                            stA(ti + 1)
                        stD(ti)
                        if ti + 1 < n5:
                            stT(ti + 1)
                        stE(ti)
                        if m3_tiles[ti][1]:
                            fold_gate()
                        if ti + 2 < n5:
                            load5(ti + 2, ti % 2)
                    fw.barrier()
                if stop_after == "M3b" and l == 0:
                    return nc
        fw.wait_all("sp")
    return nc


def host_consts():
    c = {}
    c["ident"] = np.eye(128, dtype=np.float32)
    l1 = np.arange(128)[:, None].astype(np.float64)
    k1 = np.arange(128)[None, :].astype(np.float64)
    a = 2 * np.pi * l1 * k1 / 128.0
    c["t1"] = np.stack([np.cos(a), -np.sin(a), -np.cos(a)], 1).astype(np.float32).astype(ml_dtypes.bfloat16)
    l2 = np.arange(64)[:, None, None].astype(np.float64)
    kk1 = np.arange(128)[None, :, None].astype(np.float64)
    kk2 = np.arange(64)[None, None, :].astype(np.float64)
    ang = 2 * np.pi * ((kk1 + 128 * kk2) * l2 % T) / T
    c["tw"] = np.concatenate([np.cos(ang), np.sin(ang)], 0).reshape(128, T).astype(np.float32).astype(ml_dtypes.bfloat16)
    p = np.arange(128)[:, None, None].astype(np.float64)
    t = np.arange(2)[None, :, None].astype(np.float64)
    k = np.arange(256)[None, None, :].astype(np.float64)
    a2 = 2 * np.pi * (((128 * t + p) * k) % 256) / 256.0
    c["c256"] = np.stack([np.cos(a2), -np.sin(a2)], 2).astype(np.float32).astype(ml_dtypes.bfloat16)
    j = np.arange(64)[:, None].astype(np.float64)
    ch = np.arange(64)[None, :].astype(np.float64)
    a3 = 2 * np.pi * j * ch / 64.0
    cs = np.zeros((64, 2, 2, 128), np.float64)
    for gl in range(2):
        cs[:, 0, gl, gl * 64:(gl + 1) * 64] = np.cos(a3)
        cs[:, 1, gl, gl * 64:(gl + 1) * 64] = np.sin(a3)
    c["cspad"] = cs.astype(np.float32)
    quarter = D // 4
    freqs = 10000.0 ** (-np.arange(quarter, dtype=np.float32) / np.float32(quarter))
    r = np.repeat(np.arange(T // 64, dtype=np.float32), 64)
    col = np.tile(np.arange(64, dtype=np.float32), T // 64)

    def enc(pv):
        an = pv[:, None].astype(np.float32) * freqs[None, :].astype(np.float32)
        return np.concatenate([np.sin(an), np.cos(an)], -1)

    c["pos"] = np.concatenate([enc(r), enc(col)], -1).astype(np.float32)
    return c


_WNAMES = ["w_mod", "b_mod", "w_in", "conv_w", "conv_b", "lru_wa", "lru_ba", "lru_wx", "lru_bx", "lru_lam", "sg_ws", "sg_b",
           "fourier_w", "g_mix", "w_out", "ln1_g", "ln1_b", "w_up", "w_down", "ln2_g", "ln2_b"]


def make_in_maps(inputs, n_cores=N_CORES):
    consts = host_consts()
    pos = consts.pop("pos")
    shared = {n: np.ascontiguousarray(np.asarray(inputs[n], dtype=np.float32)) for n in _WNAMES}
    shared.update(consts)
    x = np.asarray(inputs["x"], dtype=np.float32)
    c = np.asarray(inputs["c"], dtype=np.float32)
    ctx = np.asarray(inputs["ctx"], dtype=np.float32)
    c_ctx = np.asarray(inputs["c_ctx"], dtype=np.float32)
    maps = []
    for core in range(n_cores):
        b, h = core // 2, core % 2
        m = dict(shared)
        m["x"] = np.ascontiguousarray(x[b, h * TLOC:(h + 1) * TLOC])
        m["pos"] = np.ascontiguousarray(pos[h * TLOC:(h + 1) * TLOC])
        m["ctx"] = np.ascontiguousarray(ctx[b])
        m["cc"] = np.ascontiguousarray(np.stack([c[b], c_ctx], 0))
        rm = np.zeros((128, 2), np.float32)
        rm[:, h] = 1.0
        m["rmask"] = rm
        maps.append(m)
    return maps


def kernel(**inputs):
    nc = build_program()
    maps = make_in_maps(inputs)
    res = run_bass_kernel_spmd(nc, maps, core_ids=list(range(N_CORES)))
    outs = [np.asarray(r["out"], dtype=np.float32) for r in res.results]
    return np.stack([np.concatenate([outs[2 * b], outs[2 * b + 1]], 0) for b in range(N_CORES // 2)], 0)
```

```python
import math
from contextlib import ExitStack

import numpy as np
import ml_dtypes
import concourse.bass as bass
import concourse.mybir as mybir
from concourse.bass_utils import run_bass_kernel_spmd

F32 = mybir.dt.float32
BF16 = mybir.dt.bfloat16
I32 = mybir.dt.int32
ALU = mybir.AluOpType
AF = mybir.ActivationFunctionType

D = 1024
T = 8192
TC = 256
TT = T + TC
TL = 256
NT = T // TL
DEPTH = 2
DA, DB, DC = 512, 256, 256
DIN = 1792
DFF = 2816
NF = DFF // 128
EPS = 1e-6
ALPHA = (2 * DEPTH) ** 0.25
EPS_POST = EPS / (ALPHA * ALPHA)
N_CORES = 8
TLOC = T // 2
NTL = TLOC // TL

SAME_ENG_SYNC = True
N_DMA_SEMS = 40


class FW:
    def __init__(self, nc, es):
        self.nc = nc
        self.es = es
        self.engs = {"pe": nc.tensor, "act": nc.scalar, "dve": nc.vector, "pool": nc.gpsimd, "sp": nc.sync}
        self.sems = {}
        self.cnt = {}
        for e in self.engs:
            self.sems[e] = es.enter_context(nc.semaphore("s_" + e))
            self.cnt[e] = 0
        self.dsem = {}
        for q in ("sp", "pool"):
            lst = []
            for i in range(N_DMA_SEMS):
                key = "d_%s_%d" % (q, i)
                self.sems[key] = es.enter_context(nc.semaphore(key))
                self.cnt[key] = 0
                lst.append(key)
            self.dsem[q] = [lst, 0]
        self.seen = {e: {} for e in self.engs}
        self.lastw = {}
        self.readers = {}
        self.ninst = 0

    def _wait(self, eng, ev):
        sk, v, prod = ev
        if prod == "pe" and eng == "pe":
            return
        if prod == eng and not SAME_ENG_SYNC:
            return
        if self.seen[eng].get(sk, 0) >= v:
            return
        self.engs[eng].wait_ge(self.sems[sk], v)
        self.seen[eng][sk] = v

    def _deps(self, eng, reads, writes):
        for k in reads:
            ev = self.lastw.get(k)
            if ev is not None:
                self._wait(eng, ev)
        for k in writes:
            ev = self.lastw.get(k)
            if ev is not None:
                self._wait(eng, ev)
            for ev in list(self.readers.get(k, {}).values()):
                self._wait(eng, ev)

    def _record(self, ev, reads, writes):
        for k in writes:
            self.lastw[k] = ev
            self.readers[k] = {}
        for k in reads:
            if k in writes:
                continue
            self.readers.setdefault(k, {})[ev[0]] = ev

    def op(self, eng, fn, reads=(), writes=()):
        self._deps(eng, reads, writes)
        inst = fn(self.engs[eng])
        self.cnt[eng] += 1
        inst.then_inc(self.sems[eng], 1)
        ev = (eng, self.cnt[eng], eng)
        self._record(ev, reads, writes)
        self.ninst += 1
        return ev

    def dma(self, q, out, in_, reads=(), writes=(), **kw):
        self._deps(q, reads, writes)
        lst, idx = self.dsem[q]
        sk = lst[idx % len(lst)]
        self.dsem[q][1] = idx + 1
        if self.cnt[sk] > 0:
            self._wait(q, (sk, self.cnt[sk], "dma"))
        inst = self.engs[q].dma_start(out=out, in_=in_, **kw)
        self.cnt[sk] += 16
        inst.then_inc(self.sems[sk], 16)
        ev = (sk, self.cnt[sk], "dma")
        self._record(ev, reads, writes)
        self.ninst += 1
        return ev

    def barrier(self):
        for e in self.engs:
            for e2 in ("pe", "act", "dve", "pool"):
                if e2 != e and self.cnt[e2] > 0:
                    self._wait(e, (e2, self.cnt[e2], e2))
            if self.cnt.get("cc", 0) > 0:
                self._wait(e, ("cc", self.cnt["cc"], "dma"))
            for q in self.dsem:
                for sk in self.dsem[q][0]:
                    if self.cnt[sk] > 0:
                        self._wait(e, (sk, self.cnt[sk], "dma"))
        self.lastw.clear()
        self.readers.clear()

    def collective(self, kind, in_ap, out_ap, groups, reads, writes):
        self._deps("pool", reads, writes)
        if "cc" not in self.sems:
            self.sems["cc"] = self.es.enter_context(self.nc.semaphore("s_cc"))
            self.cnt["cc"] = 0
        inst = self.nc.gpsimd.collective_compute(kind, ALU.bypass, replica_groups=groups, ins=[in_ap], outs=[out_ap])
        self.cnt["cc"] += 1
        inst.then_inc(self.sems["cc"], 1)
        ev = ("cc", self.cnt["cc"], "dma")
        self._record(ev, reads, writes)
        return ev

    def wait_all(self, eng="sp"):
        for ev in list(self.lastw.values()):
            self._wait(eng, ev)
        for d in list(self.readers.values()):
            for ev in list(d.values()):
                self._wait(eng, ev)
        for q in self.dsem:
            for sk in self.dsem[q][0]:
                if self.cnt[sk] > 0:
                    self._wait(eng, (sk, self.cnt[sk], "dma"))
        for e in ("pe", "act", "dve", "pool"):
            if self.cnt[e] > 0:
                self._wait(eng, (e, self.cnt[e], e))


def build_program(stop_after=None):
    nc = bass.Bass("TRN2", target_bir_lowering=False, num_devices=8)

    def din(name, shape, dt=F32):
        return nc.dram_tensor(name, list(shape), dt, kind="ExternalInput").ap()

    dbg = stop_after is not None

    CC_BUFS = ("XAL", "GGL", "ZL", "XAF", "GGF", "ZF")

    def dscr(name, shape, dt=F32):
        ext = dbg and name not in CC_BUFS
        return nc.dram_tensor(name, list(shape), dt, kind="ExternalOutput" if ext else "Internal").ap()

    x_d = din("x", [TLOC, D])
    ctx_d = din("ctx", [TC, D])
    cc_d = din("cc", [2, D])
    pos_d = din("pos", [TLOC, D])
    rmask_d = din("rmask", [128, 2])
    wmod_d = din("w_mod", [DEPTH, D, 6 * D])
    bmod_d = din("b_mod", [DEPTH, 6 * D])
    win_d = din("w_in", [DEPTH, D, DIN])
    convw_d = din("conv_w", [DEPTH, 4, DA])
    convb_d = din("conv_b", [DEPTH, DA])
    wa_d = din("lru_wa", [DEPTH, 2, 8, 64, 64])
    ba_d = din("lru_ba", [DEPTH, 2, DA])
    wx_d = din("lru_wx", [DEPTH, 2, 8, 64, 64])
    bx_d = din("lru_bx", [DEPTH, 2, DA])
    lam_d = din("lru_lam", [DEPTH, 2, DA])
    ws_d = din("sg_ws", [DEPTH, 4, 128, 128])
    sgb_d = din("sg_b", [DEPTH, 4, 128])
    wf_d = din("fourier_w", [DEPTH, 4, 64, 64])
    gmix_d = din("g_mix", [DEPTH, D])
    wout_d = din("w_out", [DEPTH, D, D])
    ln1g_d = din("ln1_g", [DEPTH, D])
    ln1b_d = din("ln1_b", [DEPTH, D])
    wup_d = din("w_up", [DEPTH, D, 2 * DFF])
    wdn_d = din("w_down", [DEPTH, DFF, D])
    ln2g_d = din("ln2_g", [DEPTH, D])
    ln2b_d = din("ln2_b", [DEPTH, D])
    ident_d = din("ident", [128, 128])
    t1_d = din("t1", [128, 3, 128], BF16)
    tw_d = din("tw", [128, T], BF16)
    c256_d = din("c256", [128, 2, 2, 256], BF16)
    cspad_d = din("cspad", [64, 2, 2, 128])
    out_d = nc.dram_tensor("out", [TLOC, D], F32, kind="ExternalOutput").ap()

    XB = dscr("XB", [TLOC + TC, D])
    XAL = dscr("XAL", [DA, TLOC])
    GGL = dscr("GGL", [DA, TLOC])
    ZL = dscr("ZL", [DC, TLOC], BF16)
    XAF = dscr("XAF", [2 * DA, TLOC])
    GGF = dscr("GGF", [2 * DA, TLOC])
    ZF = dscr("ZF", [2 * DC, TLOC], BF16)
    XAC = dscr("XAC", [DA, TC])
    GGC = dscr("GGC", [DA, TC])
    YBL = dscr("YBL", [DB, TLOC + TC], BF16)
    XC = dscr("XC", [DA, TT])
    HF = dscr("HF", [DA, TT])
    YT = dscr("YT", [D, TT], BF16)
    GD = dscr("GD", [128, 2, 64, 256], BF16)
    MODS = dscr("MODS", [2, 6 * D])

    XCv = XC.rearrange("(c p) t -> p c t", p=128)
    XALv = XAL.rearrange("(c p) t -> p c t", p=128)
    GGLv = GGL.rearrange("(c p) t -> p c t", p=128)
    ZLv = ZL.rearrange("(c p) t -> p c t", p=128)
    XACv = XAC.rearrange("(c p) t -> p c t", p=128)
    GGCv = GGC.rearrange("(c p) t -> p c t", p=128)
    YBLv = YBL.rearrange("(c p) t -> p c t", p=128)
    XAFv = XAF.rearrange("(c r p) t -> r p c t", r=2, p=128)
    GGFv = GGF.rearrange("(c r p) t -> r p c t", r=2, p=128)
    ZFv = ZF.rearrange("(r c p) t -> r p c t", r=2, p=128)
    PAIRS = [[0, 1], [2, 3], [4, 5], [6, 7]]
    HFv = HF.rearrange("(c p) t -> p c t", p=128)
    YTv = YT.rearrange("(c p) t -> p c t", p=128)

    TILES = [(T, True)] + [(TL * i, False) for i in range(NT)]
    LTILES = [(TLOC, True)] + [(TL * i, False) for i in range(NTL)]
    import os as _os
    if _os.environ.get("DBG_NT"):
        LTILES = LTILES[:int(_os.environ["DBG_NT"])]

    with ExitStack() as es:
        fw = FW(nc, es)
        op = fw.op
        dma = fw.dma

        uid = [0]

        def sb(es_, name, shape, dt=F32):
            uid[0] += 1
            return es_.enter_context(nc.sbuf_tensor("%s_s%d" % (name, uid[0]), list(shape), dt))

        def ps(es_, name, shape, dt=F32):
            uid[0] += 1
            return es_.enter_context(nc.psum_tensor("%s_p%d" % (name, uid[0]), list(shape), dt))

        es.enter_context(nc.allow_non_contiguous_dma(reason="small strided parameter loads"))

        ident = sb(es, "ident", [128, 128])
        ones_f = sb(es, "ones_f", [128, 128])
        dma("sp", ident[:], ident_d[:, :], writes=["ident"])
        op("pool", lambda e: e.memset(ones_f[:], 1.0), writes=["ones_f"])

        def rsqrt(out, x, tmp, kout, kx, ktmp, eng="pool", iters=3):
            xi = x.bitcast(I32)
            oi = out.bitcast(I32)
            op("dve", lambda e: e.tensor_scalar(out=oi, in0=xi, scalar1=1, scalar2=None,
                                                op0=ALU.arith_shift_right), reads=[kx], writes=[kout])
            op("dve", lambda e: e.tensor_scalar(out=oi, in0=oi, scalar1=-1.0, scalar2=float(0x5F3759DF),
                                                op0=ALU.mult, op1=ALU.add), writes=[kout])
            for _ in range(iters):
                op(eng, lambda e: e.tensor_tensor(out=tmp, in0=x, in1=out, op=ALU.mult), reads=[kx, kout], writes=[ktmp])
                op(eng, lambda e: e.tensor_tensor(out=tmp, in0=tmp, in1=out, op=ALU.mult), reads=[kout], writes=[ktmp])
                op(eng, lambda e: e.tensor_scalar(out=tmp, in0=tmp, scalar1=-0.5, scalar2=1.5,
                                                  op0=ALU.mult, op1=ALU.add), writes=[ktmp])
                op(eng, lambda e: e.tensor_tensor(out=out, in0=out, in1=tmp, op=ALU.mult), reads=[ktmp], writes=[kout])

        def resid_rows(l, col0, is_ctx):
            if l == 0:
                return (ctx_d[0:TL, :] if is_ctx else x_d[col0:col0 + TL, :])
            return XB[col0:col0 + TL, :]

        def load_resid(l, col0, is_ctx, xt, kx, pt=None, kp=None, from_xb=False, save_xb=False):
            if from_xb and not is_ctx:
                src = XB[col0:col0 + TL, :]
            else:
                src = resid_rows(l, col0, is_ctx)
            dma("sp", xt[:], src.rearrange("(s p) d -> p s d", p=128), reads=(["XB"] if (from_xb or l > 0) else []), writes=[kx])
            if l == 0 and not is_ctx and not from_xb:
                dma("sp", pt[:], pos_d[col0:col0 + TL, :].rearrange("(s p) d -> p s d", p=128), writes=[kp])
                op("pool", lambda e: e.tensor_tensor(out=xt[:], in0=xt[:], in1=pt[:], op=ALU.add),
                   reads=[kp], writes=[kx])
                if save_xb:
                    dma("sp", XB[col0:col0 + TL, :].rearrange("(s p) d -> p s d", p=128), xt[:], reads=[kx], writes=["XB"])

        def ln_tile(xt, kx, xh, kxh, scr, eps, eng="pool", iters=3):
            st, mv, ve, rs, tmp = scr["st"], scr["mv"], scr["ve"], scr["rs"], scr["tmp"]
            kk = scr["k"]
            for s in range(2):
                for h in range(2):
                    op("dve", lambda e: e.bn_stats(out=st[:, s, h, :], in_=xt[:, s, h * 512:(h + 1) * 512]),
                       reads=[kx], writes=[kk + "st"])
                op("dve", lambda e: e.bn_aggr(out=mv[:, s, :], in_=st[:, s, :, :].rearrange("p a b -> p (a b)")),
                   reads=[kk + "st"], writes=[kk + "mv"])
            op("dve", lambda e: e.tensor_scalar(out=ve[:], in0=mv[:, :, 1], scalar1=float(eps), scalar2=None,
                                                op0=ALU.add), reads=[kk + "mv"], writes=[kk + "ve"])
            rsqrt(rs[:], ve[:], tmp[:], kk + "rs", kk + "ve", kk + "tmp", eng=eng, iters=iters)
            for s in range(2):
                op("dve", lambda e: e.tensor_scalar(out=xh[:, s, :], in0=xt[:, s, :], scalar1=mv[:, s, 0:1],
                                                    scalar2=rs[:, s:s + 1], op0=ALU.subtract, op1=ALU.mult),
                   reads=[kx, kk + "mv", kk + "rs"], writes=[kxh])

        def ln_stats(xt, kx, scr, eps, iters=3):
            st, mv, ve, rs, tmp = scr["st"], scr["mv"], scr["ve"], scr["rs"], scr["tmp"]
            kk = scr["k"]
            for s in range(2):
                for h in range(2):
                    op("dve", lambda e: e.bn_stats(out=st[:, s, h, :], in_=xt[:, s, h * 512:(h + 1) * 512]),
                       reads=(kx if isinstance(kx, list) else [kx]), writes=[kk + "st%d%d" % (s, h)])
                op("dve", lambda e: e.bn_aggr(out=mv[:, s, :], in_=st[:, s, :, :].rearrange("p a b -> p (a b)")),
                   reads=[kk + "st%d0" % s, kk + "st%d1" % s], writes=[kk + "mv"])
            op("dve", lambda e: e.tensor_scalar(out=ve[:], in0=mv[:, :, 1], scalar1=float(eps), scalar2=None,
                                                op0=ALU.add), reads=[kk + "mv"], writes=[kk + "ve"])
            rsqrt(rs[:], ve[:], tmp[:], kk + "rs", kk + "ve", kk + "tmp", iters=iters)

        def ln_apply(xt, kx, xh, kxh, scr):
            mv, rs = scr["mv"], scr["rs"]
            kk = scr["k"]
            for s in range(2):
                op("dve", lambda e: e.tensor_scalar(out=xh[:, s, :], in0=xt[:, s, :], scalar1=mv[:, s, 0:1],
                                                    scalar2=rs[:, s:s + 1], op0=ALU.subtract, op1=ALU.mult),
                   reads=[kx, kk + "mv", kk + "rs"], writes=[kxh])

        def ln_scr(es_, name):
            return dict(st=sb(es_, name + "st", [128, 2, 2, 6]), mv=sb(es_, name + "mv", [128, 2, 2]),
                        ve=sb(es_, name + "ve", [128, 2]), rs=sb(es_, name + "rs", [128, 2]),
                        tmp=sb(es_, name + "tmp", [128, 2]), k=name)

        def transpose_mod(xh, kxh, hT, khT, tps, modc, strm, jsc, jsh):
            for kp in range(4):
                tp, ktp = tps[kp % 2]
                for j in range(2):
                    k = 2 * kp + j
                    for s in range(2):
                        op("pe", lambda e: e.transpose(out=tp[:, j, s * 128:(s + 1) * 128],
                                                       in_=xh[:, s, k * 128:(k + 1) * 128], identity=ident[:]),
                           reads=[kxh, "ident"], writes=[ktp])
                for j in range(2):
                    k = 2 * kp + j
                    op("act", lambda e: e.activation(out=hT[:, k, :], in_=tp[:, j, :], func=AF.Identity,
                                                     scale=modc[:, strm, jsc, k:k + 1], bias=modc[:, strm, jsh, k:k + 1]),
                       reads=["modc"], writes=[ktp, khT])

        for l in range(DEPTH):
            last = (l == DEPTH - 1)
            with ExitStack() as el:
                with ExitStack() as ep:
                    ep.enter_context(nc.named_scope("P0_l%d" % l))
                    cct = sb(ep, "cct", [128, 8, 2])
                    sct = sb(ep, "sct", [128, 8, 2])
                    scbc = sb(ep, "scbc", [128, 8, 128])
                    bmbc = sb(ep, "bmbc", [128, 6 * D])
                    modbc = sb(ep, "modbc", [128, 6 * D])
                    wm = [sb(ep, "wm%d" % i, [128, 3072]) for i in range(3)]
                    pmod = [ps(ep, "pmod%d" % i, [128, 512]) for i in range(6)]
                    for s in range(2):
                        dma("sp", cct[:, :, s], cc_d[s, :].rearrange("(k p) -> p k", p=128), writes=["cct"])
                    dma("sp", bmbc[:], bmod_d[l:l + 1, :].partition_broadcast(128), writes=["bmbc"])
                    op("act", lambda e: e.activation(out=sct[:], in_=cct[:], func=AF.Tanh, scale=0.5), reads=["cct"], writes=["sct"])
                    op("dve", lambda e: e.tensor_scalar(out=sct[:], in0=sct[:], scalar1=0.5, scalar2=0.5, op0=ALU.mult, op1=ALU.add), writes=["sct"])
                    op("dve", lambda e: e.tensor_tensor(out=sct[:], in0=sct[:], in1=cct[:], op=ALU.mult), reads=["cct"], writes=["sct"])
                    for k in range(8):
                        for s in range(2):
                            op("dve", lambda e: e.tensor_scalar(out=scbc[:, k, 64 * s:64 * s + 64], in0=ones_f[:, 0:64],
                                                                scalar1=sct[:, k, s:s + 1], scalar2=None, op0=ALU.mult),
                               reads=["sct", "ones_f"], writes=["scbc"])
                    ld = 0
                    for half in range(2):
                        for k in range(8):
                            w = wm[ld % 3]
                            kw = "wm%d" % (ld % 3)
                            ld += 1
                            dma("sp", w[:], wmod_d[l, k * 128:(k + 1) * 128, half * 3072:(half + 1) * 3072], writes=[kw])
                            for n in range(6):
                                op("pe", lambda e: e.matmul(pmod[n][:], lhsT=scbc[:, k, :], rhs=w[:, n * 512:(n + 1) * 512],
                                                            start=(k == 0), stop=(k == 7)),
                                   reads=["scbc", kw], writes=["pmod%d" % n])
                        for n in range(6):
                            c0 = half * 3072 + n * 512
                            op("dve", lambda e: e.tensor_tensor(out=modbc[:, c0:c0 + 512], in0=pmod[n][:], in1=bmbc[:, c0:c0 + 512],
                                                                op=ALU.add), reads=["bmbc"], writes=["pmod%d" % n, "modbc"])
                    dma("sp", MODS[0:1, :], modbc[0:1, :], reads=["modbc"], writes=["MODS"])
                    dma("sp", MODS[1:2, :], modbc[64:65, :], reads=["modbc"], writes=["MODS"])
                    fw.barrier()

                modc = sb(el, "modc", [128, 2, 6, 8])
                gmixc = sb(el, "gmixc", [128, 8])
                for s in range(2):
                    for j in range(6):
                        dma("sp", modc[:, s, j, :], MODS[s, j * D:(j + 1) * D].rearrange("(k p) -> p k", p=128), reads=["MODS"], writes=["modc"])
                dma("sp", gmixc[:], gmix_d[l, :].rearrange("(k p) -> p k", p=128), writes=["gmixc"])
                for j in (1, 4):
                    op("dve", lambda e: e.tensor_scalar(out=modc[:, :, j, :], in0=modc[:, :, j, :], scalar1=1.0, scalar2=None,
                                                        op0=ALU.add), writes=["modc"])

                with ExitStack() as ez:
                    zT = sb(ez, "zT", [128, 2, T], BF16)
                    zTc = sb(ez, "zTc", [128, 2, TC], BF16)
                    with ExitStack() as e1:
                        e1.enter_context(nc.named_scope("M1_l%d" % l))
                        win = sb(e1, "win", [128, 8, DIN], BF16)
                        for k in range(8):
                            dma("pool", win[:, k, :], win_d[l, k * 128:(k + 1) * 128, :], writes=["win"])
                        wsT = sb(e1, "wsT", [128, 4, 128], BF16)
                        wsr = sb(e1, "wsr", [128, 4, 128])
                        bsT = sb(e1, "bsT", [128, 4])
                        dma("sp", wsr[:], ws_d[l].rearrange("h p q -> p h q"), writes=["wsr"])
                        dma("sp", bsT[:], sgb_d[l].rearrange("h p -> p h"), writes=["bsT"])
                        xt = [sb(e1, "xt%d" % i, [128, 2, D]) for i in range(2)]
                        pt = [sb(e1, "pt%d" % i, [128, 2, D]) for i in range(2)] if l == 0 else [None, None]
                        xh = sb(e1, "xh", [128, 2, D])
                        hT = [sb(e1, "hT%d" % i, [128, 8, TL], BF16) for i in range(2)]
                        lns = ln_scr(e1, "l1")
                        xaS = [sb(e1, "xaS%d" % i, [128, 4, TL]) for i in range(2)]
                        ggS = [sb(e1, "ggS%d" % i, [128, 4, TL]) for i in range(2)]
                        yBT = [sb(e1, "yBT%d" % i, [128, 2, TL], BF16) for i in range(2)]
                        zS = [sb(e1, "zS%d" % i, [128, 2, TL], BF16) for i in range(2)]
                        zB = [sb(e1, "zB%d" % i, [128, 512]) for i in range(2)]
                        yb = [sb(e1, "yb%d" % i, [128, 256]) for i in range(2)]
                        vh = sb(e1, "vh", [128, 256], BF16)
                        bst = sb(e1, "bst", [128, 4, 6])
                        bmv = sb(e1, "bmv", [128, 4, 2])
                        bve = sb(e1, "bve", [128, 4])
                        brs = sb(e1, "brs", [128, 4])
                        btmp = sb(e1, "btmp", [128, 4])
                        ssB = sb(e1, "ssB", [128, 2])
                        rB = sb(e1, "rB", [128, 2])
                        rsB = sb(e1, "rsB", [128, 2])
                        rtmp = sb(e1, "rtmp", [128, 2])
                        junk = sb(e1, "junk", [128, 256])
                        tp0 = ps(e1, "tp0", [128, 2, TL]); tp1 = ps(e1, "tp1", [128, 2, TL])
                        tps = [(tp0, "tp0"), (tp1, "tp1")]
                        pa = [ps(e1, "pa%d" % i, [128, 2, TL]) for i in range(2)]
                        pb = [ps(e1, "pb%d" % i, [128, 512]) for i in range(2)]
                        pss = ps(e1, "pss", [128, 512])
                        pyt = ps(e1, "pyt", [128, 2, TL])
                        for h in range(4):
                            op("pe", lambda e: e.transpose(out=pb[0][:, h * 128:(h + 1) * 128], in_=wsr[:, h, :], identity=ident[:]),
                               reads=["wsr", "ident"], writes=["pb0"])
                        op("dve", lambda e: e.tensor_copy(out=wsT[:].rearrange("p h q -> p (h q)"), in_=pb[0][:]), writes=["pb0", "wsT"])

                        tl = LTILES
                        nM = len(tl)
                        zBt = [[sb(e1, "zBt%d_%d" % (i, s_), [128, 512]) for s_ in range(2)] for i in range(2)]

                        def m1_load(ti):
                            nb = ti % 2
                            load_resid(l, tl[ti][0], tl[ti][1], xt[nb], "xt%d" % nb, pt[nb], "pt%d" % nb, save_xb=True)

                        def m1_A(ti):
                            b = ti % 2
                            ln_tile(xt[b], "xt%d" % b, xh, "xh", lns, EPS, eng="dve", iters=2)

                        def m1_T(ti):
                            col0, is_ctx = tl[ti]
                            b = ti % 2
                            transpose_mod(xh, "xh", hT[b], "hT%d" % b, tps, modc, 1 if is_ctx else 0, 1, 0)

                        def m1_P(ti):
                            col0, is_ctx = tl[ti]
                            b = ti % 2
                            khT = "hT%d" % b
                            pai = 0
                            only_xa = last and is_ctx
                            for grp in range(2 if only_xa else 4):
                                p_, kp_ = pa[pai % 2], "pa%d" % (pai % 2)
                                pai += 1
                                for j in range(2):
                                    oc = grp * 2 + j
                                    for k in range(8):
                                        op("pe", lambda e: e.matmul(p_[:, j, :], lhsT=win[:, k, oc * 128:(oc + 1) * 128], rhs=hT[b][:, k, :],
                                                                    start=(k == 0), stop=(k == 7)), reads=["win", khT], writes=[kp_])
                                if grp < 2:
                                    op("act", lambda e: e.activation(out=xaS[b][:, 2 * grp:2 * grp + 2, :], in_=p_[:], func=AF.Identity),
                                       writes=[kp_, "xaS%d" % b])
                                else:
                                    g2 = grp - 2
                                    op("act", lambda e: e.activation(out=ggS[b][:, 2 * g2:2 * g2 + 2, :], in_=p_[:], func=AF.Gelu),
                                       writes=[kp_, "ggS%d" % b])
                            dma("sp", XACv[:, :, :] if is_ctx else XALv[:, :, col0:col0 + TL], xaS[b][:], reads=["xaS%d" % b], writes=["XA"])
                            if only_xa:
                                return
                            dma("sp", GGCv[:, :, :] if is_ctx else GGLv[:, :, col0:col0 + TL], ggS[b][:], reads=["ggS%d" % b], writes=["GG"])
                            p_, kp_ = pa[pai % 2], "pa%d" % (pai % 2)
                            for j in range(2):
                                for k in range(8):
                                    op("pe", lambda e: e.matmul(p_[:, j, :], lhsT=win[:, k, 1536 + j * 128:1536 + (j + 1) * 128], rhs=hT[b][:, k, :],
                                                                start=(k == 0), stop=(k == 7)), reads=["win", khT], writes=[kp_])
                            if is_ctx:
                                op("act", lambda e: e.activation(out=zTc[:, :, :], in_=p_[:], func=AF.Identity), writes=[kp_, "zTc"])
                            else:
                                op("act", lambda e: e.activation(out=zS[b][:], in_=p_[:], func=AF.Identity), writes=[kp_, "zS%d" % b])
                                dma("sp", ZLv[:, :, col0:col0 + TL], zS[b][:], reads=["zS%d" % b], writes=["ZL"])

                        def m1_Bproj(ti):
                            col0, is_ctx = tl[ti]
                            if last and is_ctx:
                                return
                            b = ti % 2
                            for s in range(2):
                                p_, kp_ = pb[s], "pb%d" % s
                                for k in range(8):
                                    op("pe", lambda e: e.matmul(p_[:], lhsT=hT[b][:, k, s * 128:(s + 1) * 128], rhs=win[:, k, 1024:1536],
                                                                start=(k == 0), stop=(k == 7)), reads=["win", "hT%d" % b], writes=[kp_])
                                op("act", lambda e: e.activation(out=zBt[b][s][:], in_=p_[:], func=AF.Gelu), writes=[kp_, "zBt%d_%d" % (b, s)])

                        bmv8 = sb(e1, "bmv8", [128, 8, 2])
                        bve8 = sb(e1, "bve8", [128, 8])
                        brs8 = sb(e1, "brs8", [128, 8])
                        btmp8 = sb(e1, "btmp8", [128, 8])
                        bst8 = sb(e1, "bst8", [128, 8, 6])
                        vh2 = sb(e1, "vh2", [128, 2, 256], BF16)
                        ybp = [[sb(e1, "ybp%d_%d" % (i, s_), [128, 256]) for s_ in range(2)] for i in range(2)]
                        ssBp = [sb(e1, "ssBp%d" % i, [128, 2]) for i in range(2)]

                        def skipB(ti):
                            return last and tl[ti][1]

                        def m1_B1a(ti):
                            if skipB(ti):
                                return
                            b = ti % 2
                            for s in range(2):
                                zb, kzb = zBt[b][s], "zBt%d_%d" % (b, s)
                                for h in range(4):
                                    op("dve", lambda e: e.bn_stats(out=bst8[:, 4 * s + h, :], in_=zb[:, 256 + 64 * h:256 + 64 * h + 64]),
                                       reads=[kzb], writes=["bst8_%d" % (4 * s + h)])
                                for h in range(4):
                                    op("dve", lambda e: e.bn_aggr(out=bmv8[:, 4 * s + h, :], in_=bst8[:, 4 * s + h, :]),
                                       reads=["bst8_%d" % (4 * s + h)], writes=["bmv8"])
                            op("dve", lambda e: e.tensor_scalar(out=bve8[:], in0=bmv8[:, :, 1], scalar1=EPS, scalar2=None, op0=ALU.add),
                               reads=["bmv8"], writes=["bve8"])
                            rsqrt(brs8[:], bve8[:], btmp8[:], "brs8", "bve8", "btmp8", iters=2)
                            for s in range(2):
                                zb, kzb = zBt[b][s], "zBt%d_%d" % (b, s)
                                for h in range(4):
                                    op("dve", lambda e: e.tensor_scalar(out=vh2[:, s, 64 * h:64 * h + 64], in0=zb[:, 256 + 64 * h:256 + 64 * h + 64],
                                                                        scalar1=bmv8[:, 4 * s + h, 0:1], scalar2=brs8[:, 4 * s + h:4 * s + h + 1],
                                                                        op0=ALU.subtract, op1=ALU.mult),
                                       reads=[kzb, "bmv8", "brs8"], writes=["vh2_%d" % (4 * s + h)])

                        def m1_smm(ti):
                            if skipB(ti):
                                return
                            for s in range(2):
                                for h in range(4):
                                    op("pe", lambda e: e.matmul(pss[:, 256 * s + 64 * h:256 * s + 64 * h + 64], lhsT=wsT[:, h, :], rhs=vh2[:, s, 64 * h:64 * h + 64],
                                                                start=True, stop=True), reads=["wsT", "vh2_%d" % (4 * s + h)], writes=["pss"])

                        def m1_B1b(ti):
                            if skipB(ti):
                                return
                            b = ti % 2
                            for s in range(2):
                                zb, kzb = zBt[b][s], "zBt%d_%d" % (b, s)
                                for h in range(4):
                                    op("dve", lambda e: e.scalar_tensor_tensor(out=ybp[b][s][:, 64 * h:64 * h + 64], in0=pss[:, 256 * s + 64 * h:256 * s + 64 * h + 64],
                                                                               scalar=bsT[:, h:h + 1], in1=zb[:, 64 * h:64 * h + 64],
                                                                               op0=ALU.add, op1=ALU.mult),
                                       reads=["bsT", kzb], writes=["pss", "ybp%d_%d_%d" % (b, s, h)])
                            for s in range(2):
                                op("act", lambda e: e.activation(out=junk[:], in_=ybp[b][s][:], func=AF.Square, accum_out=ssBp[b][:, s:s + 1]),
                                   reads=["ybp%d_%d_%d" % (b, s, h) for h in range(4)], writes=["junk", "ssBp%d" % b])

                        def m1_B2(ti):
                            if skipB(ti):
                                return
                            col0, is_ctx = tl[ti]
                            b = ti % 2
                            op("dve", lambda e: e.tensor_scalar(out=rB[:], in0=ssBp[b][:], scalar1=1.0 / DB, scalar2=EPS, op0=ALU.mult, op1=ALU.add),
                               reads=["ssBp%d" % b], writes=["rB"])
                            rsqrt(rsB[:], rB[:], rtmp[:], "rsB", "rB", "rtmp", iters=2)
                            for s in range(2):
                                kyb = ["ybp%d_%d_%d" % (b, s, h) for h in range(4)]
                                op("dve", lambda e: e.tensor_scalar(out=ybp[b][s][:], in0=ybp[b][s][:], scalar1=rsB[:, s:s + 1], scalar2=None, op0=ALU.mult),
                                   reads=["rsB"], writes=kyb)
                                for c in range(2):
                                    op("pe", lambda e: e.transpose(out=pyt[:, c, s * 128:(s + 1) * 128], in_=ybp[b][s][:, c * 128:(c + 1) * 128],
                                                                   identity=ident[:]), reads=kyb + ["ident"], writes=["pyt"])
                            for c in range(2):
                                op("act", lambda e: e.activation(out=yBT[b][:, c, :], in_=pyt[:, c, :], func=AF.Identity, scale=gmixc[:, 4 + c:5 + c]),
                                   reads=["gmixc"], writes=["pyt", "yBT%d" % b])
                            dma("sp", YBLv[:, :, col0:col0 + TL], yBT[b][:], reads=["yBT%d" % b], writes=["YBL"])

                        m1_load(0)
                        if nM > 1:
                            m1_load(1)
                        m1_A(0)
                        m1_T(0)
                        for ti in range(nM + 2):
                            if ti < nM:
                                m1_P(ti)
                            if ti + 1 < nM:
                                m1_A(ti + 1)
                            if 2 <= ti:
                                m1_B2(ti - 2)
                            if ti + 1 < nM:
                                m1_T(ti + 1)
                            if 1 <= ti <= nM:
                                m1_B1a(ti - 1)
                            if ti < nM:
                                m1_Bproj(ti)
                            if 1 <= ti <= nM:
                                m1_smm(ti - 1)
                                m1_B1b(ti - 1)
                            if ti + 2 < nM:
                                m1_load(ti + 2)
                        fw.barrier()
                    fw.collective("AllGather", ZL[:, :], ZF[:, :], PAIRS, ["ZL"], ["ZF"])
                    fw.barrier()
                    for c_ in range(4):
                        fw.collective("AllGather", XAL[c_ * 128:(c_ + 1) * 128, :], XAF[c_ * 256:(c_ + 1) * 256, :], PAIRS, ["XA"], ["XAF"])
                    for c_ in range(4):
                        fw.collective("AllGather", GGL[c_ * 128:(c_ + 1) * 128, :], GGF[c_ * 256:(c_ + 1) * 256, :], PAIRS, ["GG"], ["GGF"])
                    if stop_after == "M1" and l == 0:
                        xafd = nc.dram_tensor("XAFd", [2 * DA, TLOC], F32, kind="ExternalOutput").ap()
                        zfd = nc.dram_tensor("ZFd", [2 * DC, TLOC], BF16, kind="ExternalOutput").ap()
                        dma("sp", xafd[:, :], XAF[:, :], writes=["xafd"])
                        dma("sp", zfd[:, :], ZF[:, :], writes=["zfd"])
                        fw.wait_all("sp")
                        return nc

                    if True:
                        with ExitStack() as e2:
                            e2.enter_context(nc.named_scope("DFT_l%d" % l))
                            t1 = sb(e2, "t1", [128, 3, 128], BF16)
                            tw = sb(e2, "tw", [128, 128, 64], BF16)
                            c256 = sb(e2, "c256", [128, 2, 2, 256], BF16)
                            cspad = sb(e2, "cspad", [64, 2, 2, 128])
                            wft = sb(e2, "wft", [64, 4, 64])
                            abd = sb(e2, "abd", [128, 2, 256], BF16)
                            abdc = sb(e2, "abdc", [128, 2, 256], BF16)
                            XTs = sb(e2, "XTs", [128, 2, T])
                            XTc = sb(e2, "XTc", [128, 2, TC])
                            Yp = [sb(e2, "Yp%d" % i, [128, 512], BF16) for i in range(2)]
                            Gs = [sb(e2, "Gs%d" % i, [128, 2, 8, 256], BF16) for i in range(2)]
                            Rb = [sb(e2, "Rb%d" % i, [128, 8, 256], BF16) for i in range(2)]
                            sq = sb(e2, "sq", [128, 2, TL])
                            rr = sb(e2, "rr", [128, TL])
                            rrs = sb(e2, "rrs", [128, TL])
                            rrt = sb(e2, "rrt", [128, TL])
                            yCT = [sb(e2, "yCT%d" % i, [128, 2, TL], BF16) for i in range(2)]
                            pY = [ps(e2, "pY%d" % i, [128, 512]) for i in range(2)]
                            pG = [ps(e2, "pG%d" % i, [128, 2, 256]) for i in range(2)]
                            pX = [ps(e2, "pX%d" % i, [128, 8, 64]) for i in range(2)]
                            pS = ps(e2, "pS", [128, 512])
                            pS1 = ps(e2, "pS1", [128, 512])
                            for r_ in range(2):
                                dma("sp", zT[:, :, r_ * TLOC:(r_ + 1) * TLOC], ZFv[r_], writes=["zT"])
                            dma("sp", t1[:], t1_d[:, :, :], writes=["t1"])
                            dma("sp", tw[:].rearrange("p a b -> p (a b)"), tw_d[:, :], writes=["tw"])
                            dma("sp", c256[:], c256_d[:, :, :, :], writes=["c256"])
                            dma("sp", cspad[:], cspad_d[:, :, :, :], writes=["cspad"])
                            dma("sp", wft[:], wf_d[l].rearrange("g j e -> j g e"), writes=["wft"])
                            for cc in range(2):
                                for pq in range(2):
                                    for gl in range(2):
                                        op("pe", lambda e: e.matmul(pY[0][:, pq * 128 + gl * 64:pq * 128 + gl * 64 + 64],
                                                                    lhsT=cspad[:, pq, gl, :], rhs=wft[:, 2 * cc + gl, :], start=True, stop=True),
                                           reads=["cspad", "wft"], writes=["pY0"])
                                op("act", lambda e: e.activation(out=abd[:, cc, :], in_=pY[0][:, 0:256], func=AF.Identity,
                                                                 scale=1.0 / math.sqrt(T * 64.0)), writes=["pY0", "abd"])
                                op("act", lambda e: e.activation(out=abdc[:, cc, :], in_=pY[0][:, 0:256], func=AF.Identity,
                                                                 scale=1.0 / math.sqrt(TC * 64.0)), writes=["pY0", "abdc"])

                            rr2 = [sb(e2, "rr2_%d" % i, [128, TL]) for i in range(2)]
                            rrs2 = [sb(e2, "rrs2_%d" % i, [128, TL]) for i in range(2)]
                            rrt2 = [sb(e2, "rrt2_%d" % i, [128, TL]) for i in range(2)]

                            def rmsc1(src, b):
                                op("act", lambda e: e.activation(out=sq[:], in_=src, func=AF.Square), reads=["XT"], writes=["sq"])
                                pS_ = pS if b == 0 else pS1
                                for c in range(2):
                                    op("pe", lambda e: e.matmul(pS_[:, 0:TL], lhsT=ones_f[:], rhs=sq[:, c, :], start=(c == 0), stop=(c == 1)),
                                       reads=["ones_f", "sq"], writes=["pS%d" % b])

                            def rmsc2(b):
                                pS_ = pS if b == 0 else pS1
                                op("act", lambda e: e.activation(out=rr2[b][:], in_=pS_[:, 0:TL], func=AF.Ln, scale=1.0 / DC, bias=EPS),
                                   writes=["pS%d" % b, "rr2_%d" % b])
                                op("act", lambda e: e.activation(out=rrs2[b][:], in_=rr2[b][:], func=AF.Exp, scale=-0.5),
                                   reads=["rr2_%d" % b], writes=["rrs2_%d" % b])

                            def rmsc3(src, col0, b):
                                for c in range(2):
                                    op("dve", lambda e: e.scalar_tensor_tensor(out=yCT[b][:, c, :], in0=src[:, c, :], scalar=gmixc[:, 6 + c:7 + c],
                                                                               in1=rrs2[b][:], op0=ALU.mult, op1=ALU.mult),
                                       reads=["XT", "gmixc", "rrs2_%d" % b], writes=["yCT%d_%d" % (b, c)])
                                dma("sp", YTv[:, 6:8, col0:col0 + TL], yCT[b][:], reads=["yCT%d_0" % b, "yCT%d_1" % b], writes=["YT"])

                            def rms_store_c(src, col0, b):
                                rmsc1(src, b)
                                rmsc2(b)
                                rmsc3(src, col0, b)

                            if not last:
                                Ypc = [sb(e2, "Ypc%d" % i, [128, 512], BF16) for i in range(2)]
                                for t in range(2):
                                    for cc in range(2):
                                        op("pe", lambda e: e.matmul(pY[t][:, cc * 256:(cc + 1) * 256], lhsT=zTc[:, cc, t * 128:(t + 1) * 128],
                                                                    rhs=abdc[:, cc, :], start=True, stop=True), reads=["zTc", "abdc"], writes=["pY%d" % t])
                                    op("act", lambda e: e.activation(out=Ypc[t][:], in_=pY[t][:], func=AF.Identity), writes=["pY%d" % t, "Ypc%d" % t])
                                for cc in range(2):
                                    n = 0
                                    for t in range(2):
                                        for pq in range(2):
                                            op("pe", lambda e: e.matmul(pG[cc][:].rearrange("p a b -> p (a b)")[:, 0:256],
                                                                        lhsT=Ypc[t][:, cc * 256 + pq * 128:cc * 256 + (pq + 1) * 128],
                                                                        rhs=c256[:, t, pq, :], start=(n == 0), stop=(n == 3)),
                                               reads=["Ypc%d" % t, "c256"], writes=["pG%d" % cc])
                                            n += 1
                                    op("dve", lambda e: e.tensor_copy(out=XTc[:, cc, :], in_=pG[cc][:].rearrange("p a b -> p (a b)")[:, 0:256]),
                                       writes=["pG%d" % cc, "XT"])
                                rms_store_c(XTc[:, :, :], T, 0)

                            zTv = zT[:].rearrange("p c (a b) -> p c b a", b=64)
                            GDv = GD
                            for l2 in range(64):
                                b = l2 % 2
                                for cc in range(2):
                                    op("pe", lambda e: e.matmul(pY[b][:, cc * 256:(cc + 1) * 256], lhsT=zTv[:, cc, l2, :], rhs=abd[:, cc, :],
                                                                start=True, stop=True), reads=["zT", "abd"], writes=["pY%d" % b])
                                op("act", lambda e: e.activation(out=Yp[b][:], in_=pY[b][:], func=AF.Identity), writes=["pY%d" % b, "Yp%d" % b])
                                Ypv = Yp[b][:].rearrange("p (c q j) -> p q c j", c=2, q=2)
                                combos = [(0, 0, 0), (0, 1, 1), (1, 0, 1), (1, 1, 2)]
                                for (ri, pq, ti_) in combos:
                                    op("pe", lambda e: e.matmul(pG[b][:, ri, :].rearrange("p (c j) -> p c j", c=2), lhsT=t1[:, ti_, :], rhs=Ypv[:, pq, :, :],
                                                                start=(pq == 0), stop=(pq == 1)), reads=["t1", "Yp%d" % b], writes=["pG%d" % b])
                                gb = (l2 // 8) % 2
                                op("dve", lambda e: e.tensor_copy(out=Gs[gb][:, :, l2 % 8, :], in_=pG[b][:]), writes=["pG%d" % b, "Gs%d" % gb])
                                if l2 % 8 == 7:
                                    l0 = l2 - 7
                                    dma("sp", GDv[:, :, l0:l0 + 8, :], Gs[gb][:], reads=["Gs%d" % gb], writes=["GD"])
                            XTv = XTs[:].rearrange("p c (k2 k1) -> p c k1 k2", k1=128)
                            for kb in range(16):
                                b = kb % 2
                                dma("sp", Rb[b][:], GD[kb * 8:(kb + 1) * 8, :, :, :].rearrange("k r l c -> (r l) k c"), reads=["GD"], writes=["Rb%d" % b])
                                for cc in range(2):
                                    px, kpx = pX[cc], "pX%d" % cc
                                    for r in range(8):
                                        op("pe", lambda e: e.matmul(px[:, r, :], lhsT=Rb[b][:, r, cc * 128:(cc + 1) * 128], rhs=tw[:, kb * 8 + r, :],
                                                                    start=True, stop=True), reads=["Rb%d" % b, "tw"], writes=[kpx])
                                    op("act" if cc == 0 else "dve",
                                       (lambda e: e.activation(out=XTv[:, cc, kb * 8:(kb + 1) * 8, :], in_=px[:], func=AF.Identity)) if cc == 0 else
                                       (lambda e: e.tensor_copy(out=XTv[:, cc, kb * 8:(kb + 1) * 8, :], in_=px[:])),
                                       writes=[kpx, "XT"])
                            for ti in range(NT + 2):
                                if ti < NT:
                                    rmsc1(XTs[:, :, ti * TL:(ti + 1) * TL], ti % 2)
                                if 0 <= ti - 1 < NT:
                                    rmsc2((ti - 1) % 2)
                                if 0 <= ti - 2 < NT:
                                    t2 = ti - 2
                                    rmsc3(XTs[:, :, t2 * TL:(t2 + 1) * TL], t2 * TL, t2 % 2)
                            fw.barrier()
                if stop_after == "DFT" and l == 0:
                    return nc

                with ExitStack() as e3:
                    wg = sb(e3, "wg", [128, 2, 2, 4, 128], BF16)
                    cw = sb(e3, "cw", [128, 4, 4])
                    cb = sb(e3, "cb", [128, 4])
                    gb_ = sb(e3, "gbias", [128, 2, 2, 4])
                    lam = sb(e3, "lam", [128, 2, 4])
                    hnsp = sb(e3, "hnsp", [128, 2, 4])
                    nsp = sb(e3, "nsp", [128, 2, 4])
                    op("pool", lambda e: e.memset(wg[:], 0.0), writes=["wg"])
                    for ax, wd in enumerate((wa_d, wx_d)):
                        for h2 in range(2):
                            for d_ in range(2):
                                src = wd[l, d_].rearrange("(c h) i e -> h i c e", h=2)[h2]
                                dma("pool", wg[64 * h2:64 * h2 + 64, d_, ax, :, 64 * h2:64 * h2 + 64], src, writes=["wg"])
                    for k in range(4):
                        dma("sp", cw[:, :, k], convw_d[l, k, :].rearrange("(c p) -> p c", p=128), writes=["cw"])
                    dma("sp", cb[:], convb_d[l].rearrange("(c p) -> p c", p=128), writes=["cb"])
                    for d_ in range(2):
                        dma("sp", gb_[:, 0, d_, :], ba_d[l, d_, :].rearrange("(c p) -> p c", p=128), writes=["gbias"])
                        dma("sp", gb_[:, 1, d_, :], bx_d[l, d_, :].rearrange("(c p) -> p c", p=128), writes=["gbias"])
                        dma("sp", lam[:, d_, :], lam_d[l, d_, :].rearrange("(c p) -> p c", p=128), writes=["lam"])
                    op("dve", lambda e: e.tensor_scalar(out=gb_[:], in0=gb_[:], scalar1=0.5, scalar2=None, op0=ALU.mult), writes=["gbias"])
                    op("act", lambda e: e.activation(out=lam[:], in_=lam[:], func=AF.Exp, scale=-1.0), writes=["lam"])
                    op("act", lambda e: e.activation(out=lam[:], in_=lam[:], func=AF.Ln, bias=1.0), writes=["lam"])
                    op("dve", lambda e: e.tensor_scalar(out=hnsp[:], in0=lam[:], scalar1=-4.0, scalar2=None, op0=ALU.mult), reads=["lam"], writes=["hnsp"])
                    op("dve", lambda e: e.tensor_scalar(out=nsp[:], in0=lam[:], scalar1=-8.0, scalar2=None, op0=ALU.mult), reads=["lam"], writes=["nsp"])

                    xaH = [sb(e3, "xaH%d" % i, [128, 4, TL + 3]) for i in range(2)]
                    xc = [sb(e3, "xc%d" % i, [128, 4, TL]) for i in range(2)]
                    xcb = [sb(e3, "xcb%d" % i, [128, 4, TL], BF16) for i in range(2)]
                    hS = [sb(e3, "hS%d" % i, [128, 4, TL]) for i in range(2)]
                    hfL = [sb(e3, "hfL%d" % i, [128, 4, TL]) for i in range(2)]
                    ggL = [sb(e3, "ggL%d" % i, [128, 4, TL]) for i in range(2)]
                    tr = sb(e3, "tr", [128, 4, TL])
                    tiS = [sb(e3, "tiS%d" % i, [128, 4, TL]) for i in range(2)]
                    aS = [sb(e3, "aS%d" % i, [128, 4, TL]) for i in range(2)]
                    mS = [sb(e3, "mS%d" % i, [128, 4, TL]) for i in range(2)]
                    uS = sb(e3, "uS", [128, 4, TL])
                    ya = [sb(e3, "ya%d" % i, [128, 4, TL]) for i in range(3)]
                    sq = sb(e3, "sqa", [128, 4, TL])
                    rr = [sb(e3, "rra%d" % i, [128, TL]) for i in range(2)]
                    rrs = [sb(e3, "rrsa%d" % i, [128, TL]) for i in range(2)]
                    rrt = [sb(e3, "rrta%d" % i, [128, TL]) for i in range(2)]
                    yAT = [sb(e3, "yAT%d" % i, [128, 4, TL], BF16) for i in range(2)]
                    pg = [ps(e3, "pg%d" % i, [128, 2, TL]) for i in range(4)]
                    pSs = [ps(e3, "pSa%d" % i, [128, 512]) for i in range(2)]

                    def xck(b):
                        return ["xc%d_%d" % (b, c) for c in range(4)]

                    def gates(d, b):
                        for c in range(4):
                            for ax in range(2):
                                op("pe", lambda e: e.matmul(pg[c][:, ax, :], lhsT=wg[:, d, ax, c, :], rhs=xcb[b][:, c, :], start=True, stop=True),
                                   reads=["wg", "xcb%d" % b], writes=["pg%d" % c])
                        for c in range(4):
                            op("act", lambda e: e.activation(out=tr[:, c, :], in_=pg[c][:, 0, :], func=AF.Tanh, scale=0.5, bias=gb_[:, 0, d, c:c + 1]),
                               reads=["gbias"], writes=["pg%d" % c, "tr"])
                            op("act", lambda e: e.activation(out=tiS[b][:, c, :], in_=pg[c][:, 1, :], func=AF.Tanh, scale=0.5, bias=gb_[:, 1, d, c:c + 1]),
                               reads=["gbias"], writes=["pg%d" % c, "tiS%d" % b])
                        for c in range(4):
                            op("act", lambda e: e.activation(out=aS[b][:, c, :], in_=tr[:, c, :], func=AF.Exp, scale=hnsp[:, d, c:c + 1], bias=hnsp[:, d, c:c + 1]),
                               reads=["tr", "hnsp"], writes=["aS%d" % b])
                            op("act", lambda e: e.activation(out=mS[b][:, c, :], in_=tr[:, c, :], func=AF.Exp, scale=nsp[:, d, c:c + 1], bias=nsp[:, d, c:c + 1]),
                               reads=["tr", "nsp"], writes=["mS%d" % b])
                        op("act", lambda e: e.activation(out=mS[b][:], in_=mS[b][:], func=AF.Sqrt, scale=-0.25, bias=0.25), writes=["mS%d" % b])

                    def make_u(b):
                        op("dve", lambda e: e.scalar_tensor_tensor(out=uS[:], in0=tiS[b][:], scalar=1.0, in1=xc[b][:], op0=ALU.add, op1=ALU.mult),
                           reads=["tiS%d" % b] + xck(b), writes=["uS"])
                        op("dve", lambda e: e.tensor_tensor(out=uS[:], in0=uS[:], in1=mS[b][:], op=ALU.mult), reads=["mS%d" % b], writes=["uS"])

                    scope_ = nc.named_scope("M2a_l%d" % l)
                    scope_.__enter__()

                    def load_xa(ti, b):
                        col0, is_ctx = TILES[ti]
                        first = is_ctx or col0 == 0
                        lastt = is_ctx or col0 == T - TL
                        lo = 0 if first else 2
                        hi = 0 if lastt else 1
                        if first:
                            op("pool", lambda e: e.memset(xaH[b][:, :, 0:2], 0.0), writes=["xaH%d" % b])
                        if lastt:
                            op("pool", lambda e: e.memset(xaH[b][:, :, TL + 2:TL + 3], 0.0), writes=["xaH%d" % b])
                        if is_ctx:
                            dma("sp", xaH[b][:, :, 2:TL + 2], XACv[:, :, :], writes=["xaH%d" % b])
                        else:
                            g0, g1 = col0 - lo, col0 + TL + hi
                            d0 = 2 - lo
                            for r_ in range(2):
                                a0, a1 = max(g0, r_ * TLOC), min(g1, (r_ + 1) * TLOC)
                                if a1 > a0:
                                    dma("sp", xaH[b][:, :, d0 + a0 - g0:d0 + a1 - g0], XAFv[r_][:, :, a0 - r_ * TLOC:a1 - r_ * TLOC],
                                        reads=["XAF"], writes=["xaH%d" % b])

                    def f0(ti):
                        col0, is_ctx = TILES[ti]
                        b = ti % 2
                        for c in range(4):
                            op("dve", lambda e: e.tensor_scalar(out=xc[b][:, c, :], in0=xaH[b][:, c, 0:TL], scalar1=cw[:, c, 0:1], scalar2=cb[:, c:c + 1],
                                                                op0=ALU.mult, op1=ALU.add), reads=["xaH%d" % b, "cw", "cb"], writes=["xc%d_%d" % (b, c)])
                        for k in range(1, 4):
                            for c in range(4):
                                op("dve", lambda e: e.scalar_tensor_tensor(out=xc[b][:, c, :], in0=xaH[b][:, c, k:k + TL], scalar=cw[:, c, k:k + 1],
                                                                           in1=xc[b][:, c, :], op0=ALU.mult, op1=ALU.add),
                                   reads=["xaH%d" % b, "cw"], writes=["xc%d_%d" % (b, c)])
                        op("act", lambda e: e.activation(out=xcb[b][:], in_=xc[b][:], func=AF.Identity), reads=xck(b), writes=["xcb%d" % b])
                        dma("sp", XCv[:, :, col0:col0 + TL], xc[b][:], reads=xck(b), writes=["XC"])
                        gates(0, b)

                    def f1(ti):
                        col0, is_ctx = TILES[ti]
                        b = ti % 2
                        pb_ = (ti - 1) % 2
                        make_u(b)
                        for c in range(4):
                            init = 0.0 if ti == 0 else hS[pb_][:, c, TL - 1:TL]
                            op("dve", lambda e: e.tensor_tensor_scan(out=hS[b][:, c, :], data0=aS[b][:, c, :], data1=uS[:, c, :], initial=init,
                                                                     op0=ALU.mult, op1=ALU.add),
                               reads=["aS%d" % b, "uS"] + ([] if ti == 0 else ["hS%d" % pb_]), writes=["hS%d" % b])
                        if not (last and is_ctx):
                            dma("sp", HFv[:, :, col0:col0 + TL], hS[b][:], reads=["hS%d" % b], writes=["HF"])

                    nT = len(TILES)
                    load_xa(0, 0)
                    load_xa(1, 1)
                    f0(0)
                    for ti in range(nT):
                        if ti + 2 < nT:
                            load_xa(ti + 2, ti % 2)
                        if ti + 1 < nT:
                            f0(ti + 1)
                        f1(ti)
                    fw.barrier()
                    scope_.__exit__(None, None, None)
                    if stop_after == "M2a" and l == 0:
                        return nc
                    scope_ = nc.named_scope("M2b_l%d" % l)
                    scope_.__enter__()

                    order = [0] + list(range(NT, 0, -1))
                    nO = len(order)

                    def load_x(oi):
                        col0, is_ctx = TILES[order[oi]]
                        b = oi % 2
                        dma("sp", xc[b][:], XCv[:, :, col0:col0 + TL], reads=["XC"], writes=xck(b))

                    def load_hg(oi):
                        col0, is_ctx = TILES[order[oi]]
                        b = oi % 2
                        if not (last and is_ctx):
                            dma("sp", hfL[b][:], HFv[:, :, col0:col0 + TL], reads=["HF"], writes=["hfL%d" % b])
                            ggsrc = GGCv[:, :, :] if is_ctx else GGFv[col0 // TLOC][:, :, col0 % TLOC:col0 % TLOC + TL]
                            dma("sp", ggL[b][:], ggsrc, reads=["GGF"], writes=["ggL%d" % b])

                    def b0(oi):
                        b = oi % 2
                        op("act", lambda e: e.activation(out=xcb[b][:], in_=xc[b][:], func=AF.Identity), reads=xck(b), writes=["xcb%d" % b])
                        gates(1, b)

                    def b1(oi):
                        col0, is_ctx = TILES[order[oi]]
                        b = oi % 2
                        pb_ = (oi - 1) % 2
                        y3 = oi % 3
                        make_u(b)
                        for c in range(4):
                            init = 0.0 if oi == 0 else hS[pb_][:, c, 0:1]
                            op("dve", lambda e: e.tensor_tensor_scan(out=hS[b][:, c, ::-1], data0=aS[b][:, c, ::-1], data1=uS[:, c, ::-1], initial=init,
                                                                     op0=ALU.mult, op1=ALU.add),
                               reads=["aS%d" % b, "uS"] + ([] if oi == 0 else ["hS%d" % pb_]), writes=["hS%d" % b])
                        if last and is_ctx:
                            return
                        op("pool", lambda e: e.tensor_tensor(out=ya[y3][:], in0=hS[b][:], in1=hfL[b][:], op=ALU.add), reads=["hS%d" % b, "hfL%d" % b], writes=["ya%d" % y3])
                        op("pool", lambda e: e.tensor_tensor(out=ya[y3][:], in0=ya[y3][:], in1=ggL[b][:], op=ALU.mult), reads=["ggL%d" % b], writes=["ya%d" % y3])
                        op("act", lambda e: e.activation(out=sq[:], in_=ya[y3][:], func=AF.Square), reads=["ya%d" % y3], writes=["sqa"])
                        for c in range(4):
                            op("pe", lambda e: e.matmul(pSs[b][:, 0:TL], lhsT=ones_f[:], rhs=sq[:, c, :], start=(c == 0), stop=(c == 3)),
                               reads=["ones_f", "sqa"], writes=["pSa%d" % b])

                    def b2a(oi):
                        col0, is_ctx = TILES[order[oi]]
                        b = oi % 2
                        if last and is_ctx:
                            return
                        op("dve", lambda e: e.tensor_scalar(out=rr[b][:], in0=pSs[b][:, 0:TL], scalar1=1.0 / DA, scalar2=EPS, op0=ALU.mult, op1=ALU.add),
                           writes=["pSa%d" % b, "rra%d" % b])
                        rsqrt(rrs[b][:], rr[b][:], rrt[b][:], "rrsa%d" % b, "rra%d" % b, "rrta%d" % b, iters=2)

                    def b2b(oi):
                        col0, is_ctx = TILES[order[oi]]
                        b = oi % 2
                        y3 = oi % 3
                        if last and is_ctx:
                            return
                        for c in range(4):
                            op("dve", lambda e: e.scalar_tensor_tensor(out=yAT[b][:, c, :], in0=ya[y3][:, c, :], scalar=gmixc[:, c:c + 1], in1=rrs[b][:],
                                                                       op0=ALU.mult, op1=ALU.mult), reads=["ya%d" % y3, "gmixc", "rrsa%d" % b], writes=["yAT%d_%d" % (b, c)])
                        dma("sp", YTv[:, 0:4, col0:col0 + TL], yAT[b][:], reads=["yAT%d_%d" % (b, c) for c in range(4)], writes=["YT"])

                    load_x(0)
                    load_hg(0)
                    load_x(1)
                    b0(0)
                    for oi in range(nO + 2):
                        if oi + 1 < nO:
                            load_hg(oi + 1)
                            b0(oi + 1)
                        if oi < nO:
                            b1(oi)
                        if oi + 2 < nO:
                            load_x(oi + 2)
                        if 0 <= oi - 1 < nO:
                            b2a(oi - 1)
                        if 0 <= oi - 2 < nO:
                            b2b(oi - 2)
                    fw.barrier()
                    scope_.__exit__(None, None, None)
                if stop_after == "M2b" and l == 0:
                    return nc

                m3_tiles = LTILES[1:] if last else LTILES
                wdn = sb(el, "wdn", [128, NF, D], BF16)
                with ExitStack() as e4:
                    e4.enter_context(nc.named_scope("M3a_l%d" % l))
                    wout = sb(e4, "wout", [128, 8, D], BF16)
                    for k in range(8):
                        dma("pool", wout[:, k, :], wout_d[l, k * 128:(k + 1) * 128, :], writes=["wout"])
                    for f in range(NF):
                        dma("pool", wdn[:, f, :], wdn_d[l, f * 128:(f + 1) * 128, :], writes=["wdn"])
                    gate = sb(e4, "gate1", [128, 2, D])
                    lng = sb(e4, "ln1g", [128, D])
                    lnb = sb(e4, "ln1b", [128, D])
                    for s in range(2):
                        dma("sp", gate[:, s, :], MODS[s:s + 1, 2 * D:3 * D].partition_broadcast(128), reads=["MODS"], writes=["gate1"])
                    dma("sp", lng[:], ln1g_d[l:l + 1, :].partition_broadcast(128), writes=["ln1g"])
                    dma("sp", lnb[:], ln1b_d[l:l + 1, :].partition_broadcast(128), writes=["ln1b"])
                    op("dve", lambda e: e.tensor_scalar(out=gate[:], in0=gate[:], scalar1=1.0 / ALPHA, scalar2=None, op0=ALU.mult), writes=["gate1"])
                    rmask = sb(e4, "rmask", [128, 2])
                    dma("sp", rmask[:], rmask_d[:, :], writes=["rmask"])
                    woutL = sb(e4, "woutL", [128, 8, D], BF16)
                    wsel = [sb(e4, "wsel%d" % r_, [128, 6, D], BF16) for r_ in range(2)]
                    woutC = sb(e4, "woutC", [128, 8, D], BF16) if not last else None
                    AC = (0, 1, 2, 3, 6, 7)
                    for k in range(8):
                        op("dve", lambda e: e.tensor_tensor(out=woutL[:, k, :], in0=wout[:, k, :], in1=gate[:, 0, :], op=ALU.mult),
                           reads=["wout", "gate1"], writes=["woutL"])
                        if not last:
                            op("pool", lambda e: e.tensor_tensor(out=woutC[:, k, :], in0=wout[:, k, :], in1=gate[:, 1, :], op=ALU.mult),
                               reads=["wout", "gate1"], writes=["woutC"])
                    for r_ in range(2):
                        for j, k in enumerate(AC):
                            if r_ == 0:
                                op("dve", lambda e: e.tensor_scalar(out=wsel[r_][:, j, :], in0=woutL[:, k, :], scalar1=rmask[:, r_:r_ + 1], scalar2=None, op0=ALU.mult),
                                   reads=["woutL", "rmask"], writes=["wsel%d_%d" % (r_, j)])
                            else:
                                op("act", lambda e: e.activation(out=wsel[r_][:, j, :], in_=woutL[:, k, :], func=AF.Identity, scale=rmask[:, r_:r_ + 1]),
                                   reads=["woutL", "rmask"], writes=["wsel%d_%d" % (r_, j)])
                    yTb = [sb(e4, "yTb%d" % i, [128, 2, TL], BF16) for i in range(3)]
                    yC = [[sb(e4, "yC%d_%d" % (i, r_), [128, 6, TL], BF16) for r_ in range(2)] for i in range(3)]
                    xt = [sb(e4, "xt%d" % i, [128, 2, D]) for i in range(2)]
                    rt = [sb(e4, "rt%d" % i, [128, 2, D]) for i in range(2)]
                    lns3 = [ln_scr(e4, "l3a"), ln_scr(e4, "l3b")]
                    po = [ps(e4, "po%d" % i, [128, 512]) for i in range(8)]
                    n3 = len(m3_tiles)

                    def loadY(ti):
                        col0, is_ctx = m3_tiles[ti]
                        y = ti % 3
                        dma("sp", yTb[y][:], YBLv[:, :, col0:col0 + TL], reads=["YBL"], writes=["yTb%d" % y])
                        if is_ctx:
                            dma("sp", yC[y][0][:, 0:4, :], YTv[:, 0:4, T:T + TL], reads=["YT"], writes=["yC%d_0" % y])
                            dma("sp", yC[y][0][:, 4:6, :], YTv[:, 6:8, T:T + TL], reads=["YT"], writes=["yC%d_0" % y])
                        else:
                            for r_ in range(2):
                                g0 = r_ * TLOC + col0
                                dma("sp", yC[y][r_][:, 0:4, :], YTv[:, 0:4, g0:g0 + TL], reads=["YT"], writes=["yC%d_%d" % (y, r_)])
                                dma("sp", yC[y][r_][:, 4:6, :], YTv[:, 6:8, g0:g0 + TL], reads=["YT"], writes=["yC%d_%d" % (y, r_)])

                    def loadX(ti):
                        col0, is_ctx = m3_tiles[ti]
                        b = ti % 2
                        load_resid(l, col0, is_ctx, xt[b], "xt%d" % b, None, None, from_xb=True)

                    def mm3(ti):
                        col0, is_ctx = m3_tiles[ti]
                        b = ti % 2
                        y = ti % 3
                        for s in range(2):
                            for hf_ in range(2):
                                p_, kp_ = po[4 * b + 2 * s + hf_], "po%d" % (4 * b + 2 * s + hf_)
                                cs = slice(hf_ * 512, (hf_ + 1) * 512)
                                ts_ = slice(s * 128, (s + 1) * 128)
                                steps = []
                                wB = woutC if is_ctx else woutL
                                for j in range(2):
                                    steps.append((yTb[y][:, j, ts_], wB[:, 4 + j, cs], ["yTb%d" % y, "woutC" if is_ctx else "woutL"]))
                                if is_ctx:
                                    for j, k in enumerate(AC):
                                        steps.append((yC[y][0][:, j, ts_], woutC[:, k, cs], ["yC%d_0" % y, "woutC"]))
                                else:
                                    for r_ in range(2):
                                        for j in range(6):
                                            steps.append((yC[y][r_][:, j, ts_], wsel[r_][:, j, cs], ["yC%d_%d" % (y, r_), "wsel%d_%d" % (r_, j)]))
                                for n_, (lh, rh, rd) in enumerate(steps):
                                    op("pe", lambda e: e.matmul(p_[:], lhsT=lh, rhs=rh, start=(n_ == 0), stop=(n_ == len(steps) - 1)), reads=rd, writes=[kp_])

                    def res3(ti):
                        b = ti % 2
                        for s in range(2):
                            for hf_ in range(2):
                                p_, kp_ = po[4 * b + 2 * s + hf_], "po%d" % (4 * b + 2 * s + hf_)
                                op("dve", lambda e: e.tensor_tensor(out=rt[b][:, s, hf_ * 512:(hf_ + 1) * 512], in0=p_[:],
                                                                    in1=xt[b][:, s, hf_ * 512:(hf_ + 1) * 512], op=ALU.add),
                                   reads=["xt%d" % b], writes=[kp_, "rt%d_%d" % (b, s)])

                    def stats3(ti):
                        b = ti % 2
                        ln_stats(rt[b], ["rt%d_0" % b, "rt%d_1" % b], lns3[b], EPS_POST, iters=2)

                    def norm3(ti):
                        col0, is_ctx = m3_tiles[ti]
                        b = ti % 2
                        mv, rs, kk = lns3[b]["mv"], lns3[b]["rs"], lns3[b]["k"]
                        for s in range(2):
                            kr = "rt%d_%d" % (b, s)
                            op("dve", lambda e: e.tensor_scalar(out=rt[b][:, s, :], in0=rt[b][:, s, :], scalar1=mv[:, s, 0:1],
                                                                scalar2=rs[:, s:s + 1], op0=ALU.subtract, op1=ALU.mult),
                               reads=[kk + "mv", kk + "rs"], writes=[kr])
                            op("dve", lambda e: e.tensor_tensor(out=rt[b][:, s, :], in0=rt[b][:, s, :], in1=lng[:], op=ALU.mult), reads=["ln1g"], writes=[kr])
                            op("pool", lambda e: e.tensor_tensor(out=rt[b][:, s, :], in0=rt[b][:, s, :], in1=lnb[:], op=ALU.add), reads=["ln1b"], writes=[kr])

                    def store3(ti):
                        col0, is_ctx = m3_tiles[ti]
                        b = ti % 2
                        dma("sp", XB[col0:col0 + TL, :].rearrange("(s p) d -> p s d", p=128), rt[b][:], reads=["rt%d_0" % b, "rt%d_1" % b], writes=["XB"])

                    for i_ in range(min(3, n3)):
                        loadY(i_)
                    for i_ in range(min(2, n3)):
                        loadX(i_)
                    mm3(0)
                    res3(0)
                    stats3(0)
                    for ti in range(n3):
                        if ti + 3 < n3:
                            loadY(ti + 3)
                        if ti + 2 < n3:
                            loadX(ti + 2)
                        if ti >= 1:
                            store3(ti - 1)
                        if ti + 1 < n3:
                            mm3(ti + 1)
                            res3(ti + 1)
                            stats3(ti + 1)
                        norm3(ti)
                    store3(n3 - 1)
                    fw.barrier()
                if stop_after == "M3a" and l == 0:
                    return nc

                with ExitStack() as e5:
                    e5.enter_context(nc.named_scope("M3b_l%d" % l))
                    wup = sb(e5, "wup", [128, 8, 2 * DFF], BF16)
                    for k in range(8):
                        for c0 in range(0, 2 * DFF, 2048):
                            c1 = min(c0 + 2048, 2 * DFF)
                            dma("pool", wup[:, k, c0:c1], wup_d[l, k * 128:(k + 1) * 128, c0:c1], writes=["wup"])
                    gate = sb(e5, "gate2", [128, 2, D])
                    lng = sb(e5, "ln2g", [128, D])
                    lnb = sb(e5, "ln2b", [128, D])
                    for s in range(2):
                        dma("sp", gate[:, s, :], MODS[s:s + 1, 5 * D:6 * D].partition_broadcast(128), reads=["MODS"], writes=["gate2"])
                    dma("sp", lng[:], ln2g_d[l:l + 1, :].partition_broadcast(128), writes=["ln2g"])
                    dma("sp", lnb[:], ln2b_d[l:l + 1, :].partition_broadcast(128), writes=["ln2b"])
                    op("dve", lambda e: e.tensor_scalar(out=gate[:], in0=gate[:], scalar1=1.0 / ALPHA, scalar2=None, op0=ALU.mult), writes=["gate2"])
                    xt = [sb(e5, "xu%d" % i, [128, 2, D]) for i in range(2)]
                    xh = sb(e5, "xhu", [128, 2, D])
                    h2T = [sb(e5, "h2T%d" % i, [128, 8, TL], BF16) for i in range(2)]
                    actT = sb(e5, "actT", [128, NF, TL], BF16)
                    sg = [sb(e5, "sg%d" % i, [128, TL]) for i in range(2)]
                    lns = ln_scr(e5, "l5")
                    lns2 = ln_scr(e5, "l6")
                    tp0 = ps(e5, "tq0", [128, 2, TL]); tp1 = ps(e5, "tq1", [128, 2, TL])
                    tps = [(tp0, "tq0"), (tp1, "tq1")]
                    pu = [ps(e5, "pu%d" % i, [128, 2, TL]) for i in range(2)]
                    pd = [ps(e5, "pd%d" % i, [128, 512]) for i in range(4)]

                    def load5(ti, b):
                        col0, is_ctx = m3_tiles[ti]
                        dma("sp", xt[b][:], XB[col0:col0 + TL, :].rearrange("(s p) d -> p s d", p=128), reads=["XB"], writes=["xu%d" % b])

                    su = [sb(e5, "su%d" % i, [128, TL]) for i in range(2)]
                    n5 = len(m3_tiles)
                    folded = [False]

                    def fold_gate():
                        for f in range(NF):
                            op("dve" if f % 2 == 0 else "pool",
                               lambda e: e.tensor_tensor(out=wdn[:, f, :], in0=wdn[:, f, :], in1=gate[:, 0, :], op=ALU.mult), reads=["gate2"], writes=["wdn"])
                        folded[0] = True

                    def stA(ti):
                        b = ti % 2
                        ln_tile(xt[b], "xu%d" % b, xh, "xhu", lns, EPS, eng="dve", iters=2)

                    def stT(ti):
                        col0, is_ctx = m3_tiles[ti]
                        b = ti % 2
                        transpose_mod(xh, "xhu", h2T[b], "h2T%d" % b, tps, modc, 1 if is_ctx else 0, 4, 3)

                    def stU(ti):
                        b = ti % 2
                        for f in range(NF):
                            p_, kp_ = pu[f % 2], "pu%d" % (f % 2)
                            for j in range(2):
                                c0 = j * DFF + f * 128
                                for k in range(8):
                                    op("pe", lambda e: e.matmul(p_[:, j, :], lhsT=wup[:, k, c0:c0 + 128], rhs=h2T[b][:, k, :], start=(k == 0), stop=(k == 7)),
                                       reads=["wup", "h2T%d" % b], writes=[kp_])
                            op("act", lambda e: e.activation(out=sg[f % 2][:], in_=p_[:, 0, :], func=AF.Silu), writes=[kp_, "sg%d" % (f % 2)])
                            op("act", lambda e: e.activation(out=su[f % 2][:], in_=p_[:, 1, :], func=AF.Identity), writes=[kp_, "su%d" % (f % 2)])
                            op("pool", lambda e: e.tensor_tensor(out=actT[:, f, :], in0=su[f % 2][:], in1=sg[f % 2][:], op=ALU.mult),
                               reads=["sg%d" % (f % 2), "su%d" % (f % 2)], writes=["actT"])

                    def stD(ti):
                        for s in range(2):
                            for hf_ in range(2):
                                p_, kp_ = pd[2 * s + hf_], "pd%d" % (2 * s + hf_)
                                for f in range(NF):
                                    op("pe", lambda e: e.matmul(p_[:], lhsT=actT[:, f, s * 128:(s + 1) * 128], rhs=wdn[:, f, hf_ * 512:(hf_ + 1) * 512],
                                                                start=(f == 0), stop=(f == NF - 1)), reads=["wdn", "actT"], writes=[kp_])

                    def stE(ti):
                        col0, is_ctx = m3_tiles[ti]
                        b = ti % 2
                        kx = "xu%d" % b
                        for s in range(2):
                            for hf_ in range(2):
                                p_, kp_ = pd[2 * s + hf_], "pd%d" % (2 * s + hf_)
                                cs = slice(hf_ * 512, (hf_ + 1) * 512)
                                if folded[0]:
                                    op("dve", lambda e: e.tensor_tensor(out=xt[b][:, s, cs], in0=p_[:], in1=xt[b][:, s, cs], op=ALU.add), writes=[kp_, kx])
                                else:
                                    op("dve", lambda e: e.tensor_tensor(out=xh[:, s, cs], in0=p_[:], in1=gate[:, 1 if is_ctx else 0, cs], op=ALU.mult),
                                       reads=["gate2"], writes=[kp_, "xhu"])
                        if not folded[0]:
                            op("pool", lambda e: e.tensor_tensor(out=xt[b][:], in0=xt[b][:], in1=xh[:], op=ALU.add), reads=["xhu"], writes=[kx])
                        ln_tile(xt[b], kx, xt[b], kx, lns2, EPS_POST, eng="dve", iters=3)
                        for s in range(2):
                            op("dve", lambda e: e.tensor_tensor(out=xt[b][:, s, :], in0=xt[b][:, s, :], in1=lng[:], op=ALU.mult), reads=["ln2g"], writes=[kx])
                            op("dve", lambda e: e.tensor_tensor(out=xt[b][:, s, :], in0=xt[b][:, s, :], in1=lnb[:], op=ALU.add), reads=["ln2b"], writes=[kx])
                        dst = out_d[col0:col0 + TL, :] if last else XB[col0:col0 + TL, :]
                        dma("sp", dst.rearrange("(s p) d -> p s d", p=128), xt[b][:], reads=[kx], writes=["XB"])

                    load5(0, 0)
                    if n5 > 1:
                        load5(1, 1)
                    if not m3_tiles[0][1]:
                        fold_gate()
                    stA(0)
                    stT(0)
                    for ti in range(n5):
                        stU(ti)
                        if ti + 1 < n5:
                            stA(ti + 1)
                        stD(ti)
                        if ti + 1 < n5:
                            stT(ti + 1)
                        stE(ti)
                        if m3_tiles[ti][1]:
                            fold_gate()
                        if ti + 2 < n5:
                            load5(ti + 2, ti % 2)
                    fw.barrier()
                if stop_after == "M3b" and l == 0:
                    return nc
        fw.wait_all("sp")
    return nc


def host_consts():
    c = {}
    c["ident"] = np.eye(128, dtype=np.float32)
    l1 = np.arange(128)[:, None].astype(np.float64)
    k1 = np.arange(128)[None, :].astype(np.float64)
    a = 2 * np.pi * l1 * k1 / 128.0
    c["t1"] = np.stack([np.cos(a), -np.sin(a), -np.cos(a)], 1).astype(np.float32).astype(ml_dtypes.bfloat16)
    l2 = np.arange(64)[:, None, None].astype(np.float64)
    kk1 = np.arange(128)[None, :, None].astype(np.float64)
    kk2 = np.arange(64)[None, None, :].astype(np.float64)
    ang = 2 * np.pi * ((kk1 + 128 * kk2) * l2 % T) / T
    c["tw"] = np.concatenate([np.cos(ang), np.sin(ang)], 0).reshape(128, T).astype(np.float32).astype(ml_dtypes.bfloat16)
    p = np.arange(128)[:, None, None].astype(np.float64)
    t = np.arange(2)[None, :, None].astype(np.float64)
    k = np.arange(256)[None, None, :].astype(np.float64)
    a2 = 2 * np.pi * (((128 * t + p) * k) % 256) / 256.0
    c["c256"] = np.stack([np.cos(a2), -np.sin(a2)], 2).astype(np.float32).astype(ml_dtypes.bfloat16)
    j = np.arange(64)[:, None].astype(np.float64)
    ch = np.arange(64)[None, :].astype(np.float64)
    a3 = 2 * np.pi * j * ch / 64.0
    cs = np.zeros((64, 2, 2, 128), np.float64)
    for gl in range(2):
        cs[:, 0, gl, gl * 64:(gl + 1) * 64] = np.cos(a3)
        cs[:, 1, gl, gl * 64:(gl + 1) * 64] = np.sin(a3)
    c["cspad"] = cs.astype(np.float32)
    quarter = D // 4
    freqs = 10000.0 ** (-np.arange(quarter, dtype=np.float32) / np.float32(quarter))
    r = np.repeat(np.arange(T // 64, dtype=np.float32), 64)
    col = np.tile(np.arange(64, dtype=np.float32), T // 64)

    def enc(pv):
        an = pv[:, None].astype(np.float32) * freqs[None, :].astype(np.float32)
        return np.concatenate([np.sin(an), np.cos(an)], -1)

    c["pos"] = np.concatenate([enc(r), enc(col)], -1).astype(np.float32)
    return c


_WNAMES = ["w_mod", "b_mod", "w_in", "conv_w", "conv_b", "lru_wa", "lru_ba", "lru_wx", "lru_bx", "lru_lam", "sg_ws", "sg_b",
           "fourier_w", "g_mix", "w_out", "ln1_g", "ln1_b", "w_up", "w_down", "ln2_g", "ln2_b"]


def make_in_maps(inputs, n_cores=N_CORES):
    consts = host_consts()
    pos = consts.pop("pos")
    shared = {n: np.ascontiguousarray(np.asarray(inputs[n], dtype=np.float32)) for n in _WNAMES}
    shared.update(consts)
    x = np.asarray(inputs["x"], dtype=np.float32)
    c = np.asarray(inputs["c"], dtype=np.float32)
    ctx = np.asarray(inputs["ctx"], dtype=np.float32)
    c_ctx = np.asarray(inputs["c_ctx"], dtype=np.float32)
    maps = []
    for core in range(n_cores):
        b, h = core // 2, core % 2
        m = dict(shared)
        m["x"] = np.ascontiguousarray(x[b, h * TLOC:(h + 1) * TLOC])
        m["pos"] = np.ascontiguousarray(pos[h * TLOC:(h + 1) * TLOC])
        m["ctx"] = np.ascontiguousarray(ctx[b])
        m["cc"] = np.ascontiguousarray(np.stack([c[b], c_ctx], 0))
        rm = np.zeros((128, 2), np.float32)
        rm[:, h] = 1.0
        m["rmask"] = rm
        maps.append(m)
    return maps


def kernel(**inputs):
    nc = build_program()
    maps = make_in_maps(inputs)
    res = run_bass_kernel_spmd(nc, maps, core_ids=list(range(N_CORES)))
    outs = [np.asarray(r["out"], dtype=np.float32) for r in res.results]
    return np.stack([np.concatenate([outs[2 * b], outs[2 * b + 1]], 0) for b in range(N_CORES // 2)], 0)
```

```python
import math
from contextlib import ExitStack

import numpy as np
import ml_dtypes
import concourse.bass as bass
import concourse.mybir as mybir
from concourse.bass_utils import run_bass_kernel_spmd

F32 = mybir.dt.float32
BF16 = mybir.dt.bfloat16
I32 = mybir.dt.int32
ALU = mybir.AluOpType
AF = mybir.ActivationFunctionType

D = 1024
T = 8192
TC = 256
TT = T + TC
TL = 256
NT = T // TL
DEPTH = 2
DA, DB, DC = 512, 256, 256
DIN = 1792
DFF = 2816
NF = DFF // 128
EPS = 1e-6
ALPHA = (2 * DEPTH) ** 0.25
EPS_POST = EPS / (ALPHA * ALPHA)
N_CORES = 8
TLOC = T // 2
NTL = TLOC // TL

SAME_ENG_SYNC = True
N_DMA_SEMS = 40


class FW:
    def __init__(self, nc, es):
        self.nc = nc
        self.es = es
        self.engs = {"pe": nc.tensor, "act": nc.scalar, "dve": nc.vector, "pool": nc.gpsimd, "sp": nc.sync}
        self.sems = {}
        self.cnt = {}
        for e in self.engs:
            self.sems[e] = es.enter_context(nc.semaphore("s_" + e))
            self.cnt[e] = 0
        self.dsem = {}
        for q in ("sp", "pool"):
            lst = []
            for i in range(N_DMA_SEMS):
                key = "d_%s_%d" % (q, i)
                self.sems[key] = es.enter_context(nc.semaphore(key))
                self.cnt[key] = 0
                lst.append(key)
            self.dsem[q] = [lst, 0]
        self.seen = {e: {} for e in self.engs}
        self.lastw = {}
        self.readers = {}
        self.ninst = 0

    def _wait(self, eng, ev):
        sk, v, prod = ev
        if prod == "pe" and eng == "pe":
            return
        if prod == eng and not SAME_ENG_SYNC:
            return
        if self.seen[eng].get(sk, 0) >= v:
            return
        self.engs[eng].wait_ge(self.sems[sk], v)
        self.seen[eng][sk] = v

    def _deps(self, eng, reads, writes):
        for k in reads:
            ev = self.lastw.get(k)
            if ev is not None:
                self._wait(eng, ev)
        for k in writes:
            ev = self.lastw.get(k)
            if ev is not None:
                self._wait(eng, ev)
            for ev in list(self.readers.get(k, {}).values()):
                self._wait(eng, ev)

    def _record(self, ev, reads, writes):
        for k in writes:
            self.lastw[k] = ev
            self.readers[k] = {}
        for k in reads:
            if k in writes:
                continue
            self.readers.setdefault(k, {})[ev[0]] = ev

    def op(self, eng, fn, reads=(), writes=()):
        self._deps(eng, reads, writes)
        inst = fn(self.engs[eng])
        self.cnt[eng] += 1
        inst.then_inc(self.sems[eng], 1)
        ev = (eng, self.cnt[eng], eng)
        self._record(ev, reads, writes)
        self.ninst += 1
        return ev

    def dma(self, q, out, in_, reads=(), writes=(), **kw):
        self._deps(q, reads, writes)
        lst, idx = self.dsem[q]
        sk = lst[idx % len(lst)]
        self.dsem[q][1] = idx + 1
        if self.cnt[sk] > 0:
            self._wait(q, (sk, self.cnt[sk], "dma"))
        inst = self.engs[q].dma_start(out=out, in_=in_, **kw)
        self.cnt[sk] += 16
        inst.then_inc(self.sems[sk], 16)
        ev = (sk, self.cnt[sk], "dma")
        self._record(ev, reads, writes)
        self.ninst += 1
        return ev

    def barrier(self):
        for e in self.engs:
            for e2 in ("pe", "act", "dve", "pool"):
                if e2 != e and self.cnt[e2] > 0:
                    self._wait(e, (e2, self.cnt[e2], e2))
            if self.cnt.get("cc", 0) > 0:
                self._wait(e, ("cc", self.cnt["cc"], "dma"))
            for q in self.dsem:
                for sk in self.dsem[q][0]:
                    if self.cnt[sk] > 0:
                        self._wait(e, (sk, self.cnt[sk], "dma"))
        self.lastw.clear()
        self.readers.clear()

    def collective(self, kind, in_ap, out_ap, groups, reads, writes):
        self._deps("pool", reads, writes)
        if "cc" not in self.sems:
            self.sems["cc"] = self.es.enter_context(self.nc.semaphore("s_cc"))
            self.cnt["cc"] = 0
        inst = self.nc.gpsimd.collective_compute(kind, ALU.bypass, replica_groups=groups, ins=[in_ap], outs=[out_ap])
        self.cnt["cc"] += 1
        inst.then_inc(self.sems["cc"], 1)
        ev = ("cc", self.cnt["cc"], "dma")
        self._record(ev, reads, writes)
        return ev

    def wait_all(self, eng="sp"):
        for ev in list(self.lastw.values()):
            self._wait(eng, ev)
        for d in list(self.readers.values()):
            for ev in list(d.values()):
                self._wait(eng, ev)
        for q in self.dsem:
            for sk in self.dsem[q][0]:
                if self.cnt[sk] > 0:
                    self._wait(eng, (sk, self.cnt[sk], "dma"))
        for e in ("pe", "act", "dve", "pool"):
            if self.cnt[e] > 0:
                self._wait(eng, (e, self.cnt[e], e))


def build_program(stop_after=None):
    nc = bass.Bass("TRN2", target_bir_lowering=False, num_devices=8)

    def din(name, shape, dt=F32):
        return nc.dram_tensor(name, list(shape), dt, kind="ExternalInput").ap()

    dbg = stop_after is not None

    CC_BUFS = ("XAL", "GGL", "ZL", "XAF", "GGF", "ZF")

    def dscr(name, shape, dt=F32):
        ext = dbg and name not in CC_BUFS
        return nc.dram_tensor(name, list(shape), dt, kind="ExternalOutput" if ext else "Internal").ap()

    x_d = din("x", [TLOC, D])
    ctx_d = din("ctx", [TC, D])
    cc_d = din("cc", [2, D])
    pos_d = din("pos", [TLOC, D])
    rmask_d = din("rmask", [128, 2])
    wmod_d = din("w_mod", [DEPTH, D, 6 * D])
    bmod_d = din("b_mod", [DEPTH, 6 * D])
    win_d = din("w_in", [DEPTH, D, DIN])
    convw_d = din("conv_w", [DEPTH, 4, DA])
    convb_d = din("conv_b", [DEPTH, DA])
    wa_d = din("lru_wa", [DEPTH, 2, 8, 64, 64])
    ba_d = din("lru_ba", [DEPTH, 2, DA])
    wx_d = din("lru_wx", [DEPTH, 2, 8, 64, 64])
    bx_d = din("lru_bx", [DEPTH, 2, DA])
    lam_d = din("lru_lam", [DEPTH, 2, DA])
    ws_d = din("sg_ws", [DEPTH, 4, 128, 128])
    sgb_d = din("sg_b", [DEPTH, 4, 128])
    wf_d = din("fourier_w", [DEPTH, 4, 64, 64])
    gmix_d = din("g_mix", [DEPTH, D])
    wout_d = din("w_out", [DEPTH, D, D])
    ln1g_d = din("ln1_g", [DEPTH, D])
    ln1b_d = din("ln1_b", [DEPTH, D])
    wup_d = din("w_up", [DEPTH, D, 2 * DFF])
    wdn_d = din("w_down", [DEPTH, DFF, D])
    ln2g_d = din("ln2_g", [DEPTH, D])
    ln2b_d = din("ln2_b", [DEPTH, D])
    ident_d = din("ident", [128, 128])
    t1_d = din("t1", [128, 3, 128], BF16)
    tw_d = din("tw", [128, T], BF16)
    c256_d = din("c256", [128, 2, 2, 256], BF16)
    cspad_d = din("cspad", [64, 2, 2, 128])
    out_d = nc.dram_tensor("out", [TLOC, D], F32, kind="ExternalOutput").ap()

    XB = dscr("XB", [TLOC + TC, D])
    XAL = dscr("XAL", [DA, TLOC])
    GGL = dscr("GGL", [DA, TLOC])
    ZL = dscr("ZL", [DC, TLOC], BF16)
    XAF = dscr("XAF", [2 * DA, TLOC])
    GGF = dscr("GGF", [2 * DA, TLOC])
    ZF = dscr("ZF", [2 * DC, TLOC], BF16)
    XAC = dscr("XAC", [DA, TC])
    GGC = dscr("GGC", [DA, TC])
    YBL = dscr("YBL", [DB, TLOC + TC], BF16)
    XC = dscr("XC", [DA, TT])
    HF = dscr("HF", [DA, TT])
    YT = dscr("YT", [D, TT], BF16)
    GD = dscr("GD", [128, 2, 64, 256], BF16)
    MODS = dscr("MODS", [2, 6 * D])

    XCv = XC.rearrange("(c p) t -> p c t", p=128)
    XALv = XAL.rearrange("(c p) t -> p c t", p=128)
    GGLv = GGL.rearrange("(c p) t -> p c t", p=128)
    ZLv = ZL.rearrange("(c p) t -> p c t", p=128)
    XACv = XAC.rearrange("(c p) t -> p c t", p=128)
    GGCv = GGC.rearrange("(c p) t -> p c t", p=128)
    YBLv = YBL.rearrange("(c p) t -> p c t", p=128)
    XAFv = XAF.rearrange("(c r p) t -> r p c t", r=2, p=128)
    GGFv = GGF.rearrange("(c r p) t -> r p c t", r=2, p=128)
    ZFv = ZF.rearrange("(r c p) t -> r p c t", r=2, p=128)
    PAIRS = [[0, 1], [2, 3], [4, 5], [6, 7]]
    HFv = HF.rearrange("(c p) t -> p c t", p=128)
    YTv = YT.rearrange("(c p) t -> p c t", p=128)

    TILES = [(T, True)] + [(TL * i, False) for i in range(NT)]
    LTILES = [(TLOC, True)] + [(TL * i, False) for i in range(NTL)]
    import os as _os
    if _os.environ.get("DBG_NT"):
        LTILES = LTILES[:int(_os.environ["DBG_NT"])]

    with ExitStack() as es:
        fw = FW(nc, es)
        op = fw.op
        dma = fw.dma

        uid = [0]

        def sb(es_, name, shape, dt=F32):
            uid[0] += 1
            return es_.enter_context(nc.sbuf_tensor("%s_s%d" % (name, uid[0]), list(shape), dt))

        def ps(es_, name, shape, dt=F32):
            uid[0] += 1
            return es_.enter_context(nc.psum_tensor("%s_p%d" % (name, uid[0]), list(shape), dt))

        es.enter_context(nc.allow_non_contiguous_dma(reason="small strided parameter loads"))

        ident = sb(es, "ident", [128, 128])
        ones_f = sb(es, "ones_f", [128, 128])
        dma("sp", ident[:], ident_d[:, :], writes=["ident"])
        op("pool", lambda e: e.memset(ones_f[:], 1.0), writes=["ones_f"])

        def rsqrt(out, x, tmp, kout, kx, ktmp, eng="pool", iters=3):
            xi = x.bitcast(I32)
            oi = out.bitcast(I32)
            op("dve", lambda e: e.tensor_scalar(out=oi, in0=xi, scalar1=1, scalar2=None,
                                                op0=ALU.arith_shift_right), reads=[kx], writes=[kout])
            op("dve", lambda e: e.tensor_scalar(out=oi, in0=oi, scalar1=-1.0, scalar2=float(0x5F3759DF),
                                                op0=ALU.mult, op1=ALU.add), writes=[kout])
            for _ in range(iters):
                op(eng, lambda e: e.tensor_tensor(out=tmp, in0=x, in1=out, op=ALU.mult), reads=[kx, kout], writes=[ktmp])
                op(eng, lambda e: e.tensor_tensor(out=tmp, in0=tmp, in1=out, op=ALU.mult), reads=[kout], writes=[ktmp])
                op(eng, lambda e: e.tensor_scalar(out=tmp, in0=tmp, scalar1=-0.5, scalar2=1.5,
                                                  op0=ALU.mult, op1=ALU.add), writes=[ktmp])
                op(eng, lambda e: e.tensor_tensor(out=out, in0=out, in1=tmp, op=ALU.mult), reads=[ktmp], writes=[kout])

        def resid_rows(l, col0, is_ctx):
            if l == 0:
                return (ctx_d[0:TL, :] if is_ctx else x_d[col0:col0 + TL, :])
            return XB[col0:col0 + TL, :]

        def load_resid(l, col0, is_ctx, xt, kx, pt=None, kp=None, from_xb=False, save_xb=False):
            if from_xb and not is_ctx:
                src = XB[col0:col0 + TL, :]
            else:
                src = resid_rows(l, col0, is_ctx)
            dma("sp", xt[:], src.rearrange("(s p) d -> p s d", p=128), reads=(["XB"] if (from_xb or l > 0) else []), writes=[kx])
            if l == 0 and not is_ctx and not from_xb:
                dma("sp", pt[:], pos_d[col0:col0 + TL, :].rearrange("(s p) d -> p s d", p=128), writes=[kp])
                op("pool", lambda e: e.tensor_tensor(out=xt[:], in0=xt[:], in1=pt[:], op=ALU.add),
                   reads=[kp], writes=[kx])
                if save_xb:
                    dma("sp", XB[col0:col0 + TL, :].rearrange("(s p) d -> p s d", p=128), xt[:], reads=[kx], writes=["XB"])

        def ln_tile(xt, kx, xh, kxh, scr, eps, eng="pool", iters=3):
            st, mv, ve, rs, tmp = scr["st"], scr["mv"], scr["ve"], scr["rs"], scr["tmp"]
            kk = scr["k"]
            for s in range(2):
                for h in range(2):
                    op("dve", lambda e: e.bn_stats(out=st[:, s, h, :], in_=xt[:, s, h * 512:(h + 1) * 512]),
                       reads=[kx], writes=[kk + "st"])
                op("dve", lambda e: e.bn_aggr(out=mv[:, s, :], in_=st[:, s, :, :].rearrange("p a b -> p (a b)")),
                   reads=[kk + "st"], writes=[kk + "mv"])
            op("dve", lambda e: e.tensor_scalar(out=ve[:], in0=mv[:, :, 1], scalar1=float(eps), scalar2=None,
                                                op0=ALU.add), reads=[kk + "mv"], writes=[kk + "ve"])
            rsqrt(rs[:], ve[:], tmp[:], kk + "rs", kk + "ve", kk + "tmp", eng=eng, iters=iters)
            for s in range(2):
                op("dve", lambda e: e.tensor_scalar(out=xh[:, s, :], in0=xt[:, s, :], scalar1=mv[:, s, 0:1],
                                                    scalar2=rs[:, s:s + 1], op0=ALU.subtract, op1=ALU.mult),
                   reads=[kx, kk + "mv", kk + "rs"], writes=[kxh])

        def ln_stats(xt, kx, scr, eps, iters=3):
            st, mv, ve, rs, tmp = scr["st"], scr["mv"], scr["ve"], scr["rs"], scr["tmp"]
            kk = scr["k"]
            for s in range(2):
                for h in range(2):
                    op("dve", lambda e: e.bn_stats(out=st[:, s, h, :], in_=xt[:, s, h * 512:(h + 1) * 512]),
                       reads=(kx if isinstance(kx, list) else [kx]), writes=[kk + "st%d%d" % (s, h)])
                op("dve", lambda e: e.bn_aggr(out=mv[:, s, :], in_=st[:, s, :, :].rearrange("p a b -> p (a b)")),
                   reads=[kk + "st%d0" % s, kk + "st%d1" % s], writes=[kk + "mv"])
            op("dve", lambda e: e.tensor_scalar(out=ve[:], in0=mv[:, :, 1], scalar1=float(eps), scalar2=None,
                                                op0=ALU.add), reads=[kk + "mv"], writes=[kk + "ve"])
            rsqrt(rs[:], ve[:], tmp[:], kk + "rs", kk + "ve", kk + "tmp", iters=iters)

        def ln_apply(xt, kx, xh, kxh, scr):
            mv, rs = scr["mv"], scr["rs"]
            kk = scr["k"]
            for s in range(2):
                op("dve", lambda e: e.tensor_scalar(out=xh[:, s, :], in0=xt[:, s, :], scalar1=mv[:, s, 0:1],
                                                    scalar2=rs[:, s:s + 1], op0=ALU.subtract, op1=ALU.mult),
                   reads=[kx, kk + "mv", kk + "rs"], writes=[kxh])

        def ln_scr(es_, name):
            return dict(st=sb(es_, name + "st", [128, 2, 2, 6]), mv=sb(es_, name + "mv", [128, 2, 2]),
                        ve=sb(es_, name + "ve", [128, 2]), rs=sb(es_, name + "rs", [128, 2]),
                        tmp=sb(es_, name + "tmp", [128, 2]), k=name)

        def transpose_mod(xh, kxh, hT, khT, tps, modc, strm, jsc, jsh):
            for kp in range(4):
                tp, ktp = tps[kp % 2]
                for j in range(2):
                    k = 2 * kp + j
                    for s in range(2):
                        op("pe", lambda e: e.transpose(out=tp[:, j, s * 128:(s + 1) * 128],
                                                       in_=xh[:, s, k * 128:(k + 1) * 128], identity=ident[:]),
                           reads=[kxh, "ident"], writes=[ktp])
                for j in range(2):
                    k = 2 * kp + j
                    op("act", lambda e: e.activation(out=hT[:, k, :], in_=tp[:, j, :], func=AF.Identity,
                                                     scale=modc[:, strm, jsc, k:k + 1], bias=modc[:, strm, jsh, k:k + 1]),
                       reads=["modc"], writes=[ktp, khT])

        for l in range(DEPTH):
            last = (l == DEPTH - 1)
            with ExitStack() as el:
                with ExitStack() as ep:
                    ep.enter_context(nc.named_scope("P0_l%d" % l))
                    cct = sb(ep, "cct", [128, 8, 2])
                    sct = sb(ep, "sct", [128, 8, 2])
                    scbc = sb(ep, "scbc", [128, 8, 128])
                    bmbc = sb(ep, "bmbc", [128, 6 * D])
                    modbc = sb(ep, "modbc", [128, 6 * D])
                    wm = [sb(ep, "wm%d" % i, [128, 3072]) for i in range(3)]
                    pmod = [ps(ep, "pmod%d" % i, [128, 512]) for i in range(6)]
                    for s in range(2):
                        dma("sp", cct[:, :, s], cc_d[s, :].rearrange("(k p) -> p k", p=128), writes=["cct"])
                    dma("sp", bmbc[:], bmod_d[l:l + 1, :].partition_broadcast(128), writes=["bmbc"])
                    op("act", lambda e: e.activation(out=sct[:], in_=cct[:], func=AF.Tanh, scale=0.5), reads=["cct"], writes=["sct"])
                    op("dve", lambda e: e.tensor_scalar(out=sct[:], in0=sct[:], scalar1=0.5, scalar2=0.5, op0=ALU.mult, op1=ALU.add), writes=["sct"])
                    op("dve", lambda e: e.tensor_tensor(out=sct[:], in0=sct[:], in1=cct[:], op=ALU.mult), reads=["cct"], writes=["sct"])
                    for k in range(8):
                        for s in range(2):
                            op("dve", lambda e: e.tensor_scalar(out=scbc[:, k, 64 * s:64 * s + 64], in0=ones_f[:, 0:64],
                                                                scalar1=sct[:, k, s:s + 1], scalar2=None, op0=ALU.mult),
                               reads=["sct", "ones_f"], writes=["scbc"])
                    ld = 0
                    for half in range(2):
                        for k in range(8):
                            w = wm[ld % 3]
                            kw = "wm%d" % (ld % 3)
                            ld += 1
                            dma("sp", w[:], wmod_d[l, k * 128:(k + 1) * 128, half * 3072:(half + 1) * 3072], writes=[kw])
                            for n in range(6):
                                op("pe", lambda e: e.matmul(pmod[n][:], lhsT=scbc[:, k, :], rhs=w[:, n * 512:(n + 1) * 512],
                                                            start=(k == 0), stop=(k == 7)),
                                   reads=["scbc", kw], writes=["pmod%d" % n])
                        for n in range(6):
                            c0 = half * 3072 + n * 512
                            op("dve", lambda e: e.tensor_tensor(out=modbc[:, c0:c0 + 512], in0=pmod[n][:], in1=bmbc[:, c0:c0 + 512],
                                                                op=ALU.add), reads=["bmbc"], writes=["pmod%d" % n, "modbc"])
                    dma("sp", MODS[0:1, :], modbc[0:1, :], reads=["modbc"], writes=["MODS"])
                    dma("sp", MODS[1:2, :], modbc[64:65, :], reads=["modbc"], writes=["MODS"])
                    fw.barrier()

                modc = sb(el, "modc", [128, 2, 6, 8])
                gmixc = sb(el, "gmixc", [128, 8])
                for s in range(2):
                    for j in range(6):
                        dma("sp", modc[:, s, j, :], MODS[s, j * D:(j + 1) * D].rearrange("(k p) -> p k", p=128), reads=["MODS"], writes=["modc"])
                dma("sp", gmixc[:], gmix_d[l, :].rearrange("(k p) -> p k", p=128), writes=["gmixc"])
                for j in (1, 4):
                    op("dve", lambda e: e.tensor_scalar(out=modc[:, :, j, :], in0=modc[:, :, j, :], scalar1=1.0, scalar2=None,
                                                        op0=ALU.add), writes=["modc"])

                with ExitStack() as ez:
                    zT = sb(ez, "zT", [128, 2, T], BF16)
                    zTc = sb(ez, "zTc", [128, 2, TC], BF16)
                    with ExitStack() as e1:
                        e1.enter_context(nc.named_scope("M1_l%d" % l))
                        win = sb(e1, "win", [128, 8, DIN], BF16)
                        for k in range(8):
                            dma("pool", win[:, k, :], win_d[l, k * 128:(k + 1) * 128, :], writes=["win"])
                        wsT = sb(e1, "wsT", [128, 4, 128], BF16)
                        wsr = sb(e1, "wsr", [128, 4, 128])
                        bsT = sb(e1, "bsT", [128, 4])
                        dma("sp", wsr[:], ws_d[l].rearrange("h p q -> p h q"), writes=["wsr"])
                        dma("sp", bsT[:], sgb_d[l].rearrange("h p -> p h"), writes=["bsT"])
                        xt = [sb(e1, "xt%d" % i, [128, 2, D]) for i in range(2)]
                        pt = [sb(e1, "pt%d" % i, [128, 2, D]) for i in range(2)] if l == 0 else [None, None]
                        xh = sb(e1, "xh", [128, 2, D])
                        hT = [sb(e1, "hT%d" % i, [128, 8, TL], BF16) for i in range(2)]
                        lns = ln_scr(e1, "l1")
                        xaS = [sb(e1, "xaS%d" % i, [128, 4, TL]) for i in range(2)]
                        ggS = [sb(e1, "ggS%d" % i, [128, 4, TL]) for i in range(2)]
                        yBT = [sb(e1, "yBT%d" % i, [128, 2, TL], BF16) for i in range(2)]
                        zS = [sb(e1, "zS%d" % i, [128, 2, TL], BF16) for i in range(2)]
                        zB = [sb(e1, "zB%d" % i, [128, 512]) for i in range(2)]
                        yb = [sb(e1, "yb%d" % i, [128, 256]) for i in range(2)]
                        vh = sb(e1, "vh", [128, 256], BF16)
                        bst = sb(e1, "bst", [128, 4, 6])
                        bmv = sb(e1, "bmv", [128, 4, 2])
                        bve = sb(e1, "bve", [128, 4])
                        brs = sb(e1, "brs", [128, 4])
                        btmp = sb(e1, "btmp", [128, 4])
                        ssB = sb(e1, "ssB", [128, 2])
                        rB = sb(e1, "rB", [128, 2])
                        rsB = sb(e1, "rsB", [128, 2])
                        rtmp = sb(e1, "rtmp", [128, 2])
                        junk = sb(e1, "junk", [128, 256])
                        tp0 = ps(e1, "tp0", [128, 2, TL]); tp1 = ps(e1, "tp1", [128, 2, TL])
                        tps = [(tp0, "tp0"), (tp1, "tp1")]
                        pa = [ps(e1, "pa%d" % i, [128, 2, TL]) for i in range(2)]
                        pb = [ps(e1, "pb%d" % i, [128, 512]) for i in range(2)]
                        pss = ps(e1, "pss", [128, 512])
                        pyt = ps(e1, "pyt", [128, 2, TL])
                        for h in range(4):
                            op("pe", lambda e: e.transpose(out=pb[0][:, h * 128:(h + 1) * 128], in_=wsr[:, h, :], identity=ident[:]),
                               reads=["wsr", "ident"], writes=["pb0"])
                        op("dve", lambda e: e.tensor_copy(out=wsT[:].rearrange("p h q -> p (h q)"), in_=pb[0][:]), writes=["pb0", "wsT"])

                        tl = LTILES
                        nM = len(tl)
                        zBt = [[sb(e1, "zBt%d_%d" % (i, s_), [128, 512]) for s_ in range(2)] for i in range(2)]

                        def m1_load(ti):
                            nb = ti % 2
                            load_resid(l, tl[ti][0], tl[ti][1], xt[nb], "xt%d" % nb, pt[nb], "pt%d" % nb, save_xb=True)

                        def m1_A(ti):
                            b = ti % 2
                            ln_tile(xt[b], "xt%d" % b, xh, "xh", lns, EPS, eng="dve", iters=2)

                        def m1_T(ti):
                            col0, is_ctx = tl[ti]
                            b = ti % 2
                            transpose_mod(xh, "xh", hT[b], "hT%d" % b, tps, modc, 1 if is_ctx else 0, 1, 0)

                        def m1_P(ti):
                            col0, is_ctx = tl[ti]
                            b = ti % 2
                            khT = "hT%d" % b
                            pai = 0
                            only_xa = last and is_ctx
                            for grp in range(2 if only_xa else 4):
                                p_, kp_ = pa[pai % 2], "pa%d" % (pai % 2)
                                pai += 1
                                for j in range(2):
                                    oc = grp * 2 + j
                                    for k in range(8):
                                        op("pe", lambda e: e.matmul(p_[:, j, :], lhsT=win[:, k, oc * 128:(oc + 1) * 128], rhs=hT[b][:, k, :],
                                                                    start=(k == 0), stop=(k == 7)), reads=["win", khT], writes=[kp_])
                                if grp < 2:
                                    op("act", lambda e: e.activation(out=xaS[b][:, 2 * grp:2 * grp + 2, :], in_=p_[:], func=AF.Identity),
                                       writes=[kp_, "xaS%d" % b])
                                else:
                                    g2 = grp - 2
                                    op("act", lambda e: e.activation(out=ggS[b][:, 2 * g2:2 * g2 + 2, :], in_=p_[:], func=AF.Gelu),
                                       writes=[kp_, "ggS%d" % b])
                            dma("sp", XACv[:, :, :] if is_ctx else XALv[:, :, col0:col0 + TL], xaS[b][:], reads=["xaS%d" % b], writes=["XA"])
                            if only_xa:
                                return
                            dma("sp", GGCv[:, :, :] if is_ctx else GGLv[:, :, col0:col0 + TL], ggS[b][:], reads=["ggS%d" % b], writes=["GG"])
                            p_, kp_ = pa[pai % 2], "pa%d" % (pai % 2)
                            for j in range(2):
                                for k in range(8):
                                    op("pe", lambda e: e.matmul(p_[:, j, :], lhsT=win[:, k, 1536 + j * 128:1536 + (j + 1) * 128], rhs=hT[b][:, k, :],
                                                                start=(k == 0), stop=(k == 7)), reads=["win", khT], writes=[kp_])
                            if is_ctx:
                                op("act", lambda e: e.activation(out=zTc[:, :, :], in_=p_[:], func=AF.Identity), writes=[kp_, "zTc"])
                            else:
                                op("act", lambda e: e.activation(out=zS[b][:], in_=p_[:], func=AF.Identity), writes=[kp_, "zS%d" % b])
                                dma("sp", ZLv[:, :, col0:col0 + TL], zS[b][:], reads=["zS%d" % b], writes=["ZL"])

                        def m1_Bproj(ti):
                            col0, is_ctx = tl[ti]
                            if last and is_ctx:
                                return
                            b = ti % 2
                            for s in range(2):
                                p_, kp_ = pb[s], "pb%d" % s
                                for k in range(8):
                                    op("pe", lambda e: e.matmul(p_[:], lhsT=hT[b][:, k, s * 128:(s + 1) * 128], rhs=win[:, k, 1024:1536],
                                                                start=(k == 0), stop=(k == 7)), reads=["win", "hT%d" % b], writes=[kp_])
                                op("act", lambda e: e.activation(out=zBt[b][s][:], in_=p_[:], func=AF.Gelu), writes=[kp_, "zBt%d_%d" % (b, s)])

                        bmv8 = sb(e1, "bmv8", [128, 8, 2])
                        bve8 = sb(e1, "bve8", [128, 8])
                        brs8 = sb(e1, "brs8", [128, 8])
                        btmp8 = sb(e1, "btmp8", [128, 8])
                        bst8 = sb(e1, "bst8", [128, 8, 6])
                        vh2 = sb(e1, "vh2", [128, 2, 256], BF16)
                        ybp = [[sb(e1, "ybp%d_%d" % (i, s_), [128, 256]) for s_ in range(2)] for i in range(2)]
                        ssBp = [sb(e1, "ssBp%d" % i, [128, 2]) for i in range(2)]

                        def skipB(ti):
                            return last and tl[ti][1]

                        def m1_B1a(ti):
                            if skipB(ti):
                                return
                            b = ti % 2
                            for s in range(2):
                                zb, kzb = zBt[b][s], "zBt%d_%d" % (b, s)
                                for h in range(4):
                                    op("dve", lambda e: e.bn_stats(out=bst8[:, 4 * s + h, :], in_=zb[:, 256 + 64 * h:256 + 64 * h + 64]),
                                       reads=[kzb], writes=["bst8_%d" % (4 * s + h)])
                                for h in range(4):
                                    op("dve", lambda e: e.bn_aggr(out=bmv8[:, 4 * s + h, :], in_=bst8[:, 4 * s + h, :]),
                                       reads=["bst8_%d" % (4 * s + h)], writes=["bmv8"])
                            op("dve", lambda e: e.tensor_scalar(out=bve8[:], in0=bmv8[:, :, 1], scalar1=EPS, scalar2=None, op0=ALU.add),
                               reads=["bmv8"], writes=["bve8"])
                            rsqrt(brs8[:], bve8[:], btmp8[:], "brs8", "bve8", "btmp8", iters=2)
                            for s in range(2):
                                zb, kzb = zBt[b][s], "zBt%d_%d" % (b, s)
                                for h in range(4):
                                    op("dve", lambda e: e.tensor_scalar(out=vh2[:, s, 64 * h:64 * h + 64], in0=zb[:, 256 + 64 * h:256 + 64 * h + 64],
                                                                        scalar1=bmv8[:, 4 * s + h, 0:1], scalar2=brs8[:, 4 * s + h:4 * s + h + 1],
                                                                        op0=ALU.subtract, op1=ALU.mult),
                                       reads=[kzb, "bmv8", "brs8"], writes=["vh2_%d" % (4 * s + h)])

                        def m1_smm(ti):
                            if skipB(ti):
                                return
                            for s in range(2):
                                for h in range(4):
                                    op("pe", lambda e: e.matmul(pss[:, 256 * s + 64 * h:256 * s + 64 * h + 64], lhsT=wsT[:, h, :], rhs=vh2[:, s, 64 * h:64 * h + 64],
                                                                start=True, stop=True), reads=["wsT", "vh2_%d" % (4 * s + h)], writes=["pss"])

                        def m1_B1b(ti):
                            if skipB(ti):
                                return
                            b = ti % 2
                            for s in range(2):
                                zb, kzb = zBt[b][s], "zBt%d_%d" % (b, s)
                                for h in range(4):
                                    op("dve", lambda e: e.scalar_tensor_tensor(out=ybp[b][s][:, 64 * h:64 * h + 64], in0=pss[:, 256 * s + 64 * h:256 * s + 64 * h + 64],
                                                                               scalar=bsT[:, h:h + 1], in1=zb[:, 64 * h:64 * h + 64],
                                                                               op0=ALU.add, op1=ALU.mult),
                                       reads=["bsT", kzb], writes=["pss", "ybp%d_%d_%d" % (b, s, h)])
                            for s in range(2):
                                op("act", lambda e: e.activation(out=junk[:], in_=ybp[b][s][:], func=AF.Square, accum_out=ssBp[b][:, s:s + 1]),
                                   reads=["ybp%d_%d_%d" % (b, s, h) for h in range(4)], writes=["junk", "ssBp%d" % b])

                        def m1_B2(ti):
                            if skipB(ti):
                                return
                            col0, is_ctx = tl[ti]
                            b = ti % 2
                            op("dve", lambda e: e.tensor_scalar(out=rB[:], in0=ssBp[b][:], scalar1=1.0 / DB, scalar2=EPS, op0=ALU.mult, op1=ALU.add),
                               reads=["ssBp%d" % b], writes=["rB"])
                            rsqrt(rsB[:], rB[:], rtmp[:], "rsB", "rB", "rtmp", iters=2)
                            for s in range(2):
                                kyb = ["ybp%d_%d_%d" % (b, s, h) for h in range(4)]
                                op("dve", lambda e: e.tensor_scalar(out=ybp[b][s][:], in0=ybp[b][s][:], scalar1=rsB[:, s:s + 1], scalar2=None, op0=ALU.mult),
                                   reads=["rsB"], writes=kyb)
                                for c in range(2):
                                    op("pe", lambda e: e.transpose(out=pyt[:, c, s * 128:(s + 1) * 128], in_=ybp[b][s][:, c * 128:(c + 1) * 128],
                                                                   identity=ident[:]), reads=kyb + ["ident"], writes=["pyt"])
                            for c in range(2):
                                op("act", lambda e: e.activation(out=yBT[b][:, c, :], in_=pyt[:, c, :], func=AF.Identity, scale=gmixc[:, 4 + c:5 + c]),
                                   reads=["gmixc"], writes=["pyt", "yBT%d" % b])
                            dma("sp", YBLv[:, :, col0:col0 + TL], yBT[b][:], reads=["yBT%d" % b], writes=["YBL"])

                        m1_load(0)
                        if nM > 1:
                            m1_load(1)
                        m1_A(0)
                        m1_T(0)
                        for ti in range(nM + 2):
                            if ti < nM:
                                m1_P(ti)
                            if ti + 1 < nM:
                                m1_A(ti + 1)
                            if 2 <= ti:
                                m1_B2(ti - 2)
                            if ti + 1 < nM:
                                m1_T(ti + 1)
                            if 1 <= ti <= nM:
                                m1_B1a(ti - 1)
                            if ti < nM:
                                m1_Bproj(ti)
                            if 1 <= ti <= nM:
                                m1_smm(ti - 1)
                                m1_B1b(ti - 1)
                            if ti + 2 < nM:
                                m1_load(ti + 2)
                        fw.barrier()
                    fw.collective("AllGather", ZL[:, :], ZF[:, :], PAIRS, ["ZL"], ["ZF"])
                    fw.barrier()
                    for c_ in range(4):
                        fw.collective("AllGather", XAL[c_ * 128:(c_ + 1) * 128, :], XAF[c_ * 256:(c_ + 1) * 256, :], PAIRS, ["XA"], ["XAF"])
                    for c_ in range(4):
                        fw.collective("AllGather", GGL[c_ * 128:(c_ + 1) * 128, :], GGF[c_ * 256:(c_ + 1) * 256, :], PAIRS, ["GG"], ["GGF"])
                    if stop_after == "M1" and l == 0:
                        xafd = nc.dram_tensor("XAFd", [2 * DA, TLOC], F32, kind="ExternalOutput").ap()
                        zfd = nc.dram_tensor("ZFd", [2 * DC, TLOC], BF16, kind="ExternalOutput").ap()
                        dma("sp", xafd[:, :], XAF[:, :], writes=["xafd"])
                        dma("sp", zfd[:, :], ZF[:, :], writes=["zfd"])
                        fw.wait_all("sp")
                        return nc

                    if True:
                        with ExitStack() as e2:
                            e2.enter_context(nc.named_scope("DFT_l%d" % l))
                            t1 = sb(e2, "t1", [128, 3, 128], BF16)
                            tw = sb(e2, "tw", [128, 128, 64], BF16)
                            c256 = sb(e2, "c256", [128, 2, 2, 256], BF16)
                            cspad = sb(e2, "cspad", [64, 2, 2, 128])
                            wft = sb(e2, "wft", [64, 4, 64])
                            abd = sb(e2, "abd", [128, 2, 256], BF16)
                            abdc = sb(e2, "abdc", [128, 2, 256], BF16)
                            XTs = sb(e2, "XTs", [128, 2, T])
                            XTc = sb(e2, "XTc", [128, 2, TC])
                            Yp = [sb(e2, "Yp%d" % i, [128, 512], BF16) for i in range(2)]
                            Gs = [sb(e2, "Gs%d" % i, [128, 2, 8, 256], BF16) for i in range(2)]
                            Rb = [sb(e2, "Rb%d" % i, [128, 8, 256], BF16) for i in range(2)]
                            sq = sb(e2, "sq", [128, 2, TL])
                            rr = sb(e2, "rr", [128, TL])
                            rrs = sb(e2, "rrs", [128, TL])
                            rrt = sb(e2, "rrt", [128, TL])
                            yCT = [sb(e2, "yCT%d" % i, [128, 2, TL], BF16) for i in range(2)]
                            pY = [ps(e2, "pY%d" % i, [128, 512]) for i in range(2)]
                            pG = [ps(e2, "pG%d" % i, [128, 2, 256]) for i in range(2)]
                            pX = [ps(e2, "pX%d" % i, [128, 8, 64]) for i in range(2)]
                            pS = ps(e2, "pS", [128, 512])
                            pS1 = ps(e2, "pS1", [128, 512])
                            for r_ in range(2):
                                dma("sp", zT[:, :, r_ * TLOC:(r_ + 1) * TLOC], ZFv[r_], writes=["zT"])
                            dma("sp", t1[:], t1_d[:, :, :], writes=["t1"])
                            dma("sp", tw[:].rearrange("p a b -> p (a b)"), tw_d[:, :], writes=["tw"])
                            dma("sp", c256[:], c256_d[:, :, :, :], writes=["c256"])
                            dma("sp", cspad[:], cspad_d[:, :, :, :], writes=["cspad"])
                            dma("sp", wft[:], wf_d[l].rearrange("g j e -> j g e"), writes=["wft"])
                            for cc in range(2):
                                for pq in range(2):
                                    for gl in range(2):
                                        op("pe", lambda e: e.matmul(pY[0][:, pq * 128 + gl * 64:pq * 128 + gl * 64 + 64],
                                                                    lhsT=cspad[:, pq, gl, :], rhs=wft[:, 2 * cc + gl, :], start=True, stop=True),
                                           reads=["cspad", "wft"], writes=["pY0"])
                                op("act", lambda e: e.activation(out=abd[:, cc, :], in_=pY[0][:, 0:256], func=AF.Identity,
                                                                 scale=1.0 / math.sqrt(T * 64.0)), writes=["pY0", "abd"])
                                op("act", lambda e: e.activation(out=abdc[:, cc, :], in_=pY[0][:, 0:256], func=AF.Identity,
                                                                 scale=1.0 / math.sqrt(TC * 64.0)), writes=["pY0", "abdc"])

                            rr2 = [sb(e2, "rr2_%d" % i, [128, TL]) for i in range(2)]
                            rrs2 = [sb(e2, "rrs2_%d" % i, [128, TL]) for i in range(2)]
                            rrt2 = [sb(e2, "rrt2_%d" % i, [128, TL]) for i in range(2)]

                            def rmsc1(src, b):
                                op("act", lambda e: e.activation(out=sq[:], in_=src, func=AF.Square), reads=["XT"], writes=["sq"])
                                pS_ = pS if b == 0 else pS1
                                for c in range(2):
                                    op("pe", lambda e: e.matmul(pS_[:, 0:TL], lhsT=ones_f[:], rhs=sq[:, c, :], start=(c == 0), stop=(c == 1)),
                                       reads=["ones_f", "sq"], writes=["pS%d" % b])

                            def rmsc2(b):
                                pS_ = pS if b == 0 else pS1
                                op("act", lambda e: e.activation(out=rr2[b][:], in_=pS_[:, 0:TL], func=AF.Ln, scale=1.0 / DC, bias=EPS),
                                   writes=["pS%d" % b, "rr2_%d" % b])
                                op("act", lambda e: e.activation(out=rrs2[b][:], in_=rr2[b][:], func=AF.Exp, scale=-0.5),
                                   reads=["rr2_%d" % b], writes=["rrs2_%d" % b])

                            def rmsc3(src, col0, b):
                                for c in range(2):
                                    op("dve", lambda e: e.scalar_tensor_tensor(out=yCT[b][:, c, :], in0=src[:, c, :], scalar=gmixc[:, 6 + c:7 + c],
                                                                               in1=rrs2[b][:], op0=ALU.mult, op1=ALU.mult),
                                       reads=["XT", "gmixc", "rrs2_%d" % b], writes=["yCT%d_%d" % (b, c)])
                                dma("sp", YTv[:, 6:8, col0:col0 + TL], yCT[b][:], reads=["yCT%d_0" % b, "yCT%d_1" % b], writes=["YT"])

                            def rms_store_c(src, col0, b):
                                rmsc1(src, b)
                                rmsc2(b)
                                rmsc3(src, col0, b)

                            if not last:
                                Ypc = [sb(e2, "Ypc%d" % i, [128, 512], BF16) for i in range(2)]
                                for t in range(2):
                                    for cc in range(2):
                                        op("pe", lambda e: e.matmul(pY[t][:, cc * 256:(cc + 1) * 256], lhsT=zTc[:, cc, t * 128:(t + 1) * 128],
                                                                    rhs=abdc[:, cc, :], start=True, stop=True), reads=["zTc", "abdc"], writes=["pY%d" % t])
                                    op("act", lambda e: e.activation(out=Ypc[t][:], in_=pY[t][:], func=AF.Identity), writes=["pY%d" % t, "Ypc%d" % t])
                                for cc in range(2):
                                    n = 0
                                    for t in range(2):
                                        for pq in range(2):
                                            op("pe", lambda e: e.matmul(pG[cc][:].rearrange("p a b -> p (a b)")[:, 0:256],
                                                                        lhsT=Ypc[t][:, cc * 256 + pq * 128:cc * 256 + (pq + 1) * 128],
                                                                        rhs=c256[:, t, pq, :], start=(n == 0), stop=(n == 3)),
                                               reads=["Ypc%d" % t, "c256"], writes=["pG%d" % cc])
                                            n += 1
                                    op("dve", lambda e: e.tensor_copy(out=XTc[:, cc, :], in_=pG[cc][:].rearrange("p a b -> p (a b)")[:, 0:256]),
                                       writes=["pG%d" % cc, "XT"])
                                rms_store_c(XTc[:, :, :], T, 0)

                            zTv = zT[:].rearrange("p c (a b) -> p c b a", b=64)
                            GDv = GD
                            for l2 in range(64):
                                b = l2 % 2
                                for cc in range(2):
                                    op("pe", lambda e: e.matmul(pY[b][:, cc * 256:(cc + 1) * 256], lhsT=zTv[:, cc, l2, :], rhs=abd[:, cc, :],
                                                                start=True, stop=True), reads=["zT", "abd"], writes=["pY%d" % b])
                                op("act", lambda e: e.activation(out=Yp[b][:], in_=pY[b][:], func=AF.Identity), writes=["pY%d" % b, "Yp%d" % b])
                                Ypv = Yp[b][:].rearrange("p (c q j) -> p q c j", c=2, q=2)
                                combos = [(0, 0, 0), (0, 1, 1), (1, 0, 1), (1, 1, 2)]
                                for (ri, pq, ti_) in combos:
                                    op("pe", lambda e: e.matmul(pG[b][:, ri, :].rearrange("p (c j) -> p c j", c=2), lhsT=t1[:, ti_, :], rhs=Ypv[:, pq, :, :],
                                                                start=(pq == 0), stop=(pq == 1)), reads=["t1", "Yp%d" % b], writes=["pG%d" % b])
                                gb = (l2 // 8) % 2
                                op("dve", lambda e: e.tensor_copy(out=Gs[gb][:, :, l2 % 8, :], in_=pG[b][:]), writes=["pG%d" % b, "Gs%d" % gb])
                                if l2 % 8 == 7:
                                    l0 = l2 - 7
                                    dma("sp", GDv[:, :, l0:l0 + 8, :], Gs[gb][:], reads=["Gs%d" % gb], writes=["GD"])
                            XTv = XTs[:].rearrange("p c (k2 k1) -> p c k1 k2", k1=128)
                            for kb in range(16):
                                b = kb % 2
                                dma("sp", Rb[b][:], GD[kb * 8:(kb + 1) * 8, :, :, :].rearrange("k r l c -> (r l) k c"), reads=["GD"], writes=["Rb%d" % b])
                                for cc in range(2):
                                    px, kpx = pX[cc], "pX%d" % cc
                                    for r in range(8):
                                        op("pe", lambda e: e.matmul(px[:, r, :], lhsT=Rb[b][:, r, cc * 128:(cc + 1) * 128], rhs=tw[:, kb * 8 + r, :],
                                                                    start=True, stop=True), reads=["Rb%d" % b, "tw"], writes=[kpx])
                                    op("act" if cc == 0 else "dve",
                                       (lambda e: e.activation(out=XTv[:, cc, kb * 8:(kb + 1) * 8, :], in_=px[:], func=AF.Identity)) if cc == 0 else
                                       (lambda e: e.tensor_copy(out=XTv[:, cc, kb * 8:(kb + 1) * 8, :], in_=px[:])),
                                       writes=[kpx, "XT"])
                            for ti in range(NT + 2):
                                if ti < NT:
                                    rmsc1(XTs[:, :, ti * TL:(ti + 1) * TL], ti % 2)
                                if 0 <= ti - 1 < NT:
                                    rmsc2((ti - 1) % 2)
                                if 0 <= ti - 2 < NT:
                                    t2 = ti - 2
                                    rmsc3(XTs[:, :, t2 * TL:(t2 + 1) * TL], t2 * TL, t2 % 2)
                            fw.barrier()
                if stop_after == "DFT" and l == 0:
                    return nc

                with ExitStack() as e3:
                    wg = sb(e3, "wg", [128, 2, 2, 4, 128], BF16)
                    cw = sb(e3, "cw", [128, 4, 4])
                    cb = sb(e3, "cb", [128, 4])
                    gb_ = sb(e3, "gbias", [128, 2, 2, 4])
                    lam = sb(e3, "lam", [128, 2, 4])
                    hnsp = sb(e3, "hnsp", [128, 2, 4])
                    nsp = sb(e3, "nsp", [128, 2, 4])
                    op("pool", lambda e: e.memset(wg[:], 0.0), writes=["wg"])
                    for ax, wd in enumerate((wa_d, wx_d)):
                        for h2 in range(2):
                            for d_ in range(2):
                                src = wd[l, d_].rearrange("(c h) i e -> h i c e", h=2)[h2]
                                dma("pool", wg[64 * h2:64 * h2 + 64, d_, ax, :, 64 * h2:64 * h2 + 64], src, writes=["wg"])
                    for k in range(4):
                        dma("sp", cw[:, :, k], convw_d[l, k, :].rearrange("(c p) -> p c", p=128), writes=["cw"])
                    dma("sp", cb[:], convb_d[l].rearrange("(c p) -> p c", p=128), writes=["cb"])
                    for d_ in range(2):
                        dma("sp", gb_[:, 0, d_, :], ba_d[l, d_, :].rearrange("(c p) -> p c", p=128), writes=["gbias"])
                        dma("sp", gb_[:, 1, d_, :], bx_d[l, d_, :].rearrange("(c p) -> p c", p=128), writes=["gbias"])
                        dma("sp", lam[:, d_, :], lam_d[l, d_, :].rearrange("(c p) -> p c", p=128), writes=["lam"])
                    op("dve", lambda e: e.tensor_scalar(out=gb_[:], in0=gb_[:], scalar1=0.5, scalar2=None, op0=ALU.mult), writes=["gbias"])
                    op("act", lambda e: e.activation(out=lam[:], in_=lam[:], func=AF.Exp, scale=-1.0), writes=["lam"])
                    op("act", lambda e: e.activation(out=lam[:], in_=lam[:], func=AF.Ln, bias=1.0), writes=["lam"])
                    op("dve", lambda e: e.tensor_scalar(out=hnsp[:], in0=lam[:], scalar1=-4.0, scalar2=None, op0=ALU.mult), reads=["lam"], writes=["hnsp"])
                    op("dve", lambda e: e.tensor_scalar(out=nsp[:], in0=lam[:], scalar1=-8.0, scalar2=None, op0=ALU.mult), reads=["lam"], writes=["nsp"])

                    xaH = [sb(e3, "xaH%d" % i, [128, 4, TL + 3]) for i in range(2)]
                    xc = [sb(e3, "xc%d" % i, [128, 4, TL]) for i in range(2)]
                    xcb = [sb(e3, "xcb%d" % i, [128, 4, TL], BF16) for i in range(2)]
                    hS = [sb(e3, "hS%d" % i, [128, 4, TL]) for i in range(2)]
                    hfL = [sb(e3, "hfL%d" % i, [128, 4, TL]) for i in range(2)]
                    ggL = [sb(e3, "ggL%d" % i, [128, 4, TL]) for i in range(2)]
                    tr = sb(e3, "tr", [128, 4, TL])
                    tiS = [sb(e3, "tiS%d" % i, [128, 4, TL]) for i in range(2)]
                    aS = [sb(e3, "aS%d" % i, [128, 4, TL]) for i in range(2)]
                    mS = [sb(e3, "mS%d" % i, [128, 4, TL]) for i in range(2)]
                    uS = sb(e3, "uS", [128, 4, TL])
                    ya = [sb(e3, "ya%d" % i, [128, 4, TL]) for i in range(3)]
                    sq = sb(e3, "sqa", [128, 4, TL])
                    rr = [sb(e3, "rra%d" % i, [128, TL]) for i in range(2)]
                    rrs = [sb(e3, "rrsa%d" % i, [128, TL]) for i in range(2)]
                    rrt = [sb(e3, "rrta%d" % i, [128, TL]) for i in range(2)]
                    yAT = [sb(e3, "yAT%d" % i, [128, 4, TL], BF16) for i in range(2)]
                    pg = [ps(e3, "pg%d" % i, [128, 2, TL]) for i in range(4)]
                    pSs = [ps(e3, "pSa%d" % i, [128, 512]) for i in range(2)]

                    def xck(b):
                        return ["xc%d_%d" % (b, c) for c in range(4)]

                    def gates(d, b):
                        for c in range(4):
                            for ax in range(2):
                                op("pe", lambda e: e.matmul(pg[c][:, ax, :], lhsT=wg[:, d, ax, c, :], rhs=xcb[b][:, c, :], start=True, stop=True),
                                   reads=["wg", "xcb%d" % b], writes=["pg%d" % c])
                        for c in range(4):
                            op("act", lambda e: e.activation(out=tr[:, c, :], in_=pg[c][:, 0, :], func=AF.Tanh, scale=0.5, bias=gb_[:, 0, d, c:c + 1]),
                               reads=["gbias"], writes=["pg%d" % c, "tr"])
                            op("act", lambda e: e.activation(out=tiS[b][:, c, :], in_=pg[c][:, 1, :], func=AF.Tanh, scale=0.5, bias=gb_[:, 1, d, c:c + 1]),
                               reads=["gbias"], writes=["pg%d" % c, "tiS%d" % b])
                        for c in range(4):
                            op("act", lambda e: e.activation(out=aS[b][:, c, :], in_=tr[:, c, :], func=AF.Exp, scale=hnsp[:, d, c:c + 1], bias=hnsp[:, d, c:c + 1]),
                               reads=["tr", "hnsp"], writes=["aS%d" % b])
                            op("act", lambda e: e.activation(out=mS[b][:, c, :], in_=tr[:, c, :], func=AF.Exp, scale=nsp[:, d, c:c + 1], bias=nsp[:, d, c:c + 1]),
                               reads=["tr", "nsp"], writes=["mS%d" % b])
                        op("act", lambda e: e.activation(out=mS[b][:], in_=mS[b][:], func=AF.Sqrt, scale=-0.25, bias=0.25), writes=["mS%d" % b])

                    def make_u(b):
                        op("dve", lambda e: e.scalar_tensor_tensor(out=uS[:], in0=tiS[b][:], scalar=1.0, in1=xc[b][:], op0=ALU.add, op1=ALU.mult),
                           reads=["tiS%d" % b] + xck(b), writes=["uS"])
                        op("dve", lambda e: e.tensor_tensor(out=uS[:], in0=uS[:], in1=mS[b][:], op=ALU.mult), reads=["mS%d" % b], writes=["uS"])

                    scope_ = nc.named_scope("M2a_l%d" % l)
                    scope_.__enter__()

                    def load_xa(ti, b):
                        col0, is_ctx = TILES[ti]
                        first = is_ctx or col0 == 0
                        lastt = is_ctx or col0 == T - TL
                        lo = 0 if first else 2
                        hi = 0 if lastt else 1
                        if first:
                            op("pool", lambda e: e.memset(xaH[b][:, :, 0:2], 0.0), writes=["xaH%d" % b])
                        if lastt:
                            op("pool", lambda e: e.memset(xaH[b][:, :, TL + 2:TL + 3], 0.0), writes=["xaH%d" % b])
                        if is_ctx:
                            dma("sp", xaH[b][:, :, 2:TL + 2], XACv[:, :, :], writes=["xaH%d" % b])
                        else:
                            g0, g1 = col0 - lo, col0 + TL + hi
                            d0 = 2 - lo
                            for r_ in range(2):
                                a0, a1 = max(g0, r_ * TLOC), min(g1, (r_ + 1) * TLOC)
                                if a1 > a0:
                                    dma("sp", xaH[b][:, :, d0 + a0 - g0:d0 + a1 - g0], XAFv[r_][:, :, a0 - r_ * TLOC:a1 - r_ * TLOC],
                                        reads=["XAF"], writes=["xaH%d" % b])

                    def f0(ti):
                        col0, is_ctx = TILES[ti]
                        b = ti % 2
                        for c in range(4):
                            op("dve", lambda e: e.tensor_scalar(out=xc[b][:, c, :], in0=xaH[b][:, c, 0:TL], scalar1=cw[:, c, 0:1], scalar2=cb[:, c:c + 1],
                                                                op0=ALU.mult, op1=ALU.add), reads=["xaH%d" % b, "cw", "cb"], writes=["xc%d_%d" % (b, c)])
                        for k in range(1, 4):
                            for c in range(4):
                                op("dve", lambda e: e.scalar_tensor_tensor(out=xc[b][:, c, :], in0=xaH[b][:, c, k:k + TL], scalar=cw[:, c, k:k + 1],
                                                                           in1=xc[b][:, c, :], op0=ALU.mult, op1=ALU.add),
                                   reads=["xaH%d" % b, "cw"], writes=["xc%d_%d" % (b, c)])
                        op("act", lambda e: e.activation(out=xcb[b][:], in_=xc[b][:], func=AF.Identity), reads=xck(b), writes=["xcb%d" % b])
                        dma("sp", XCv[:, :, col0:col0 + TL], xc[b][:], reads=xck(b), writes=["XC"])
                        gates(0, b)

                    def f1(ti):
                        col0, is_ctx = TILES[ti]
                        b = ti % 2
                        pb_ = (ti - 1) % 2
                        make_u(b)
                        for c in range(4):
                            init = 0.0 if ti == 0 else hS[pb_][:, c, TL - 1:TL]
                            op("dve", lambda e: e.tensor_tensor_scan(out=hS[b][:, c, :], data0=aS[b][:, c, :], data1=uS[:, c, :], initial=init,
                                                                     op0=ALU.mult, op1=ALU.add),
                               reads=["aS%d" % b, "uS"] + ([] if ti == 0 else ["hS%d" % pb_]), writes=["hS%d" % b])
                        if not (last and is_ctx):
                            dma("sp", HFv[:, :, col0:col0 + TL], hS[b][:], reads=["hS%d" % b], writes=["HF"])

                    nT = len(TILES)
                    load_xa(0, 0)
                    load_xa(1, 1)
                    f0(0)
                    for ti in range(nT):
                        if ti + 2 < nT:
                            load_xa(ti + 2, ti % 2)
                        if ti + 1 < nT:
                            f0(ti + 1)
                        f1(ti)
                    fw.barrier()
                    scope_.__exit__(None, None, None)
                    if stop_after == "M2a" and l == 0:
                        return nc
                    scope_ = nc.named_scope("M2b_l%d" % l)
                    scope_.__enter__()

                    order = [0] + list(range(NT, 0, -1))
                    nO = len(order)

                    def load_x(oi):
                        col0, is_ctx = TILES[order[oi]]
                        b = oi % 2
                        dma("sp", xc[b][:], XCv[:, :, col0:col0 + TL], reads=["XC"], writes=xck(b))

                    def load_hg(oi):
                        col0, is_ctx = TILES[order[oi]]
                        b = oi % 2
                        if not (last and is_ctx):
                            dma("sp", hfL[b][:], HFv[:, :, col0:col0 + TL], reads=["HF"], writes=["hfL%d" % b])
                            ggsrc = GGCv[:, :, :] if is_ctx else GGFv[col0 // TLOC][:, :, col0 % TLOC:col0 % TLOC + TL]
                            dma("sp", ggL[b][:], ggsrc, reads=["GGF"], writes=["ggL%d" % b])

                    def b0(oi):
                        b = oi % 2
                        op("act", lambda e: e.activation(out=xcb[b][:], in_=xc[b][:], func=AF.Identity), reads=xck(b), writes=["xcb%d" % b])
                        gates(1, b)

                    def b1(oi):
                        col0, is_ctx = TILES[order[oi]]
                        b = oi % 2
                        pb_ = (oi - 1) % 2
                        y3 = oi % 3
                        make_u(b)
                        for c in range(4):
                            init = 0.0 if oi == 0 else hS[pb_][:, c, 0:1]
                            op("dve", lambda e: e.tensor_tensor_scan(out=hS[b][:, c, ::-1], data0=aS[b][:, c, ::-1], data1=uS[:, c, ::-1], initial=init,
                                                                     op0=ALU.mult, op1=ALU.add),
                               reads=["aS%d" % b, "uS"] + ([] if oi == 0 else ["hS%d" % pb_]), writes=["hS%d" % b])
                        if last and is_ctx:
                            return
                        op("pool", lambda e: e.tensor_tensor(out=ya[y3][:], in0=hS[b][:], in1=hfL[b][:], op=ALU.add), reads=["hS%d" % b, "hfL%d" % b], writes=["ya%d" % y3])
                        op("pool", lambda e: e.tensor_tensor(out=ya[y3][:], in0=ya[y3][:], in1=ggL[b][:], op=ALU.mult), reads=["ggL%d" % b], writes=["ya%d" % y3])
                        op("act", lambda e: e.activation(out=sq[:], in_=ya[y3][:], func=AF.Square), reads=["ya%d" % y3], writes=["sqa"])
                        for c in range(4):
                            op("pe", lambda e: e.matmul(pSs[b][:, 0:TL], lhsT=ones_f[:], rhs=sq[:, c, :], start=(c == 0), stop=(c == 3)),
                               reads=["ones_f", "sqa"], writes=["pSa%d" % b])

                    def b2a(oi):
                        col0, is_ctx = TILES[order[oi]]
                        b = oi % 2
                        if last and is_ctx:
                            return
                        op("dve", lambda e: e.tensor_scalar(out=rr[b][:], in0=pSs[b][:, 0:TL], scalar1=1.0 / DA, scalar2=EPS, op0=ALU.mult, op1=ALU.add),
                           writes=["pSa%d" % b, "rra%d" % b])
                        rsqrt(rrs[b][:], rr[b][:], rrt[b][:], "rrsa%d" % b, "rra%d" % b, "rrta%d" % b, iters=2)

                    def b2b(oi):
                        col0, is_ctx = TILES[order[oi]]
                        b = oi % 2
                        y3 = oi % 3
                        if last and is_ctx:
                            return
                        for c in range(4):
                            op("dve", lambda e: e.scalar_tensor_tensor(out=yAT[b][:, c, :], in0=ya[y3][:, c, :], scalar=gmixc[:, c:c + 1], in1=rrs[b][:],
                                                                       op0=ALU.mult, op1=ALU.mult), reads=["ya%d" % y3, "gmixc", "rrsa%d" % b], writes=["yAT%d_%d" % (b, c)])
                        dma("sp", YTv[:, 0:4, col0:col0 + TL], yAT[b][:], reads=["yAT%d_%d" % (b, c) for c in range(4)], writes=["YT"])

                    load_x(0)
                    load_hg(0)
                    load_x(1)
                    b0(0)
                    for oi in range(nO + 2):
                        if oi + 1 < nO:
                            load_hg(oi + 1)
                            b0(oi + 1)
                        if oi < nO:
                            b1(oi)
                        if oi + 2 < nO:
                            load_x(oi + 2)
                        if 0 <= oi - 1 < nO:
                            b2a(oi - 1)
                        if 0 <= oi - 2 < nO:
                            b2b(oi - 2)
                    fw.barrier()
                    scope_.__exit__(None, None, None)
                if stop_after == "M2b" and l == 0:
                    return nc

                m3_tiles = LTILES[1:] if last else LTILES
                wdn = sb(el, "wdn", [128, NF, D], BF16)
                with ExitStack() as e4:
                    e4.enter_context(nc.named_scope("M3a_l%d" % l))
                    wout = sb(e4, "wout", [128, 8, D], BF16)
                    for k in range(8):
                        dma("pool", wout[:, k, :], wout_d[l, k * 128:(k + 1) * 128, :], writes=["wout"])
                    for f in range(NF):
                        dma("pool", wdn[:, f, :], wdn_d[l, f * 128:(f + 1) * 128, :], writes=["wdn"])
                    gate = sb(e4, "gate1", [128, 2, D])
                    lng = sb(e4, "ln1g", [128, D])
                    lnb = sb(e4, "ln1b", [128, D])
                    for s in range(2):
                        dma("sp", gate[:, s, :], MODS[s:s + 1, 2 * D:3 * D].partition_broadcast(128), reads=["MODS"], writes=["gate1"])
                    dma("sp", lng[:], ln1g_d[l:l + 1, :].partition_broadcast(128), writes=["ln1g"])
                    dma("sp", lnb[:], ln1b_d[l:l + 1, :].partition_broadcast(128), writes=["ln1b"])
                    op("dve", lambda e: e.tensor_scalar(out=gate[:], in0=gate[:], scalar1=1.0 / ALPHA, scalar2=None, op0=ALU.mult), writes=["gate1"])
                    rmask = sb(e4, "rmask", [128, 2])
                    dma("sp", rmask[:], rmask_d[:, :], writes=["rmask"])
                    woutL = sb(e4, "woutL", [128, 8, D], BF16)
                    wsel = [sb(e4, "wsel%d" % r_, [128, 6, D], BF16) for r_ in range(2)]
                    woutC = sb(e4, "woutC", [128, 8, D], BF16) if not last else None
                    AC = (0, 1, 2, 3, 6, 7)
                    for k in range(8):
                        op("dve", lambda e: e.tensor_tensor(out=woutL[:, k, :], in0=wout[:, k, :], in1=gate[:, 0, :], op=ALU.mult),
                           reads=["wout", "gate1"], writes=["woutL"])
                        if not last:
                            op("pool", lambda e: e.tensor_tensor(out=woutC[:, k, :], in0=wout[:, k, :], in1=gate[:, 1, :], op=ALU.mult),
                               reads=["wout", "gate1"], writes=["woutC"])
                    for r_ in range(2):
                        for j, k in enumerate(AC):
                            if r_ == 0:
                                op("dve", lambda e: e.tensor_scalar(out=wsel[r_][:, j, :], in0=woutL[:, k, :], scalar1=rmask[:, r_:r_ + 1], scalar2=None, op0=ALU.mult),
                                   reads=["woutL", "rmask"], writes=["wsel%d_%d" % (r_, j)])
                            else:
                                op("act", lambda e: e.activation(out=wsel[r_][:, j, :], in_=woutL[:, k, :], func=AF.Identity, scale=rmask[:, r_:r_ + 1]),
                                   reads=["woutL", "rmask"], writes=["wsel%d_%d" % (r_, j)])
                    yTb = [sb(e4, "yTb%d" % i, [128, 2, TL], BF16) for i in range(3)]
                    yC = [[sb(e4, "yC%d_%d" % (i, r_), [128, 6, TL], BF16) for r_ in range(2)] for i in range(3)]
                    xt = [sb(e4, "xt%d" % i, [128, 2, D]) for i in range(2)]
                    rt = [sb(e4, "rt%d" % i, [128, 2, D]) for i in range(2)]
                    lns3 = [ln_scr(e4, "l3a"), ln_scr(e4, "l3b")]
                    po = [ps(e4, "po%d" % i, [128, 512]) for i in range(8)]
                    n3 = len(m3_tiles)

                    def loadY(ti):
                        col0, is_ctx = m3_tiles[ti]
                        y = ti % 3
                        dma("sp", yTb[y][:], YBLv[:, :, col0:col0 + TL], reads=["YBL"], writes=["yTb%d" % y])
                        if is_ctx:
                            dma("sp", yC[y][0][:, 0:4, :], YTv[:, 0:4, T:T + TL], reads=["YT"], writes=["yC%d_0" % y])
                            dma("sp", yC[y][0][:, 4:6, :], YTv[:, 6:8, T:T + TL], reads=["YT"], writes=["yC%d_0" % y])
                        else:
                            for r_ in range(2):
                                g0 = r_ * TLOC + col0
                                dma("sp", yC[y][r_][:, 0:4, :], YTv[:, 0:4, g0:g0 + TL], reads=["YT"], writes=["yC%d_%d" % (y, r_)])
                                dma("sp", yC[y][r_][:, 4:6, :], YTv[:, 6:8, g0:g0 + TL], reads=["YT"], writes=["yC%d_%d" % (y, r_)])

                    def loadX(ti):
                        col0, is_ctx = m3_tiles[ti]
                        b = ti % 2
                        load_resid(l, col0, is_ctx, xt[b], "xt%d" % b, None, None, from_xb=True)

                    def mm3(ti):
                        col0, is_ctx = m3_tiles[ti]
                        b = ti % 2
                        y = ti % 3
                        for s in range(2):
                            for hf_ in range(2):
                                p_, kp_ = po[4 * b + 2 * s + hf_], "po%d" % (4 * b + 2 * s + hf_)
                                cs = slice(hf_ * 512, (hf_ + 1) * 512)
                                ts_ = slice(s * 128, (s + 1) * 128)
                                steps = []
                                wB = woutC if is_ctx else woutL
                                for j in range(2):
                                    steps.append((yTb[y][:, j, ts_], wB[:, 4 + j, cs], ["yTb%d" % y, "woutC" if is_ctx else "woutL"]))
                                if is_ctx:
                                    for j, k in enumerate(AC):
                                        steps.append((yC[y][0][:, j, ts_], woutC[:, k, cs], ["yC%d_0" % y, "woutC"]))
                                else:
                                    for r_ in range(2):
                                        for j in range(6):
                                            steps.append((yC[y][r_][:, j, ts_], wsel[r_][:, j, cs], ["yC%d_%d" % (y, r_), "wsel%d_%d" % (r_, j)]))
                                for n_, (lh, rh, rd) in enumerate(steps):
                                    op("pe", lambda e: e.matmul(p_[:], lhsT=lh, rhs=rh, start=(n_ == 0), stop=(n_ == len(steps) - 1)), reads=rd, writes=[kp_])

                    def res3(ti):
                        b = ti % 2
                        for s in range(2):
                            for hf_ in range(2):
                                p_, kp_ = po[4 * b + 2 * s + hf_], "po%d" % (4 * b + 2 * s + hf_)
                                op("dve", lambda e: e.tensor_tensor(out=rt[b][:, s, hf_ * 512:(hf_ + 1) * 512], in0=p_[:],
                                                                    in1=xt[b][:, s, hf_ * 512:(hf_ + 1) * 512], op=ALU.add),
                                   reads=["xt%d" % b], writes=[kp_, "rt%d_%d" % (b, s)])

                    def stats3(ti):
                        b = ti % 2
                        ln_stats(rt[b], ["rt%d_0" % b, "rt%d_1" % b], lns3[b], EPS_POST, iters=2)

                    def norm3(ti):
                        col0, is_ctx = m3_tiles[ti]
                        b = ti % 2
                        mv, rs, kk = lns3[b]["mv"], lns3[b]["rs"], lns3[b]["k"]
                        for s in range(2):
                            kr = "rt%d_%d" % (b, s)
                            op("dve", lambda e: e.tensor_scalar(out=rt[b][:, s, :], in0=rt[b][:, s, :], scalar1=mv[:, s, 0:1],
                                                                scalar2=rs[:, s:s + 1], op0=ALU.subtract, op1=ALU.mult),
                               reads=[kk + "mv", kk + "rs"], writes=[kr])
                            op("dve", lambda e: e.tensor_tensor(out=rt[b][:, s, :], in0=rt[b][:, s, :], in1=lng[:], op=ALU.mult), reads=["ln1g"], writes=[kr])
                            op("pool", lambda e: e.tensor_tensor(out=rt[b][:, s, :], in0=rt[b][:, s, :], in1=lnb[:], op=ALU.add), reads=["ln1b"], writes=[kr])

                    def store3(ti):
                        col0, is_ctx = m3_tiles[ti]
                        b = ti % 2
                        dma("sp", XB[col0:col0 + TL, :].rearrange("(s p) d -> p s d", p=128), rt[b][:], reads=["rt%d_0" % b, "rt%d_1" % b], writes=["XB"])

                    for i_ in range(min(3, n3)):
                        loadY(i_)
                    for i_ in range(min(2, n3)):
                        loadX(i_)
                    mm3(0)
                    res3(0)
                    stats3(0)
                    for ti in range(n3):
                        if ti + 3 < n3:
                            loadY(ti + 3)
                        if ti + 2 < n3:
                            loadX(ti + 2)
                        if ti >= 1:
                            store3(ti - 1)
                        if ti + 1 < n3:
                            mm3(ti + 1)
                            res3(ti + 1)
                            stats3(ti + 1)
                        norm3(ti)
                    store3(n3 - 1)
                    fw.barrier()
                if stop_after == "M3a" and l == 0:
                    return nc

                with ExitStack() as e5:
                    e5.enter_context(nc.named_scope("M3b_l%d" % l))
                    wup = sb(e5, "wup", [128, 8, 2 * DFF], BF16)
                    for k in range(8):
                        for c0 in range(0, 2 * DFF, 2048):
                            c1 = min(c0 + 2048, 2 * DFF)
                            dma("pool", wup[:, k, c0:c1], wup_d[l, k * 128:(k + 1) * 128, c0:c1], writes=["wup"])
                    gate = sb(e5, "gate2", [128, 2, D])
                    lng = sb(e5, "ln2g", [128, D])
                    lnb = sb(e5, "ln2b", [128, D])
                    for s in range(2):
                        dma("sp", gate[:, s, :], MODS[s:s + 1, 5 * D:6 * D].partition_broadcast(128), reads=["MODS"], writes=["gate2"])
                    dma("sp", lng[:], ln2g_d[l:l + 1, :].partition_broadcast(128), writes=["ln2g"])
                    dma("sp", lnb[:], ln2b_d[l:l + 1, :].partition_broadcast(128), writes=["ln2b"])
                    op("dve", lambda e: e.tensor_scalar(out=gate[:], in0=gate[:], scalar1=1.0 / ALPHA, scalar2=None, op0=ALU.mult), writes=["gate2"])
                    xt = [sb(e5, "xu%d" % i, [128, 2, D]) for i in range(2)]
                    xh = sb(e5, "xhu", [128, 2, D])
                    h2T = [sb(e5, "h2T%d" % i, [128, 8, TL], BF16) for i in range(2)]
                    actT = sb(e5, "actT", [128, NF, TL], BF16)
                    sg = [sb(e5, "sg%d" % i, [128, TL]) for i in range(2)]
                    lns = ln_scr(e5, "l5")
                    lns2 = ln_scr(e5, "l6")
                    tp0 = ps(e5, "tq0", [128, 2, TL]); tp1 = ps(e5, "tq1", [128, 2, TL])
                    tps = [(tp0, "tq0"), (tp1, "tq1")]
                    pu = [ps(e5, "pu%d" % i, [128, 2, TL]) for i in range(2)]
                    pd = [ps(e5, "pd%d" % i, [128, 512]) for i in range(4)]

                    def load5(ti, b):
                        col0, is_ctx = m3_tiles[ti]
                        dma("sp", xt[b][:], XB[col0:col0 + TL, :].rearrange("(s p) d -> p s d", p=128), reads=["XB"], writes=["xu%d" % b])

                    su = [sb(e5, "su%d" % i, [128, TL]) for i in range(2)]
                    n5 = len(m3_tiles)
                    folded = [False]

                    def fold_gate():
                        for f in range(NF):
                            op("dve" if f % 2 == 0 else "pool",
                               lambda e: e.tensor_tensor(out=wdn[:, f, :], in0=wdn[:, f, :], in1=gate[:, 0, :], op=ALU.mult), reads=["gate2"], writes=["wdn"])
                        folded[0] = True

                    def stA(ti):
                        b = ti % 2
                        ln_tile(xt[b], "xu%d" % b, xh, "xhu", lns, EPS, eng="dve", iters=2)

                    def stT(ti):
                        col0, is_ctx = m3_tiles[ti]
                        b = ti % 2
                        transpose_mod(xh, "xhu", h2T[b], "h2T%d" % b, tps, modc, 1 if is_ctx else 0, 4, 3)

                    def stU(ti):
                        b = ti % 2
                        for f in range(NF):
                            p_, kp_ = pu[f % 2], "pu%d" % (f % 2)
                            for j in range(2):
                                c0 = j * DFF + f * 128
                                for k in range(8):
                                    op("pe", lambda e: e.matmul(p_[:, j, :], lhsT=wup[:, k, c0:c0 + 128], rhs=h2T[b][:, k, :], start=(k == 0), stop=(k == 7)),
                                       reads=["wup", "h2T%d" % b], writes=[kp_])
                            op("act", lambda e: e.activation(out=sg[f % 2][:], in_=p_[:, 0, :], func=AF.Silu), writes=[kp_, "sg%d" % (f % 2)])
                            op("act", lambda e: e.activation(out=su[f % 2][:], in_=p_[:, 1, :], func=AF.Identity), writes=[kp_, "su%d" % (f % 2)])
                            op("pool", lambda e: e.tensor_tensor(out=actT[:, f, :], in0=su[f % 2][:], in1=sg[f % 2][:], op=ALU.mult),
                               reads=["sg%d" % (f % 2), "su%d" % (f % 2)], writes=["actT"])

                    def stD(ti):
                        for s in range(2):
                            for hf_ in range(2):
                                p_, kp_ = pd[2 * s + hf_], "pd%d" % (2 * s + hf_)
                                for f in range(NF):
                                    op("pe", lambda e: e.matmul(p_[:], lhsT=actT[:, f, s * 128:(s + 1) * 128], rhs=wdn[:, f, hf_ * 512:(hf_ + 1) * 512],
                                                                start=(f == 0), stop=(f == NF - 1)), reads=["wdn", "actT"], writes=[kp_])

                    def stE(ti):
                        col0, is_ctx = m3_tiles[ti]
                        b = ti % 2
                        kx = "xu%d" % b
                        for s in range(2):
                            for hf_ in range(2):
                                p_, kp_ = pd[2 * s + hf_], "pd%d" % (2 * s + hf_)
                                cs = slice(hf_ * 512, (hf_ + 1) * 512)
                                if folded[0]:
                                    op("dve", lambda e: e.tensor_tensor(out=xt[b][:, s, cs], in0=p_[:], in1=xt[b][:, s, cs], op=ALU.add), writes=[kp_, kx])
                                else:
                                    op("dve", lambda e: e.tensor_tensor(out=xh[:, s, cs], in0=p_[:], in1=gate[:, 1 if is_ctx else 0, cs], op=ALU.mult),
                                       reads=["gate2"], writes=[kp_, "xhu"])
                        if not folded[0]:
                            op("pool", lambda e: e.tensor_tensor(out=xt[b][:], in0=xt[b][:], in1=xh[:], op=ALU.add), reads=["xhu"], writes=[kx])
                        ln_tile(xt[b], kx, xt[b], kx, lns2, EPS_POST, eng="dve", iters=3)
                        for s in range(2):
                            op("dve", lambda e: e.tensor_tensor(out=xt[b][:, s, :], in0=xt[b][:, s, :], in1=lng[:], op=ALU.mult), reads=["ln2g"], writes=[kx])
                            op("dve", lambda e: e.tensor_tensor(out=xt[b][:, s, :], in0=xt[b][:, s, :], in1=lnb[:], op=ALU.add), reads=["ln2b"], writes=[kx])
                        dst = out_d[col0:col0 + TL, :] if last else XB[col0:col0 + TL, :]
                        dma("sp", dst.rearrange("(s p) d -> p s d", p=128), xt[b][:], reads=[kx], writes=["XB"])

                    load5(0, 0)
                    if n5 > 1:
                        load5(1, 1)
                    if not m3_tiles[0][1]:
                        fold_gate()
                    stA(0)
                    stT(0)
                    for ti in range(n5):
                        stU(ti)
                        if ti + 1 < n5:
                            stA(ti + 1)
                            stT(ti + 1)
                        stD(ti)
                        stE(ti)
                        if m3_tiles[ti][1]:
                            fold_gate()
                        if ti + 2 < n5:
                            load5(ti + 2, ti % 2)
                    fw.barrier()
                if stop_after == "M3b" and l == 0:
                    return nc
        fw.wait_all("sp")
    return nc


def host_consts():
    c = {}
    c["ident"] = np.eye(128, dtype=np.float32)
    l1 = np.arange(128)[:, None].astype(np.float64)
    k1 = np.arange(128)[None, :].astype(np.float64)
    a = 2 * np.pi * l1 * k1 / 128.0
    c["t1"] = np.stack([np.cos(a), -np.sin(a), -np.cos(a)], 1).astype(np.float32).astype(ml_dtypes.bfloat16)
    l2 = np.arange(64)[:, None, None].astype(np.float64)
    kk1 = np.arange(128)[None, :, None].astype(np.float64)
    kk2 = np.arange(64)[None, None, :].astype(np.float64)
    ang = 2 * np.pi * ((kk1 + 128 * kk2) * l2 % T) / T
    c["tw"] = np.concatenate([np.cos(ang), np.sin(ang)], 0).reshape(128, T).astype(np.float32).astype(ml_dtypes.bfloat16)
    p = np.arange(128)[:, None, None].astype(np.float64)
    t = np.arange(2)[None, :, None].astype(np.float64)
    k = np.arange(256)[None, None, :].astype(np.float64)
    a2 = 2 * np.pi * (((128 * t + p) * k) % 256) / 256.0
    c["c256"] = np.stack([np.cos(a2), -np.sin(a2)], 2).astype(np.float32).astype(ml_dtypes.bfloat16)
    j = np.arange(64)[:, None].astype(np.float64)
    ch = np.arange(64)[None, :].astype(np.float64)
    a3 = 2 * np.pi * j * ch / 64.0
    cs = np.zeros((64, 2, 2, 128), np.float64)
    for gl in range(2):
        cs[:, 0, gl, gl * 64:(gl + 1) * 64] = np.cos(a3)
        cs[:, 1, gl, gl * 64:(gl + 1) * 64] = np.sin(a3)
    c["cspad"] = cs.astype(np.float32)
    quarter = D // 4
    freqs = 10000.0 ** (-np.arange(quarter, dtype=np.float32) / np.float32(quarter))
    r = np.repeat(np.arange(T // 64, dtype=np.float32), 64)
    col = np.tile(np.arange(64, dtype=np.float32), T // 64)

    def enc(pv):
        an = pv[:, None].astype(np.float32) * freqs[None, :].astype(np.float32)
        return np.concatenate([np.sin(an), np.cos(an)], -1)

    c["pos"] = np.concatenate([enc(r), enc(col)], -1).astype(np.float32)
    return c


_WNAMES = ["w_mod", "b_mod", "w_in", "conv_w", "conv_b", "lru_wa", "lru_ba", "lru_wx", "lru_bx", "lru_lam", "sg_ws", "sg_b",
           "fourier_w", "g_mix", "w_out", "ln1_g", "ln1_b", "w_up", "w_down", "ln2_g", "ln2_b"]


def make_in_maps(inputs, n_cores=N_CORES):
    consts = host_consts()
    pos = consts.pop("pos")
    shared = {n: np.ascontiguousarray(np.asarray(inputs[n], dtype=np.float32)) for n in _WNAMES}
    shared.update(consts)
    x = np.asarray(inputs["x"], dtype=np.float32)
    c = np.asarray(inputs["c"], dtype=np.float32)
    ctx = np.asarray(inputs["ctx"], dtype=np.float32)
    c_ctx = np.asarray(inputs["c_ctx"], dtype=np.float32)
    maps = []
    for core in range(n_cores):
        b, h = core // 2, core % 2
        m = dict(shared)
        m["x"] = np.ascontiguousarray(x[b, h * TLOC:(h + 1) * TLOC])
        m["pos"] = np.ascontiguousarray(pos[h * TLOC:(h + 1) * TLOC])
        m["ctx"] = np.ascontiguousarray(ctx[b])
        m["cc"] = np.ascontiguousarray(np.stack([c[b], c_ctx], 0))
        rm = np.zeros((128, 2), np.float32)
        rm[:, h] = 1.0
        m["rmask"] = rm
        maps.append(m)
    return maps


def kernel(**inputs):
    nc = build_program()
    maps = make_in_maps(inputs)
    res = run_bass_kernel_spmd(nc, maps, core_ids=list(range(N_CORES)))
    outs = [np.asarray(r["out"], dtype=np.float32) for r in res.results]
    return np.stack([np.concatenate([outs[2 * b], outs[2 * b + 1]], 0) for b in range(N_CORES // 2)], 0)
```

```python
import math
from contextlib import ExitStack

import numpy as np
import ml_dtypes
import concourse.bass as bass
import concourse.mybir as mybir
from concourse.bass_utils import run_bass_kernel_spmd

F32 = mybir.dt.float32
BF16 = mybir.dt.bfloat16
I32 = mybir.dt.int32
ALU = mybir.AluOpType
AF = mybir.ActivationFunctionType

D = 1024
T = 8192
TC = 256
TT = T + TC
TL = 256
NT = T // TL
DEPTH = 2
DA, DB, DC = 512, 256, 256
DIN = 1792
DFF = 2816
NF = DFF // 128
EPS = 1e-6
ALPHA = (2 * DEPTH) ** 0.25
EPS_POST = EPS / (ALPHA * ALPHA)
N_CORES = 8
TLOC = T // 2
NTL = TLOC // TL

SAME_ENG_SYNC = True
N_DMA_SEMS = 40


class FW:
    def __init__(self, nc, es):
        self.nc = nc
        self.es = es
        self.engs = {"pe": nc.tensor, "act": nc.scalar, "dve": nc.vector, "pool": nc.gpsimd, "sp": nc.sync}
        self.sems = {}
        self.cnt = {}
        for e in self.engs:
            self.sems[e] = es.enter_context(nc.semaphore("s_" + e))
            self.cnt[e] = 0
        self.dsem = {}
        for q in ("sp", "pool"):
            lst = []
            for i in range(N_DMA_SEMS):
                key = "d_%s_%d" % (q, i)
                self.sems[key] = es.enter_context(nc.semaphore(key))
                self.cnt[key] = 0
                lst.append(key)
            self.dsem[q] = [lst, 0]
        self.seen = {e: {} for e in self.engs}
        self.lastw = {}
        self.readers = {}
        self.ninst = 0

    def _wait(self, eng, ev):
        sk, v, prod = ev
        if prod == "pe" and eng == "pe":
            return
        if prod == eng and not SAME_ENG_SYNC:
            return
        if self.seen[eng].get(sk, 0) >= v:
            return
        self.engs[eng].wait_ge(self.sems[sk], v)
        self.seen[eng][sk] = v

    def _deps(self, eng, reads, writes):
        for k in reads:
            ev = self.lastw.get(k)
            if ev is not None:
                self._wait(eng, ev)
        for k in writes:
            ev = self.lastw.get(k)
            if ev is not None:
                self._wait(eng, ev)
            for ev in list(self.readers.get(k, {}).values()):
                self._wait(eng, ev)

    def _record(self, ev, reads, writes):
        for k in writes:
            self.lastw[k] = ev
            self.readers[k] = {}
        for k in reads:
            if k in writes:
                continue
            self.readers.setdefault(k, {})[ev[0]] = ev

    def op(self, eng, fn, reads=(), writes=()):
        self._deps(eng, reads, writes)
        inst = fn(self.engs[eng])
        self.cnt[eng] += 1
        inst.then_inc(self.sems[eng], 1)
        ev = (eng, self.cnt[eng], eng)
        self._record(ev, reads, writes)
        self.ninst += 1
        return ev

    def dma(self, q, out, in_, reads=(), writes=(), **kw):
        self._deps(q, reads, writes)
        lst, idx = self.dsem[q]
        sk = lst[idx % len(lst)]
        self.dsem[q][1] = idx + 1
        if self.cnt[sk] > 0:
            self._wait(q, (sk, self.cnt[sk], "dma"))
        inst = self.engs[q].dma_start(out=out, in_=in_, **kw)
        self.cnt[sk] += 16
        inst.then_inc(self.sems[sk], 16)
        ev = (sk, self.cnt[sk], "dma")
        self._record(ev, reads, writes)
        self.ninst += 1
        return ev

    def barrier(self):
        for e in self.engs:
            for e2 in ("pe", "act", "dve", "pool"):
                if e2 != e and self.cnt[e2] > 0:
                    self._wait(e, (e2, self.cnt[e2], e2))
            if self.cnt.get("cc", 0) > 0:
                self._wait(e, ("cc", self.cnt["cc"], "dma"))
            for q in self.dsem:
                for sk in self.dsem[q][0]:
                    if self.cnt[sk] > 0:
                        self._wait(e, (sk, self.cnt[sk], "dma"))
        self.lastw.clear()
        self.readers.clear()

    def collective(self, kind, in_ap, out_ap, groups, reads, writes):
        self._deps("pool", reads, writes)
        if "cc" not in self.sems:
            self.sems["cc"] = self.es.enter_context(self.nc.semaphore("s_cc"))
            self.cnt["cc"] = 0
        inst = self.nc.gpsimd.collective_compute(kind, ALU.bypass, replica_groups=groups, ins=[in_ap], outs=[out_ap])
        self.cnt["cc"] += 1
        inst.then_inc(self.sems["cc"], 1)
        ev = ("cc", self.cnt["cc"], "dma")
        self._record(ev, reads, writes)
        return ev

    def wait_all(self, eng="sp"):
        for ev in list(self.lastw.values()):
            self._wait(eng, ev)
        for d in list(self.readers.values()):
            for ev in list(d.values()):
                self._wait(eng, ev)
        for q in self.dsem:
            for sk in self.dsem[q][0]:
                if self.cnt[sk] > 0:
                    self._wait(eng, (sk, self.cnt[sk], "dma"))
        for e in ("pe", "act", "dve", "pool"):
            if self.cnt[e] > 0:
                self._wait(eng, (e, self.cnt[e], e))


def build_program(stop_after=None):
    nc = bass.Bass("TRN2", target_bir_lowering=False, num_devices=8)

    def din(name, shape, dt=F32):
        return nc.dram_tensor(name, list(shape), dt, kind="ExternalInput").ap()

    dbg = stop_after is not None

    CC_BUFS = ("XAL", "GGL", "ZL", "XAF", "GGF", "ZF")

    def dscr(name, shape, dt=F32):
        ext = dbg and name not in CC_BUFS
        return nc.dram_tensor(name, list(shape), dt, kind="ExternalOutput" if ext else "Internal").ap()

    x_d = din("x", [TLOC, D])
    ctx_d = din("ctx", [TC, D])
    cc_d = din("cc", [2, D])
    pos_d = din("pos", [TLOC, D])
    rmask_d = din("rmask", [128, 2])
    wmod_d = din("w_mod", [DEPTH, D, 6 * D])
    bmod_d = din("b_mod", [DEPTH, 6 * D])
    win_d = din("w_in", [DEPTH, D, DIN])
    convw_d = din("conv_w", [DEPTH, 4, DA])
    convb_d = din("conv_b", [DEPTH, DA])
    wa_d = din("lru_wa", [DEPTH, 2, 8, 64, 64])
    ba_d = din("lru_ba", [DEPTH, 2, DA])
    wx_d = din("lru_wx", [DEPTH, 2, 8, 64, 64])
    bx_d = din("lru_bx", [DEPTH, 2, DA])
    lam_d = din("lru_lam", [DEPTH, 2, DA])
    ws_d = din("sg_ws", [DEPTH, 4, 128, 128])
    sgb_d = din("sg_b", [DEPTH, 4, 128])
    wf_d = din("fourier_w", [DEPTH, 4, 64, 64])
    gmix_d = din("g_mix", [DEPTH, D])
    wout_d = din("w_out", [DEPTH, D, D])
    ln1g_d = din("ln1_g", [DEPTH, D])
    ln1b_d = din("ln1_b", [DEPTH, D])
    wup_d = din("w_up", [DEPTH, D, 2 * DFF])
    wdn_d = din("w_down", [DEPTH, DFF, D])
    ln2g_d = din("ln2_g", [DEPTH, D])
    ln2b_d = din("ln2_b", [DEPTH, D])
    ident_d = din("ident", [128, 128])
    t1_d = din("t1", [128, 3, 128], BF16)
    tw_d = din("tw", [128, T], BF16)
    c256_d = din("c256", [128, 2, 2, 256], BF16)
    cspad_d = din("cspad", [64, 2, 2, 128])
    out_d = nc.dram_tensor("out", [TLOC, D], F32, kind="ExternalOutput").ap()

    XB = dscr("XB", [TLOC + TC, D])
    XAL = dscr("XAL", [DA, TLOC])
    GGL = dscr("GGL", [DA, TLOC])
    ZL = dscr("ZL", [DC, TLOC], BF16)
    XAF = dscr("XAF", [2 * DA, TLOC])
    GGF = dscr("GGF", [2 * DA, TLOC])
    ZF = dscr("ZF", [2 * DC, TLOC], BF16)
    XAC = dscr("XAC", [DA, TC])
    GGC = dscr("GGC", [DA, TC])
    YBL = dscr("YBL", [DB, TLOC + TC], BF16)
    XC = dscr("XC", [DA, TT])
    HF = dscr("HF", [DA, TT])
    YT = dscr("YT", [D, TT], BF16)
    GD = dscr("GD", [128, 2, 64, 256], BF16)
    MODS = dscr("MODS", [2, 6 * D])

    XCv = XC.rearrange("(c p) t -> p c t", p=128)
    XALv = XAL.rearrange("(c p) t -> p c t", p=128)
    GGLv = GGL.rearrange("(c p) t -> p c t", p=128)
    ZLv = ZL.rearrange("(c p) t -> p c t", p=128)
    XACv = XAC.rearrange("(c p) t -> p c t", p=128)
    GGCv = GGC.rearrange("(c p) t -> p c t", p=128)
    YBLv = YBL.rearrange("(c p) t -> p c t", p=128)
    XAFv = XAF.rearrange("(c r p) t -> r p c t", r=2, p=128)
    GGFv = GGF.rearrange("(c r p) t -> r p c t", r=2, p=128)
    ZFv = ZF.rearrange("(r c p) t -> r p c t", r=2, p=128)
    PAIRS = [[0, 1], [2, 3], [4, 5], [6, 7]]
    HFv = HF.rearrange("(c p) t -> p c t", p=128)
    YTv = YT.rearrange("(c p) t -> p c t", p=128)

    TILES = [(T, True)] + [(TL * i, False) for i in range(NT)]
    LTILES = [(TLOC, True)] + [(TL * i, False) for i in range(NTL)]
    import os as _os
    if _os.environ.get("DBG_NT"):
        LTILES = LTILES[:int(_os.environ["DBG_NT"])]

    with ExitStack() as es:
        fw = FW(nc, es)
        op = fw.op
        dma = fw.dma

        uid = [0]

        def sb(es_, name, shape, dt=F32):
            uid[0] += 1
            return es_.enter_context(nc.sbuf_tensor("%s_s%d" % (name, uid[0]), list(shape), dt))

        def ps(es_, name, shape, dt=F32):
            uid[0] += 1
            return es_.enter_context(nc.psum_tensor("%s_p%d" % (name, uid[0]), list(shape), dt))

        es.enter_context(nc.allow_non_contiguous_dma(reason="small strided parameter loads"))

        ident = sb(es, "ident", [128, 128])
        ones_f = sb(es, "ones_f", [128, 128])
        dma("sp", ident[:], ident_d[:, :], writes=["ident"])
        op("pool", lambda e: e.memset(ones_f[:], 1.0), writes=["ones_f"])

        def rsqrt(out, x, tmp, kout, kx, ktmp, eng="pool", iters=3):
            xi = x.bitcast(I32)
            oi = out.bitcast(I32)
            op("dve", lambda e: e.tensor_scalar(out=oi, in0=xi, scalar1=1, scalar2=None,
                                                op0=ALU.arith_shift_right), reads=[kx], writes=[kout])
            op("dve", lambda e: e.tensor_scalar(out=oi, in0=oi, scalar1=-1.0, scalar2=float(0x5F3759DF),
                                                op0=ALU.mult, op1=ALU.add), writes=[kout])
            for _ in range(iters):
                op(eng, lambda e: e.tensor_tensor(out=tmp, in0=x, in1=out, op=ALU.mult), reads=[kx, kout], writes=[ktmp])
                op(eng, lambda e: e.tensor_tensor(out=tmp, in0=tmp, in1=out, op=ALU.mult), reads=[kout], writes=[ktmp])
                op(eng, lambda e: e.tensor_scalar(out=tmp, in0=tmp, scalar1=-0.5, scalar2=1.5,
                                                  op0=ALU.mult, op1=ALU.add), writes=[ktmp])
                op(eng, lambda e: e.tensor_tensor(out=out, in0=out, in1=tmp, op=ALU.mult), reads=[ktmp], writes=[kout])

        def resid_rows(l, col0, is_ctx):
            if l == 0:
                return (ctx_d[0:TL, :] if is_ctx else x_d[col0:col0 + TL, :])
            return XB[col0:col0 + TL, :]

        def load_resid(l, col0, is_ctx, xt, kx, pt=None, kp=None, from_xb=False, save_xb=False):
            if from_xb and not is_ctx:
                src = XB[col0:col0 + TL, :]
            else:
                src = resid_rows(l, col0, is_ctx)
            dma("sp", xt[:], src.rearrange("(s p) d -> p s d", p=128), reads=(["XB_%d" % col0] if (from_xb or l > 0) else []), writes=[kx])
            if l == 0 and not is_ctx and not from_xb:
                dma("sp", pt[:], pos_d[col0:col0 + TL, :].rearrange("(s p) d -> p s d", p=128), writes=[kp])
                op("pool", lambda e: e.tensor_tensor(out=xt[:], in0=xt[:], in1=pt[:], op=ALU.add),
                   reads=[kp], writes=[kx])
                if save_xb:
                    dma("sp", XB[col0:col0 + TL, :].rearrange("(s p) d -> p s d", p=128), xt[:], reads=[kx], writes=["XB_%d" % col0])

        def ln_tile(xt, kx, xh, kxh, scr, eps, eng="pool", iters=3):
            st, mv, ve, rs, tmp = scr["st"], scr["mv"], scr["ve"], scr["rs"], scr["tmp"]
            kk = scr["k"]
            for s in range(2):
                for h in range(2):
                    op("dve", lambda e: e.bn_stats(out=st[:, s, h, :], in_=xt[:, s, h * 512:(h + 1) * 512]),
                       reads=[kx], writes=[kk + "st"])
                op("dve", lambda e: e.bn_aggr(out=mv[:, s, :], in_=st[:, s, :, :].rearrange("p a b -> p (a b)")),
                   reads=[kk + "st"], writes=[kk + "mv"])
            op("dve", lambda e: e.tensor_scalar(out=ve[:], in0=mv[:, :, 1], scalar1=float(eps), scalar2=None,
                                                op0=ALU.add), reads=[kk + "mv"], writes=[kk + "ve"])
            rsqrt(rs[:], ve[:], tmp[:], kk + "rs", kk + "ve", kk + "tmp", eng=eng, iters=iters)
            for s in range(2):
                op("dve", lambda e: e.tensor_scalar(out=xh[:, s, :], in0=xt[:, s, :], scalar1=mv[:, s, 0:1],
                                                    scalar2=rs[:, s:s + 1], op0=ALU.subtract, op1=ALU.mult),
                   reads=[kx, kk + "mv", kk + "rs"], writes=[kxh])

        def ln_stats(xt, kx, scr, eps, iters=3):
            st, mv, ve, rs, tmp = scr["st"], scr["mv"], scr["ve"], scr["rs"], scr["tmp"]
            kk = scr["k"]
            for s in range(2):
                for h in range(2):
                    op("dve", lambda e: e.bn_stats(out=st[:, s, h, :], in_=xt[:, s, h * 512:(h + 1) * 512]),
                       reads=(kx if isinstance(kx, list) else [kx]), writes=[kk + "st%d%d" % (s, h)])
                op("dve", lambda e: e.bn_aggr(out=mv[:, s, :], in_=st[:, s, :, :].rearrange("p a b -> p (a b)")),
                   reads=[kk + "st%d0" % s, kk + "st%d1" % s], writes=[kk + "mv"])
            op("dve", lambda e: e.tensor_scalar(out=ve[:], in0=mv[:, :, 1], scalar1=float(eps), scalar2=None,
                                                op0=ALU.add), reads=[kk + "mv"], writes=[kk + "ve"])
            rsqrt(rs[:], ve[:], tmp[:], kk + "rs", kk + "ve", kk + "tmp", iters=iters)

        def ln_apply(xt, kx, xh, kxh, scr):
            mv, rs = scr["mv"], scr["rs"]
            kk = scr["k"]
            for s in range(2):
                op("dve", lambda e: e.tensor_scalar(out=xh[:, s, :], in0=xt[:, s, :], scalar1=mv[:, s, 0:1],
                                                    scalar2=rs[:, s:s + 1], op0=ALU.subtract, op1=ALU.mult),
                   reads=[kx, kk + "mv", kk + "rs"], writes=[kxh])

        def ln_scr(es_, name):
            return dict(st=sb(es_, name + "st", [128, 2, 2, 6]), mv=sb(es_, name + "mv", [128, 2, 2]),
                        ve=sb(es_, name + "ve", [128, 2]), rs=sb(es_, name + "rs", [128, 2]),
                        tmp=sb(es_, name + "tmp", [128, 2]), k=name)

        def transpose_mod(xh, kxh, hT, khT, tps, modc, strm, jsc, jsh):
            for kp in range(4):
                tp, ktp = tps[kp % 2]
                for j in range(2):
                    k = 2 * kp + j
                    for s in range(2):
                        op("pe", lambda e: e.transpose(out=tp[:, j, s * 128:(s + 1) * 128],
                                                       in_=xh[:, s, k * 128:(k + 1) * 128], identity=ident[:]),
                           reads=[kxh, "ident"], writes=[ktp])
                for j in range(2):
                    k = 2 * kp + j
                    op("act", lambda e: e.activation(out=hT[:, k, :], in_=tp[:, j, :], func=AF.Identity,
                                                     scale=modc[:, strm, jsc, k:k + 1], bias=modc[:, strm, jsh, k:k + 1]),
                       reads=["modc"], writes=[ktp, khT])

        for l in range(DEPTH):
            last = (l == DEPTH - 1)
            with ExitStack() as el:
                with ExitStack() as ep:
                    ep.enter_context(nc.named_scope("P0_l%d" % l))
                    cct = sb(ep, "cct", [128, 8, 2])
                    sct = sb(ep, "sct", [128, 8, 2])
                    scbc = sb(ep, "scbc", [128, 8, 128])
                    bmbc = sb(ep, "bmbc", [128, 6 * D])
                    modbc = sb(ep, "modbc", [128, 6 * D])
                    wm = [sb(ep, "wm%d" % i, [128, 3072]) for i in range(3)]
                    pmod = [ps(ep, "pmod%d" % i, [128, 512]) for i in range(6)]
                    for s in range(2):
                        dma("sp", cct[:, :, s], cc_d[s, :].rearrange("(k p) -> p k", p=128), writes=["cct"])
                    dma("sp", bmbc[:], bmod_d[l:l + 1, :].partition_broadcast(128), writes=["bmbc"])
                    op("act", lambda e: e.activation(out=sct[:], in_=cct[:], func=AF.Tanh, scale=0.5), reads=["cct"], writes=["sct"])
                    op("dve", lambda e: e.tensor_scalar(out=sct[:], in0=sct[:], scalar1=0.5, scalar2=0.5, op0=ALU.mult, op1=ALU.add), writes=["sct"])
                    op("dve", lambda e: e.tensor_tensor(out=sct[:], in0=sct[:], in1=cct[:], op=ALU.mult), reads=["cct"], writes=["sct"])
                    for k in range(8):
                        for s in range(2):
                            op("dve", lambda e: e.tensor_scalar(out=scbc[:, k, 64 * s:64 * s + 64], in0=ones_f[:, 0:64],
                                                                scalar1=sct[:, k, s:s + 1], scalar2=None, op0=ALU.mult),
                               reads=["sct", "ones_f"], writes=["scbc"])
                    ld = 0
                    for half in range(2):
                        for k in range(8):
                            w = wm[ld % 3]
                            kw = "wm%d" % (ld % 3)
                            ld += 1
                            dma("sp", w[:], wmod_d[l, k * 128:(k + 1) * 128, half * 3072:(half + 1) * 3072], writes=[kw])
                            for n in range(6):
                                op("pe", lambda e: e.matmul(pmod[n][:], lhsT=scbc[:, k, :], rhs=w[:, n * 512:(n + 1) * 512],
                                                            start=(k == 0), stop=(k == 7)),
                                   reads=["scbc", kw], writes=["pmod%d" % n])
                        for n in range(6):
                            c0 = half * 3072 + n * 512
                            op("dve", lambda e: e.tensor_tensor(out=modbc[:, c0:c0 + 512], in0=pmod[n][:], in1=bmbc[:, c0:c0 + 512],
                                                                op=ALU.add), reads=["bmbc"], writes=["pmod%d" % n, "modbc"])
                    dma("sp", MODS[0:1, :], modbc[0:1, :], reads=["modbc"], writes=["MODS"])
                    dma("sp", MODS[1:2, :], modbc[64:65, :], reads=["modbc"], writes=["MODS"])
                    fw.barrier()

                modc = sb(el, "modc", [128, 2, 6, 8])
                gmixc = sb(el, "gmixc", [128, 8])
                for s in range(2):
                    for j in range(6):
                        dma("sp", modc[:, s, j, :], MODS[s, j * D:(j + 1) * D].rearrange("(k p) -> p k", p=128), reads=["MODS"], writes=["modc"])
                dma("sp", gmixc[:], gmix_d[l, :].rearrange("(k p) -> p k", p=128), writes=["gmixc"])
                for j in (1, 4):
                    op("dve", lambda e: e.tensor_scalar(out=modc[:, :, j, :], in0=modc[:, :, j, :], scalar1=1.0, scalar2=None,
                                                        op0=ALU.add), writes=["modc"])

                with ExitStack() as ez:
                    zT = sb(ez, "zT", [128, 2, T], BF16)
                    zTc = sb(ez, "zTc", [128, 2, TC], BF16)
                    with ExitStack() as e1:
                        e1.enter_context(nc.named_scope("M1_l%d" % l))
                        win = sb(e1, "win", [128, 8, DIN], BF16)
                        for k in range(8):
                            dma("pool", win[:, k, :], win_d[l, k * 128:(k + 1) * 128, :], writes=["win"])
                        wsT = sb(e1, "wsT", [128, 4, 128], BF16)
                        wsr = sb(e1, "wsr", [128, 4, 128])
                        bsT = sb(e1, "bsT", [128, 4])
                        dma("sp", wsr[:], ws_d[l].rearrange("h p q -> p h q"), writes=["wsr"])
                        dma("sp", bsT[:], sgb_d[l].rearrange("h p -> p h"), writes=["bsT"])
                        xt = [sb(e1, "xt%d" % i, [128, 2, D]) for i in range(2)]
                        pt = [sb(e1, "pt%d" % i, [128, 2, D]) for i in range(2)] if l == 0 else [None, None]
                        xh = sb(e1, "xh", [128, 2, D])
                        hT = [sb(e1, "hT%d" % i, [128, 8, TL], BF16) for i in range(2)]
                        lns = ln_scr(e1, "l1")
                        xaS = [sb(e1, "xaS%d" % i, [128, 4, TL]) for i in range(2)]
                        ggS = [sb(e1, "ggS%d" % i, [128, 4, TL]) for i in range(2)]
                        yBT = [sb(e1, "yBT%d" % i, [128, 2, TL], BF16) for i in range(2)]
                        zS = [sb(e1, "zS%d" % i, [128, 2, TL], BF16) for i in range(2)]
                        zB = [sb(e1, "zB%d" % i, [128, 512]) for i in range(2)]
                        yb = [sb(e1, "yb%d" % i, [128, 256]) for i in range(2)]
                        vh = sb(e1, "vh", [128, 256], BF16)
                        bst = sb(e1, "bst", [128, 4, 6])
                        bmv = sb(e1, "bmv", [128, 4, 2])
                        bve = sb(e1, "bve", [128, 4])
                        brs = sb(e1, "brs", [128, 4])
                        btmp = sb(e1, "btmp", [128, 4])
                        ssB = sb(e1, "ssB", [128, 2])
                        rB = sb(e1, "rB", [128, 2])
                        rsB = sb(e1, "rsB", [128, 2])
                        rtmp = sb(e1, "rtmp", [128, 2])
                        junk = sb(e1, "junk", [128, 256])
                        tp0 = ps(e1, "tp0", [128, 2, TL]); tp1 = ps(e1, "tp1", [128, 2, TL])
                        tps = [(tp0, "tp0"), (tp1, "tp1")]
                        pa = [ps(e1, "pa%d" % i, [128, 2, TL]) for i in range(2)]
                        pb = [ps(e1, "pb%d" % i, [128, 512]) for i in range(2)]
                        pss = ps(e1, "pss", [128, 512])
                        pyt = ps(e1, "pyt", [128, 2, TL])
                        for h in range(4):
                            op("pe", lambda e: e.transpose(out=pb[0][:, h * 128:(h + 1) * 128], in_=wsr[:, h, :], identity=ident[:]),
                               reads=["wsr", "ident"], writes=["pb0"])
                        op("dve", lambda e: e.tensor_copy(out=wsT[:].rearrange("p h q -> p (h q)"), in_=pb[0][:]), writes=["pb0", "wsT"])

                        tl = LTILES
                        nM = len(tl)
                        zBt = [[sb(e1, "zBt%d_%d" % (i, s_), [128, 512]) for s_ in range(2)] for i in range(2)]

                        def m1_load(ti):
                            nb = ti % 2
                            load_resid(l, tl[ti][0], tl[ti][1], xt[nb], "xt%d" % nb, pt[nb], "pt%d" % nb, save_xb=True)

                        def m1_A(ti):
                            b = ti % 2
                            ln_tile(xt[b], "xt%d" % b, xh, "xh", lns, EPS, eng="dve", iters=2)

                        def m1_T(ti):
                            col0, is_ctx = tl[ti]
                            b = ti % 2
                            transpose_mod(xh, "xh", hT[b], "hT%d" % b, tps, modc, 1 if is_ctx else 0, 1, 0)

                        def m1_P(ti):
                            col0, is_ctx = tl[ti]
                            b = ti % 2
                            khT = "hT%d" % b
                            pai = 0
                            only_xa = last and is_ctx
                            for grp in range(2 if only_xa else 4):
                                p_, kp_ = pa[pai % 2], "pa%d" % (pai % 2)
                                pai += 1
                                for j in range(2):
                                    oc = grp * 2 + j
                                    for k in range(8):
                                        op("pe", lambda e: e.matmul(p_[:, j, :], lhsT=win[:, k, oc * 128:(oc + 1) * 128], rhs=hT[b][:, k, :],
                                                                    start=(k == 0), stop=(k == 7)), reads=["win", khT], writes=[kp_])
                                if grp < 2:
                                    op("act", lambda e: e.activation(out=xaS[b][:, 2 * grp:2 * grp + 2, :], in_=p_[:], func=AF.Identity),
                                       writes=[kp_, "xaS%d" % b])
                                else:
                                    g2 = grp - 2
                                    op("act", lambda e: e.activation(out=ggS[b][:, 2 * g2:2 * g2 + 2, :], in_=p_[:], func=AF.Gelu),
                                       writes=[kp_, "ggS%d" % b])
                            dma("sp", XACv[:, :, :] if is_ctx else XALv[:, :, col0:col0 + TL], xaS[b][:], reads=["xaS%d" % b], writes=["XA"])
                            if only_xa:
                                return
                            dma("sp", GGCv[:, :, :] if is_ctx else GGLv[:, :, col0:col0 + TL], ggS[b][:], reads=["ggS%d" % b], writes=["GG"])
                            p_, kp_ = pa[pai % 2], "pa%d" % (pai % 2)
                            for j in range(2):
                                for k in range(8):
                                    op("pe", lambda e: e.matmul(p_[:, j, :], lhsT=win[:, k, 1536 + j * 128:1536 + (j + 1) * 128], rhs=hT[b][:, k, :],
                                                                start=(k == 0), stop=(k == 7)), reads=["win", khT], writes=[kp_])
                            if is_ctx:
                                op("act", lambda e: e.activation(out=zTc[:, :, :], in_=p_[:], func=AF.Identity), writes=[kp_, "zTc"])
                            else:
                                op("act", lambda e: e.activation(out=zS[b][:], in_=p_[:], func=AF.Identity), writes=[kp_, "zS%d" % b])
                                dma("sp", ZLv[:, :, col0:col0 + TL], zS[b][:], reads=["zS%d" % b], writes=["ZL"])

                        def m1_Bproj(ti):
                            col0, is_ctx = tl[ti]
                            if last and is_ctx:
                                return
                            b = ti % 2
                            for s in range(2):
                                p_, kp_ = pb[s], "pb%d" % s
                                for k in range(8):
                                    op("pe", lambda e: e.matmul(p_[:], lhsT=hT[b][:, k, s * 128:(s + 1) * 128], rhs=win[:, k, 1024:1536],
                                                                start=(k == 0), stop=(k == 7)), reads=["win", "hT%d" % b], writes=[kp_])
                                op("act", lambda e: e.activation(out=zBt[b][s][:], in_=p_[:], func=AF.Gelu), writes=[kp_, "zBt%d_%d" % (b, s)])

                        bmv8 = sb(e1, "bmv8", [128, 8, 2])
                        bve8 = sb(e1, "bve8", [128, 8])
                        brs8 = sb(e1, "brs8", [128, 8])
                        btmp8 = sb(e1, "btmp8", [128, 8])
                        bst8 = sb(e1, "bst8", [128, 8, 6])
                        vh2 = sb(e1, "vh2", [128, 2, 256], BF16)
                        ybp = [[sb(e1, "ybp%d_%d" % (i, s_), [128, 256]) for s_ in range(2)] for i in range(2)]
                        ssBp = [sb(e1, "ssBp%d" % i, [128, 2]) for i in range(2)]

                        def skipB(ti):
                            return last and tl[ti][1]

                        def m1_B1a(ti):
                            if skipB(ti):
                                return
                            b = ti % 2
                            for s in range(2):
                                zb, kzb = zBt[b][s], "zBt%d_%d" % (b, s)
                                for h in range(4):
                                    op("dve", lambda e: e.bn_stats(out=bst8[:, 4 * s + h, :], in_=zb[:, 256 + 64 * h:256 + 64 * h + 64]),
                                       reads=[kzb], writes=["bst8_%d" % (4 * s + h)])
                                for h in range(4):
                                    op("dve", lambda e: e.bn_aggr(out=bmv8[:, 4 * s + h, :], in_=bst8[:, 4 * s + h, :]),
                                       reads=["bst8_%d" % (4 * s + h)], writes=["bmv8"])
                            op("dve", lambda e: e.tensor_scalar(out=bve8[:], in0=bmv8[:, :, 1], scalar1=EPS, scalar2=None, op0=ALU.add),
                               reads=["bmv8"], writes=["bve8"])
                            rsqrt(brs8[:], bve8[:], btmp8[:], "brs8", "bve8", "btmp8", iters=2)
                            for s in range(2):
                                zb, kzb = zBt[b][s], "zBt%d_%d" % (b, s)
                                for h in range(4):
                                    op("dve", lambda e: e.tensor_scalar(out=vh2[:, s, 64 * h:64 * h + 64], in0=zb[:, 256 + 64 * h:256 + 64 * h + 64],
                                                                        scalar1=bmv8[:, 4 * s + h, 0:1], scalar2=brs8[:, 4 * s + h:4 * s + h + 1],
                                                                        op0=ALU.subtract, op1=ALU.mult),
                                       reads=[kzb, "bmv8", "brs8"], writes=["vh2_%d" % (4 * s + h)])

                        def m1_smm(ti):
                            if skipB(ti):
                                return
                            for s in range(2):
                                for h in range(4):
                                    op("pe", lambda e: e.matmul(pss[:, 256 * s + 64 * h:256 * s + 64 * h + 64], lhsT=wsT[:, h, :], rhs=vh2[:, s, 64 * h:64 * h + 64],
                                                                start=True, stop=True), reads=["wsT", "vh2_%d" % (4 * s + h)], writes=["pss"])

                        def m1_B1b(ti):
                            if skipB(ti):
                                return
                            b = ti % 2
                            for s in range(2):
                                zb, kzb = zBt[b][s], "zBt%d_%d" % (b, s)
                                for h in range(4):
                                    op("dve", lambda e: e.scalar_tensor_tensor(out=ybp[b][s][:, 64 * h:64 * h + 64], in0=pss[:, 256 * s + 64 * h:256 * s + 64 * h + 64],
                                                                               scalar=bsT[:, h:h + 1], in1=zb[:, 64 * h:64 * h + 64],
                                                                               op0=ALU.add, op1=ALU.mult),
                                       reads=["bsT", kzb], writes=["pss", "ybp%d_%d_%d" % (b, s, h)])
                            for s in range(2):
                                op("act", lambda e: e.activation(out=junk[:], in_=ybp[b][s][:], func=AF.Square, accum_out=ssBp[b][:, s:s + 1]),
                                   reads=["ybp%d_%d_%d" % (b, s, h) for h in range(4)], writes=["junk", "ssBp%d" % b])

                        def m1_B2(ti):
                            if skipB(ti):
                                return
                            col0, is_ctx = tl[ti]
                            b = ti % 2
                            op("dve", lambda e: e.tensor_scalar(out=rB[:], in0=ssBp[b][:], scalar1=1.0 / DB, scalar2=EPS, op0=ALU.mult, op1=ALU.add),
                               reads=["ssBp%d" % b], writes=["rB"])
                            rsqrt(rsB[:], rB[:], rtmp[:], "rsB", "rB", "rtmp", iters=2)
                            for s in range(2):
                                kyb = ["ybp%d_%d_%d" % (b, s, h) for h in range(4)]
                                op("dve", lambda e: e.tensor_scalar(out=ybp[b][s][:], in0=ybp[b][s][:], scalar1=rsB[:, s:s + 1], scalar2=None, op0=ALU.mult),
                                   reads=["rsB"], writes=kyb)
                                for c in range(2):
                                    op("pe", lambda e: e.transpose(out=pyt[:, c, s * 128:(s + 1) * 128], in_=ybp[b][s][:, c * 128:(c + 1) * 128],
                                                                   identity=ident[:]), reads=kyb + ["ident"], writes=["pyt"])
                            for c in range(2):
                                op("act", lambda e: e.activation(out=yBT[b][:, c, :], in_=pyt[:, c, :], func=AF.Identity, scale=gmixc[:, 4 + c:5 + c]),
                                   reads=["gmixc"], writes=["pyt", "yBT%d" % b])
                            dma("sp", YBLv[:, :, col0:col0 + TL], yBT[b][:], reads=["yBT%d" % b], writes=["YBL"])

                        m1_load(0)
                        if nM > 1:
                            m1_load(1)
                        m1_A(0)
                        m1_T(0)
                        for ti in range(nM + 2):
                            if ti < nM:
                                m1_P(ti)
                            if ti + 1 < nM:
                                m1_A(ti + 1)
                            if 2 <= ti:
                                m1_B2(ti - 2)
                            if ti + 1 < nM:
                                m1_T(ti + 1)
                            if 1 <= ti <= nM:
                                m1_B1a(ti - 1)
                            if ti < nM:
                                m1_Bproj(ti)
                            if 1 <= ti <= nM:
                                m1_smm(ti - 1)
                                m1_B1b(ti - 1)
                            if ti + 2 < nM:
                                m1_load(ti + 2)
                        fw.barrier()
                    fw.collective("AllGather", ZL[:, :], ZF[:, :], PAIRS, ["ZL"], ["ZF"])
                    fw.barrier()
                    for c_ in range(4):
                        fw.collective("AllGather", XAL[c_ * 128:(c_ + 1) * 128, :], XAF[c_ * 256:(c_ + 1) * 256, :], PAIRS, ["XA"], ["XAF"])
                    for c_ in range(4):
                        fw.collective("AllGather", GGL[c_ * 128:(c_ + 1) * 128, :], GGF[c_ * 256:(c_ + 1) * 256, :], PAIRS, ["GG"], ["GGF"])
                    if stop_after == "M1" and l == 0:
                        xafd = nc.dram_tensor("XAFd", [2 * DA, TLOC], F32, kind="ExternalOutput").ap()
                        zfd = nc.dram_tensor("ZFd", [2 * DC, TLOC], BF16, kind="ExternalOutput").ap()
                        dma("sp", xafd[:, :], XAF[:, :], writes=["xafd"])
                        dma("sp", zfd[:, :], ZF[:, :], writes=["zfd"])
                        fw.wait_all("sp")
                        return nc

                    if True:
                        with ExitStack() as e2:
                            e2.enter_context(nc.named_scope("DFT_l%d" % l))
                            t1 = sb(e2, "t1", [128, 3, 128], BF16)
                            tw = sb(e2, "tw", [128, 128, 64], BF16)
                            c256 = sb(e2, "c256", [128, 2, 2, 256], BF16)
                            cspad = sb(e2, "cspad", [64, 2, 2, 128])
                            wft = sb(e2, "wft", [64, 4, 64])
                            abd = sb(e2, "abd", [128, 2, 256], BF16)
                            abdc = sb(e2, "abdc", [128, 2, 256], BF16)
                            XTs = sb(e2, "XTs", [128, 2, T])
                            XTc = sb(e2, "XTc", [128, 2, TC])
                            Yp = [sb(e2, "Yp%d" % i, [128, 512], BF16) for i in range(2)]
                            Gs = [sb(e2, "Gs%d" % i, [128, 2, 8, 256], BF16) for i in range(2)]
                            Rb = [sb(e2, "Rb%d" % i, [128, 8, 256], BF16) for i in range(2)]
                            sq = sb(e2, "sq", [128, 2, TL])
                            rr = sb(e2, "rr", [128, TL])
                            rrs = sb(e2, "rrs", [128, TL])
                            rrt = sb(e2, "rrt", [128, TL])
                            yCT = [sb(e2, "yCT%d" % i, [128, 2, TL], BF16) for i in range(2)]
                            pY = [ps(e2, "pY%d" % i, [128, 512]) for i in range(2)]
                            pG = [ps(e2, "pG%d" % i, [128, 2, 256]) for i in range(2)]
                            pX = [ps(e2, "pX%d" % i, [128, 8, 64]) for i in range(2)]
                            pS = ps(e2, "pS", [128, 512])
                            pS1 = ps(e2, "pS1", [128, 512])
                            for r_ in range(2):
                                dma("sp", zT[:, :, r_ * TLOC:(r_ + 1) * TLOC], ZFv[r_], writes=["zT"])
                            dma("sp", t1[:], t1_d[:, :, :], writes=["t1"])
                            dma("sp", tw[:].rearrange("p a b -> p (a b)"), tw_d[:, :], writes=["tw"])
                            dma("sp", c256[:], c256_d[:, :, :, :], writes=["c256"])
                            dma("sp", cspad[:], cspad_d[:, :, :, :], writes=["cspad"])
                            dma("sp", wft[:], wf_d[l].rearrange("g j e -> j g e"), writes=["wft"])
                            for cc in range(2):
                                for pq in range(2):
                                    for gl in range(2):
                                        op("pe", lambda e: e.matmul(pY[0][:, pq * 128 + gl * 64:pq * 128 + gl * 64 + 64],
                                                                    lhsT=cspad[:, pq, gl, :], rhs=wft[:, 2 * cc + gl, :], start=True, stop=True),
                                           reads=["cspad", "wft"], writes=["pY0"])
                                op("act", lambda e: e.activation(out=abd[:, cc, :], in_=pY[0][:, 0:256], func=AF.Identity,
                                                                 scale=1.0 / math.sqrt(T * 64.0)), writes=["pY0", "abd"])
                                op("act", lambda e: e.activation(out=abdc[:, cc, :], in_=pY[0][:, 0:256], func=AF.Identity,
                                                                 scale=1.0 / math.sqrt(TC * 64.0)), writes=["pY0", "abdc"])

                            rr2 = [sb(e2, "rr2_%d" % i, [128, TL]) for i in range(2)]
                            rrs2 = [sb(e2, "rrs2_%d" % i, [128, TL]) for i in range(2)]
                            rrt2 = [sb(e2, "rrt2_%d" % i, [128, TL]) for i in range(2)]

                            def rmsc1(src, b):
                                op("act", lambda e: e.activation(out=sq[:], in_=src, func=AF.Square), reads=["XT"], writes=["sq"])
                                pS_ = pS if b == 0 else pS1
                                for c in range(2):
                                    op("pe", lambda e: e.matmul(pS_[:, 0:TL], lhsT=ones_f[:], rhs=sq[:, c, :], start=(c == 0), stop=(c == 1)),
                                       reads=["ones_f", "sq"], writes=["pS%d" % b])

                            def rmsc2(b):
                                pS_ = pS if b == 0 else pS1
                                op("act", lambda e: e.activation(out=rr2[b][:], in_=pS_[:, 0:TL], func=AF.Ln, scale=1.0 / DC, bias=EPS),
                                   writes=["pS%d" % b, "rr2_%d" % b])
                                op("act", lambda e: e.activation(out=rrs2[b][:], in_=rr2[b][:], func=AF.Exp, scale=-0.5),
                                   reads=["rr2_%d" % b], writes=["rrs2_%d" % b])

                            def rmsc3(src, col0, b):
                                for c in range(2):
                                    op("dve", lambda e: e.scalar_tensor_tensor(out=yCT[b][:, c, :], in0=src[:, c, :], scalar=gmixc[:, 6 + c:7 + c],
                                                                               in1=rrs2[b][:], op0=ALU.mult, op1=ALU.mult),
                                       reads=["XT", "gmixc", "rrs2_%d" % b], writes=["yCT%d_%d" % (b, c)])
                                dma("sp", YTv[:, 6:8, col0:col0 + TL], yCT[b][:], reads=["yCT%d_0" % b, "yCT%d_1" % b], writes=["YT"])

                            def rms_store_c(src, col0, b):
                                rmsc1(src, b)
                                rmsc2(b)
                                rmsc3(src, col0, b)

                            if not last:
                                Ypc = [sb(e2, "Ypc%d" % i, [128, 512], BF16) for i in range(2)]
                                for t in range(2):
                                    for cc in range(2):
                                        op("pe", lambda e: e.matmul(pY[t][:, cc * 256:(cc + 1) * 256], lhsT=zTc[:, cc, t * 128:(t + 1) * 128],
                                                                    rhs=abdc[:, cc, :], start=True, stop=True), reads=["zTc", "abdc"], writes=["pY%d" % t])
                                    op("act", lambda e: e.activation(out=Ypc[t][:], in_=pY[t][:], func=AF.Identity), writes=["pY%d" % t, "Ypc%d" % t])
                                for cc in range(2):
                                    n = 0
                                    for t in range(2):
                                        for pq in range(2):
                                            op("pe", lambda e: e.matmul(pG[cc][:].rearrange("p a b -> p (a b)")[:, 0:256],
                                                                        lhsT=Ypc[t][:, cc * 256 + pq * 128:cc * 256 + (pq + 1) * 128],
                                                                        rhs=c256[:, t, pq, :], start=(n == 0), stop=(n == 3)),
                                               reads=["Ypc%d" % t, "c256"], writes=["pG%d" % cc])
                                            n += 1
                                    op("dve", lambda e: e.tensor_copy(out=XTc[:, cc, :], in_=pG[cc][:].rearrange("p a b -> p (a b)")[:, 0:256]),
                                       writes=["pG%d" % cc, "XT"])
                                rms_store_c(XTc[:, :, :], T, 0)

                            zTv = zT[:].rearrange("p c (a b) -> p c b a", b=64)
                            GDv = GD
                            for l2 in range(64):
                                b = l2 % 2
                                for cc in range(2):
                                    op("pe", lambda e: e.matmul(pY[b][:, cc * 256:(cc + 1) * 256], lhsT=zTv[:, cc, l2, :], rhs=abd[:, cc, :],
                                                                start=True, stop=True), reads=["zT", "abd"], writes=["pY%d" % b])
                                op("act", lambda e: e.activation(out=Yp[b][:], in_=pY[b][:], func=AF.Identity), writes=["pY%d" % b, "Yp%d" % b])
                                Ypv = Yp[b][:].rearrange("p (c q j) -> p q c j", c=2, q=2)
                                combos = [(0, 0, 0), (0, 1, 1), (1, 0, 1), (1, 1, 2)]
                                for (ri, pq, ti_) in combos:
                                    op("pe", lambda e: e.matmul(pG[b][:, ri, :].rearrange("p (c j) -> p c j", c=2), lhsT=t1[:, ti_, :], rhs=Ypv[:, pq, :, :],
                                                                start=(pq == 0), stop=(pq == 1)), reads=["t1", "Yp%d" % b], writes=["pG%d" % b])
                                gb = (l2 // 8) % 2
                                op("dve", lambda e: e.tensor_copy(out=Gs[gb][:, :, l2 % 8, :], in_=pG[b][:]), writes=["pG%d" % b, "Gs%d" % gb])
                                if l2 % 8 == 7:
                                    l0 = l2 - 7
                                    dma("sp", GDv[:, :, l0:l0 + 8, :], Gs[gb][:], reads=["Gs%d" % gb], writes=["GD"])
                            XTv = XTs[:].rearrange("p c (k2 k1) -> p c k1 k2", k1=128)
                            for kb in range(16):
                                b = kb % 2
                                dma("sp", Rb[b][:], GD[kb * 8:(kb + 1) * 8, :, :, :].rearrange("k r l c -> (r l) k c"), reads=["GD"], writes=["Rb%d" % b])
                                for cc in range(2):
                                    px, kpx = pX[cc], "pX%d" % cc
                                    for r in range(8):
                                        op("pe", lambda e: e.matmul(px[:, r, :], lhsT=Rb[b][:, r, cc * 128:(cc + 1) * 128], rhs=tw[:, kb * 8 + r, :],
                                                                    start=True, stop=True), reads=["Rb%d" % b, "tw"], writes=[kpx])
                                    op("act" if cc == 0 else "dve",
                                       (lambda e: e.activation(out=XTv[:, cc, kb * 8:(kb + 1) * 8, :], in_=px[:], func=AF.Identity)) if cc == 0 else
                                       (lambda e: e.tensor_copy(out=XTv[:, cc, kb * 8:(kb + 1) * 8, :], in_=px[:])),
                                       writes=[kpx, "XT"])
                            for ti in range(NT + 2):
                                if ti < NT:
                                    rmsc1(XTs[:, :, ti * TL:(ti + 1) * TL], ti % 2)
                                if 0 <= ti - 1 < NT:
                                    rmsc2((ti - 1) % 2)
                                if 0 <= ti - 2 < NT:
                                    t2 = ti - 2
                                    rmsc3(XTs[:, :, t2 * TL:(t2 + 1) * TL], t2 * TL, t2 % 2)
                            fw.barrier()
                if stop_after == "DFT" and l == 0:
                    return nc

                with ExitStack() as e3:
                    wg = sb(e3, "wg", [128, 2, 2, 4, 128], BF16)
                    cw = sb(e3, "cw", [128, 4, 4])
                    cb = sb(e3, "cb", [128, 4])
                    gb_ = sb(e3, "gbias", [128, 2, 2, 4])
                    lam = sb(e3, "lam", [128, 2, 4])
                    hnsp = sb(e3, "hnsp", [128, 2, 4])
                    nsp = sb(e3, "nsp", [128, 2, 4])
                    op("pool", lambda e: e.memset(wg[:], 0.0), writes=["wg"])
                    for ax, wd in enumerate((wa_d, wx_d)):
                        for h2 in range(2):
                            for d_ in range(2):
                                src = wd[l, d_].rearrange("(c h) i e -> h i c e", h=2)[h2]
                                dma("pool", wg[64 * h2:64 * h2 + 64, d_, ax, :, 64 * h2:64 * h2 + 64], src, writes=["wg"])
                    for k in range(4):
                        dma("sp", cw[:, :, k], convw_d[l, k, :].rearrange("(c p) -> p c", p=128), writes=["cw"])
                    dma("sp", cb[:], convb_d[l].rearrange("(c p) -> p c", p=128), writes=["cb"])
                    for d_ in range(2):
                        dma("sp", gb_[:, 0, d_, :], ba_d[l, d_, :].rearrange("(c p) -> p c", p=128), writes=["gbias"])
                        dma("sp", gb_[:, 1, d_, :], bx_d[l, d_, :].rearrange("(c p) -> p c", p=128), writes=["gbias"])
                        dma("sp", lam[:, d_, :], lam_d[l, d_, :].rearrange("(c p) -> p c", p=128), writes=["lam"])
                    op("dve", lambda e: e.tensor_scalar(out=gb_[:], in0=gb_[:], scalar1=0.5, scalar2=None, op0=ALU.mult), writes=["gbias"])
                    op("act", lambda e: e.activation(out=lam[:], in_=lam[:], func=AF.Exp, scale=-1.0), writes=["lam"])
                    op("act", lambda e: e.activation(out=lam[:], in_=lam[:], func=AF.Ln, bias=1.0), writes=["lam"])
                    op("dve", lambda e: e.tensor_scalar(out=hnsp[:], in0=lam[:], scalar1=-4.0, scalar2=None, op0=ALU.mult), reads=["lam"], writes=["hnsp"])
                    op("dve", lambda e: e.tensor_scalar(out=nsp[:], in0=lam[:], scalar1=-8.0, scalar2=None, op0=ALU.mult), reads=["lam"], writes=["nsp"])

                    xaH = [sb(e3, "xaH%d" % i, [128, 4, TL + 3]) for i in range(2)]
                    xc = [sb(e3, "xc%d" % i, [128, 4, TL]) for i in range(2)]
                    xcb = [sb(e3, "xcb%d" % i, [128, 4, TL], BF16) for i in range(2)]
                    hS = [sb(e3, "hS%d" % i, [128, 4, TL]) for i in range(2)]
                    hfL = [sb(e3, "hfL%d" % i, [128, 4, TL]) for i in range(2)]
                    ggL = [sb(e3, "ggL%d" % i, [128, 4, TL]) for i in range(2)]
                    tr = sb(e3, "tr", [128, 4, TL])
                    tiS = [sb(e3, "tiS%d" % i, [128, 4, TL]) for i in range(2)]
                    aS = [sb(e3, "aS%d" % i, [128, 4, TL]) for i in range(2)]
                    mS = [sb(e3, "mS%d" % i, [128, 4, TL]) for i in range(2)]
                    uS = sb(e3, "uS", [128, 4, TL])
                    ya = [sb(e3, "ya%d" % i, [128, 4, TL]) for i in range(3)]
                    sq = sb(e3, "sqa", [128, 4, TL])
                    rr = [sb(e3, "rra%d" % i, [128, TL]) for i in range(2)]
                    rrs = [sb(e3, "rrsa%d" % i, [128, TL]) for i in range(2)]
                    rrt = [sb(e3, "rrta%d" % i, [128, TL]) for i in range(2)]
                    yAT = [sb(e3, "yAT%d" % i, [128, 4, TL], BF16) for i in range(2)]
                    pg = [ps(e3, "pg%d" % i, [128, 2, TL]) for i in range(4)]
                    pSs = [ps(e3, "pSa%d" % i, [128, 512]) for i in range(2)]

                    def xck(b):
                        return ["xc%d_%d" % (b, c) for c in range(4)]

                    def gates(d, b):
                        for c in range(4):
                            for ax in range(2):
                                op("pe", lambda e: e.matmul(pg[c][:, ax, :], lhsT=wg[:, d, ax, c, :], rhs=xcb[b][:, c, :], start=True, stop=True),
                                   reads=["wg", "xcb%d" % b], writes=["pg%d" % c])
                        for c in range(4):
                            op("act", lambda e: e.activation(out=tr[:, c, :], in_=pg[c][:, 0, :], func=AF.Tanh, scale=0.5, bias=gb_[:, 0, d, c:c + 1]),
                               reads=["gbias"], writes=["pg%d" % c, "tr"])
                            op("act", lambda e: e.activation(out=tiS[b][:, c, :], in_=pg[c][:, 1, :], func=AF.Tanh, scale=0.5, bias=gb_[:, 1, d, c:c + 1]),
                               reads=["gbias"], writes=["pg%d" % c, "tiS%d" % b])
                        for c in range(4):
                            op("act", lambda e: e.activation(out=aS[b][:, c, :], in_=tr[:, c, :], func=AF.Exp, scale=hnsp[:, d, c:c + 1], bias=hnsp[:, d, c:c + 1]),
                               reads=["tr", "hnsp"], writes=["aS%d" % b])
                            op("act", lambda e: e.activation(out=mS[b][:, c, :], in_=tr[:, c, :], func=AF.Exp, scale=nsp[:, d, c:c + 1], bias=nsp[:, d, c:c + 1]),
                               reads=["tr", "nsp"], writes=["mS%d" % b])
                        op("act", lambda e: e.activation(out=mS[b][:], in_=mS[b][:], func=AF.Sqrt, scale=-0.25, bias=0.25), writes=["mS%d" % b])

                    def make_u(b):
                        op("dve", lambda e: e.scalar_tensor_tensor(out=uS[:], in0=tiS[b][:], scalar=1.0, in1=xc[b][:], op0=ALU.add, op1=ALU.mult),
                           reads=["tiS%d" % b] + xck(b), writes=["uS"])
                        op("dve", lambda e: e.tensor_tensor(out=uS[:], in0=uS[:], in1=mS[b][:], op=ALU.mult), reads=["mS%d" % b], writes=["uS"])

                    scope_ = nc.named_scope("M2a_l%d" % l)
                    scope_.__enter__()

                    def load_xa(ti, b):
                        col0, is_ctx = TILES[ti]
                        first = is_ctx or col0 == 0
                        lastt = is_ctx or col0 == T - TL
                        lo = 0 if first else 2
                        hi = 0 if lastt else 1
                        if first:
                            op("pool", lambda e: e.memset(xaH[b][:, :, 0:2], 0.0), writes=["xaH%d" % b])
                        if lastt:
                            op("pool", lambda e: e.memset(xaH[b][:, :, TL + 2:TL + 3], 0.0), writes=["xaH%d" % b])
                        if is_ctx:
                            dma("sp", xaH[b][:, :, 2:TL + 2], XACv[:, :, :], writes=["xaH%d" % b])
                        else:
                            g0, g1 = col0 - lo, col0 + TL + hi
                            d0 = 2 - lo
                            for r_ in range(2):
                                a0, a1 = max(g0, r_ * TLOC), min(g1, (r_ + 1) * TLOC)
                                if a1 > a0:
                                    dma("sp", xaH[b][:, :, d0 + a0 - g0:d0 + a1 - g0], XAFv[r_][:, :, a0 - r_ * TLOC:a1 - r_ * TLOC],
                                        reads=["XAF"], writes=["xaH%d" % b])

                    def f0(ti):
                        col0, is_ctx = TILES[ti]
                        b = ti % 2
                        for c in range(4):
                            op("dve", lambda e: e.tensor_scalar(out=xc[b][:, c, :], in0=xaH[b][:, c, 0:TL], scalar1=cw[:, c, 0:1], scalar2=cb[:, c:c + 1],
                                                                op0=ALU.mult, op1=ALU.add), reads=["xaH%d" % b, "cw", "cb"], writes=["xc%d_%d" % (b, c)])
                        for k in range(1, 4):
                            for c in range(4):
                                op("dve", lambda e: e.scalar_tensor_tensor(out=xc[b][:, c, :], in0=xaH[b][:, c, k:k + TL], scalar=cw[:, c, k:k + 1],
                                                                           in1=xc[b][:, c, :], op0=ALU.mult, op1=ALU.add),
                                   reads=["xaH%d" % b, "cw"], writes=["xc%d_%d" % (b, c)])
                        op("act", lambda e: e.activation(out=xcb[b][:], in_=xc[b][:], func=AF.Identity), reads=xck(b), writes=["xcb%d" % b])
                        dma("sp", XCv[:, :, col0:col0 + TL], xc[b][:], reads=xck(b), writes=["XC"])
                        gates(0, b)

                    def f1(ti):
                        col0, is_ctx = TILES[ti]
                        b = ti % 2
                        pb_ = (ti - 1) % 2
                        make_u(b)
                        for c in range(4):
                            init = 0.0 if ti == 0 else hS[pb_][:, c, TL - 1:TL]
                            op("dve", lambda e: e.tensor_tensor_scan(out=hS[b][:, c, :], data0=aS[b][:, c, :], data1=uS[:, c, :], initial=init,
                                                                     op0=ALU.mult, op1=ALU.add),
                               reads=["aS%d" % b, "uS"] + ([] if ti == 0 else ["hS%d" % pb_]), writes=["hS%d" % b])
                        if not (last and is_ctx):
                            dma("sp", HFv[:, :, col0:col0 + TL], hS[b][:], reads=["hS%d" % b], writes=["HF"])

                    nT = len(TILES)
                    load_xa(0, 0)
                    load_xa(1, 1)
                    f0(0)
                    for ti in range(nT):
                        if ti + 2 < nT:
                            load_xa(ti + 2, ti % 2)
                        if ti + 1 < nT:
                            f0(ti + 1)
                        f1(ti)
                    fw.barrier()
                    scope_.__exit__(None, None, None)
                    if stop_after == "M2a" and l == 0:
                        return nc
                    scope_ = nc.named_scope("M2b_l%d" % l)
                    scope_.__enter__()

                    order = [0] + list(range(NT, 0, -1))
                    nO = len(order)

                    def load_x(oi):
                        col0, is_ctx = TILES[order[oi]]
                        b = oi % 2
                        dma("sp", xc[b][:], XCv[:, :, col0:col0 + TL], reads=["XC"], writes=xck(b))

                    def load_hg(oi):
                        col0, is_ctx = TILES[order[oi]]
                        b = oi % 2
                        if not (last and is_ctx):
                            dma("sp", hfL[b][:], HFv[:, :, col0:col0 + TL], reads=["HF"], writes=["hfL%d" % b])
                            ggsrc = GGCv[:, :, :] if is_ctx else GGFv[col0 // TLOC][:, :, col0 % TLOC:col0 % TLOC + TL]
                            dma("sp", ggL[b][:], ggsrc, reads=["GGF"], writes=["ggL%d" % b])

                    def b0(oi):
                        b = oi % 2
                        op("act", lambda e: e.activation(out=xcb[b][:], in_=xc[b][:], func=AF.Identity), reads=xck(b), writes=["xcb%d" % b])
                        gates(1, b)

                    def b1(oi):
                        col0, is_ctx = TILES[order[oi]]
                        b = oi % 2
                        pb_ = (oi - 1) % 2
                        y3 = oi % 3
                        make_u(b)
                        for c in range(4):
                            init = 0.0 if oi == 0 else hS[pb_][:, c, 0:1]
                            op("dve", lambda e: e.tensor_tensor_scan(out=hS[b][:, c, ::-1], data0=aS[b][:, c, ::-1], data1=uS[:, c, ::-1], initial=init,
                                                                     op0=ALU.mult, op1=ALU.add),
                               reads=["aS%d" % b, "uS"] + ([] if oi == 0 else ["hS%d" % pb_]), writes=["hS%d" % b])
                        if last and is_ctx:
                            return
                        op("pool", lambda e: e.tensor_tensor(out=ya[y3][:], in0=hS[b][:], in1=hfL[b][:], op=ALU.add), reads=["hS%d" % b, "hfL%d" % b], writes=["ya%d" % y3])
                        op("pool", lambda e: e.tensor_tensor(out=ya[y3][:], in0=ya[y3][:], in1=ggL[b][:], op=ALU.mult), reads=["ggL%d" % b], writes=["ya%d" % y3])
                        op("act", lambda e: e.activation(out=sq[:], in_=ya[y3][:], func=AF.Square), reads=["ya%d" % y3], writes=["sqa"])
                        for c in range(4):
                            op("pe", lambda e: e.matmul(pSs[b][:, 0:TL], lhsT=ones_f[:], rhs=sq[:, c, :], start=(c == 0), stop=(c == 3)),
                               reads=["ones_f", "sqa"], writes=["pSa%d" % b])

                    def b2a(oi):
                        col0, is_ctx = TILES[order[oi]]
                        b = oi % 2
                        if last and is_ctx:
                            return
                        op("dve", lambda e: e.tensor_scalar(out=rr[b][:], in0=pSs[b][:, 0:TL], scalar1=1.0 / DA, scalar2=EPS, op0=ALU.mult, op1=ALU.add),
                           writes=["pSa%d" % b, "rra%d" % b])
                        rsqrt(rrs[b][:], rr[b][:], rrt[b][:], "rrsa%d" % b, "rra%d" % b, "rrta%d" % b, iters=2)

                    def b2b(oi):
                        col0, is_ctx = TILES[order[oi]]
                        b = oi % 2
                        y3 = oi % 3
                        if last and is_ctx:
                            return
                        for c in range(4):
                            op("dve", lambda e: e.scalar_tensor_tensor(out=yAT[b][:, c, :], in0=ya[y3][:, c, :], scalar=gmixc[:, c:c + 1], in1=rrs[b][:],
                                                                       op0=ALU.mult, op1=ALU.mult), reads=["ya%d" % y3, "gmixc", "rrsa%d" % b], writes=["yAT%d_%d" % (b, c)])
                        dma("sp", YTv[:, 0:4, col0:col0 + TL], yAT[b][:], reads=["yAT%d_%d" % (b, c) for c in range(4)], writes=["YT"])

                    load_x(0)
                    load_hg(0)
                    load_x(1)
                    b0(0)
                    for oi in range(nO + 2):
                        if oi + 1 < nO:
                            load_hg(oi + 1)
                            b0(oi + 1)
                        if oi < nO:
                            b1(oi)
                        if oi + 2 < nO:
                            load_x(oi + 2)
                        if 0 <= oi - 1 < nO:
                            b2a(oi - 1)
                        if 0 <= oi - 2 < nO:
                            b2b(oi - 2)
                    fw.barrier()
                    scope_.__exit__(None, None, None)
                if stop_after == "M2b" and l == 0:
                    return nc

                m3_tiles = LTILES[1:] if last else LTILES
                wdn = sb(el, "wdn", [128, NF, D], BF16)
                with ExitStack() as e4:
                    e4.enter_context(nc.named_scope("M3a_l%d" % l))
                    wout = sb(e4, "wout", [128, 8, D], BF16)
                    for k in range(8):
                        dma("pool", wout[:, k, :], wout_d[l, k * 128:(k + 1) * 128, :], writes=["wout"])
                    for f in range(NF):
                        dma("pool", wdn[:, f, :], wdn_d[l, f * 128:(f + 1) * 128, :], writes=["wdn"])
                    gate = sb(e4, "gate1", [128, 2, D])
                    lng = sb(e4, "ln1g", [128, D])
                    lnb = sb(e4, "ln1b", [128, D])
                    for s in range(2):
                        dma("sp", gate[:, s, :], MODS[s:s + 1, 2 * D:3 * D].partition_broadcast(128), reads=["MODS"], writes=["gate1"])
                    dma("sp", lng[:], ln1g_d[l:l + 1, :].partition_broadcast(128), writes=["ln1g"])
                    dma("sp", lnb[:], ln1b_d[l:l + 1, :].partition_broadcast(128), writes=["ln1b"])
                    op("dve", lambda e: e.tensor_scalar(out=gate[:], in0=gate[:], scalar1=1.0 / ALPHA, scalar2=None, op0=ALU.mult), writes=["gate1"])
                    rmask = sb(e4, "rmask", [128, 2])
                    dma("sp", rmask[:], rmask_d[:, :], writes=["rmask"])
                    woutL = sb(e4, "woutL", [128, 8, D], BF16)
                    wsel = [sb(e4, "wsel%d" % r_, [128, 6, D], BF16) for r_ in range(2)]
                    woutC = sb(e4, "woutC", [128, 8, D], BF16) if not last else None
                    AC = (0, 1, 2, 3, 6, 7)
                    for k in range(8):
                        op("dve", lambda e: e.tensor_tensor(out=woutL[:, k, :], in0=wout[:, k, :], in1=gate[:, 0, :], op=ALU.mult),
                           reads=["wout", "gate1"], writes=["woutL"])
                        if not last:
                            op("pool", lambda e: e.tensor_tensor(out=woutC[:, k, :], in0=wout[:, k, :], in1=gate[:, 1, :], op=ALU.mult),
                               reads=["wout", "gate1"], writes=["woutC"])
                    for r_ in range(2):
                        for j, k in enumerate(AC):
                            if r_ == 0:
                                op("dve", lambda e: e.tensor_scalar(out=wsel[r_][:, j, :], in0=woutL[:, k, :], scalar1=rmask[:, r_:r_ + 1], scalar2=None, op0=ALU.mult),
                                   reads=["woutL", "rmask"], writes=["wsel%d_%d" % (r_, j)])
                            else:
                                op("act", lambda e: e.activation(out=wsel[r_][:, j, :], in_=woutL[:, k, :], func=AF.Identity, scale=rmask[:, r_:r_ + 1]),
                                   reads=["woutL", "rmask"], writes=["wsel%d_%d" % (r_, j)])
                    yTb = [sb(e4, "yTb%d" % i, [128, 2, TL], BF16) for i in range(3)]
                    yC = [[sb(e4, "yC%d_%d" % (i, r_), [128, 6, TL], BF16) for r_ in range(2)] for i in range(3)]
                    xt = [sb(e4, "xt%d" % i, [128, 2, D]) for i in range(2)]
                    rt = [sb(e4, "rt%d" % i, [128, 2, D]) for i in range(2)]
                    lns3 = [ln_scr(e4, "l3a"), ln_scr(e4, "l3b")]
                    po = [ps(e4, "po%d" % i, [128, 512]) for i in range(8)]
                    n3 = len(m3_tiles)

                    def loadY(ti):
                        col0, is_ctx = m3_tiles[ti]
                        y = ti % 3
                        dma("sp", yTb[y][:], YBLv[:, :, col0:col0 + TL], reads=["YBL"], writes=["yTb%d" % y])
                        if is_ctx:
                            dma("sp", yC[y][0][:, 0:4, :], YTv[:, 0:4, T:T + TL], reads=["YT"], writes=["yC%d_0" % y])
                            dma("sp", yC[y][0][:, 4:6, :], YTv[:, 6:8, T:T + TL], reads=["YT"], writes=["yC%d_0" % y])
                        else:
                            for r_ in range(2):
                                g0 = r_ * TLOC + col0
                                dma("sp", yC[y][r_][:, 0:4, :], YTv[:, 0:4, g0:g0 + TL], reads=["YT"], writes=["yC%d_%d" % (y, r_)])
                                dma("sp", yC[y][r_][:, 4:6, :], YTv[:, 6:8, g0:g0 + TL], reads=["YT"], writes=["yC%d_%d" % (y, r_)])

                    def loadX(ti):
                        col0, is_ctx = m3_tiles[ti]
                        b = ti % 2
                        load_resid(l, col0, is_ctx, xt[b], "xt%d" % b, None, None, from_xb=True)

                    def mm3(ti):
                        col0, is_ctx = m3_tiles[ti]
                        b = ti % 2
                        y = ti % 3
                        for s in range(2):
                            for hf_ in range(2):
                                p_, kp_ = po[4 * b + 2 * s + hf_], "po%d" % (4 * b + 2 * s + hf_)
                                cs = slice(hf_ * 512, (hf_ + 1) * 512)
                                ts_ = slice(s * 128, (s + 1) * 128)
                                steps = []
                                wB = woutC if is_ctx else woutL
                                for j in range(2):
                                    steps.append((yTb[y][:, j, ts_], wB[:, 4 + j, cs], ["yTb%d" % y, "woutC" if is_ctx else "woutL"]))
                                if is_ctx:
                                    for j, k in enumerate(AC):
                                        steps.append((yC[y][0][:, j, ts_], woutC[:, k, cs], ["yC%d_0" % y, "woutC"]))
                                else:
                                    for r_ in range(2):
                                        for j in range(6):
                                            steps.append((yC[y][r_][:, j, ts_], wsel[r_][:, j, cs], ["yC%d_%d" % (y, r_), "wsel%d_%d" % (r_, j)]))
                                for n_, (lh, rh, rd) in enumerate(steps):
                                    op("pe", lambda e: e.matmul(p_[:], lhsT=lh, rhs=rh, start=(n_ == 0), stop=(n_ == len(steps) - 1)), reads=rd, writes=[kp_])

                    def res3(ti):
                        b = ti % 2
                        for s in range(2):
                            for hf_ in range(2):
                                p_, kp_ = po[4 * b + 2 * s + hf_], "po%d" % (4 * b + 2 * s + hf_)
                                op("dve", lambda e: e.tensor_tensor(out=rt[b][:, s, hf_ * 512:(hf_ + 1) * 512], in0=p_[:],
                                                                    in1=xt[b][:, s, hf_ * 512:(hf_ + 1) * 512], op=ALU.add),
                                   reads=["xt%d" % b], writes=[kp_, "rt%d_%d" % (b, s)])

                    def stats3(ti):
                        b = ti % 2
                        ln_stats(rt[b], ["rt%d_0" % b, "rt%d_1" % b], lns3[b], EPS_POST, iters=2)

                    def norm3(ti):
                        col0, is_ctx = m3_tiles[ti]
                        b = ti % 2
                        mv, rs, kk = lns3[b]["mv"], lns3[b]["rs"], lns3[b]["k"]
                        for s in range(2):
                            kr = "rt%d_%d" % (b, s)
                            op("dve", lambda e: e.tensor_scalar(out=rt[b][:, s, :], in0=rt[b][:, s, :], scalar1=mv[:, s, 0:1],
                                                                scalar2=rs[:, s:s + 1], op0=ALU.subtract, op1=ALU.mult),
                               reads=[kk + "mv", kk + "rs"], writes=[kr])
                            op("dve", lambda e: e.tensor_tensor(out=rt[b][:, s, :], in0=rt[b][:, s, :], in1=lng[:], op=ALU.mult), reads=["ln1g"], writes=[kr])
                            op("pool", lambda e: e.tensor_tensor(out=rt[b][:, s, :], in0=rt[b][:, s, :], in1=lnb[:], op=ALU.add), reads=["ln1b"], writes=[kr])

                    def store3(ti):
                        col0, is_ctx = m3_tiles[ti]
                        b = ti % 2
                        dma("sp", XB[col0:col0 + TL, :].rearrange("(s p) d -> p s d", p=128), rt[b][:], reads=["rt%d_0" % b, "rt%d_1" % b], writes=["XB_%d" % col0])

                    for i_ in range(min(3, n3)):
                        loadY(i_)
                    for i_ in range(min(2, n3)):
                        loadX(i_)
                    mm3(0)
                    res3(0)
                    stats3(0)
                    for ti in range(n3):
                        if ti + 3 < n3:
                            loadY(ti + 3)
                        if ti + 2 < n3:
                            loadX(ti + 2)
                        if ti >= 1:
                            store3(ti - 1)
                        if ti + 1 < n3:
                            mm3(ti + 1)
                            res3(ti + 1)
                            stats3(ti + 1)
                        norm3(ti)
                    store3(n3 - 1)
                    fw.barrier()
                if stop_after == "M3a" and l == 0:
                    return nc

                with ExitStack() as e5:
                    e5.enter_context(nc.named_scope("M3b_l%d" % l))
                    wup = sb(e5, "wup", [128, 8, 2 * DFF], BF16)
                    for k in range(8):
                        for c0 in range(0, 2 * DFF, 2048):
                            c1 = min(c0 + 2048, 2 * DFF)
                            dma("pool", wup[:, k, c0:c1], wup_d[l, k * 128:(k + 1) * 128, c0:c1], writes=["wup"])
                    gate = sb(e5, "gate2", [128, 2, D])
                    lng = sb(e5, "ln2g", [128, D])
                    lnb = sb(e5, "ln2b", [128, D])
                    for s in range(2):
                        dma("sp", gate[:, s, :], MODS[s:s + 1, 5 * D:6 * D].partition_broadcast(128), reads=["MODS"], writes=["gate2"])
                    dma("sp", lng[:], ln2g_d[l:l + 1, :].partition_broadcast(128), writes=["ln2g"])
                    dma("sp", lnb[:], ln2b_d[l:l + 1, :].partition_broadcast(128), writes=["ln2b"])
                    op("dve", lambda e: e.tensor_scalar(out=gate[:], in0=gate[:], scalar1=1.0 / ALPHA, scalar2=None, op0=ALU.mult), writes=["gate2"])
                    xt = [sb(e5, "xu%d" % i, [128, 2, D]) for i in range(2)]
                    xh = sb(e5, "xhu", [128, 2, D])
                    h2T = [sb(e5, "h2T%d" % i, [128, 8, TL], BF16) for i in range(2)]
                    actT = sb(e5, "actT", [128, NF, TL], BF16)
                    sg = [sb(e5, "sg%d" % i, [128, TL]) for i in range(2)]
                    lns = ln_scr(e5, "l5")
                    lns2 = ln_scr(e5, "l6")
                    tp0 = ps(e5, "tq0", [128, 2, TL]); tp1 = ps(e5, "tq1", [128, 2, TL])
                    tps = [(tp0, "tq0"), (tp1, "tq1")]
                    pu = [ps(e5, "pu%d" % i, [128, 2, TL]) for i in range(2)]
                    pd = [ps(e5, "pd%d" % i, [128, 512]) for i in range(4)]

                    def load5(ti, b):
                        col0, is_ctx = m3_tiles[ti]
                        dma("sp", xt[b][:], XB[col0:col0 + TL, :].rearrange("(s p) d -> p s d", p=128), reads=["XB_%d" % col0], writes=["xu%d" % b])

                    su = [sb(e5, "su%d" % i, [128, TL]) for i in range(2)]
                    n5 = len(m3_tiles)
                    folded = [False]

                    def fold_gate():
                        for f in range(NF):
                            op("dve" if f % 2 == 0 else "pool",
                               lambda e: e.tensor_tensor(out=wdn[:, f, :], in0=wdn[:, f, :], in1=gate[:, 0, :], op=ALU.mult), reads=["gate2"], writes=["wdn"])
                        folded[0] = True

                    def stA(ti):
                        b = ti % 2
                        ln_tile(xt[b], "xu%d" % b, xh, "xhu", lns, EPS, eng="dve", iters=2)

                    def stT(ti):
                        col0, is_ctx = m3_tiles[ti]
                        b = ti % 2
                        transpose_mod(xh, "xhu", h2T[b], "h2T%d" % b, tps, modc, 1 if is_ctx else 0, 4, 3)

                    def stU(ti):
                        b = ti % 2
                        for f in range(NF):
                            p_, kp_ = pu[f % 2], "pu%d" % (f % 2)
                            for j in range(2):
                                c0 = j * DFF + f * 128
                                for k in range(8):
                                    op("pe", lambda e: e.matmul(p_[:, j, :], lhsT=wup[:, k, c0:c0 + 128], rhs=h2T[b][:, k, :], start=(k == 0), stop=(k == 7)),
                                       reads=["wup", "h2T%d" % b], writes=[kp_])
                            op("act", lambda e: e.activation(out=sg[f % 2][:], in_=p_[:, 0, :], func=AF.Silu), writes=[kp_, "sg%d" % (f % 2)])
                            op("act", lambda e: e.activation(out=su[f % 2][:], in_=p_[:, 1, :], func=AF.Identity), writes=[kp_, "su%d" % (f % 2)])
                            op("pool", lambda e: e.tensor_tensor(out=actT[:, f, :], in0=su[f % 2][:], in1=sg[f % 2][:], op=ALU.mult),
                               reads=["sg%d" % (f % 2), "su%d" % (f % 2)], writes=["actT"])

                    def stD(ti):
                        for s in range(2):
                            for hf_ in range(2):
                                p_, kp_ = pd[2 * s + hf_], "pd%d" % (2 * s + hf_)
                                for f in range(NF):
                                    op("pe", lambda e: e.matmul(p_[:], lhsT=actT[:, f, s * 128:(s + 1) * 128], rhs=wdn[:, f, hf_ * 512:(hf_ + 1) * 512],
                                                                start=(f == 0), stop=(f == NF - 1)), reads=["wdn", "actT"], writes=[kp_])

                    def stE(ti):
                        col0, is_ctx = m3_tiles[ti]
                        b = ti % 2
                        kx = "xu%d" % b
                        for s in range(2):
                            for hf_ in range(2):
                                p_, kp_ = pd[2 * s + hf_], "pd%d" % (2 * s + hf_)
                                cs = slice(hf_ * 512, (hf_ + 1) * 512)
                                if folded[0]:
                                    op("dve", lambda e: e.tensor_tensor(out=xt[b][:, s, cs], in0=p_[:], in1=xt[b][:, s, cs], op=ALU.add), writes=[kp_, kx])
                                else:
                                    op("dve", lambda e: e.tensor_tensor(out=xh[:, s, cs], in0=p_[:], in1=gate[:, 1 if is_ctx else 0, cs], op=ALU.mult),
                                       reads=["gate2"], writes=[kp_, "xhu"])
                        if not folded[0]:
                            op("pool", lambda e: e.tensor_tensor(out=xt[b][:], in0=xt[b][:], in1=xh[:], op=ALU.add), reads=["xhu"], writes=[kx])
                        ln_tile(xt[b], kx, xt[b], kx, lns2, EPS_POST, eng="dve", iters=3)
                        for s in range(2):
                            op("dve", lambda e: e.tensor_tensor(out=xt[b][:, s, :], in0=xt[b][:, s, :], in1=lng[:], op=ALU.mult), reads=["ln2g"], writes=[kx])
                            op("dve", lambda e: e.tensor_tensor(out=xt[b][:, s, :], in0=xt[b][:, s, :], in1=lnb[:], op=ALU.add), reads=["ln2b"], writes=[kx])
                        dst = out_d[col0:col0 + TL, :] if last else XB[col0:col0 + TL, :]
                        dma("sp", dst.rearrange("(s p) d -> p s d", p=128), xt[b][:], reads=[kx], writes=["XB_%d" % col0])

                    load5(0, 0)
                    if n5 > 1:
                        load5(1, 1)
                    if not m3_tiles[0][1]:
                        fold_gate()
                    stA(0)
                    stT(0)
                    for ti in range(n5):
                        stU(ti)
                        if ti + 1 < n5:
                            stA(ti + 1)
                        stD(ti)
                        if ti + 1 < n5:
                            stT(ti + 1)
                        stE(ti)
                        if m3_tiles[ti][1]:
                            fold_gate()
                        if ti + 2 < n5:
                            load5(ti + 2, ti % 2)
                    fw.barrier()
                if stop_after == "M3b" and l == 0:
                    return nc
        fw.wait_all("sp")
    return nc


def host_consts():
    c = {}
    c["ident"] = np.eye(128, dtype=np.float32)
    l1 = np.arange(128)[:, None].astype(np.float64)
    k1 = np.arange(128)[None, :].astype(np.float64)
    a = 2 * np.pi * l1 * k1 / 128.0
    c["t1"] = np.stack([np.cos(a), -np.sin(a), -np.cos(a)], 1).astype(np.float32).astype(ml_dtypes.bfloat16)
    l2 = np.arange(64)[:, None, None].astype(np.float64)
    kk1 = np.arange(128)[None, :, None].astype(np.float64)
    kk2 = np.arange(64)[None, None, :].astype(np.float64)
    ang = 2 * np.pi * ((kk1 + 128 * kk2) * l2 % T) / T
    c["tw"] = np.concatenate([np.cos(ang), np.sin(ang)], 0).reshape(128, T).astype(np.float32).astype(ml_dtypes.bfloat16)
    p = np.arange(128)[:, None, None].astype(np.float64)
    t = np.arange(2)[None, :, None].astype(np.float64)
    k = np.arange(256)[None, None, :].astype(np.float64)
    a2 = 2 * np.pi * (((128 * t + p) * k) % 256) / 256.0
    c["c256"] = np.stack([np.cos(a2), -np.sin(a2)], 2).astype(np.float32).astype(ml_dtypes.bfloat16)
    j = np.arange(64)[:, None].astype(np.float64)
    ch = np.arange(64)[None, :].astype(np.float64)
    a3 = 2 * np.pi * j * ch / 64.0
    cs = np.zeros((64, 2, 2, 128), np.float64)
    for gl in range(2):
        cs[:, 0, gl, gl * 64:(gl + 1) * 64] = np.cos(a3)
        cs[:, 1, gl, gl * 64:(gl + 1) * 64] = np.sin(a3)
    c["cspad"] = cs.astype(np.float32)
    quarter = D // 4
    freqs = 10000.0 ** (-np.arange(quarter, dtype=np.float32) / np.float32(quarter))
    r = np.repeat(np.arange(T // 64, dtype=np.float32), 64)
    col = np.tile(np.arange(64, dtype=np.float32), T // 64)

    def enc(pv):
        an = pv[:, None].astype(np.float32) * freqs[None, :].astype(np.float32)
        return np.concatenate([np.sin(an), np.cos(an)], -1)

    c["pos"] = np.concatenate([enc(r), enc(col)], -1).astype(np.float32)
    return c


_WNAMES = ["w_mod", "b_mod", "w_in", "conv_w", "conv_b", "lru_wa", "lru_ba", "lru_wx", "lru_bx", "lru_lam", "sg_ws", "sg_b",
           "fourier_w", "g_mix", "w_out", "ln1_g", "ln1_b", "w_up", "w_down", "ln2_g", "ln2_b"]


def make_in_maps(inputs, n_cores=N_CORES):
    consts = host_consts()
    pos = consts.pop("pos")
    shared = {n: np.ascontiguousarray(np.asarray(inputs[n], dtype=np.float32)) for n in _WNAMES}
    shared.update(consts)
    x = np.asarray(inputs["x"], dtype=np.float32)
    c = np.asarray(inputs["c"], dtype=np.float32)
    ctx = np.asarray(inputs["ctx"], dtype=np.float32)
    c_ctx = np.asarray(inputs["c_ctx"], dtype=np.float32)
    maps = []
    for core in range(n_cores):
        b, h = core // 2, core % 2
        m = dict(shared)
        m["x"] = np.ascontiguousarray(x[b, h * TLOC:(h + 1) * TLOC])
        m["pos"] = np.ascontiguousarray(pos[h * TLOC:(h + 1) * TLOC])
        m["ctx"] = np.ascontiguousarray(ctx[b])
        m["cc"] = np.ascontiguousarray(np.stack([c[b], c_ctx], 0))
        rm = np.zeros((128, 2), np.float32)
        rm[:, h] = 1.0
        m["rmask"] = rm
        maps.append(m)
    return maps


def kernel(**inputs):
    nc = build_program()
    maps = make_in_maps(inputs)
    res = run_bass_kernel_spmd(nc, maps, core_ids=list(range(N_CORES)))
    outs = [np.asarray(r["out"], dtype=np.float32) for r in res.results]
    return np.stack([np.concatenate([outs[2 * b], outs[2 * b + 1]], 0) for b in range(N_CORES // 2)], 0)
```

```python
import math
from contextlib import ExitStack

import numpy as np
import ml_dtypes
import concourse.bass as bass
import concourse.mybir as mybir
from concourse.bass_utils import run_bass_kernel_spmd

F32 = mybir.dt.float32
BF16 = mybir.dt.bfloat16
I32 = mybir.dt.int32
ALU = mybir.AluOpType
AF = mybir.ActivationFunctionType

D = 1024
T = 8192
TC = 256
TT = T + TC
TL = 256
NT = T // TL
DEPTH = 2
DA, DB, DC = 512, 256, 256
DIN = 1792
DFF = 2816
NF = DFF // 128
EPS = 1e-6
ALPHA = (2 * DEPTH) ** 0.25
EPS_POST = EPS / (ALPHA * ALPHA)
N_CORES = 8
TLOC = T // 2
NTL = TLOC // TL

SAME_ENG_SYNC = True
N_DMA_SEMS = 40


class FW:
    def __init__(self, nc, es):
        self.nc = nc
        self.es = es
        self.engs = {"pe": nc.tensor, "act": nc.scalar, "dve": nc.vector, "pool": nc.gpsimd, "sp": nc.sync}
        self.sems = {}
        self.cnt = {}
        for e in self.engs:
            self.sems[e] = es.enter_context(nc.semaphore("s_" + e))
            self.cnt[e] = 0
        self.dsem = {}
        for q in ("sp", "pool"):
            lst = []
            for i in range(N_DMA_SEMS):
                key = "d_%s_%d" % (q, i)
                self.sems[key] = es.enter_context(nc.semaphore(key))
                self.cnt[key] = 0
                lst.append(key)
            self.dsem[q] = [lst, 0]
        self.seen = {e: {} for e in self.engs}
        self.lastw = {}
        self.readers = {}
        self.ninst = 0

    def _wait(self, eng, ev):
        sk, v, prod = ev
        if prod == "pe" and eng == "pe":
            return
        if prod == eng and not SAME_ENG_SYNC:
            return
        if self.seen[eng].get(sk, 0) >= v:
            return
        self.engs[eng].wait_ge(self.sems[sk], v)
        self.seen[eng][sk] = v

    def _deps(self, eng, reads, writes):
        for k in reads:
            ev = self.lastw.get(k)
            if ev is not None:
                self._wait(eng, ev)
        for k in writes:
            ev = self.lastw.get(k)
            if ev is not None:
                self._wait(eng, ev)
            for ev in list(self.readers.get(k, {}).values()):
                self._wait(eng, ev)

    def _record(self, ev, reads, writes):
        for k in writes:
            self.lastw[k] = ev
            self.readers[k] = {}
        for k in reads:
            if k in writes:
                continue
            self.readers.setdefault(k, {})[ev[0]] = ev

    def op(self, eng, fn, reads=(), writes=()):
        self._deps(eng, reads, writes)
        inst = fn(self.engs[eng])
        self.cnt[eng] += 1
        inst.then_inc(self.sems[eng], 1)
        ev = (eng, self.cnt[eng], eng)
        self._record(ev, reads, writes)
        self.ninst += 1
        return ev

    def dma(self, q, out, in_, reads=(), writes=(), **kw):
        self._deps(q, reads, writes)
        lst, idx = self.dsem[q]
        sk = lst[idx % len(lst)]
        self.dsem[q][1] = idx + 1
        if self.cnt[sk] > 0:
            self._wait(q, (sk, self.cnt[sk], "dma"))
        inst = self.engs[q].dma_start(out=out, in_=in_, **kw)
        self.cnt[sk] += 16
        inst.then_inc(self.sems[sk], 16)
        ev = (sk, self.cnt[sk], "dma")
        self._record(ev, reads, writes)
        self.ninst += 1
        return ev

    def barrier(self):
        for e in self.engs:
            for e2 in ("pe", "act", "dve", "pool"):
                if e2 != e and self.cnt[e2] > 0:
                    self._wait(e, (e2, self.cnt[e2], e2))
            if self.cnt.get("cc", 0) > 0:
                self._wait(e, ("cc", self.cnt["cc"], "dma"))
            for q in self.dsem:
                for sk in self.dsem[q][0]:
                    if self.cnt[sk] > 0:
                        self._wait(e, (sk, self.cnt[sk], "dma"))
        self.lastw.clear()
        self.readers.clear()

    def collective(self, kind, in_ap, out_ap, groups, reads, writes):
        self._deps("pool", reads, writes)
        if "cc" not in self.sems:
            self.sems["cc"] = self.es.enter_context(self.nc.semaphore("s_cc"))
            self.cnt["cc"] = 0
        inst = self.nc.gpsimd.collective_compute(kind, ALU.bypass, replica_groups=groups, ins=[in_ap], outs=[out_ap])
        self.cnt["cc"] += 1
        inst.then_inc(self.sems["cc"], 1)
        ev = ("cc", self.cnt["cc"], "dma")
        self._record(ev, reads, writes)
        return ev

    def wait_all(self, eng="sp"):
        for ev in list(self.lastw.values()):
            self._wait(eng, ev)
        for d in list(self.readers.values()):
            for ev in list(d.values()):
                self._wait(eng, ev)
        for q in self.dsem:
            for sk in self.dsem[q][0]:
                if self.cnt[sk] > 0:
                    self._wait(eng, (sk, self.cnt[sk], "dma"))
        for e in ("pe", "act", "dve", "pool"):
            if self.cnt[e] > 0:
                self._wait(eng, (e, self.cnt[e], e))


def build_program(stop_after=None):
    nc = bass.Bass("TRN2", target_bir_lowering=False, num_devices=8)

    def din(name, shape, dt=F32):
        return nc.dram_tensor(name, list(shape), dt, kind="ExternalInput").ap()

    dbg = stop_after is not None

    CC_BUFS = ("XAL", "GGL", "ZL", "XAF", "GGF", "ZF")

    def dscr(name, shape, dt=F32):
        ext = dbg and name not in CC_BUFS
        return nc.dram_tensor(name, list(shape), dt, kind="ExternalOutput" if ext else "Internal").ap()

    x_d = din("x", [TLOC, D])
    ctx_d = din("ctx", [TC, D])
    cc_d = din("cc", [2, D])
    pos_d = din("pos", [TLOC, D])
    rmask_d = din("rmask", [128, 2])
    wmod_d = din("w_mod", [DEPTH, D, 6 * D])
    bmod_d = din("b_mod", [DEPTH, 6 * D])
    win_d = din("w_in", [DEPTH, D, DIN])
    convw_d = din("conv_w", [DEPTH, 4, DA])
    convb_d = din("conv_b", [DEPTH, DA])
    wa_d = din("lru_wa", [DEPTH, 2, 8, 64, 64])
    ba_d = din("lru_ba", [DEPTH, 2, DA])
    wx_d = din("lru_wx", [DEPTH, 2, 8, 64, 64])
    bx_d = din("lru_bx", [DEPTH, 2, DA])
    lam_d = din("lru_lam", [DEPTH, 2, DA])
    ws_d = din("sg_ws", [DEPTH, 4, 128, 128])
    sgb_d = din("sg_b", [DEPTH, 4, 128])
    wf_d = din("fourier_w", [DEPTH, 4, 64, 64])
    gmix_d = din("g_mix", [DEPTH, D])
    wout_d = din("w_out", [DEPTH, D, D])
    ln1g_d = din("ln1_g", [DEPTH, D])
    ln1b_d = din("ln1_b", [DEPTH, D])
    wup_d = din("w_up", [DEPTH, D, 2 * DFF])
    wdn_d = din("w_down", [DEPTH, DFF, D])
    ln2g_d = din("ln2_g", [DEPTH, D])
    ln2b_d = din("ln2_b", [DEPTH, D])
    ident_d = din("ident", [128, 128])
    t1_d = din("t1", [128, 3, 128], BF16)
    tw_d = din("tw", [128, T], BF16)
    c256_d = din("c256", [128, 2, 2, 256], BF16)
    cspad_d = din("cspad", [64, 2, 2, 128])
    out_d = nc.dram_tensor("out", [TLOC, D], F32, kind="ExternalOutput").ap()

    XB = dscr("XB", [TLOC + TC, D])
    XAL = dscr("XAL", [DA, TLOC])
    GGL = dscr("GGL", [DA, TLOC])
    ZL = dscr("ZL", [DC, TLOC], BF16)
    XAF = dscr("XAF", [2 * DA, TLOC])
    GGF = dscr("GGF", [2 * DA, TLOC])
    ZF = dscr("ZF", [2 * DC, TLOC], BF16)
    XAC = dscr("XAC", [DA, TC])
    GGC = dscr("GGC", [DA, TC])
    YBL = dscr("YBL", [DB, TLOC + TC], BF16)
    XC = dscr("XC", [DA, TT])
    HF = dscr("HF", [DA, TT])
    YT = dscr("YT", [D, TT], BF16)
    GD = dscr("GD", [128, 2, 64, 256], BF16)
    MODS = dscr("MODS", [2, 6 * D])

    XCv = XC.rearrange("(c p) t -> p c t", p=128)
    XALv = XAL.rearrange("(c p) t -> p c t", p=128)
    GGLv = GGL.rearrange("(c p) t -> p c t", p=128)
    ZLv = ZL.rearrange("(c p) t -> p c t", p=128)
    XACv = XAC.rearrange("(c p) t -> p c t", p=128)
    GGCv = GGC.rearrange("(c p) t -> p c t", p=128)
    YBLv = YBL.rearrange("(c p) t -> p c t", p=128)
    XAFv = XAF.rearrange("(c r p) t -> r p c t", r=2, p=128)
    GGFv = GGF.rearrange("(c r p) t -> r p c t", r=2, p=128)
    ZFv = ZF.rearrange("(r c p) t -> r p c t", r=2, p=128)
    PAIRS = [[0, 1], [2, 3], [4, 5], [6, 7]]
    HFv = HF.rearrange("(c p) t -> p c t", p=128)
    YTv = YT.rearrange("(c p) t -> p c t", p=128)

    TILES = [(T, True)] + [(TL * i, False) for i in range(NT)]
    LTILES = [(TLOC, True)] + [(TL * i, False) for i in range(NTL)]
    import os as _os
    if _os.environ.get("DBG_NT"):
        LTILES = LTILES[:int(_os.environ["DBG_NT"])]

    with ExitStack() as es:
        fw = FW(nc, es)
        op = fw.op
        dma = fw.dma

        uid = [0]

        def sb(es_, name, shape, dt=F32):
            uid[0] += 1
            return es_.enter_context(nc.sbuf_tensor("%s_s%d" % (name, uid[0]), list(shape), dt))

        def ps(es_, name, shape, dt=F32):
            uid[0] += 1
            return es_.enter_context(nc.psum_tensor("%s_p%d" % (name, uid[0]), list(shape), dt))

        es.enter_context(nc.allow_non_contiguous_dma(reason="small strided parameter loads"))

        ident = sb(es, "ident", [128, 128])
        ones_f = sb(es, "ones_f", [128, 128])
        dma("sp", ident[:], ident_d[:, :], writes=["ident"])
        op("pool", lambda e: e.memset(ones_f[:], 1.0), writes=["ones_f"])

        def rsqrt(out, x, tmp, kout, kx, ktmp, eng="pool", iters=3):
            xi = x.bitcast(I32)
            oi = out.bitcast(I32)
            op("dve", lambda e: e.tensor_scalar(out=oi, in0=xi, scalar1=1, scalar2=None,
                                                op0=ALU.arith_shift_right), reads=[kx], writes=[kout])
            op("dve", lambda e: e.tensor_scalar(out=oi, in0=oi, scalar1=-1.0, scalar2=float(0x5F3759DF),
                                                op0=ALU.mult, op1=ALU.add), writes=[kout])
            for _ in range(iters):
                op(eng, lambda e: e.tensor_tensor(out=tmp, in0=x, in1=out, op=ALU.mult), reads=[kx, kout], writes=[ktmp])
                op(eng, lambda e: e.tensor_tensor(out=tmp, in0=tmp, in1=out, op=ALU.mult), reads=[kout], writes=[ktmp])
                op(eng, lambda e: e.tensor_scalar(out=tmp, in0=tmp, scalar1=-0.5, scalar2=1.5,
                                                  op0=ALU.mult, op1=ALU.add), writes=[ktmp])
                op(eng, lambda e: e.tensor_tensor(out=out, in0=out, in1=tmp, op=ALU.mult), reads=[ktmp], writes=[kout])

        def resid_rows(l, col0, is_ctx):
            if l == 0:
                return (ctx_d[0:TL, :] if is_ctx else x_d[col0:col0 + TL, :])
            return XB[col0:col0 + TL, :]

        def load_resid(l, col0, is_ctx, xt, kx, pt=None, kp=None, from_xb=False, save_xb=False):
            if from_xb and not is_ctx:
                src = XB[col0:col0 + TL, :]
            else:
                src = resid_rows(l, col0, is_ctx)
            dma("sp", xt[:], src.rearrange("(s p) d -> p s d", p=128), reads=(["XB_%d" % col0] if (from_xb or l > 0) else []), writes=[kx])
            if l == 0 and not is_ctx and not from_xb:
                dma("sp", pt[:], pos_d[col0:col0 + TL, :].rearrange("(s p) d -> p s d", p=128), writes=[kp])
                op("pool", lambda e: e.tensor_tensor(out=xt[:], in0=xt[:], in1=pt[:], op=ALU.add),
                   reads=[kp], writes=[kx])
                if save_xb:
                    dma("sp", XB[col0:col0 + TL, :].rearrange("(s p) d -> p s d", p=128), xt[:], reads=[kx], writes=["XB_%d" % col0])

        def ln_tile(xt, kx, xh, kxh, scr, eps, eng="pool", iters=3):
            st, mv, ve, rs, tmp = scr["st"], scr["mv"], scr["ve"], scr["rs"], scr["tmp"]
            kk = scr["k"]
            for s in range(2):
                for h in range(2):
                    op("dve", lambda e: e.bn_stats(out=st[:, s, h, :], in_=xt[:, s, h * 512:(h + 1) * 512]),
                       reads=[kx], writes=[kk + "st"])
                op("dve", lambda e: e.bn_aggr(out=mv[:, s, :], in_=st[:, s, :, :].rearrange("p a b -> p (a b)")),
                   reads=[kk + "st"], writes=[kk + "mv"])
            op("dve", lambda e: e.tensor_scalar(out=ve[:], in0=mv[:, :, 1], scalar1=float(eps), scalar2=None,
                                                op0=ALU.add), reads=[kk + "mv"], writes=[kk + "ve"])
            rsqrt(rs[:], ve[:], tmp[:], kk + "rs", kk + "ve", kk + "tmp", eng=eng, iters=iters)
            for s in range(2):
                op("dve", lambda e: e.tensor_scalar(out=xh[:, s, :], in0=xt[:, s, :], scalar1=mv[:, s, 0:1],
                                                    scalar2=rs[:, s:s + 1], op0=ALU.subtract, op1=ALU.mult),
                   reads=[kx, kk + "mv", kk + "rs"], writes=[kxh])

        def ln_stats(xt, kx, scr, eps, iters=3):
            st, mv, ve, rs, tmp = scr["st"], scr["mv"], scr["ve"], scr["rs"], scr["tmp"]
            kk = scr["k"]
            for s in range(2):
                for h in range(2):
                    op("dve", lambda e: e.bn_stats(out=st[:, s, h, :], in_=xt[:, s, h * 512:(h + 1) * 512]),
                       reads=(kx if isinstance(kx, list) else [kx]), writes=[kk + "st%d%d" % (s, h)])
                op("dve", lambda e: e.bn_aggr(out=mv[:, s, :], in_=st[:, s, :, :].rearrange("p a b -> p (a b)")),
                   reads=[kk + "st%d0" % s, kk + "st%d1" % s], writes=[kk + "mv"])
            op("dve", lambda e: e.tensor_scalar(out=ve[:], in0=mv[:, :, 1], scalar1=float(eps), scalar2=None,
                                                op0=ALU.add), reads=[kk + "mv"], writes=[kk + "ve"])
            rsqrt(rs[:], ve[:], tmp[:], kk + "rs", kk + "ve", kk + "tmp", iters=iters)

        def ln_apply(xt, kx, xh, kxh, scr):
            mv, rs = scr["mv"], scr["rs"]
            kk = scr["k"]
            for s in range(2):
                op("dve", lambda e: e.tensor_scalar(out=xh[:, s, :], in0=xt[:, s, :], scalar1=mv[:, s, 0:1],
                                                    scalar2=rs[:, s:s + 1], op0=ALU.subtract, op1=ALU.mult),
                   reads=[kx, kk + "mv", kk + "rs"], writes=[kxh])

        def ln_scr(es_, name):
            return dict(st=sb(es_, name + "st", [128, 2, 2, 6]), mv=sb(es_, name + "mv", [128, 2, 2]),
                        ve=sb(es_, name + "ve", [128, 2]), rs=sb(es_, name + "rs", [128, 2]),
                        tmp=sb(es_, name + "tmp", [128, 2]), k=name)

        def transpose_mod(xh, kxh, hT, khT, tps, modc, strm, jsc, jsh):
            for kp in range(4):
                tp, ktp = tps[kp % 2]
                for j in range(2):
                    k = 2 * kp + j
                    for s in range(2):
                        op("pe", lambda e: e.transpose(out=tp[:, j, s * 128:(s + 1) * 128],
                                                       in_=xh[:, s, k * 128:(k + 1) * 128], identity=ident[:]),
                           reads=[kxh, "ident"], writes=[ktp])
                for j in range(2):
                    k = 2 * kp + j
                    op("act", lambda e: e.activation(out=hT[:, k, :], in_=tp[:, j, :], func=AF.Identity,
                                                     scale=modc[:, strm, jsc, k:k + 1], bias=modc[:, strm, jsh, k:k + 1]),
                       reads=["modc"], writes=[ktp, khT])

        for l in range(DEPTH):
            last = (l == DEPTH - 1)
            with ExitStack() as el:
                with ExitStack() as ep:
                    ep.enter_context(nc.named_scope("P0_l%d" % l))
                    cct = sb(ep, "cct", [128, 8, 2])
                    sct = sb(ep, "sct", [128, 8, 2])
                    scbc = sb(ep, "scbc", [128, 8, 128])
                    bmbc = sb(ep, "bmbc", [128, 6 * D])
                    modbc = sb(ep, "modbc", [128, 6 * D])
                    wm = [sb(ep, "wm%d" % i, [128, 3072]) for i in range(3)]
                    pmod = [ps(ep, "pmod%d" % i, [128, 512]) for i in range(6)]
                    for s in range(2):
                        dma("sp", cct[:, :, s], cc_d[s, :].rearrange("(k p) -> p k", p=128), writes=["cct"])
                    dma("sp", bmbc[:], bmod_d[l:l + 1, :].partition_broadcast(128), writes=["bmbc"])
                    op("act", lambda e: e.activation(out=sct[:], in_=cct[:], func=AF.Tanh, scale=0.5), reads=["cct"], writes=["sct"])
                    op("dve", lambda e: e.tensor_scalar(out=sct[:], in0=sct[:], scalar1=0.5, scalar2=0.5, op0=ALU.mult, op1=ALU.add), writes=["sct"])
                    op("dve", lambda e: e.tensor_tensor(out=sct[:], in0=sct[:], in1=cct[:], op=ALU.mult), reads=["cct"], writes=["sct"])
                    for k in range(8):
                        for s in range(2):
                            op("dve", lambda e: e.tensor_scalar(out=scbc[:, k, 64 * s:64 * s + 64], in0=ones_f[:, 0:64],
                                                                scalar1=sct[:, k, s:s + 1], scalar2=None, op0=ALU.mult),
                               reads=["sct", "ones_f"], writes=["scbc"])
                    ld = 0
                    for half in range(2):
                        for k in range(8):
                            w = wm[ld % 3]
                            kw = "wm%d" % (ld % 3)
                            ld += 1
                            dma("sp", w[:], wmod_d[l, k * 128:(k + 1) * 128, half * 3072:(half + 1) * 3072], writes=[kw])
                            for n in range(6):
                                op("pe", lambda e: e.matmul(pmod[n][:], lhsT=scbc[:, k, :], rhs=w[:, n * 512:(n + 1) * 512],
                                                            start=(k == 0), stop=(k == 7)),
                                   reads=["scbc", kw], writes=["pmod%d" % n])
                        for n in range(6):
                            c0 = half * 3072 + n * 512
                            op("dve", lambda e: e.tensor_tensor(out=modbc[:, c0:c0 + 512], in0=pmod[n][:], in1=bmbc[:, c0:c0 + 512],
                                                                op=ALU.add), reads=["bmbc"], writes=["pmod%d" % n, "modbc"])
                    dma("sp", MODS[0:1, :], modbc[0:1, :], reads=["modbc"], writes=["MODS"])
                    dma("sp", MODS[1:2, :], modbc[64:65, :], reads=["modbc"], writes=["MODS"])
                    fw.barrier()

                modc = sb(el, "modc", [128, 2, 6, 8])
                gmixc = sb(el, "gmixc", [128, 8])
                for s in range(2):
                    for j in range(6):
                        dma("sp", modc[:, s, j, :], MODS[s, j * D:(j + 1) * D].rearrange("(k p) -> p k", p=128), reads=["MODS"], writes=["modc"])
                dma("sp", gmixc[:], gmix_d[l, :].rearrange("(k p) -> p k", p=128), writes=["gmixc"])
                for j in (1, 4):
                    op("dve", lambda e: e.tensor_scalar(out=modc[:, :, j, :], in0=modc[:, :, j, :], scalar1=1.0, scalar2=None,
                                                        op0=ALU.add), writes=["modc"])

                with ExitStack() as ez:
                    zT = sb(ez, "zT", [128, 2, T], BF16)
                    zTc = sb(ez, "zTc", [128, 2, TC], BF16)
                    with ExitStack() as e1:
                        e1.enter_context(nc.named_scope("M1_l%d" % l))
                        win = sb(e1, "win", [128, 8, DIN], BF16)
                        for k in range(8):
                            dma("pool", win[:, k, :], win_d[l, k * 128:(k + 1) * 128, :], writes=["win"])
                        wsT = sb(e1, "wsT", [128, 4, 128], BF16)
                        wsr = sb(e1, "wsr", [128, 4, 128])
                        bsT = sb(e1, "bsT", [128, 4])
                        dma("sp", wsr[:], ws_d[l].rearrange("h p q -> p h q"), writes=["wsr"])
                        dma("sp", bsT[:], sgb_d[l].rearrange("h p -> p h"), writes=["bsT"])
                        xt = [sb(e1, "xt%d" % i, [128, 2, D]) for i in range(2)]
                        pt = [sb(e1, "pt%d" % i, [128, 2, D]) for i in range(2)] if l == 0 else [None, None]
                        xh = sb(e1, "xh", [128, 2, D])
                        hT = [sb(e1, "hT%d" % i, [128, 8, TL], BF16) for i in range(2)]
                        lns = ln_scr(e1, "l1")
                        xaS = [sb(e1, "xaS%d" % i, [128, 4, TL]) for i in range(2)]
                        ggS = [sb(e1, "ggS%d" % i, [128, 4, TL]) for i in range(2)]
                        yBT = [sb(e1, "yBT%d" % i, [128, 2, TL], BF16) for i in range(2)]
                        zS = [sb(e1, "zS%d" % i, [128, 2, TL], BF16) for i in range(2)]
                        zB = [sb(e1, "zB%d" % i, [128, 512]) for i in range(2)]
                        yb = [sb(e1, "yb%d" % i, [128, 256]) for i in range(2)]
                        vh = sb(e1, "vh", [128, 256], BF16)
                        bst = sb(e1, "bst", [128, 4, 6])
                        bmv = sb(e1, "bmv", [128, 4, 2])
                        bve = sb(e1, "bve", [128, 4])
                        brs = sb(e1, "brs", [128, 4])
                        btmp = sb(e1, "btmp", [128, 4])
                        ssB = sb(e1, "ssB", [128, 2])
                        rB = sb(e1, "rB", [128, 2])
                        rsB = sb(e1, "rsB", [128, 2])
                        rtmp = sb(e1, "rtmp", [128, 2])
                        junk = sb(e1, "junk", [128, 256])
                        tp0 = ps(e1, "tp0", [128, 2, TL]); tp1 = ps(e1, "tp1", [128, 2, TL])
                        tps = [(tp0, "tp0"), (tp1, "tp1")]
                        pa = [ps(e1, "pa%d" % i, [128, 2, TL]) for i in range(2)]
                        pb = [ps(e1, "pb%d" % i, [128, 512]) for i in range(2)]
                        pss = ps(e1, "pss", [128, 512])
                        pyt = ps(e1, "pyt", [128, 2, TL])
                        for h in range(4):
                            op("pe", lambda e: e.transpose(out=pb[0][:, h * 128:(h + 1) * 128], in_=wsr[:, h, :], identity=ident[:]),
                               reads=["wsr", "ident"], writes=["pb0"])
                        op("dve", lambda e: e.tensor_copy(out=wsT[:].rearrange("p h q -> p (h q)"), in_=pb[0][:]), writes=["pb0", "wsT"])

                        tl = LTILES
                        nM = len(tl)
                        zBt = [[sb(e1, "zBt%d_%d" % (i, s_), [128, 512]) for s_ in range(2)] for i in range(2)]

                        def m1_load(ti):
                            nb = ti % 2
                            load_resid(l, tl[ti][0], tl[ti][1], xt[nb], "xt%d" % nb, pt[nb], "pt%d" % nb, save_xb=True)

                        def m1_A(ti):
                            b = ti % 2
                            ln_tile(xt[b], "xt%d" % b, xh, "xh", lns, EPS, eng="dve", iters=2)

                        def m1_T(ti):
                            col0, is_ctx = tl[ti]
                            b = ti % 2
                            transpose_mod(xh, "xh", hT[b], "hT%d" % b, tps, modc, 1 if is_ctx else 0, 1, 0)

                        def m1_P(ti):
                            col0, is_ctx = tl[ti]
                            b = ti % 2
                            khT = "hT%d" % b
                            pai = 0
                            only_xa = last and is_ctx
                            for grp in range(2 if only_xa else 4):
                                p_, kp_ = pa[pai % 2], "pa%d" % (pai % 2)
                                pai += 1
                                for j in range(2):
                                    oc = grp * 2 + j
                                    for k in range(8):
                                        op("pe", lambda e: e.matmul(p_[:, j, :], lhsT=win[:, k, oc * 128:(oc + 1) * 128], rhs=hT[b][:, k, :],
                                                                    start=(k == 0), stop=(k == 7)), reads=["win", khT], writes=[kp_])
                                if grp < 2:
                                    op("act", lambda e: e.activation(out=xaS[b][:, 2 * grp:2 * grp + 2, :], in_=p_[:], func=AF.Identity),
                                       writes=[kp_, "xaS%d" % b])
                                else:
                                    g2 = grp - 2
                                    op("act", lambda e: e.activation(out=ggS[b][:, 2 * g2:2 * g2 + 2, :], in_=p_[:], func=AF.Gelu),
                                       writes=[kp_, "ggS%d" % b])
                            dma("sp", XACv[:, :, :] if is_ctx else XALv[:, :, col0:col0 + TL], xaS[b][:], reads=["xaS%d" % b], writes=["XA"])
                            if only_xa:
                                return
                            dma("sp", GGCv[:, :, :] if is_ctx else GGLv[:, :, col0:col0 + TL], ggS[b][:], reads=["ggS%d" % b], writes=["GG"])
                            p_, kp_ = pa[pai % 2], "pa%d" % (pai % 2)
                            for j in range(2):
                                for k in range(8):
                                    op("pe", lambda e: e.matmul(p_[:, j, :], lhsT=win[:, k, 1536 + j * 128:1536 + (j + 1) * 128], rhs=hT[b][:, k, :],
                                                                start=(k == 0), stop=(k == 7)), reads=["win", khT], writes=[kp_])
                            if is_ctx:
                                op("act", lambda e: e.activation(out=zTc[:, :, :], in_=p_[:], func=AF.Identity), writes=[kp_, "zTc"])
                            else:
                                op("act", lambda e: e.activation(out=zS[b][:], in_=p_[:], func=AF.Identity), writes=[kp_, "zS%d" % b])
                                dma("sp", ZLv[:, :, col0:col0 + TL], zS[b][:], reads=["zS%d" % b], writes=["ZL"])

                        def m1_Bproj(ti):
                            col0, is_ctx = tl[ti]
                            if last and is_ctx:
                                return
                            b = ti % 2
                            for s in range(2):
                                p_, kp_ = pb[s], "pb%d" % s
                                for k in range(8):
                                    op("pe", lambda e: e.matmul(p_[:], lhsT=hT[b][:, k, s * 128:(s + 1) * 128], rhs=win[:, k, 1024:1536],
                                                                start=(k == 0), stop=(k == 7)), reads=["win", "hT%d" % b], writes=[kp_])
                                op("act", lambda e: e.activation(out=zBt[b][s][:], in_=p_[:], func=AF.Gelu), writes=[kp_, "zBt%d_%d" % (b, s)])

                        bmv8 = sb(e1, "bmv8", [128, 8, 2])
                        bve8 = sb(e1, "bve8", [128, 8])
                        brs8 = sb(e1, "brs8", [128, 8])
                        btmp8 = sb(e1, "btmp8", [128, 8])
                        bst8 = sb(e1, "bst8", [128, 8, 6])
                        vh2 = sb(e1, "vh2", [128, 2, 256], BF16)
                        ybp = [[sb(e1, "ybp%d_%d" % (i, s_), [128, 256]) for s_ in range(2)] for i in range(2)]
                        ssBp = [sb(e1, "ssBp%d" % i, [128, 2]) for i in range(2)]

                        def skipB(ti):
                            return last and tl[ti][1]

                        def m1_B1a(ti):
                            if skipB(ti):
                                return
                            b = ti % 2
                            for s in range(2):
                                zb, kzb = zBt[b][s], "zBt%d_%d" % (b, s)
                                for h in range(4):
                                    op("dve", lambda e: e.bn_stats(out=bst8[:, 4 * s + h, :], in_=zb[:, 256 + 64 * h:256 + 64 * h + 64]),
                                       reads=[kzb], writes=["bst8_%d" % (4 * s + h)])
                                for h in range(4):
                                    op("dve", lambda e: e.bn_aggr(out=bmv8[:, 4 * s + h, :], in_=bst8[:, 4 * s + h, :]),
                                       reads=["bst8_%d" % (4 * s + h)], writes=["bmv8"])
                            op("dve", lambda e: e.tensor_scalar(out=bve8[:], in0=bmv8[:, :, 1], scalar1=EPS, scalar2=None, op0=ALU.add),
                               reads=["bmv8"], writes=["bve8"])
                            rsqrt(brs8[:], bve8[:], btmp8[:], "brs8", "bve8", "btmp8", iters=2)
                            for s in range(2):
                                zb, kzb = zBt[b][s], "zBt%d_%d" % (b, s)
                                for h in range(4):
                                    op("dve", lambda e: e.tensor_scalar(out=vh2[:, s, 64 * h:64 * h + 64], in0=zb[:, 256 + 64 * h:256 + 64 * h + 64],
                                                                        scalar1=bmv8[:, 4 * s + h, 0:1], scalar2=brs8[:, 4 * s + h:4 * s + h + 1],
                                                                        op0=ALU.subtract, op1=ALU.mult),
                                       reads=[kzb, "bmv8", "brs8"], writes=["vh2_%d" % (4 * s + h)])

                        def m1_smm(ti):
                            if skipB(ti):
                                return
                            for s in range(2):
                                for h in range(4):
                                    op("pe", lambda e: e.matmul(pss[:, 256 * s + 64 * h:256 * s + 64 * h + 64], lhsT=wsT[:, h, :], rhs=vh2[:, s, 64 * h:64 * h + 64],
                                                                start=True, stop=True), reads=["wsT", "vh2_%d" % (4 * s + h)], writes=["pss"])

                        def m1_B1b(ti):
                            if skipB(ti):
                                return
                            b = ti % 2
                            for s in range(2):
                                zb, kzb = zBt[b][s], "zBt%d_%d" % (b, s)
                                for h in range(4):
                                    op("dve", lambda e: e.scalar_tensor_tensor(out=ybp[b][s][:, 64 * h:64 * h + 64], in0=pss[:, 256 * s + 64 * h:256 * s + 64 * h + 64],
                                                                               scalar=bsT[:, h:h + 1], in1=zb[:, 64 * h:64 * h + 64],
                                                                               op0=ALU.add, op1=ALU.mult),
                                       reads=["bsT", kzb], writes=["pss", "ybp%d_%d_%d" % (b, s, h)])
                            for s in range(2):
                                op("act", lambda e: e.activation(out=junk[:], in_=ybp[b][s][:], func=AF.Square, accum_out=ssBp[b][:, s:s + 1]),
                                   reads=["ybp%d_%d_%d" % (b, s, h) for h in range(4)], writes=["junk", "ssBp%d" % b])

                        def m1_B2(ti):
                            if skipB(ti):
                                return
                            col0, is_ctx = tl[ti]
                            b = ti % 2
                            op("dve", lambda e: e.tensor_scalar(out=rB[:], in0=ssBp[b][:], scalar1=1.0 / DB, scalar2=EPS, op0=ALU.mult, op1=ALU.add),
                               reads=["ssBp%d" % b], writes=["rB"])
                            rsqrt(rsB[:], rB[:], rtmp[:], "rsB", "rB", "rtmp", iters=2)
                            for s in range(2):
                                kyb = ["ybp%d_%d_%d" % (b, s, h) for h in range(4)]
                                op("dve", lambda e: e.tensor_scalar(out=ybp[b][s][:], in0=ybp[b][s][:], scalar1=rsB[:, s:s + 1], scalar2=None, op0=ALU.mult),
                                   reads=["rsB"], writes=kyb)
                                for c in range(2):
                                    op("pe", lambda e: e.transpose(out=pyt[:, c, s * 128:(s + 1) * 128], in_=ybp[b][s][:, c * 128:(c + 1) * 128],
                                                                   identity=ident[:]), reads=kyb + ["ident"], writes=["pyt"])
                            for c in range(2):
                                op("act", lambda e: e.activation(out=yBT[b][:, c, :], in_=pyt[:, c, :], func=AF.Identity, scale=gmixc[:, 4 + c:5 + c]),
                                   reads=["gmixc"], writes=["pyt", "yBT%d" % b])
                            dma("sp", YBLv[:, :, col0:col0 + TL], yBT[b][:], reads=["yBT%d" % b], writes=["YBL"])

                        m1_load(0)
                        if nM > 1:
                            m1_load(1)
                        m1_A(0)
                        m1_T(0)
                        for ti in range(nM + 2):
                            if ti < nM:
                                m1_P(ti)
                            if ti + 1 < nM:
                                m1_A(ti + 1)
                            if 2 <= ti:
                                m1_B2(ti - 2)
                            if ti + 1 < nM:
                                m1_T(ti + 1)
                            if 1 <= ti <= nM:
                                m1_B1a(ti - 1)
                            if ti < nM:
                                m1_Bproj(ti)
                            if 1 <= ti <= nM:
                                m1_smm(ti - 1)
                                m1_B1b(ti - 1)
                            if ti + 2 < nM:
                                m1_load(ti + 2)
                        fw.barrier()
                    fw.collective("AllGather", ZL[:, :], ZF[:, :], PAIRS, ["ZL"], ["ZF"])
                    fw.barrier()
                    for c_ in range(4):
                        fw.collective("AllGather", XAL[c_ * 128:(c_ + 1) * 128, :], XAF[c_ * 256:(c_ + 1) * 256, :], PAIRS, ["XA"], ["XAF"])
                    for c_ in range(4):
                        fw.collective("AllGather", GGL[c_ * 128:(c_ + 1) * 128, :], GGF[c_ * 256:(c_ + 1) * 256, :], PAIRS, ["GG"], ["GGF"])
                    if stop_after == "M1" and l == 0:
                        xafd = nc.dram_tensor("XAFd", [2 * DA, TLOC], F32, kind="ExternalOutput").ap()
                        zfd = nc.dram_tensor("ZFd", [2 * DC, TLOC], BF16, kind="ExternalOutput").ap()
                        dma("sp", xafd[:, :], XAF[:, :], writes=["xafd"])
                        dma("sp", zfd[:, :], ZF[:, :], writes=["zfd"])
                        fw.wait_all("sp")
                        return nc

                    if True:
                        with ExitStack() as e2:
                            e2.enter_context(nc.named_scope("DFT_l%d" % l))
                            t1 = sb(e2, "t1", [128, 3, 128], BF16)
                            tw = sb(e2, "tw", [128, 128, 64], BF16)
                            c256 = sb(e2, "c256", [128, 2, 2, 256], BF16)
                            cspad = sb(e2, "cspad", [64, 2, 2, 128])
                            wft = sb(e2, "wft", [64, 4, 64])
                            abd = sb(e2, "abd", [128, 2, 256], BF16)
                            abdc = sb(e2, "abdc", [128, 2, 256], BF16)
                            XTs = sb(e2, "XTs", [128, 2, T])
                            XTc = sb(e2, "XTc", [128, 2, TC])
                            Yp = [sb(e2, "Yp%d" % i, [128, 512], BF16) for i in range(2)]
                            Gs = [sb(e2, "Gs%d" % i, [128, 2, 8, 256], BF16) for i in range(2)]
                            Rb = [sb(e2, "Rb%d" % i, [128, 8, 256], BF16) for i in range(2)]
                            sq = sb(e2, "sq", [128, 2, TL])
                            rr = sb(e2, "rr", [128, TL])
                            rrs = sb(e2, "rrs", [128, TL])
                            rrt = sb(e2, "rrt", [128, TL])
                            yCT = [sb(e2, "yCT%d" % i, [128, 2, TL], BF16) for i in range(2)]
                            pY = [ps(e2, "pY%d" % i, [128, 512]) for i in range(2)]
                            pG = [ps(e2, "pG%d" % i, [128, 2, 256]) for i in range(2)]
                            pX = [ps(e2, "pX%d" % i, [128, 8, 64]) for i in range(2)]
                            pS = ps(e2, "pS", [128, 512])
                            pS1 = ps(e2, "pS1", [128, 512])
                            for r_ in range(2):
                                dma("sp", zT[:, :, r_ * TLOC:(r_ + 1) * TLOC], ZFv[r_], writes=["zT"])
                            dma("sp", t1[:], t1_d[:, :, :], writes=["t1"])
                            dma("sp", tw[:].rearrange("p a b -> p (a b)"), tw_d[:, :], writes=["tw"])
                            dma("sp", c256[:], c256_d[:, :, :, :], writes=["c256"])
                            dma("sp", cspad[:], cspad_d[:, :, :, :], writes=["cspad"])
                            dma("sp", wft[:], wf_d[l].rearrange("g j e -> j g e"), writes=["wft"])
                            for cc in range(2):
                                for pq in range(2):
                                    for gl in range(2):
                                        op("pe", lambda e: e.matmul(pY[0][:, pq * 128 + gl * 64:pq * 128 + gl * 64 + 64],
                                                                    lhsT=cspad[:, pq, gl, :], rhs=wft[:, 2 * cc + gl, :], start=True, stop=True),
                                           reads=["cspad", "wft"], writes=["pY0"])
                                op("act", lambda e: e.activation(out=abd[:, cc, :], in_=pY[0][:, 0:256], func=AF.Identity,
                                                                 scale=1.0 / math.sqrt(T * 64.0)), writes=["pY0", "abd"])
                                op("act", lambda e: e.activation(out=abdc[:, cc, :], in_=pY[0][:, 0:256], func=AF.Identity,
                                                                 scale=1.0 / math.sqrt(TC * 64.0)), writes=["pY0", "abdc"])

                            rr2 = [sb(e2, "rr2_%d" % i, [128, TL]) for i in range(2)]
                            rrs2 = [sb(e2, "rrs2_%d" % i, [128, TL]) for i in range(2)]
                            rrt2 = [sb(e2, "rrt2_%d" % i, [128, TL]) for i in range(2)]

                            def rmsc1(src, b):
                                op("act", lambda e: e.activation(out=sq[:], in_=src, func=AF.Square), reads=["XT"], writes=["sq"])
                                pS_ = pS if b == 0 else pS1
                                for c in range(2):
                                    op("pe", lambda e: e.matmul(pS_[:, 0:TL], lhsT=ones_f[:], rhs=sq[:, c, :], start=(c == 0), stop=(c == 1)),
                                       reads=["ones_f", "sq"], writes=["pS%d" % b])

                            def rmsc2(b):
                                pS_ = pS if b == 0 else pS1
                                op("act", lambda e: e.activation(out=rr2[b][:], in_=pS_[:, 0:TL], func=AF.Ln, scale=1.0 / DC, bias=EPS),
                                   writes=["pS%d" % b, "rr2_%d" % b])
                                op("act", lambda e: e.activation(out=rrs2[b][:], in_=rr2[b][:], func=AF.Exp, scale=-0.5),
                                   reads=["rr2_%d" % b], writes=["rrs2_%d" % b])

                            def rmsc3(src, col0, b):
                                for c in range(2):
                                    op("dve", lambda e: e.scalar_tensor_tensor(out=yCT[b][:, c, :], in0=src[:, c, :], scalar=gmixc[:, 6 + c:7 + c],
                                                                               in1=rrs2[b][:], op0=ALU.mult, op1=ALU.mult),
                                       reads=["XT", "gmixc", "rrs2_%d" % b], writes=["yCT%d_%d" % (b, c)])
                                dma("sp", YTv[:, 6:8, col0:col0 + TL], yCT[b][:], reads=["yCT%d_0" % b, "yCT%d_1" % b], writes=["YT"])

                            def rms_store_c(src, col0, b):
                                rmsc1(src, b)
                                rmsc2(b)
                                rmsc3(src, col0, b)

                            if not last:
                                Ypc = [sb(e2, "Ypc%d" % i, [128, 512], BF16) for i in range(2)]
                                for t in range(2):
                                    for cc in range(2):
                                        op("pe", lambda e: e.matmul(pY[t][:, cc * 256:(cc + 1) * 256], lhsT=zTc[:, cc, t * 128:(t + 1) * 128],
                                                                    rhs=abdc[:, cc, :], start=True, stop=True), reads=["zTc", "abdc"], writes=["pY%d" % t])
                                    op("act", lambda e: e.activation(out=Ypc[t][:], in_=pY[t][:], func=AF.Identity), writes=["pY%d" % t, "Ypc%d" % t])
                                for cc in range(2):
                                    n = 0
                                    for t in range(2):
                                        for pq in range(2):
                                            op("pe", lambda e: e.matmul(pG[cc][:].rearrange("p a b -> p (a b)")[:, 0:256],
                                                                        lhsT=Ypc[t][:, cc * 256 + pq * 128:cc * 256 + (pq + 1) * 128],
                                                                        rhs=c256[:, t, pq, :], start=(n == 0), stop=(n == 3)),
                                               reads=["Ypc%d" % t, "c256"], writes=["pG%d" % cc])
                                            n += 1
                                    op("dve", lambda e: e.tensor_copy(out=XTc[:, cc, :], in_=pG[cc][:].rearrange("p a b -> p (a b)")[:, 0:256]),
                                       writes=["pG%d" % cc, "XT"])
                                rms_store_c(XTc[:, :, :], T, 0)

                            zTv = zT[:].rearrange("p c (a b) -> p c b a", b=64)
                            GDv = GD
                            for l2 in range(64):
                                b = l2 % 2
                                for cc in range(2):
                                    op("pe", lambda e: e.matmul(pY[b][:, cc * 256:(cc + 1) * 256], lhsT=zTv[:, cc, l2, :], rhs=abd[:, cc, :],
                                                                start=True, stop=True), reads=["zT", "abd"], writes=["pY%d" % b])
                                op("act", lambda e: e.activation(out=Yp[b][:], in_=pY[b][:], func=AF.Identity), writes=["pY%d" % b, "Yp%d" % b])
                                Ypv = Yp[b][:].rearrange("p (c q j) -> p q c j", c=2, q=2)
                                combos = [(0, 0, 0), (0, 1, 1), (1, 0, 1), (1, 1, 2)]
                                for (ri, pq, ti_) in combos:
                                    op("pe", lambda e: e.matmul(pG[b][:, ri, :].rearrange("p (c j) -> p c j", c=2), lhsT=t1[:, ti_, :], rhs=Ypv[:, pq, :, :],
                                                                start=(pq == 0), stop=(pq == 1)), reads=["t1", "Yp%d" % b], writes=["pG%d" % b])
                                gb = (l2 // 8) % 2
                                op("dve", lambda e: e.tensor_copy(out=Gs[gb][:, :, l2 % 8, :], in_=pG[b][:]), writes=["pG%d" % b, "Gs%d" % gb])
                                if l2 % 8 == 7:
                                    l0 = l2 - 7
                                    dma("sp", GDv[:, :, l0:l0 + 8, :], Gs[gb][:], reads=["Gs%d" % gb], writes=["GD"])
                            XTv = XTs[:].rearrange("p c (k2 k1) -> p c k1 k2", k1=128)
                            for kb in range(16):
                                b = kb % 2
                                dma("sp", Rb[b][:], GD[kb * 8:(kb + 1) * 8, :, :, :].rearrange("k r l c -> (r l) k c"), reads=["GD"], writes=["Rb%d" % b])
                                for cc in range(2):
                                    px, kpx = pX[cc], "pX%d" % cc
                                    for r in range(8):
                                        op("pe", lambda e: e.matmul(px[:, r, :], lhsT=Rb[b][:, r, cc * 128:(cc + 1) * 128], rhs=tw[:, kb * 8 + r, :],
                                                                    start=True, stop=True), reads=["Rb%d" % b, "tw"], writes=[kpx])
                                    op("act" if cc == 0 else "dve",
                                       (lambda e: e.activation(out=XTv[:, cc, kb * 8:(kb + 1) * 8, :], in_=px[:], func=AF.Identity)) if cc == 0 else
                                       (lambda e: e.tensor_copy(out=XTv[:, cc, kb * 8:(kb + 1) * 8, :], in_=px[:])),
                                       writes=[kpx, "XT"])
                            for ti in range(NT + 2):
                                if ti < NT:
                                    rmsc1(XTs[:, :, ti * TL:(ti + 1) * TL], ti % 2)
                                if 0 <= ti - 1 < NT:
                                    rmsc2((ti - 1) % 2)
                                if 0 <= ti - 2 < NT:
                                    t2 = ti - 2
                                    rmsc3(XTs[:, :, t2 * TL:(t2 + 1) * TL], t2 * TL, t2 % 2)
                            fw.barrier()
                if stop_after == "DFT" and l == 0:
                    return nc

                with ExitStack() as e3:
                    wg = sb(e3, "wg", [128, 2, 2, 4, 128], BF16)
                    cw = sb(e3, "cw", [128, 4, 4])
                    cb = sb(e3, "cb", [128, 4])
                    gb_ = sb(e3, "gbias", [128, 2, 2, 4])
                    lam = sb(e3, "lam", [128, 2, 4])
                    hnsp = sb(e3, "hnsp", [128, 2, 4])
                    nsp = sb(e3, "nsp", [128, 2, 4])
                    op("pool", lambda e: e.memset(wg[:], 0.0), writes=["wg"])
                    for ax, wd in enumerate((wa_d, wx_d)):
                        for h2 in range(2):
                            for d_ in range(2):
                                src = wd[l, d_].rearrange("(c h) i e -> h i c e", h=2)[h2]
                                dma("pool", wg[64 * h2:64 * h2 + 64, d_, ax, :, 64 * h2:64 * h2 + 64], src, writes=["wg"])
                    for k in range(4):
                        dma("sp", cw[:, :, k], convw_d[l, k, :].rearrange("(c p) -> p c", p=128), writes=["cw"])
                    dma("sp", cb[:], convb_d[l].rearrange("(c p) -> p c", p=128), writes=["cb"])
                    for d_ in range(2):
                        dma("sp", gb_[:, 0, d_, :], ba_d[l, d_, :].rearrange("(c p) -> p c", p=128), writes=["gbias"])
                        dma("sp", gb_[:, 1, d_, :], bx_d[l, d_, :].rearrange("(c p) -> p c", p=128), writes=["gbias"])
                        dma("sp", lam[:, d_, :], lam_d[l, d_, :].rearrange("(c p) -> p c", p=128), writes=["lam"])
                    op("dve", lambda e: e.tensor_scalar(out=gb_[:], in0=gb_[:], scalar1=0.5, scalar2=None, op0=ALU.mult), writes=["gbias"])
                    op("act", lambda e: e.activation(out=lam[:], in_=lam[:], func=AF.Exp, scale=-1.0), writes=["lam"])
                    op("act", lambda e: e.activation(out=lam[:], in_=lam[:], func=AF.Ln, bias=1.0), writes=["lam"])
                    op("dve", lambda e: e.tensor_scalar(out=hnsp[:], in0=lam[:], scalar1=-4.0, scalar2=None, op0=ALU.mult), reads=["lam"], writes=["hnsp"])
                    op("dve", lambda e: e.tensor_scalar(out=nsp[:], in0=lam[:], scalar1=-8.0, scalar2=None, op0=ALU.mult), reads=["lam"], writes=["nsp"])

                    xaH = [sb(e3, "xaH%d" % i, [128, 4, TL + 3]) for i in range(2)]
                    xc = [sb(e3, "xc%d" % i, [128, 4, TL]) for i in range(2)]
                    xcb = [sb(e3, "xcb%d" % i, [128, 4, TL], BF16) for i in range(2)]
                    hS = [sb(e3, "hS%d" % i, [128, 4, TL]) for i in range(2)]
                    hfL = [sb(e3, "hfL%d" % i, [128, 4, TL]) for i in range(2)]
                    ggL = [sb(e3, "ggL%d" % i, [128, 4, TL]) for i in range(2)]
                    tr = sb(e3, "tr", [128, 4, TL])
                    tiS = [sb(e3, "tiS%d" % i, [128, 4, TL]) for i in range(2)]
                    aS = [sb(e3, "aS%d" % i, [128, 4, TL]) for i in range(2)]
                    mS = [sb(e3, "mS%d" % i, [128, 4, TL]) for i in range(2)]
                    uS = sb(e3, "uS", [128, 4, TL])
                    ya = [sb(e3, "ya%d" % i, [128, 4, TL]) for i in range(3)]
                    sq = sb(e3, "sqa", [128, 4, TL])
                    rr = [sb(e3, "rra%d" % i, [128, TL]) for i in range(2)]
                    rrs = [sb(e3, "rrsa%d" % i, [128, TL]) for i in range(2)]
                    rrt = [sb(e3, "rrta%d" % i, [128, TL]) for i in range(2)]
                    yAT = [sb(e3, "yAT%d" % i, [128, 4, TL], BF16) for i in range(2)]
                    pg = [ps(e3, "pg%d" % i, [128, 2, TL]) for i in range(4)]
                    pSs = [ps(e3, "pSa%d" % i, [128, 512]) for i in range(2)]

                    def xck(b):
                        return ["xc%d_%d" % (b, c) for c in range(4)]

                    def gates(d, b):
                        for c in range(4):
                            for ax in range(2):
                                op("pe", lambda e: e.matmul(pg[c][:, ax, :], lhsT=wg[:, d, ax, c, :], rhs=xcb[b][:, c, :], start=True, stop=True),
                                   reads=["wg", "xcb%d" % b], writes=["pg%d" % c])
                        for c in range(4):
                            op("act", lambda e: e.activation(out=tr[:, c, :], in_=pg[c][:, 0, :], func=AF.Tanh, scale=0.5, bias=gb_[:, 0, d, c:c + 1]),
                               reads=["gbias"], writes=["pg%d" % c, "tr"])
                            op("act", lambda e: e.activation(out=tiS[b][:, c, :], in_=pg[c][:, 1, :], func=AF.Tanh, scale=0.5, bias=gb_[:, 1, d, c:c + 1]),
                               reads=["gbias"], writes=["pg%d" % c, "tiS%d" % b])
                        for c in range(4):
                            op("act", lambda e: e.activation(out=aS[b][:, c, :], in_=tr[:, c, :], func=AF.Exp, scale=hnsp[:, d, c:c + 1], bias=hnsp[:, d, c:c + 1]),
                               reads=["tr", "hnsp"], writes=["aS%d" % b])
                            op("act", lambda e: e.activation(out=mS[b][:, c, :], in_=tr[:, c, :], func=AF.Exp, scale=nsp[:, d, c:c + 1], bias=nsp[:, d, c:c + 1]),
                               reads=["tr", "nsp"], writes=["mS%d" % b])
                        op("act", lambda e: e.activation(out=mS[b][:], in_=mS[b][:], func=AF.Sqrt, scale=-0.25, bias=0.25), writes=["mS%d" % b])

                    def make_u(b):
                        op("dve", lambda e: e.scalar_tensor_tensor(out=uS[:], in0=tiS[b][:], scalar=1.0, in1=xc[b][:], op0=ALU.add, op1=ALU.mult),
                           reads=["tiS%d" % b] + xck(b), writes=["uS"])
                        op("dve", lambda e: e.tensor_tensor(out=uS[:], in0=uS[:], in1=mS[b][:], op=ALU.mult), reads=["mS%d" % b], writes=["uS"])

                    scope_ = nc.named_scope("M2a_l%d" % l)
                    scope_.__enter__()

                    def load_xa(ti, b):
                        col0, is_ctx = TILES[ti]
                        first = is_ctx or col0 == 0
                        lastt = is_ctx or col0 == T - TL
                        lo = 0 if first else 2
                        hi = 0 if lastt else 1
                        if first:
                            op("pool", lambda e: e.memset(xaH[b][:, :, 0:2], 0.0), writes=["xaH%d" % b])
                        if lastt:
                            op("pool", lambda e: e.memset(xaH[b][:, :, TL + 2:TL + 3], 0.0), writes=["xaH%d" % b])
                        if is_ctx:
                            dma("sp", xaH[b][:, :, 2:TL + 2], XACv[:, :, :], writes=["xaH%d" % b])
                        else:
                            g0, g1 = col0 - lo, col0 + TL + hi
                            d0 = 2 - lo
                            for r_ in range(2):
                                a0, a1 = max(g0, r_ * TLOC), min(g1, (r_ + 1) * TLOC)
                                if a1 > a0:
                                    dma("sp", xaH[b][:, :, d0 + a0 - g0:d0 + a1 - g0], XAFv[r_][:, :, a0 - r_ * TLOC:a1 - r_ * TLOC],
                                        reads=["XAF"], writes=["xaH%d" % b])

                    def f0(ti):
                        col0, is_ctx = TILES[ti]
                        b = ti % 2
                        for c in range(4):
                            op("dve", lambda e: e.tensor_scalar(out=xc[b][:, c, :], in0=xaH[b][:, c, 0:TL], scalar1=cw[:, c, 0:1], scalar2=cb[:, c:c + 1],
                                                                op0=ALU.mult, op1=ALU.add), reads=["xaH%d" % b, "cw", "cb"], writes=["xc%d_%d" % (b, c)])
                        for k in range(1, 4):
                            for c in range(4):
                                op("dve", lambda e: e.scalar_tensor_tensor(out=xc[b][:, c, :], in0=xaH[b][:, c, k:k + TL], scalar=cw[:, c, k:k + 1],
                                                                           in1=xc[b][:, c, :], op0=ALU.mult, op1=ALU.add),
                                   reads=["xaH%d" % b, "cw"], writes=["xc%d_%d" % (b, c)])
                        op("act", lambda e: e.activation(out=xcb[b][:], in_=xc[b][:], func=AF.Identity), reads=xck(b), writes=["xcb%d" % b])
                        dma("sp", XCv[:, :, col0:col0 + TL], xc[b][:], reads=xck(b), writes=["XC"])
                        gates(0, b)

                    def f1(ti):
                        col0, is_ctx = TILES[ti]
                        b = ti % 2
                        pb_ = (ti - 1) % 2
                        make_u(b)
                        for c in range(4):
                            init = 0.0 if ti == 0 else hS[pb_][:, c, TL - 1:TL]
                            op("dve", lambda e: e.tensor_tensor_scan(out=hS[b][:, c, :], data0=aS[b][:, c, :], data1=uS[:, c, :], initial=init,
                                                                     op0=ALU.mult, op1=ALU.add),
                               reads=["aS%d" % b, "uS"] + ([] if ti == 0 else ["hS%d" % pb_]), writes=["hS%d" % b])
                        if not (last and is_ctx):
                            dma("sp", HFv[:, :, col0:col0 + TL], hS[b][:], reads=["hS%d" % b], writes=["HF"])

                    nT = len(TILES)
                    load_xa(0, 0)
                    load_xa(1, 1)
                    f0(0)
                    for ti in range(nT):
                        if ti + 2 < nT:
                            load_xa(ti + 2, ti % 2)
                        if ti + 1 < nT:
                            f0(ti + 1)
                        f1(ti)
                    fw.barrier()
                    scope_.__exit__(None, None, None)
                    if stop_after == "M2a" and l == 0:
                        return nc
                    scope_ = nc.named_scope("M2b_l%d" % l)
                    scope_.__enter__()

                    order = [0] + list(range(NT, 0, -1))
                    nO = len(order)

                    def load_x(oi):
                        col0, is_ctx = TILES[order[oi]]
                        b = oi % 2
                        dma("sp", xc[b][:], XCv[:, :, col0:col0 + TL], reads=["XC"], writes=xck(b))

                    def load_hg(oi):
                        col0, is_ctx = TILES[order[oi]]
                        b = oi % 2
                        if not (last and is_ctx):
                            dma("sp", hfL[b][:], HFv[:, :, col0:col0 + TL], reads=["HF"], writes=["hfL%d" % b])
                            ggsrc = GGCv[:, :, :] if is_ctx else GGFv[col0 // TLOC][:, :, col0 % TLOC:col0 % TLOC + TL]
                            dma("sp", ggL[b][:], ggsrc, reads=["GGF"], writes=["ggL%d" % b])

                    def b0(oi):
                        b = oi % 2
                        op("act", lambda e: e.activation(out=xcb[b][:], in_=xc[b][:], func=AF.Identity), reads=xck(b), writes=["xcb%d" % b])
                        gates(1, b)

                    def b1(oi):
                        col0, is_ctx = TILES[order[oi]]
                        b = oi % 2
                        pb_ = (oi - 1) % 2
                        y3 = oi % 3
                        make_u(b)
                        for c in range(4):
                            init = 0.0 if oi == 0 else hS[pb_][:, c, 0:1]
                            op("dve", lambda e: e.tensor_tensor_scan(out=hS[b][:, c, ::-1], data0=aS[b][:, c, ::-1], data1=uS[:, c, ::-1], initial=init,
                                                                     op0=ALU.mult, op1=ALU.add),
                               reads=["aS%d" % b, "uS"] + ([] if oi == 0 else ["hS%d" % pb_]), writes=["hS%d" % b])
                        if last and is_ctx:
                            return
                        op("pool", lambda e: e.tensor_tensor(out=ya[y3][:], in0=hS[b][:], in1=hfL[b][:], op=ALU.add), reads=["hS%d" % b, "hfL%d" % b], writes=["ya%d" % y3])
                        op("pool", lambda e: e.tensor_tensor(out=ya[y3][:], in0=ya[y3][:], in1=ggL[b][:], op=ALU.mult), reads=["ggL%d" % b], writes=["ya%d" % y3])
                        op("act", lambda e: e.activation(out=sq[:], in_=ya[y3][:], func=AF.Square), reads=["ya%d" % y3], writes=["sqa"])
                        for c in range(4):
                            op("pe", lambda e: e.matmul(pSs[b][:, 0:TL], lhsT=ones_f[:], rhs=sq[:, c, :], start=(c == 0), stop=(c == 3)),
                               reads=["ones_f", "sqa"], writes=["pSa%d" % b])

                    def b2a(oi):
                        col0, is_ctx = TILES[order[oi]]
                        b = oi % 2
                        if last and is_ctx:
                            return
                        op("dve", lambda e: e.tensor_scalar(out=rr[b][:], in0=pSs[b][:, 0:TL], scalar1=1.0 / DA, scalar2=EPS, op0=ALU.mult, op1=ALU.add),
                           writes=["pSa%d" % b, "rra%d" % b])
                        rsqrt(rrs[b][:], rr[b][:], rrt[b][:], "rrsa%d" % b, "rra%d" % b, "rrta%d" % b, iters=2)

                    def b2b(oi):
                        col0, is_ctx = TILES[order[oi]]
                        b = oi % 2
                        y3 = oi % 3
                        if last and is_ctx:
                            return
                        for c in range(4):
                            op("dve", lambda e: e.scalar_tensor_tensor(out=yAT[b][:, c, :], in0=ya[y3][:, c, :], scalar=gmixc[:, c:c + 1], in1=rrs[b][:],
                                                                       op0=ALU.mult, op1=ALU.mult), reads=["ya%d" % y3, "gmixc", "rrsa%d" % b], writes=["yAT%d_%d" % (b, c)])
                        dma("sp", YTv[:, 0:4, col0:col0 + TL], yAT[b][:], reads=["yAT%d_%d" % (b, c) for c in range(4)], writes=["YT"])

                    load_x(0)
                    load_hg(0)
                    load_x(1)
                    b0(0)
                    for oi in range(nO + 2):
                        if oi + 1 < nO:
                            load_hg(oi + 1)
                            b0(oi + 1)
                        if oi < nO:
                            b1(oi)
                        if oi + 2 < nO:
                            load_x(oi + 2)
                        if 0 <= oi - 1 < nO:
                            b2a(oi - 1)
                        if 0 <= oi - 2 < nO:
                            b2b(oi - 2)
                    fw.barrier()
                    scope_.__exit__(None, None, None)
                if stop_after == "M2b" and l == 0:
                    return nc

                m3_tiles = LTILES[1:] if last else LTILES
                wdn = sb(el, "wdn", [128, NF, D], BF16)
                with ExitStack() as e4:
                    e4.enter_context(nc.named_scope("M3a_l%d" % l))
                    wout = sb(e4, "wout", [128, 8, D], BF16)
                    for k in range(8):
                        dma("pool", wout[:, k, :], wout_d[l, k * 128:(k + 1) * 128, :], writes=["wout"])
                    for f in range(NF):
                        dma("pool", wdn[:, f, :], wdn_d[l, f * 128:(f + 1) * 128, :], writes=["wdn"])
                    gate = sb(e4, "gate1", [128, 2, D])
                    lng = sb(e4, "ln1g", [128, D])
                    lnb = sb(e4, "ln1b", [128, D])
                    for s in range(2):
                        dma("sp", gate[:, s, :], MODS[s:s + 1, 2 * D:3 * D].partition_broadcast(128), reads=["MODS"], writes=["gate1"])
                    dma("sp", lng[:], ln1g_d[l:l + 1, :].partition_broadcast(128), writes=["ln1g"])
                    dma("sp", lnb[:], ln1b_d[l:l + 1, :].partition_broadcast(128), writes=["ln1b"])
                    op("dve", lambda e: e.tensor_scalar(out=gate[:], in0=gate[:], scalar1=1.0 / ALPHA, scalar2=None, op0=ALU.mult), writes=["gate1"])
                    rmask = sb(e4, "rmask", [128, 2])
                    dma("sp", rmask[:], rmask_d[:, :], writes=["rmask"])
                    woutL = sb(e4, "woutL", [128, 8, D], BF16)
                    wsel = [sb(e4, "wsel%d" % r_, [128, 6, D], BF16) for r_ in range(2)]
                    woutC = sb(e4, "woutC", [128, 8, D], BF16) if not last else None
                    AC = (0, 1, 2, 3, 6, 7)
                    for k in range(8):
                        op("dve", lambda e: e.tensor_tensor(out=woutL[:, k, :], in0=wout[:, k, :], in1=gate[:, 0, :], op=ALU.mult),
                           reads=["wout", "gate1"], writes=["woutL"])
                        if not last:
                            op("pool", lambda e: e.tensor_tensor(out=woutC[:, k, :], in0=wout[:, k, :], in1=gate[:, 1, :], op=ALU.mult),
                               reads=["wout", "gate1"], writes=["woutC"])
                    for r_ in range(2):
                        for j, k in enumerate(AC):
                            if r_ == 0:
                                op("dve", lambda e: e.tensor_scalar(out=wsel[r_][:, j, :], in0=woutL[:, k, :], scalar1=rmask[:, r_:r_ + 1], scalar2=None, op0=ALU.mult),
                                   reads=["woutL", "rmask"], writes=["wsel%d_%d" % (r_, j)])
                            else:
                                op("act", lambda e: e.activation(out=wsel[r_][:, j, :], in_=woutL[:, k, :], func=AF.Identity, scale=rmask[:, r_:r_ + 1]),
                                   reads=["woutL", "rmask"], writes=["wsel%d_%d" % (r_, j)])
                    yTb = [sb(e4, "yTb%d" % i, [128, 2, TL], BF16) for i in range(3)]
                    yC = [[sb(e4, "yC%d_%d" % (i, r_), [128, 6, TL], BF16) for r_ in range(2)] for i in range(3)]
                    xt = [sb(e4, "xt%d" % i, [128, 2, D]) for i in range(2)]
                    rt = [sb(e4, "rt%d" % i, [128, 2, D]) for i in range(3)]
                    lns3 = [ln_scr(e4, "l3a"), ln_scr(e4, "l3b")]
                    po = [ps(e4, "po%d" % i, [128, 512]) for i in range(8)]
                    n3 = len(m3_tiles)

                    def loadY(ti):
                        col0, is_ctx = m3_tiles[ti]
                        y = ti % 3
                        dma("sp", yTb[y][:], YBLv[:, :, col0:col0 + TL], reads=["YBL"], writes=["yTb%d" % y])
                        if is_ctx:
                            dma("sp", yC[y][0][:, 0:4, :], YTv[:, 0:4, T:T + TL], reads=["YT"], writes=["yC%d_0" % y])
                            dma("sp", yC[y][0][:, 4:6, :], YTv[:, 6:8, T:T + TL], reads=["YT"], writes=["yC%d_0" % y])
                        else:
                            for r_ in range(2):
                                g0 = r_ * TLOC + col0
                                dma("sp", yC[y][r_][:, 0:4, :], YTv[:, 0:4, g0:g0 + TL], reads=["YT"], writes=["yC%d_%d" % (y, r_)])
                                dma("sp", yC[y][r_][:, 4:6, :], YTv[:, 6:8, g0:g0 + TL], reads=["YT"], writes=["yC%d_%d" % (y, r_)])

                    def loadX(ti):
                        col0, is_ctx = m3_tiles[ti]
                        b = ti % 2
                        load_resid(l, col0, is_ctx, xt[b], "xt%d" % b, None, None, from_xb=True)

                    def mm3(ti):
                        col0, is_ctx = m3_tiles[ti]
                        b = ti % 2
                        y = ti % 3
                        for s in range(2):
                            for hf_ in range(2):
                                p_, kp_ = po[4 * b + 2 * s + hf_], "po%d" % (4 * b + 2 * s + hf_)
                                cs = slice(hf_ * 512, (hf_ + 1) * 512)
                                ts_ = slice(s * 128, (s + 1) * 128)
                                steps = []
                                wB = woutC if is_ctx else woutL
                                for j in range(2):
                                    steps.append((yTb[y][:, j, ts_], wB[:, 4 + j, cs], ["yTb%d" % y, "woutC" if is_ctx else "woutL"]))
                                if is_ctx:
                                    for j, k in enumerate(AC):
                                        steps.append((yC[y][0][:, j, ts_], woutC[:, k, cs], ["yC%d_0" % y, "woutC"]))
                                else:
                                    for r_ in range(2):
                                        for j in range(6):
                                            steps.append((yC[y][r_][:, j, ts_], wsel[r_][:, j, cs], ["yC%d_%d" % (y, r_), "wsel%d_%d" % (r_, j)]))
                                for n_, (lh, rh, rd) in enumerate(steps):
                                    op("pe", lambda e: e.matmul(p_[:], lhsT=lh, rhs=rh, start=(n_ == 0), stop=(n_ == len(steps) - 1)), reads=rd, writes=[kp_])

                    def res3(ti):
                        b = ti % 2
                        r3 = ti % 3
                        for s in range(2):
                            for hf_ in range(2):
                                p_, kp_ = po[4 * b + 2 * s + hf_], "po%d" % (4 * b + 2 * s + hf_)
                                op("dve", lambda e: e.tensor_tensor(out=rt[r3][:, s, hf_ * 512:(hf_ + 1) * 512], in0=p_[:],
                                                                    in1=xt[b][:, s, hf_ * 512:(hf_ + 1) * 512], op=ALU.add),
                                   reads=["xt%d" % b], writes=[kp_, "rt%d_%d" % (r3, s)])

                    def stats3(ti):
                        b = ti % 2
                        r3 = ti % 3
                        ln_stats(rt[r3], ["rt%d_0" % r3, "rt%d_1" % r3], lns3[b], EPS_POST, iters=2)

                    def norm3(ti):
                        col0, is_ctx = m3_tiles[ti]
                        b = ti % 2
                        r3 = ti % 3
                        mv, rs, kk = lns3[b]["mv"], lns3[b]["rs"], lns3[b]["k"]
                        for s in range(2):
                            kr = "rt%d_%d" % (r3, s)
                            op("dve", lambda e: e.tensor_scalar(out=rt[r3][:, s, :], in0=rt[r3][:, s, :], scalar1=mv[:, s, 0:1],
                                                                scalar2=rs[:, s:s + 1], op0=ALU.subtract, op1=ALU.mult),
                               reads=[kk + "mv", kk + "rs"], writes=[kr])
                            op("dve", lambda e: e.tensor_tensor(out=rt[r3][:, s, :], in0=rt[r3][:, s, :], in1=lng[:], op=ALU.mult), reads=["ln1g"], writes=[kr])
                            op("pool", lambda e: e.tensor_tensor(out=rt[r3][:, s, :], in0=rt[r3][:, s, :], in1=lnb[:], op=ALU.add), reads=["ln1b"], writes=[kr])

                    def store3(ti):
                        col0, is_ctx = m3_tiles[ti]
                        r3 = ti % 3
                        dma("sp", XB[col0:col0 + TL, :].rearrange("(s p) d -> p s d", p=128), rt[r3][:], reads=["rt%d_0" % r3, "rt%d_1" % r3], writes=["XB_%d" % col0])

                    for i_ in range(min(3, n3)):
                        loadY(i_)
                    for i_ in range(min(2, n3)):
                        loadX(i_)
                    mm3(0)
                    res3(0)
                    stats3(0)
                    for ti in range(n3):
                        if ti + 3 < n3:
                            loadY(ti + 3)
                        if ti + 2 < n3:
                            loadX(ti + 2)
                        if ti >= 2:
                            store3(ti - 2)
                        if ti + 1 < n3:
                            mm3(ti + 1)
                            res3(ti + 1)
                            stats3(ti + 1)
                        norm3(ti)
                    if n3 >= 2:
                        store3(n3 - 2)
                    store3(n3 - 1)
                    fw.barrier()
                if stop_after == "M3a" and l == 0:
                    return nc

                with ExitStack() as e5:
                    e5.enter_context(nc.named_scope("M3b_l%d" % l))
                    wup = sb(e5, "wup", [128, 8, 2 * DFF], BF16)
                    for k in range(8):
                        for c0 in range(0, 2 * DFF, 2048):
                            c1 = min(c0 + 2048, 2 * DFF)
                            dma("pool", wup[:, k, c0:c1], wup_d[l, k * 128:(k + 1) * 128, c0:c1], writes=["wup"])
                    gate = sb(e5, "gate2", [128, 2, D])
                    lng = sb(e5, "ln2g", [128, D])
                    lnb = sb(e5, "ln2b", [128, D])
                    for s in range(2):
                        dma("sp", gate[:, s, :], MODS[s:s + 1, 5 * D:6 * D].partition_broadcast(128), reads=["MODS"], writes=["gate2"])
                    dma("sp", lng[:], ln2g_d[l:l + 1, :].partition_broadcast(128), writes=["ln2g"])
                    dma("sp", lnb[:], ln2b_d[l:l + 1, :].partition_broadcast(128), writes=["ln2b"])
                    op("dve", lambda e: e.tensor_scalar(out=gate[:], in0=gate[:], scalar1=1.0 / ALPHA, scalar2=None, op0=ALU.mult), writes=["gate2"])
                    xt = [sb(e5, "xu%d" % i, [128, 2, D]) for i in range(2)]
                    xh = sb(e5, "xhu", [128, 2, D])
                    h2T = [sb(e5, "h2T%d" % i, [128, 8, TL], BF16) for i in range(2)]
                    actT = sb(e5, "actT", [128, NF, TL], BF16)
                    sg = [sb(e5, "sg%d" % i, [128, TL]) for i in range(2)]
                    lns = ln_scr(e5, "l5")
                    lns2 = ln_scr(e5, "l6")
                    tp0 = ps(e5, "tq0", [128, 2, TL]); tp1 = ps(e5, "tq1", [128, 2, TL])
                    tps = [(tp0, "tq0"), (tp1, "tq1")]
                    pu = [ps(e5, "pu%d" % i, [128, 2, TL]) for i in range(2)]
                    pd = [ps(e5, "pd%d" % i, [128, 512]) for i in range(4)]

                    def load5(ti, b):
                        col0, is_ctx = m3_tiles[ti]
                        dma("sp", xt[b][:], XB[col0:col0 + TL, :].rearrange("(s p) d -> p s d", p=128), reads=["XB_%d" % col0], writes=["xu%d" % b])

                    su = [sb(e5, "su%d" % i, [128, TL]) for i in range(2)]
                    n5 = len(m3_tiles)
                    folded = [False]

                    def fold_gate():
                        for f in range(NF):
                            op("dve" if f % 2 == 0 else "pool",
                               lambda e: e.tensor_tensor(out=wdn[:, f, :], in0=wdn[:, f, :], in1=gate[:, 0, :], op=ALU.mult), reads=["gate2"], writes=["wdn"])
                        folded[0] = True

                    def stA(ti):
                        b = ti % 2
                        ln_tile(xt[b], "xu%d" % b, xh, "xhu", lns, EPS, eng="dve", iters=2)

                    def stT(ti):
                        col0, is_ctx = m3_tiles[ti]
                        b = ti % 2
                        transpose_mod(xh, "xhu", h2T[b], "h2T%d" % b, tps, modc, 1 if is_ctx else 0, 4, 3)

                    def stU(ti):
                        b = ti % 2
                        for f in range(NF):
                            p_, kp_ = pu[f % 2], "pu%d" % (f % 2)
                            for j in range(2):
                                c0 = j * DFF + f * 128
                                for k in range(8):
                                    op("pe", lambda e: e.matmul(p_[:, j, :], lhsT=wup[:, k, c0:c0 + 128], rhs=h2T[b][:, k, :], start=(k == 0), stop=(k == 7)),
                                       reads=["wup", "h2T%d" % b], writes=[kp_])
                            op("act", lambda e: e.activation(out=sg[f % 2][:], in_=p_[:, 0, :], func=AF.Silu), writes=[kp_, "sg%d" % (f % 2)])
                            op("act", lambda e: e.activation(out=su[f % 2][:], in_=p_[:, 1, :], func=AF.Identity), writes=[kp_, "su%d" % (f % 2)])
                            op("pool", lambda e: e.tensor_tensor(out=actT[:, f, :], in0=su[f % 2][:], in1=sg[f % 2][:], op=ALU.mult),
                               reads=["sg%d" % (f % 2), "su%d" % (f % 2)], writes=["actT"])

                    def stD(ti):
                        for s in range(2):
                            for hf_ in range(2):
                                p_, kp_ = pd[2 * s + hf_], "pd%d" % (2 * s + hf_)
                                for f in range(NF):
                                    op("pe", lambda e: e.matmul(p_[:], lhsT=actT[:, f, s * 128:(s + 1) * 128], rhs=wdn[:, f, hf_ * 512:(hf_ + 1) * 512],
                                                                start=(f == 0), stop=(f == NF - 1)), reads=["wdn", "actT"], writes=[kp_])

                    def stE(ti):
                        col0, is_ctx = m3_tiles[ti]
                        b = ti % 2
                        kx = "xu%d" % b
                        for s in range(2):
                            for hf_ in range(2):
                                p_, kp_ = pd[2 * s + hf_], "pd%d" % (2 * s + hf_)
                                cs = slice(hf_ * 512, (hf_ + 1) * 512)
                                if folded[0]:
                                    op("dve", lambda e: e.tensor_tensor(out=xt[b][:, s, cs], in0=p_[:], in1=xt[b][:, s, cs], op=ALU.add), writes=[kp_, kx])
                                else:
                                    op("dve", lambda e: e.tensor_tensor(out=xh[:, s, cs], in0=p_[:], in1=gate[:, 1 if is_ctx else 0, cs], op=ALU.mult),
                                       reads=["gate2"], writes=[kp_, "xhu"])
                        if not folded[0]:
                            op("pool", lambda e: e.tensor_tensor(out=xt[b][:], in0=xt[b][:], in1=xh[:], op=ALU.add), reads=["xhu"], writes=[kx])
                        ln_tile(xt[b], kx, xt[b], kx, lns2, EPS_POST, eng="dve", iters=3)
                        for s in range(2):
                            op("dve", lambda e: e.tensor_tensor(out=xt[b][:, s, :], in0=xt[b][:, s, :], in1=lng[:], op=ALU.mult), reads=["ln2g"], writes=[kx])
                            op("dve", lambda e: e.tensor_tensor(out=xt[b][:, s, :], in0=xt[b][:, s, :], in1=lnb[:], op=ALU.add), reads=["ln2b"], writes=[kx])
                        dst = out_d[col0:col0 + TL, :] if last else XB[col0:col0 + TL, :]
                        dma("sp", dst.rearrange("(s p) d -> p s d", p=128), xt[b][:], reads=[kx], writes=["XB_%d" % col0])

                    load5(0, 0)
                    if n5 > 1:
                        load5(1, 1)
                    if not m3_tiles[0][1]:
                        fold_gate()
                    stA(0)
                    stT(0)
                    for ti in range(n5):
                        stU(ti)
                        if ti + 1 < n5:
                            stA(ti + 1)
                        stD(ti)
                        if ti + 1 < n5:
                            stT(ti + 1)
                        stE(ti)
                        if m3_tiles[ti][1]:
                            fold_gate()
                        if ti + 2 < n5:
                            load5(ti + 2, ti % 2)
                    fw.barrier()
                if stop_after == "M3b" and l == 0:
                    return nc
        fw.wait_all("sp")
    return nc


def host_consts():
    c = {}
    c["ident"] = np.eye(128, dtype=np.float32)
    l1 = np.arange(128)[:, None].astype(np.float64)
    k1 = np.arange(128)[None, :].astype(np.float64)
    a = 2 * np.pi * l1 * k1 / 128.0
    c["t1"] = np.stack([np.cos(a), -np.sin(a), -np.cos(a)], 1).astype(np.float32).astype(ml_dtypes.bfloat16)
    l2 = np.arange(64)[:, None, None].astype(np.float64)
    kk1 = np.arange(128)[None, :, None].astype(np.float64)
    kk2 = np.arange(64)[None, None, :].astype(np.float64)
    ang = 2 * np.pi * ((kk1 + 128 * kk2) * l2 % T) / T
    c["tw"] = np.concatenate([np.cos(ang), np.sin(ang)], 0).reshape(128, T).astype(np.float32).astype(ml_dtypes.bfloat16)
    p = np.arange(128)[:, None, None].astype(np.float64)
    t = np.arange(2)[None, :, None].astype(np.float64)
    k = np.arange(256)[None, None, :].astype(np.float64)
    a2 = 2 * np.pi * (((128 * t + p) * k) % 256) / 256.0
    c["c256"] = np.stack([np.cos(a2), -np.sin(a2)], 2).astype(np.float32).astype(ml_dtypes.bfloat16)
    j = np.arange(64)[:, None].astype(np.float64)
    ch = np.arange(64)[None, :].astype(np.float64)
    a3 = 2 * np.pi * j * ch / 64.0
    cs = np.zeros((64, 2, 2, 128), np.float64)
    for gl in range(2):
        cs[:, 0, gl, gl * 64:(gl + 1) * 64] = np.cos(a3)
        cs[:, 1, gl, gl * 64:(gl + 1) * 64] = np.sin(a3)
    c["cspad"] = cs.astype(np.float32)
    quarter = D // 4
    freqs = 10000.0 ** (-np.arange(quarter, dtype=np.float32) / np.float32(quarter))
    r = np.repeat(np.arange(T // 64, dtype=np.float32), 64)
    col = np.tile(np.arange(64, dtype=np.float32), T // 64)

    def enc(pv):
        an = pv[:, None].astype(np.float32) * freqs[None, :].astype(np.float32)
        return np.concatenate([np.sin(an), np.cos(an)], -1)

    c["pos"] = np.concatenate([enc(r), enc(col)], -1).astype(np.float32)
    return c


_WNAMES = ["w_mod", "b_mod", "w_in", "conv_w", "conv_b", "lru_wa", "lru_ba", "lru_wx", "lru_bx", "lru_lam", "sg_ws", "sg_b",
           "fourier_w", "g_mix", "w_out", "ln1_g", "ln1_b", "w_up", "w_down", "ln2_g", "ln2_b"]


def make_in_maps(inputs, n_cores=N_CORES):
    consts = host_consts()
    pos = consts.pop("pos")
    shared = {n: np.ascontiguousarray(np.asarray(inputs[n], dtype=np.float32)) for n in _WNAMES}
    shared.update(consts)
    x = np.asarray(inputs["x"], dtype=np.float32)
    c = np.asarray(inputs["c"], dtype=np.float32)
    ctx = np.asarray(inputs["ctx"], dtype=np.float32)
    c_ctx = np.asarray(inputs["c_ctx"], dtype=np.float32)
    maps = []
    for core in range(n_cores):
        b, h = core // 2, core % 2
        m = dict(shared)
        m["x"] = np.ascontiguousarray(x[b, h * TLOC:(h + 1) * TLOC])
        m["pos"] = np.ascontiguousarray(pos[h * TLOC:(h + 1) * TLOC])
        m["ctx"] = np.ascontiguousarray(ctx[b])
        m["cc"] = np.ascontiguousarray(np.stack([c[b], c_ctx], 0))
        rm = np.zeros((128, 2), np.float32)
        rm[:, h] = 1.0
        m["rmask"] = rm
        maps.append(m)
    return maps


def kernel(**inputs):
    nc = build_program()
    maps = make_in_maps(inputs)
    res = run_bass_kernel_spmd(nc, maps, core_ids=list(range(N_CORES)))
    outs = [np.asarray(r["out"], dtype=np.float32) for r in res.results]
    return np.stack([np.concatenate([outs[2 * b], outs[2 * b + 1]], 0) for b in range(N_CORES // 2)], 0)
```

```python
import math
from contextlib import ExitStack

import numpy as np
import ml_dtypes
import concourse.bass as bass
import concourse.mybir as mybir
from concourse.bass_utils import run_bass_kernel_spmd

F32 = mybir.dt.float32
BF16 = mybir.dt.bfloat16
I32 = mybir.dt.int32
ALU = mybir.AluOpType
AF = mybir.ActivationFunctionType

D = 1024
T = 8192
TC = 256
TT = T + TC
TL = 256
NT = T // TL
DEPTH = 2
DA, DB, DC = 512, 256, 256
DIN = 1792
DFF = 2816
NF = DFF // 128
EPS = 1e-6
ALPHA = (2 * DEPTH) ** 0.25
EPS_POST = EPS / (ALPHA * ALPHA)
N_CORES = 8
TLOC = T // 2
NTL = TLOC // TL

SAME_ENG_SYNC = True
N_DMA_SEMS = 40


class FW:
    def __init__(self, nc, es):
        self.nc = nc
        self.es = es
        self.engs = {"pe": nc.tensor, "act": nc.scalar, "dve": nc.vector, "pool": nc.gpsimd, "sp": nc.sync}
        self.sems = {}
        self.cnt = {}
        for e in self.engs:
            self.sems[e] = es.enter_context(nc.semaphore("s_" + e))
            self.cnt[e] = 0
        self.dsem = {}
        for q in ("sp", "pool"):
            lst = []
            for i in range(N_DMA_SEMS):
                key = "d_%s_%d" % (q, i)
                self.sems[key] = es.enter_context(nc.semaphore(key))
                self.cnt[key] = 0
                lst.append(key)
            self.dsem[q] = [lst, 0]
        self.seen = {e: {} for e in self.engs}
        self.lastw = {}
        self.readers = {}
        self.ninst = 0

    def _wait(self, eng, ev):
        sk, v, prod = ev
        if prod == "pe" and eng == "pe":
            return
        if prod == eng and not SAME_ENG_SYNC:
            return
        if self.seen[eng].get(sk, 0) >= v:
            return
        self.engs[eng].wait_ge(self.sems[sk], v)
        self.seen[eng][sk] = v

    def _deps(self, eng, reads, writes):
        for k in reads:
            ev = self.lastw.get(k)
            if ev is not None:
                self._wait(eng, ev)
        for k in writes:
            ev = self.lastw.get(k)
            if ev is not None:
                self._wait(eng, ev)
            for ev in list(self.readers.get(k, {}).values()):
                self._wait(eng, ev)

    def _record(self, ev, reads, writes):
        for k in writes:
            self.lastw[k] = ev
            self.readers[k] = {}
        for k in reads:
            if k in writes:
                continue
            self.readers.setdefault(k, {})[ev[0]] = ev

    def op(self, eng, fn, reads=(), writes=()):
        self._deps(eng, reads, writes)
        inst = fn(self.engs[eng])
        self.cnt[eng] += 1
        inst.then_inc(self.sems[eng], 1)
        ev = (eng, self.cnt[eng], eng)
        self._record(ev, reads, writes)
        self.ninst += 1
        return ev

    def dma(self, q, out, in_, reads=(), writes=(), **kw):
        self._deps(q, reads, writes)
        lst, idx = self.dsem[q]
        sk = lst[idx % len(lst)]
        self.dsem[q][1] = idx + 1
        if self.cnt[sk] > 0:
            self._wait(q, (sk, self.cnt[sk], "dma"))
        inst = self.engs[q].dma_start(out=out, in_=in_, **kw)
        self.cnt[sk] += 16
        inst.then_inc(self.sems[sk], 16)
        ev = (sk, self.cnt[sk], "dma")
        self._record(ev, reads, writes)
        self.ninst += 1
        return ev

    def barrier(self):
        for e in self.engs:
            for e2 in ("pe", "act", "dve", "pool"):
                if e2 != e and self.cnt[e2] > 0:
                    self._wait(e, (e2, self.cnt[e2], e2))
            if self.cnt.get("cc", 0) > 0:
                self._wait(e, ("cc", self.cnt["cc"], "dma"))
            for q in self.dsem:
                for sk in self.dsem[q][0]:
                    if self.cnt[sk] > 0:
                        self._wait(e, (sk, self.cnt[sk], "dma"))
        self.lastw.clear()
        self.readers.clear()

    def collective(self, kind, in_ap, out_ap, groups, reads, writes):
        self._deps("pool", reads, writes)
        if "cc" not in self.sems:
            self.sems["cc"] = self.es.enter_context(self.nc.semaphore("s_cc"))
            self.cnt["cc"] = 0
        inst = self.nc.gpsimd.collective_compute(kind, ALU.bypass, replica_groups=groups, ins=[in_ap], outs=[out_ap])
        self.cnt["cc"] += 1
        inst.then_inc(self.sems["cc"], 1)
        ev = ("cc", self.cnt["cc"], "dma")
        self._record(ev, reads, writes)
        return ev

    def wait_all(self, eng="sp"):
        for ev in list(self.lastw.values()):
            self._wait(eng, ev)
        for d in list(self.readers.values()):
            for ev in list(d.values()):
                self._wait(eng, ev)
        for q in self.dsem:
            for sk in self.dsem[q][0]:
                if self.cnt[sk] > 0:
                    self._wait(eng, (sk, self.cnt[sk], "dma"))
        for e in ("pe", "act", "dve", "pool"):
            if self.cnt[e] > 0:
                self._wait(eng, (e, self.cnt[e], e))


def build_program(stop_after=None):
    nc = bass.Bass("TRN2", target_bir_lowering=False, num_devices=8)

    def din(name, shape, dt=F32):
        return nc.dram_tensor(name, list(shape), dt, kind="ExternalInput").ap()

    dbg = stop_after is not None

    CC_BUFS = ("XAL", "GGL", "ZL", "XAF", "GGF", "ZF")

    def dscr(name, shape, dt=F32):
        ext = dbg and name not in CC_BUFS
        return nc.dram_tensor(name, list(shape), dt, kind="ExternalOutput" if ext else "Internal").ap()

    x_d = din("x", [TLOC, D])
    ctx_d = din("ctx", [TC, D])
    cc_d = din("cc", [2, D])
    pos_d = din("pos", [TLOC, D])
    rmask_d = din("rmask", [128, 2])
    wmod_d = din("w_mod", [DEPTH, D, 6 * D])
    bmod_d = din("b_mod", [DEPTH, 6 * D])
    win_d = din("w_in", [DEPTH, D, DIN])
    convw_d = din("conv_w", [DEPTH, 4, DA])
    convb_d = din("conv_b", [DEPTH, DA])
    wa_d = din("lru_wa", [DEPTH, 2, 8, 64, 64])
    ba_d = din("lru_ba", [DEPTH, 2, DA])
    wx_d = din("lru_wx", [DEPTH, 2, 8, 64, 64])
    bx_d = din("lru_bx", [DEPTH, 2, DA])
    lam_d = din("lru_lam", [DEPTH, 2, DA])
    ws_d = din("sg_ws", [DEPTH, 4, 128, 128])
    sgb_d = din("sg_b", [DEPTH, 4, 128])
    wf_d = din("fourier_w", [DEPTH, 4, 64, 64])
    gmix_d = din("g_mix", [DEPTH, D])
    wout_d = din("w_out", [DEPTH, D, D])
    ln1g_d = din("ln1_g", [DEPTH, D])
    ln1b_d = din("ln1_b", [DEPTH, D])
    wup_d = din("w_up", [DEPTH, D, 2 * DFF])
    wdn_d = din("w_down", [DEPTH, DFF, D])
    ln2g_d = din("ln2_g", [DEPTH, D])
    ln2b_d = din("ln2_b", [DEPTH, D])
    ident_d = din("ident", [128, 128])
    t1_d = din("t1", [128, 3, 128], BF16)
    tw_d = din("tw", [128, T], BF16)
    c256_d = din("c256", [128, 2, 2, 256], BF16)
    cspad_d = din("cspad", [64, 2, 2, 128])
    out_d = nc.dram_tensor("out", [TLOC, D], F32, kind="ExternalOutput").ap()

    XB = dscr("XB", [TLOC + TC, D])
    XAL = dscr("XAL", [DA, TLOC])
    GGL = dscr("GGL", [DA, TLOC])
    ZL = dscr("ZL", [DC, TLOC], BF16)
    XAF = dscr("XAF", [2 * DA, TLOC])
    GGF = dscr("GGF", [2 * DA, TLOC])
    ZF = dscr("ZF", [2 * DC, TLOC], BF16)
    XAC = dscr("XAC", [DA, TC])
    GGC = dscr("GGC", [DA, TC])
    YBL = dscr("YBL", [DB, TLOC + TC], BF16)
    XC = dscr("XC", [DA, TT])
    HF = dscr("HF", [DA, TT])
    YT = dscr("YT", [D, TT], BF16)
    GD = dscr("GD", [128, 2, 64, 256], BF16)
    MODS = dscr("MODS", [2, 6 * D])

    XCv = XC.rearrange("(c p) t -> p c t", p=128)
    XALv = XAL.rearrange("(c p) t -> p c t", p=128)
    GGLv = GGL.rearrange("(c p) t -> p c t", p=128)
    ZLv = ZL.rearrange("(c p) t -> p c t", p=128)
    XACv = XAC.rearrange("(c p) t -> p c t", p=128)
    GGCv = GGC.rearrange("(c p) t -> p c t", p=128)
    YBLv = YBL.rearrange("(c p) t -> p c t", p=128)
    XAFv = XAF.rearrange("(c r p) t -> r p c t", r=2, p=128)
    GGFv = GGF.rearrange("(c r p) t -> r p c t", r=2, p=128)
    ZFv = ZF.rearrange("(r c p) t -> r p c t", r=2, p=128)
    PAIRS = [[0, 1], [2, 3], [4, 5], [6, 7]]
    HFv = HF.rearrange("(c p) t -> p c t", p=128)
    YTv = YT.rearrange("(c p) t -> p c t", p=128)

    TILES = [(T, True)] + [(TL * i, False) for i in range(NT)]
    LTILES = [(TLOC, True)] + [(TL * i, False) for i in range(NTL)]
    import os as _os
    if _os.environ.get("DBG_NT"):
        LTILES = LTILES[:int(_os.environ["DBG_NT"])]

    with ExitStack() as es:
        fw = FW(nc, es)
        op = fw.op
        dma = fw.dma

        uid = [0]

        def sb(es_, name, shape, dt=F32):
            uid[0] += 1
            return es_.enter_context(nc.sbuf_tensor("%s_s%d" % (name, uid[0]), list(shape), dt))

        def ps(es_, name, shape, dt=F32):
            uid[0] += 1
            return es_.enter_context(nc.psum_tensor("%s_p%d" % (name, uid[0]), list(shape), dt))

        es.enter_context(nc.allow_non_contiguous_dma(reason="small strided parameter loads"))

        ident = sb(es, "ident", [128, 128])
        ones_f = sb(es, "ones_f", [128, 128])
        dma("sp", ident[:], ident_d[:, :], writes=["ident"])
        op("pool", lambda e: e.memset(ones_f[:], 1.0), writes=["ones_f"])

        def rsqrt(out, x, tmp, kout, kx, ktmp, eng="pool", iters=3):
            xi = x.bitcast(I32)
            oi = out.bitcast(I32)
            op("dve", lambda e: e.tensor_scalar(out=oi, in0=xi, scalar1=1, scalar2=None,
                                                op0=ALU.arith_shift_right), reads=[kx], writes=[kout])
            op("dve", lambda e: e.tensor_scalar(out=oi, in0=oi, scalar1=-1.0, scalar2=float(0x5F3759DF),
                                                op0=ALU.mult, op1=ALU.add), writes=[kout])
            for _ in range(iters):
                op(eng, lambda e: e.tensor_tensor(out=tmp, in0=x, in1=out, op=ALU.mult), reads=[kx, kout], writes=[ktmp])
                op(eng, lambda e: e.tensor_tensor(out=tmp, in0=tmp, in1=out, op=ALU.mult), reads=[kout], writes=[ktmp])
                op(eng, lambda e: e.tensor_scalar(out=tmp, in0=tmp, scalar1=-0.5, scalar2=1.5,
                                                  op0=ALU.mult, op1=ALU.add), writes=[ktmp])
                op(eng, lambda e: e.tensor_tensor(out=out, in0=out, in1=tmp, op=ALU.mult), reads=[ktmp], writes=[kout])

        def resid_rows(l, col0, is_ctx):
            if l == 0:
                return (ctx_d[0:TL, :] if is_ctx else x_d[col0:col0 + TL, :])
            return XB[col0:col0 + TL, :]

        def load_resid(l, col0, is_ctx, xt, kx, pt=None, kp=None, from_xb=False, save_xb=False):
            if from_xb and not is_ctx:
                src = XB[col0:col0 + TL, :]
            else:
                src = resid_rows(l, col0, is_ctx)
            dma("sp", xt[:], src.rearrange("(s p) d -> p s d", p=128), reads=(["XB_%d" % col0] if (from_xb or l > 0) else []), writes=[kx])
            if l == 0 and not is_ctx and not from_xb:
                dma("sp", pt[:], pos_d[col0:col0 + TL, :].rearrange("(s p) d -> p s d", p=128), writes=[kp])
                op("pool", lambda e: e.tensor_tensor(out=xt[:], in0=xt[:], in1=pt[:], op=ALU.add),
                   reads=[kp], writes=[kx])
                if save_xb:
                    dma("sp", XB[col0:col0 + TL, :].rearrange("(s p) d -> p s d", p=128), xt[:], reads=[kx], writes=["XB_%d" % col0])

        def ln_tile(xt, kx, xh, kxh, scr, eps, eng="pool", iters=3):
            st, mv, ve, rs, tmp = scr["st"], scr["mv"], scr["ve"], scr["rs"], scr["tmp"]
            kk = scr["k"]
            for s in range(2):
                for h in range(2):
                    op("dve", lambda e: e.bn_stats(out=st[:, s, h, :], in_=xt[:, s, h * 512:(h + 1) * 512]),
                       reads=[kx], writes=[kk + "st"])
                op("dve", lambda e: e.bn_aggr(out=mv[:, s, :], in_=st[:, s, :, :].rearrange("p a b -> p (a b)")),
                   reads=[kk + "st"], writes=[kk + "mv"])
            op("dve", lambda e: e.tensor_scalar(out=ve[:], in0=mv[:, :, 1], scalar1=float(eps), scalar2=None,
                                                op0=ALU.add), reads=[kk + "mv"], writes=[kk + "ve"])
            rsqrt(rs[:], ve[:], tmp[:], kk + "rs", kk + "ve", kk + "tmp", eng=eng, iters=iters)
            for s in range(2):
                op("dve", lambda e: e.tensor_scalar(out=xh[:, s, :], in0=xt[:, s, :], scalar1=mv[:, s, 0:1],
                                                    scalar2=rs[:, s:s + 1], op0=ALU.subtract, op1=ALU.mult),
                   reads=[kx, kk + "mv", kk + "rs"], writes=[kxh])

        def ln_stats(xt, kx, scr, eps, iters=3):
            st, mv, ve, rs, tmp = scr["st"], scr["mv"], scr["ve"], scr["rs"], scr["tmp"]
            kk = scr["k"]
            for s in range(2):
                for h in range(2):
                    op("dve", lambda e: e.bn_stats(out=st[:, s, h, :], in_=xt[:, s, h * 512:(h + 1) * 512]),
                       reads=(kx if isinstance(kx, list) else [kx]), writes=[kk + "st%d%d" % (s, h)])
                op("dve", lambda e: e.bn_aggr(out=mv[:, s, :], in_=st[:, s, :, :].rearrange("p a b -> p (a b)")),
                   reads=[kk + "st%d0" % s, kk + "st%d1" % s], writes=[kk + "mv"])
            op("dve", lambda e: e.tensor_scalar(out=ve[:], in0=mv[:, :, 1], scalar1=float(eps), scalar2=None,
                                                op0=ALU.add), reads=[kk + "mv"], writes=[kk + "ve"])
            rsqrt(rs[:], ve[:], tmp[:], kk + "rs", kk + "ve", kk + "tmp", iters=iters)

        def ln_apply(xt, kx, xh, kxh, scr):
            mv, rs = scr["mv"], scr["rs"]
            kk = scr["k"]
            for s in range(2):
                op("dve", lambda e: e.tensor_scalar(out=xh[:, s, :], in0=xt[:, s, :], scalar1=mv[:, s, 0:1],
                                                    scalar2=rs[:, s:s + 1], op0=ALU.subtract, op1=ALU.mult),
                   reads=[kx, kk + "mv", kk + "rs"], writes=[kxh])

        def ln_scr(es_, name):
            return dict(st=sb(es_, name + "st", [128, 2, 2, 6]), mv=sb(es_, name + "mv", [128, 2, 2]),
                        ve=sb(es_, name + "ve", [128, 2]), rs=sb(es_, name + "rs", [128, 2]),
                        tmp=sb(es_, name + "tmp", [128, 2]), k=name)

        def transpose_mod(xh, kxh, hT, khT, tps, modc, strm, jsc, jsh):
            for kp in range(4):
                tp, ktp = tps[kp % 2]
                for j in range(2):
                    k = 2 * kp + j
                    for s in range(2):
                        op("pe", lambda e: e.transpose(out=tp[:, j, s * 128:(s + 1) * 128],
                                                       in_=xh[:, s, k * 128:(k + 1) * 128], identity=ident[:]),
                           reads=[kxh, "ident"], writes=[ktp])
                for j in range(2):
                    k = 2 * kp + j
                    op("act", lambda e: e.activation(out=hT[:, k, :], in_=tp[:, j, :], func=AF.Identity,
                                                     scale=modc[:, strm, jsc, k:k + 1], bias=modc[:, strm, jsh, k:k + 1]),
                       reads=["modc"], writes=[ktp, khT])

        for l in range(DEPTH):
            last = (l == DEPTH - 1)
            with ExitStack() as el:
                with ExitStack() as ep:
                    ep.enter_context(nc.named_scope("P0_l%d" % l))
                    cct = sb(ep, "cct", [128, 8, 2])
                    sct = sb(ep, "sct", [128, 8, 2])
                    scbc = sb(ep, "scbc", [128, 8, 128])
                    bmbc = sb(ep, "bmbc", [128, 6 * D])
                    modbc = sb(ep, "modbc", [128, 6 * D])
                    wm = [sb(ep, "wm%d" % i, [128, 3072]) for i in range(3)]
                    pmod = [ps(ep, "pmod%d" % i, [128, 512]) for i in range(6)]
                    for s in range(2):
                        dma("sp", cct[:, :, s], cc_d[s, :].rearrange("(k p) -> p k", p=128), writes=["cct"])
                    dma("sp", bmbc[:], bmod_d[l:l + 1, :].partition_broadcast(128), writes=["bmbc"])
                    op("act", lambda e: e.activation(out=sct[:], in_=cct[:], func=AF.Tanh, scale=0.5), reads=["cct"], writes=["sct"])
                    op("dve", lambda e: e.tensor_scalar(out=sct[:], in0=sct[:], scalar1=0.5, scalar2=0.5, op0=ALU.mult, op1=ALU.add), writes=["sct"])
                    op("dve", lambda e: e.tensor_tensor(out=sct[:], in0=sct[:], in1=cct[:], op=ALU.mult), reads=["cct"], writes=["sct"])
                    for k in range(8):
                        for s in range(2):
                            op("dve", lambda e: e.tensor_scalar(out=scbc[:, k, 64 * s:64 * s + 64], in0=ones_f[:, 0:64],
                                                                scalar1=sct[:, k, s:s + 1], scalar2=None, op0=ALU.mult),
                               reads=["sct", "ones_f"], writes=["scbc"])
                    ld = 0
                    for half in range(2):
                        for k in range(8):
                            w = wm[ld % 3]
                            kw = "wm%d" % (ld % 3)
                            ld += 1
                            dma("sp", w[:], wmod_d[l, k * 128:(k + 1) * 128, half * 3072:(half + 1) * 3072], writes=[kw])
                            for n in range(6):
                                op("pe", lambda e: e.matmul(pmod[n][:], lhsT=scbc[:, k, :], rhs=w[:, n * 512:(n + 1) * 512],
                                                            start=(k == 0), stop=(k == 7)),
                                   reads=["scbc", kw], writes=["pmod%d" % n])
                        for n in range(6):
                            c0 = half * 3072 + n * 512
                            op("dve", lambda e: e.tensor_tensor(out=modbc[:, c0:c0 + 512], in0=pmod[n][:], in1=bmbc[:, c0:c0 + 512],
                                                                op=ALU.add), reads=["bmbc"], writes=["pmod%d" % n, "modbc"])
                    dma("sp", MODS[0:1, :], modbc[0:1, :], reads=["modbc"], writes=["MODS"])
                    dma("sp", MODS[1:2, :], modbc[64:65, :], reads=["modbc"], writes=["MODS"])
                    fw.barrier()

                modc = sb(el, "modc", [128, 2, 6, 8])
                gmixc = sb(el, "gmixc", [128, 8])
                for s in range(2):
                    for j in range(6):
                        dma("sp", modc[:, s, j, :], MODS[s, j * D:(j + 1) * D].rearrange("(k p) -> p k", p=128), reads=["MODS"], writes=["modc"])
                dma("sp", gmixc[:], gmix_d[l, :].rearrange("(k p) -> p k", p=128), writes=["gmixc"])
                for j in (1, 4):
                    op("dve", lambda e: e.tensor_scalar(out=modc[:, :, j, :], in0=modc[:, :, j, :], scalar1=1.0, scalar2=None,
                                                        op0=ALU.add), writes=["modc"])

                with ExitStack() as ez:
                    zT = sb(ez, "zT", [128, 2, T], BF16)
                    zTc = sb(ez, "zTc", [128, 2, TC], BF16)
                    with ExitStack() as e1:
                        e1.enter_context(nc.named_scope("M1_l%d" % l))
                        win = sb(e1, "win", [128, 8, DIN], BF16)
                        for k in range(8):
                            dma("pool", win[:, k, :], win_d[l, k * 128:(k + 1) * 128, :], writes=["win"])
                        wsT = sb(e1, "wsT", [128, 4, 128], BF16)
                        wsr = sb(e1, "wsr", [128, 4, 128])
                        bsT = sb(e1, "bsT", [128, 4])
                        dma("sp", wsr[:], ws_d[l].rearrange("h p q -> p h q"), writes=["wsr"])
                        dma("sp", bsT[:], sgb_d[l].rearrange("h p -> p h"), writes=["bsT"])
                        xt = [sb(e1, "xt%d" % i, [128, 2, D]) for i in range(2)]
                        pt = [sb(e1, "pt%d" % i, [128, 2, D]) for i in range(2)] if l == 0 else [None, None]
                        xh = sb(e1, "xh", [128, 2, D])
                        hT = [sb(e1, "hT%d" % i, [128, 8, TL], BF16) for i in range(2)]
                        lns = ln_scr(e1, "l1")
                        xaS = [sb(e1, "xaS%d" % i, [128, 4, TL]) for i in range(2)]
                        ggS = [sb(e1, "ggS%d" % i, [128, 4, TL]) for i in range(2)]
                        yBT = [sb(e1, "yBT%d" % i, [128, 2, TL], BF16) for i in range(2)]
                        zS = [sb(e1, "zS%d" % i, [128, 2, TL], BF16) for i in range(2)]
                        zB = [sb(e1, "zB%d" % i, [128, 512]) for i in range(2)]
                        yb = [sb(e1, "yb%d" % i, [128, 256]) for i in range(2)]
                        vh = sb(e1, "vh", [128, 256], BF16)
                        bst = sb(e1, "bst", [128, 4, 6])
                        bmv = sb(e1, "bmv", [128, 4, 2])
                        bve = sb(e1, "bve", [128, 4])
                        brs = sb(e1, "brs", [128, 4])
                        btmp = sb(e1, "btmp", [128, 4])
                        ssB = sb(e1, "ssB", [128, 2])
                        rB = sb(e1, "rB", [128, 2])
                        rsB = sb(e1, "rsB", [128, 2])
                        rtmp = sb(e1, "rtmp", [128, 2])
                        junk = sb(e1, "junk", [128, 256])
                        tp0 = ps(e1, "tp0", [128, 2, TL]); tp1 = ps(e1, "tp1", [128, 2, TL])
                        tps = [(tp0, "tp0"), (tp1, "tp1")]
                        pa = [ps(e1, "pa%d" % i, [128, 2, TL]) for i in range(2)]
                        pb = [ps(e1, "pb%d" % i, [128, 512]) for i in range(2)]
                        pss = ps(e1, "pss", [128, 512])
                        pyt = ps(e1, "pyt", [128, 2, TL])
                        for h in range(4):
                            op("pe", lambda e: e.transpose(out=pb[0][:, h * 128:(h + 1) * 128], in_=wsr[:, h, :], identity=ident[:]),
                               reads=["wsr", "ident"], writes=["pb0"])
                        op("dve", lambda e: e.tensor_copy(out=wsT[:].rearrange("p h q -> p (h q)"), in_=pb[0][:]), writes=["pb0", "wsT"])

                        tl = LTILES
                        nM = len(tl)
                        zBt = [[sb(e1, "zBt%d_%d" % (i, s_), [128, 512]) for s_ in range(2)] for i in range(2)]

                        def m1_load(ti):
                            nb = ti % 2
                            load_resid(l, tl[ti][0], tl[ti][1], xt[nb], "xt%d" % nb, pt[nb], "pt%d" % nb, save_xb=True)

                        def m1_A(ti):
                            b = ti % 2
                            ln_tile(xt[b], "xt%d" % b, xh, "xh", lns, EPS, eng="dve", iters=2)

                        def m1_T(ti):
                            col0, is_ctx = tl[ti]
                            b = ti % 2
                            transpose_mod(xh, "xh", hT[b], "hT%d" % b, tps, modc, 1 if is_ctx else 0, 1, 0)

                        def m1_P(ti):
                            col0, is_ctx = tl[ti]
                            b = ti % 2
                            khT = "hT%d" % b
                            pai = 0
                            only_xa = last and is_ctx
                            for grp in range(2 if only_xa else 4):
                                p_, kp_ = pa[pai % 2], "pa%d" % (pai % 2)
                                pai += 1
                                for j in range(2):
                                    oc = grp * 2 + j
                                    for k in range(8):
                                        op("pe", lambda e: e.matmul(p_[:, j, :], lhsT=win[:, k, oc * 128:(oc + 1) * 128], rhs=hT[b][:, k, :],
                                                                    start=(k == 0), stop=(k == 7)), reads=["win", khT], writes=[kp_])
                                if grp < 2:
                                    op("act", lambda e: e.activation(out=xaS[b][:, 2 * grp:2 * grp + 2, :], in_=p_[:], func=AF.Identity),
                                       writes=[kp_, "xaS%d" % b])
                                else:
                                    g2 = grp - 2
                                    op("act", lambda e: e.activation(out=ggS[b][:, 2 * g2:2 * g2 + 2, :], in_=p_[:], func=AF.Gelu),
                                       writes=[kp_, "ggS%d" % b])
                            dma("sp", XACv[:, :, :] if is_ctx else XALv[:, :, col0:col0 + TL], xaS[b][:], reads=["xaS%d" % b], writes=["XA"])
                            if only_xa:
                                return
                            dma("sp", GGCv[:, :, :] if is_ctx else GGLv[:, :, col0:col0 + TL], ggS[b][:], reads=["ggS%d" % b], writes=["GG"])
                            p_, kp_ = pa[pai % 2], "pa%d" % (pai % 2)
                            for j in range(2):
                                for k in range(8):
                                    op("pe", lambda e: e.matmul(p_[:, j, :], lhsT=win[:, k, 1536 + j * 128:1536 + (j + 1) * 128], rhs=hT[b][:, k, :],
                                                                start=(k == 0), stop=(k == 7)), reads=["win", khT], writes=[kp_])
                            if is_ctx:
                                op("act", lambda e: e.activation(out=zTc[:, :, :], in_=p_[:], func=AF.Identity), writes=[kp_, "zTc"])
                            else:
                                op("act", lambda e: e.activation(out=zS[b][:], in_=p_[:], func=AF.Identity), writes=[kp_, "zS%d" % b])
                                dma("sp", ZLv[:, :, col0:col0 + TL], zS[b][:], reads=["zS%d" % b], writes=["ZL"])

                        def m1_Bproj(ti):
                            col0, is_ctx = tl[ti]
                            if last and is_ctx:
                                return
                            b = ti % 2
                            for s in range(2):
                                p_, kp_ = pb[s], "pb%d" % s
                                for k in range(8):
                                    op("pe", lambda e: e.matmul(p_[:], lhsT=hT[b][:, k, s * 128:(s + 1) * 128], rhs=win[:, k, 1024:1536],
                                                                start=(k == 0), stop=(k == 7)), reads=["win", "hT%d" % b], writes=[kp_])
                                op("act", lambda e: e.activation(out=zBt[b][s][:], in_=p_[:], func=AF.Gelu), writes=[kp_, "zBt%d_%d" % (b, s)])

                        bmv8 = sb(e1, "bmv8", [128, 8, 2])
                        bve8 = sb(e1, "bve8", [128, 8])
                        brs8 = sb(e1, "brs8", [128, 8])
                        btmp8 = sb(e1, "btmp8", [128, 8])
                        bst8 = sb(e1, "bst8", [128, 8, 6])
                        vh2 = sb(e1, "vh2", [128, 2, 256], BF16)
                        ybp = [[sb(e1, "ybp%d_%d" % (i, s_), [128, 256]) for s_ in range(2)] for i in range(2)]
                        ssBp = [sb(e1, "ssBp%d" % i, [128, 2]) for i in range(2)]

                        def skipB(ti):
                            return last and tl[ti][1]

                        def m1_B1a(ti):
                            if skipB(ti):
                                return
                            b = ti % 2
                            for s in range(2):
                                zb, kzb = zBt[b][s], "zBt%d_%d" % (b, s)
                                for h in range(4):
                                    op("dve", lambda e: e.bn_stats(out=bst8[:, 4 * s + h, :], in_=zb[:, 256 + 64 * h:256 + 64 * h + 64]),
                                       reads=[kzb], writes=["bst8_%d" % (4 * s + h)])
                                for h in range(4):
                                    op("dve", lambda e: e.bn_aggr(out=bmv8[:, 4 * s + h, :], in_=bst8[:, 4 * s + h, :]),
                                       reads=["bst8_%d" % (4 * s + h)], writes=["bmv8"])
                            op("dve", lambda e: e.tensor_scalar(out=bve8[:], in0=bmv8[:, :, 1], scalar1=EPS, scalar2=None, op0=ALU.add),
                               reads=["bmv8"], writes=["bve8"])
                            rsqrt(brs8[:], bve8[:], btmp8[:], "brs8", "bve8", "btmp8", iters=2)
                            for s in range(2):
                                zb, kzb = zBt[b][s], "zBt%d_%d" % (b, s)
                                for h in range(4):
                                    op("dve", lambda e: e.tensor_scalar(out=vh2[:, s, 64 * h:64 * h + 64], in0=zb[:, 256 + 64 * h:256 + 64 * h + 64],
                                                                        scalar1=bmv8[:, 4 * s + h, 0:1], scalar2=brs8[:, 4 * s + h:4 * s + h + 1],
                                                                        op0=ALU.subtract, op1=ALU.mult),
                                       reads=[kzb, "bmv8", "brs8"], writes=["vh2_%d" % (4 * s + h)])

                        def m1_smm(ti):
                            if skipB(ti):
                                return
                            for s in range(2):
                                for h in range(4):
                                    op("pe", lambda e: e.matmul(pss[:, 256 * s + 64 * h:256 * s + 64 * h + 64], lhsT=wsT[:, h, :], rhs=vh2[:, s, 64 * h:64 * h + 64],
                                                                start=True, stop=True), reads=["wsT", "vh2_%d" % (4 * s + h)], writes=["pss"])

                        def m1_B1b(ti):
                            if skipB(ti):
                                return
                            b = ti % 2
                            for s in range(2):
                                zb, kzb = zBt[b][s], "zBt%d_%d" % (b, s)
                                for h in range(4):
                                    op("dve", lambda e: e.scalar_tensor_tensor(out=ybp[b][s][:, 64 * h:64 * h + 64], in0=pss[:, 256 * s + 64 * h:256 * s + 64 * h + 64],
                                                                               scalar=bsT[:, h:h + 1], in1=zb[:, 64 * h:64 * h + 64],
                                                                               op0=ALU.add, op1=ALU.mult),
                                       reads=["bsT", kzb], writes=["pss", "ybp%d_%d_%d" % (b, s, h)])
                            for s in range(2):
                                op("act", lambda e: e.activation(out=junk[:], in_=ybp[b][s][:], func=AF.Square, accum_out=ssBp[b][:, s:s + 1]),
                                   reads=["ybp%d_%d_%d" % (b, s, h) for h in range(4)], writes=["junk", "ssBp%d" % b])

                        def m1_B2(ti):
                            if skipB(ti):
                                return
                            col0, is_ctx = tl[ti]
                            b = ti % 2
                            op("dve", lambda e: e.tensor_scalar(out=rB[:], in0=ssBp[b][:], scalar1=1.0 / DB, scalar2=EPS, op0=ALU.mult, op1=ALU.add),
                               reads=["ssBp%d" % b], writes=["rB"])
                            rsqrt(rsB[:], rB[:], rtmp[:], "rsB", "rB", "rtmp", iters=2)
                            for s in range(2):
                                kyb = ["ybp%d_%d_%d" % (b, s, h) for h in range(4)]
                                op("dve", lambda e: e.tensor_scalar(out=ybp[b][s][:], in0=ybp[b][s][:], scalar1=rsB[:, s:s + 1], scalar2=None, op0=ALU.mult),
                                   reads=["rsB"], writes=kyb)
                                for c in range(2):
                                    op("pe", lambda e: e.transpose(out=pyt[:, c, s * 128:(s + 1) * 128], in_=ybp[b][s][:, c * 128:(c + 1) * 128],
                                                                   identity=ident[:]), reads=kyb + ["ident"], writes=["pyt"])
                            for c in range(2):
                                op("act", lambda e: e.activation(out=yBT[b][:, c, :], in_=pyt[:, c, :], func=AF.Identity, scale=gmixc[:, 4 + c:5 + c]),
                                   reads=["gmixc"], writes=["pyt", "yBT%d" % b])
                            dma("sp", YBLv[:, :, col0:col0 + TL], yBT[b][:], reads=["yBT%d" % b], writes=["YBL"])

                        m1_load(0)
                        if nM > 1:
                            m1_load(1)
                        m1_A(0)
                        m1_T(0)
                        for ti in range(nM + 2):
                            if ti + 2 < nM:
                                m1_load(ti + 2)
                            if ti < nM:
                                m1_P(ti)
                            if ti + 1 < nM:
                                m1_A(ti + 1)
                            if 2 <= ti:
                                m1_B2(ti - 2)
                            if ti + 1 < nM:
                                m1_T(ti + 1)
                            if 1 <= ti <= nM:
                                m1_B1a(ti - 1)
                            if ti < nM:
                                m1_Bproj(ti)
                            if 1 <= ti <= nM:
                                m1_smm(ti - 1)
                                m1_B1b(ti - 1)
                        fw.barrier()
                    fw.collective("AllGather", ZL[:, :], ZF[:, :], PAIRS, ["ZL"], ["ZF"])
                    fw.barrier()
                    for c_ in range(4):
                        fw.collective("AllGather", XAL[c_ * 128:(c_ + 1) * 128, :], XAF[c_ * 256:(c_ + 1) * 256, :], PAIRS, ["XA"], ["XAF"])
                    for c_ in range(4):
                        fw.collective("AllGather", GGL[c_ * 128:(c_ + 1) * 128, :], GGF[c_ * 256:(c_ + 1) * 256, :], PAIRS, ["GG"], ["GGF"])
                    if stop_after == "M1" and l == 0:
                        xafd = nc.dram_tensor("XAFd", [2 * DA, TLOC], F32, kind="ExternalOutput").ap()
                        zfd = nc.dram_tensor("ZFd", [2 * DC, TLOC], BF16, kind="ExternalOutput").ap()
                        dma("sp", xafd[:, :], XAF[:, :], writes=["xafd"])
                        dma("sp", zfd[:, :], ZF[:, :], writes=["zfd"])
                        fw.wait_all("sp")
                        return nc

                    if True:
                        with ExitStack() as e2:
                            e2.enter_context(nc.named_scope("DFT_l%d" % l))
                            t1 = sb(e2, "t1", [128, 3, 128], BF16)
                            tw = sb(e2, "tw", [128, 128, 64], BF16)
                            c256 = sb(e2, "c256", [128, 2, 2, 256], BF16)
                            cspad = sb(e2, "cspad", [64, 2, 2, 128])
                            wft = sb(e2, "wft", [64, 4, 64])
                            abd = sb(e2, "abd", [128, 2, 256], BF16)
                            abdc = sb(e2, "abdc", [128, 2, 256], BF16)
                            XTs = sb(e2, "XTs", [128, 2, T])
                            XTc = sb(e2, "XTc", [128, 2, TC])
                            Yp = [sb(e2, "Yp%d" % i, [128, 512], BF16) for i in range(2)]
                            Gs = [sb(e2, "Gs%d" % i, [128, 2, 8, 256], BF16) for i in range(2)]
                            Rb = [sb(e2, "Rb%d" % i, [128, 8, 256], BF16) for i in range(2)]
                            sq = sb(e2, "sq", [128, 2, TL])
                            rr = sb(e2, "rr", [128, TL])
                            rrs = sb(e2, "rrs", [128, TL])
                            rrt = sb(e2, "rrt", [128, TL])
                            yCT = [sb(e2, "yCT%d" % i, [128, 2, TL], BF16) for i in range(2)]
                            pY = [ps(e2, "pY%d" % i, [128, 512]) for i in range(2)]
                            pG = [ps(e2, "pG%d" % i, [128, 2, 256]) for i in range(2)]
                            pX = [ps(e2, "pX%d" % i, [128, 8, 64]) for i in range(2)]
                            pS = ps(e2, "pS", [128, 512])
                            pS1 = ps(e2, "pS1", [128, 512])
                            for r_ in range(2):
                                dma("sp", zT[:, :, r_ * TLOC:(r_ + 1) * TLOC], ZFv[r_], writes=["zT"])
                            dma("sp", t1[:], t1_d[:, :, :], writes=["t1"])
                            dma("sp", tw[:].rearrange("p a b -> p (a b)"), tw_d[:, :], writes=["tw"])
                            dma("sp", c256[:], c256_d[:, :, :, :], writes=["c256"])
                            dma("sp", cspad[:], cspad_d[:, :, :, :], writes=["cspad"])
                            dma("sp", wft[:], wf_d[l].rearrange("g j e -> j g e"), writes=["wft"])
                            for cc in range(2):
                                for pq in range(2):
                                    for gl in range(2):
                                        op("pe", lambda e: e.matmul(pY[0][:, pq * 128 + gl * 64:pq * 128 + gl * 64 + 64],
                                                                    lhsT=cspad[:, pq, gl, :], rhs=wft[:, 2 * cc + gl, :], start=True, stop=True),
                                           reads=["cspad", "wft"], writes=["pY0"])
                                op("act", lambda e: e.activation(out=abd[:, cc, :], in_=pY[0][:, 0:256], func=AF.Identity,
                                                                 scale=1.0 / math.sqrt(T * 64.0)), writes=["pY0", "abd"])
                                op("act", lambda e: e.activation(out=abdc[:, cc, :], in_=pY[0][:, 0:256], func=AF.Identity,
                                                                 scale=1.0 / math.sqrt(TC * 64.0)), writes=["pY0", "abdc"])

                            rr2 = [sb(e2, "rr2_%d" % i, [128, TL]) for i in range(2)]
                            rrs2 = [sb(e2, "rrs2_%d" % i, [128, TL]) for i in range(2)]
                            rrt2 = [sb(e2, "rrt2_%d" % i, [128, TL]) for i in range(2)]

                            def rmsc1(src, b):
                                op("act", lambda e: e.activation(out=sq[:], in_=src, func=AF.Square), reads=["XT"], writes=["sq"])
                                pS_ = pS if b == 0 else pS1
                                for c in range(2):
                                    op("pe", lambda e: e.matmul(pS_[:, 0:TL], lhsT=ones_f[:], rhs=sq[:, c, :], start=(c == 0), stop=(c == 1)),
                                       reads=["ones_f", "sq"], writes=["pS%d" % b])

                            def rmsc2(b):
                                pS_ = pS if b == 0 else pS1
                                op("act", lambda e: e.activation(out=rr2[b][:], in_=pS_[:, 0:TL], func=AF.Ln, scale=1.0 / DC, bias=EPS),
                                   writes=["pS%d" % b, "rr2_%d" % b])
                                op("act", lambda e: e.activation(out=rrs2[b][:], in_=rr2[b][:], func=AF.Exp, scale=-0.5),
                                   reads=["rr2_%d" % b], writes=["rrs2_%d" % b])

                            def rmsc3(src, col0, b):
                                for c in range(2):
                                    op("dve", lambda e: e.scalar_tensor_tensor(out=yCT[b][:, c, :], in0=src[:, c, :], scalar=gmixc[:, 6 + c:7 + c],
                                                                               in1=rrs2[b][:], op0=ALU.mult, op1=ALU.mult),
                                       reads=["XT", "gmixc", "rrs2_%d" % b], writes=["yCT%d_%d" % (b, c)])
                                dma("sp", YTv[:, 6:8, col0:col0 + TL], yCT[b][:], reads=["yCT%d_0" % b, "yCT%d_1" % b], writes=["YT"])

                            def rms_store_c(src, col0, b):
                                rmsc1(src, b)
                                rmsc2(b)
                                rmsc3(src, col0, b)

                            if not last:
                                Ypc = [sb(e2, "Ypc%d" % i, [128, 512], BF16) for i in range(2)]
                                for t in range(2):
                                    for cc in range(2):
                                        op("pe", lambda e: e.matmul(pY[t][:, cc * 256:(cc + 1) * 256], lhsT=zTc[:, cc, t * 128:(t + 1) * 128],
                                                                    rhs=abdc[:, cc, :], start=True, stop=True), reads=["zTc", "abdc"], writes=["pY%d" % t])
                                    op("act", lambda e: e.activation(out=Ypc[t][:], in_=pY[t][:], func=AF.Identity), writes=["pY%d" % t, "Ypc%d" % t])
                                for cc in range(2):
                                    n = 0
                                    for t in range(2):
                                        for pq in range(2):
                                            op("pe", lambda e: e.matmul(pG[cc][:].rearrange("p a b -> p (a b)")[:, 0:256],
                                                                        lhsT=Ypc[t][:, cc * 256 + pq * 128:cc * 256 + (pq + 1) * 128],
                                                                        rhs=c256[:, t, pq, :], start=(n == 0), stop=(n == 3)),
                                               reads=["Ypc%d" % t, "c256"], writes=["pG%d" % cc])
                                            n += 1
                                    op("dve", lambda e: e.tensor_copy(out=XTc[:, cc, :], in_=pG[cc][:].rearrange("p a b -> p (a b)")[:, 0:256]),
                                       writes=["pG%d" % cc, "XT"])
                                rms_store_c(XTc[:, :, :], T, 0)

                            zTv = zT[:].rearrange("p c (a b) -> p c b a", b=64)
                            GDv = GD
                            for l2 in range(64):
                                b = l2 % 2
                                for cc in range(2):
                                    op("pe", lambda e: e.matmul(pY[b][:, cc * 256:(cc + 1) * 256], lhsT=zTv[:, cc, l2, :], rhs=abd[:, cc, :],
                                                                start=True, stop=True), reads=["zT", "abd"], writes=["pY%d" % b])
                                op("act", lambda e: e.activation(out=Yp[b][:], in_=pY[b][:], func=AF.Identity), writes=["pY%d" % b, "Yp%d" % b])
                                Ypv = Yp[b][:].rearrange("p (c q j) -> p q c j", c=2, q=2)
                                combos = [(0, 0, 0), (0, 1, 1), (1, 0, 1), (1, 1, 2)]
                                for (ri, pq, ti_) in combos:
                                    op("pe", lambda e: e.matmul(pG[b][:, ri, :].rearrange("p (c j) -> p c j", c=2), lhsT=t1[:, ti_, :], rhs=Ypv[:, pq, :, :],
                                                                start=(pq == 0), stop=(pq == 1)), reads=["t1", "Yp%d" % b], writes=["pG%d" % b])
                                gb = (l2 // 8) % 2
                                op("dve", lambda e: e.tensor_copy(out=Gs[gb][:, :, l2 % 8, :], in_=pG[b][:]), writes=["pG%d" % b, "Gs%d" % gb])
                                if l2 % 8 == 7:
                                    l0 = l2 - 7
                                    dma("sp", GDv[:, :, l0:l0 + 8, :], Gs[gb][:], reads=["Gs%d" % gb], writes=["GD"])
                            XTv = XTs[:].rearrange("p c (k2 k1) -> p c k1 k2", k1=128)
                            def load_rb(kb):
                                dma("sp", Rb[kb % 2][:], GD[kb * 8:(kb + 1) * 8, :, :, :].rearrange("k r l c -> (r l) k c"), reads=["GD"], writes=["Rb%d" % (kb % 2)])

                            load_rb(0)
                            load_rb(1)
                            for kb in range(16):
                                b = kb % 2
                                for cc in range(2):
                                    px, kpx = pX[cc], "pX%d" % cc
                                    for r in range(8):
                                        op("pe", lambda e: e.matmul(px[:, r, :], lhsT=Rb[b][:, r, cc * 128:(cc + 1) * 128], rhs=tw[:, kb * 8 + r, :],
                                                                    start=True, stop=True), reads=["Rb%d" % b, "tw"], writes=[kpx])
                                    op("act" if cc == 0 else "dve",
                                       (lambda e: e.activation(out=XTv[:, cc, kb * 8:(kb + 1) * 8, :], in_=px[:], func=AF.Identity)) if cc == 0 else
                                       (lambda e: e.tensor_copy(out=XTv[:, cc, kb * 8:(kb + 1) * 8, :], in_=px[:])),
                                       writes=[kpx, "XT"])
                                if kb + 2 < 16:
                                    load_rb(kb + 2)
                            for ti in range(NT + 2):
                                if ti < NT:
                                    rmsc1(XTs[:, :, ti * TL:(ti + 1) * TL], ti % 2)
                                if 0 <= ti - 1 < NT:
                                    rmsc2((ti - 1) % 2)
                                if 0 <= ti - 2 < NT:
                                    t2 = ti - 2
                                    rmsc3(XTs[:, :, t2 * TL:(t2 + 1) * TL], t2 * TL, t2 % 2)
                            fw.barrier()
                if stop_after == "DFT" and l == 0:
                    return nc

                with ExitStack() as e3:
                    wg = sb(e3, "wg", [128, 2, 2, 4, 128], BF16)
                    cw = sb(e3, "cw", [128, 4, 4])
                    cb = sb(e3, "cb", [128, 4])
                    gb_ = sb(e3, "gbias", [128, 2, 2, 4])
                    lam = sb(e3, "lam", [128, 2, 4])
                    hnsp = sb(e3, "hnsp", [128, 2, 4])
                    nsp = sb(e3, "nsp", [128, 2, 4])
                    op("pool", lambda e: e.memset(wg[:], 0.0), writes=["wg"])
                    for ax, wd in enumerate((wa_d, wx_d)):
                        for h2 in range(2):
                            for d_ in range(2):
                                src = wd[l, d_].rearrange("(c h) i e -> h i c e", h=2)[h2]
                                dma("pool", wg[64 * h2:64 * h2 + 64, d_, ax, :, 64 * h2:64 * h2 + 64], src, writes=["wg"])
                    for k in range(4):
                        dma("sp", cw[:, :, k], convw_d[l, k, :].rearrange("(c p) -> p c", p=128), writes=["cw"])
                    dma("sp", cb[:], convb_d[l].rearrange("(c p) -> p c", p=128), writes=["cb"])
                    for d_ in range(2):
                        dma("sp", gb_[:, 0, d_, :], ba_d[l, d_, :].rearrange("(c p) -> p c", p=128), writes=["gbias"])
                        dma("sp", gb_[:, 1, d_, :], bx_d[l, d_, :].rearrange("(c p) -> p c", p=128), writes=["gbias"])
                        dma("sp", lam[:, d_, :], lam_d[l, d_, :].rearrange("(c p) -> p c", p=128), writes=["lam"])
                    op("dve", lambda e: e.tensor_scalar(out=gb_[:], in0=gb_[:], scalar1=0.5, scalar2=None, op0=ALU.mult), writes=["gbias"])
                    op("act", lambda e: e.activation(out=lam[:], in_=lam[:], func=AF.Exp, scale=-1.0), writes=["lam"])
                    op("act", lambda e: e.activation(out=lam[:], in_=lam[:], func=AF.Ln, bias=1.0), writes=["lam"])
                    op("dve", lambda e: e.tensor_scalar(out=hnsp[:], in0=lam[:], scalar1=-4.0, scalar2=None, op0=ALU.mult), reads=["lam"], writes=["hnsp"])
                    op("dve", lambda e: e.tensor_scalar(out=nsp[:], in0=lam[:], scalar1=-8.0, scalar2=None, op0=ALU.mult), reads=["lam"], writes=["nsp"])

                    xaH = [sb(e3, "xaH%d" % i, [128, 4, TL + 3]) for i in range(2)]
                    xc = [sb(e3, "xc%d" % i, [128, 4, TL]) for i in range(2)]
                    xcb = [sb(e3, "xcb%d" % i, [128, 4, TL], BF16) for i in range(2)]
                    hS = [sb(e3, "hS%d" % i, [128, 4, TL]) for i in range(2)]
                    hfL = [sb(e3, "hfL%d" % i, [128, 4, TL]) for i in range(2)]
                    ggL = [sb(e3, "ggL%d" % i, [128, 4, TL]) for i in range(2)]
                    tr = sb(e3, "tr", [128, 4, TL])
                    tiS = [sb(e3, "tiS%d" % i, [128, 4, TL]) for i in range(2)]
                    aS = [sb(e3, "aS%d" % i, [128, 4, TL]) for i in range(2)]
                    mS = [sb(e3, "mS%d" % i, [128, 4, TL]) for i in range(2)]
                    uS = sb(e3, "uS", [128, 4, TL])
                    ya = [sb(e3, "ya%d" % i, [128, 4, TL]) for i in range(3)]
                    sq = sb(e3, "sqa", [128, 4, TL])
                    rr = [sb(e3, "rra%d" % i, [128, TL]) for i in range(2)]
                    rrs = [sb(e3, "rrsa%d" % i, [128, TL]) for i in range(2)]
                    rrt = [sb(e3, "rrta%d" % i, [128, TL]) for i in range(2)]
                    yAT = [sb(e3, "yAT%d" % i, [128, 4, TL], BF16) for i in range(2)]
                    pg = [ps(e3, "pg%d" % i, [128, 2, TL]) for i in range(4)]
                    pSs = [ps(e3, "pSa%d" % i, [128, 512]) for i in range(2)]

                    def xck(b):
                        return ["xc%d_%d" % (b, c) for c in range(4)]

                    def gates(d, b):
                        for c in range(4):
                            for ax in range(2):
                                op("pe", lambda e: e.matmul(pg[c][:, ax, :], lhsT=wg[:, d, ax, c, :], rhs=xcb[b][:, c, :], start=True, stop=True),
                                   reads=["wg", "xcb%d" % b], writes=["pg%d" % c])
                        for c in range(4):
                            op("act", lambda e: e.activation(out=tr[:, c, :], in_=pg[c][:, 0, :], func=AF.Tanh, scale=0.5, bias=gb_[:, 0, d, c:c + 1]),
                               reads=["gbias"], writes=["pg%d" % c, "tr"])
                            op("act", lambda e: e.activation(out=tiS[b][:, c, :], in_=pg[c][:, 1, :], func=AF.Tanh, scale=0.5, bias=gb_[:, 1, d, c:c + 1]),
                               reads=["gbias"], writes=["pg%d" % c, "tiS%d" % b])
                        for c in range(4):
                            op("act", lambda e: e.activation(out=aS[b][:, c, :], in_=tr[:, c, :], func=AF.Exp, scale=hnsp[:, d, c:c + 1], bias=hnsp[:, d, c:c + 1]),
                               reads=["tr", "hnsp"], writes=["aS%d" % b])
                            op("act", lambda e: e.activation(out=mS[b][:, c, :], in_=tr[:, c, :], func=AF.Exp, scale=nsp[:, d, c:c + 1], bias=nsp[:, d, c:c + 1]),
                               reads=["tr", "nsp"], writes=["mS%d" % b])
                        op("act", lambda e: e.activation(out=mS[b][:], in_=mS[b][:], func=AF.Sqrt, scale=-0.25, bias=0.25), writes=["mS%d" % b])

                    def make_u(b):
                        op("dve", lambda e: e.scalar_tensor_tensor(out=uS[:], in0=tiS[b][:], scalar=1.0, in1=xc[b][:], op0=ALU.add, op1=ALU.mult),
                           reads=["tiS%d" % b] + xck(b), writes=["uS"])
                        op("dve", lambda e: e.tensor_tensor(out=uS[:], in0=uS[:], in1=mS[b][:], op=ALU.mult), reads=["mS%d" % b], writes=["uS"])

                    scope_ = nc.named_scope("M2a_l%d" % l)
                    scope_.__enter__()

                    def load_xa(ti, b):
                        col0, is_ctx = TILES[ti]
                        first = is_ctx or col0 == 0
                        lastt = is_ctx or col0 == T - TL
                        lo = 0 if first else 2
                        hi = 0 if lastt else 1
                        if first:
                            op("pool", lambda e: e.memset(xaH[b][:, :, 0:2], 0.0), writes=["xaH%d" % b])
                        if lastt:
                            op("pool", lambda e: e.memset(xaH[b][:, :, TL + 2:TL + 3], 0.0), writes=["xaH%d" % b])
                        if is_ctx:
                            dma("sp", xaH[b][:, :, 2:TL + 2], XACv[:, :, :], writes=["xaH%d" % b])
                        else:
                            g0, g1 = col0 - lo, col0 + TL + hi
                            d0 = 2 - lo
                            for r_ in range(2):
                                a0, a1 = max(g0, r_ * TLOC), min(g1, (r_ + 1) * TLOC)
                                if a1 > a0:
                                    dma("sp", xaH[b][:, :, d0 + a0 - g0:d0 + a1 - g0], XAFv[r_][:, :, a0 - r_ * TLOC:a1 - r_ * TLOC],
                                        reads=["XAF"], writes=["xaH%d" % b])

                    def f0(ti):
                        col0, is_ctx = TILES[ti]
                        b = ti % 2
                        for c in range(4):
                            op("dve", lambda e: e.tensor_scalar(out=xc[b][:, c, :], in0=xaH[b][:, c, 0:TL], scalar1=cw[:, c, 0:1], scalar2=cb[:, c:c + 1],
                                                                op0=ALU.mult, op1=ALU.add), reads=["xaH%d" % b, "cw", "cb"], writes=["xc%d_%d" % (b, c)])
                        for k in range(1, 4):
                            for c in range(4):
                                op("dve", lambda e: e.scalar_tensor_tensor(out=xc[b][:, c, :], in0=xaH[b][:, c, k:k + TL], scalar=cw[:, c, k:k + 1],
                                                                           in1=xc[b][:, c, :], op0=ALU.mult, op1=ALU.add),
                                   reads=["xaH%d" % b, "cw"], writes=["xc%d_%d" % (b, c)])
                        op("act", lambda e: e.activation(out=xcb[b][:], in_=xc[b][:], func=AF.Identity), reads=xck(b), writes=["xcb%d" % b])
                        dma("sp", XCv[:, :, col0:col0 + TL], xc[b][:], reads=xck(b), writes=["XC"])
                        gates(0, b)

                    def f1(ti):
                        col0, is_ctx = TILES[ti]
                        b = ti % 2
                        pb_ = (ti - 1) % 2
                        make_u(b)
                        for c in range(4):
                            init = 0.0 if ti == 0 else hS[pb_][:, c, TL - 1:TL]
                            op("dve", lambda e: e.tensor_tensor_scan(out=hS[b][:, c, :], data0=aS[b][:, c, :], data1=uS[:, c, :], initial=init,
                                                                     op0=ALU.mult, op1=ALU.add),
                               reads=["aS%d" % b, "uS"] + ([] if ti == 0 else ["hS%d" % pb_]), writes=["hS%d" % b])
                        if not (last and is_ctx):
                            dma("sp", HFv[:, :, col0:col0 + TL], hS[b][:], reads=["hS%d" % b], writes=["HF"])

                    nT = len(TILES)
                    load_xa(0, 0)
                    load_xa(1, 1)
                    f0(0)
                    for ti in range(nT):
                        if ti + 2 < nT:
                            load_xa(ti + 2, ti % 2)
                        if ti + 1 < nT:
                            f0(ti + 1)
                        f1(ti)
                    fw.barrier()
                    scope_.__exit__(None, None, None)
                    if stop_after == "M2a" and l == 0:
                        return nc
                    scope_ = nc.named_scope("M2b_l%d" % l)
                    scope_.__enter__()

                    order = [0] + list(range(NT, 0, -1))
                    nO = len(order)

                    def load_x(oi):
                        col0, is_ctx = TILES[order[oi]]
                        b = oi % 2
                        dma("sp", xc[b][:], XCv[:, :, col0:col0 + TL], reads=["XC"], writes=xck(b))

                    def load_hg(oi):
                        col0, is_ctx = TILES[order[oi]]
                        b = oi % 2
                        if not (last and is_ctx):
                            dma("sp", hfL[b][:], HFv[:, :, col0:col0 + TL], reads=["HF"], writes=["hfL%d" % b])
                            ggsrc = GGCv[:, :, :] if is_ctx else GGFv[col0 // TLOC][:, :, col0 % TLOC:col0 % TLOC + TL]
                            dma("sp", ggL[b][:], ggsrc, reads=["GGF"], writes=["ggL%d" % b])

                    def b0(oi):
                        b = oi % 2
                        op("act", lambda e: e.activation(out=xcb[b][:], in_=xc[b][:], func=AF.Identity), reads=xck(b), writes=["xcb%d" % b])
                        gates(1, b)

                    def b1(oi):
                        col0, is_ctx = TILES[order[oi]]
                        b = oi % 2
                        pb_ = (oi - 1) % 2
                        y3 = oi % 3
                        make_u(b)
                        for c in range(4):
                            init = 0.0 if oi == 0 else hS[pb_][:, c, 0:1]
                            op("dve", lambda e: e.tensor_tensor_scan(out=hS[b][:, c, ::-1], data0=aS[b][:, c, ::-1], data1=uS[:, c, ::-1], initial=init,
                                                                     op0=ALU.mult, op1=ALU.add),
                               reads=["aS%d" % b, "uS"] + ([] if oi == 0 else ["hS%d" % pb_]), writes=["hS%d" % b])
                        if last and is_ctx:
                            return
                        op("pool", lambda e: e.tensor_tensor(out=ya[y3][:], in0=hS[b][:], in1=hfL[b][:], op=ALU.add), reads=["hS%d" % b, "hfL%d" % b], writes=["ya%d" % y3])
                        op("pool", lambda e: e.tensor_tensor(out=ya[y3][:], in0=ya[y3][:], in1=ggL[b][:], op=ALU.mult), reads=["ggL%d" % b], writes=["ya%d" % y3])
                        op("act", lambda e: e.activation(out=sq[:], in_=ya[y3][:], func=AF.Square), reads=["ya%d" % y3], writes=["sqa"])
                        for c in range(4):
                            op("pe", lambda e: e.matmul(pSs[b][:, 0:TL], lhsT=ones_f[:], rhs=sq[:, c, :], start=(c == 0), stop=(c == 3)),
                               reads=["ones_f", "sqa"], writes=["pSa%d" % b])

                    def b2a(oi):
                        col0, is_ctx = TILES[order[oi]]
                        b = oi % 2
                        if last and is_ctx:
                            return
                        op("dve", lambda e: e.tensor_scalar(out=rr[b][:], in0=pSs[b][:, 0:TL], scalar1=1.0 / DA, scalar2=EPS, op0=ALU.mult, op1=ALU.add),
                           writes=["pSa%d" % b, "rra%d" % b])
                        rsqrt(rrs[b][:], rr[b][:], rrt[b][:], "rrsa%d" % b, "rra%d" % b, "rrta%d" % b, iters=2)

                    def b2b(oi):
                        col0, is_ctx = TILES[order[oi]]
                        b = oi % 2
                        y3 = oi % 3
                        if last and is_ctx:
                            return
                        for c in range(4):
                            op("dve", lambda e: e.scalar_tensor_tensor(out=yAT[b][:, c, :], in0=ya[y3][:, c, :], scalar=gmixc[:, c:c + 1], in1=rrs[b][:],
                                                                       op0=ALU.mult, op1=ALU.mult), reads=["ya%d" % y3, "gmixc", "rrsa%d" % b], writes=["yAT%d_%d" % (b, c)])
                        dma("sp", YTv[:, 0:4, col0:col0 + TL], yAT[b][:], reads=["yAT%d_%d" % (b, c) for c in range(4)], writes=["YT"])

                    load_x(0)
                    load_hg(0)
                    load_x(1)
                    b0(0)
                    for oi in range(nO + 2):
                        if oi + 1 < nO:
                            load_hg(oi + 1)
                            b0(oi + 1)
                        if oi < nO:
                            b1(oi)
                        if oi + 2 < nO:
                            load_x(oi + 2)
                        if 0 <= oi - 1 < nO:
                            b2a(oi - 1)
                        if 0 <= oi - 2 < nO:
                            b2b(oi - 2)
                    fw.barrier()
                    scope_.__exit__(None, None, None)
                if stop_after == "M2b" and l == 0:
                    return nc

                m3_tiles = LTILES[1:] if last else LTILES
                wdn = sb(el, "wdn", [128, NF, D], BF16)
                with ExitStack() as e4:
                    e4.enter_context(nc.named_scope("M3a_l%d" % l))
                    wout = sb(e4, "wout", [128, 8, D], BF16)
                    for k in range(8):
                        dma("pool", wout[:, k, :], wout_d[l, k * 128:(k + 1) * 128, :], writes=["wout"])
                    for f in range(NF):
                        dma("pool", wdn[:, f, :], wdn_d[l, f * 128:(f + 1) * 128, :], writes=["wdn"])
                    gate = sb(e4, "gate1", [128, 2, D])
                    lng = sb(e4, "ln1g", [128, D])
                    lnb = sb(e4, "ln1b", [128, D])
                    for s in range(2):
                        dma("sp", gate[:, s, :], MODS[s:s + 1, 2 * D:3 * D].partition_broadcast(128), reads=["MODS"], writes=["gate1"])
                    dma("sp", lng[:], ln1g_d[l:l + 1, :].partition_broadcast(128), writes=["ln1g"])
                    dma("sp", lnb[:], ln1b_d[l:l + 1, :].partition_broadcast(128), writes=["ln1b"])
                    op("dve", lambda e: e.tensor_scalar(out=gate[:], in0=gate[:], scalar1=1.0 / ALPHA, scalar2=None, op0=ALU.mult), writes=["gate1"])
                    rmask = sb(e4, "rmask", [128, 2])
                    dma("sp", rmask[:], rmask_d[:, :], writes=["rmask"])
                    woutL = sb(e4, "woutL", [128, 8, D], BF16)
                    wsel = [sb(e4, "wsel%d" % r_, [128, 6, D], BF16) for r_ in range(2)]
                    woutC = sb(e4, "woutC", [128, 8, D], BF16) if not last else None
                    AC = (0, 1, 2, 3, 6, 7)
                    for k in range(8):
                        op("dve", lambda e: e.tensor_tensor(out=woutL[:, k, :], in0=wout[:, k, :], in1=gate[:, 0, :], op=ALU.mult),
                           reads=["wout", "gate1"], writes=["woutL"])
                        if not last:
                            op("pool", lambda e: e.tensor_tensor(out=woutC[:, k, :], in0=wout[:, k, :], in1=gate[:, 1, :], op=ALU.mult),
                               reads=["wout", "gate1"], writes=["woutC"])
                    for r_ in range(2):
                        for j, k in enumerate(AC):
                            if r_ == 0:
                                op("dve", lambda e: e.tensor_scalar(out=wsel[r_][:, j, :], in0=woutL[:, k, :], scalar1=rmask[:, r_:r_ + 1], scalar2=None, op0=ALU.mult),
                                   reads=["woutL", "rmask"], writes=["wsel%d_%d" % (r_, j)])
                            else:
                                op("act", lambda e: e.activation(out=wsel[r_][:, j, :], in_=woutL[:, k, :], func=AF.Identity, scale=rmask[:, r_:r_ + 1]),
                                   reads=["woutL", "rmask"], writes=["wsel%d_%d" % (r_, j)])
                    yTb = [sb(e4, "yTb%d" % i, [128, 2, TL], BF16) for i in range(3)]
                    yC = [[sb(e4, "yC%d_%d" % (i, r_), [128, 6, TL], BF16) for r_ in range(2)] for i in range(3)]
                    xt = [sb(e4, "xt%d" % i, [128, 2, D]) for i in range(2)]
                    rt = [sb(e4, "rt%d" % i, [128, 2, D]) for i in range(3)]
                    lns3 = [ln_scr(e4, "l3a"), ln_scr(e4, "l3b")]
                    po = [ps(e4, "po%d" % i, [128, 512]) for i in range(8)]
                    n3 = len(m3_tiles)

                    def loadY(ti):
                        col0, is_ctx = m3_tiles[ti]
                        y = ti % 3
                        dma("sp", yTb[y][:], YBLv[:, :, col0:col0 + TL], reads=["YBL"], writes=["yTb%d" % y])
                        if is_ctx:
                            dma("sp", yC[y][0][:, 0:4, :], YTv[:, 0:4, T:T + TL], reads=["YT"], writes=["yC%d_0" % y])
                            dma("sp", yC[y][0][:, 4:6, :], YTv[:, 6:8, T:T + TL], reads=["YT"], writes=["yC%d_0" % y])
                        else:
                            for r_ in range(2):
                                g0 = r_ * TLOC + col0
                                dma("sp", yC[y][r_][:, 0:4, :], YTv[:, 0:4, g0:g0 + TL], reads=["YT"], writes=["yC%d_%d" % (y, r_)])
                                dma("sp", yC[y][r_][:, 4:6, :], YTv[:, 6:8, g0:g0 + TL], reads=["YT"], writes=["yC%d_%d" % (y, r_)])

                    def loadX(ti):
                        col0, is_ctx = m3_tiles[ti]
                        b = ti % 2
                        load_resid(l, col0, is_ctx, xt[b], "xt%d" % b, None, None, from_xb=True)

                    def mm3(ti):
                        col0, is_ctx = m3_tiles[ti]
                        b = ti % 2
                        y = ti % 3
                        for s in range(2):
                            for hf_ in range(2):
                                p_, kp_ = po[4 * b + 2 * s + hf_], "po%d" % (4 * b + 2 * s + hf_)
                                cs = slice(hf_ * 512, (hf_ + 1) * 512)
                                ts_ = slice(s * 128, (s + 1) * 128)
                                steps = []
                                wB = woutC if is_ctx else woutL
                                for j in range(2):
                                    steps.append((yTb[y][:, j, ts_], wB[:, 4 + j, cs], ["yTb%d" % y, "woutC" if is_ctx else "woutL"]))
                                if is_ctx:
                                    for j, k in enumerate(AC):
                                        steps.append((yC[y][0][:, j, ts_], woutC[:, k, cs], ["yC%d_0" % y, "woutC"]))
                                else:
                                    for r_ in range(2):
                                        for j in range(6):
                                            steps.append((yC[y][r_][:, j, ts_], wsel[r_][:, j, cs], ["yC%d_%d" % (y, r_), "wsel%d_%d" % (r_, j)]))
                                for n_, (lh, rh, rd) in enumerate(steps):
                                    op("pe", lambda e: e.matmul(p_[:], lhsT=lh, rhs=rh, start=(n_ == 0), stop=(n_ == len(steps) - 1)), reads=rd, writes=[kp_])

                    def res3(ti):
                        b = ti % 2
                        r3 = ti % 3
                        for s in range(2):
                            for hf_ in range(2):
                                p_, kp_ = po[4 * b + 2 * s + hf_], "po%d" % (4 * b + 2 * s + hf_)
                                op("dve", lambda e: e.tensor_tensor(out=rt[r3][:, s, hf_ * 512:(hf_ + 1) * 512], in0=p_[:],
                                                                    in1=xt[b][:, s, hf_ * 512:(hf_ + 1) * 512], op=ALU.add),
                                   reads=["xt%d" % b], writes=[kp_, "rt%d_%d" % (r3, s)])

                    def stats3(ti):
                        b = ti % 2
                        r3 = ti % 3
                        ln_stats(rt[r3], ["rt%d_0" % r3, "rt%d_1" % r3], lns3[b], EPS_POST, iters=2)

                    def norm3(ti):
                        col0, is_ctx = m3_tiles[ti]
                        b = ti % 2
                        r3 = ti % 3
                        mv, rs, kk = lns3[b]["mv"], lns3[b]["rs"], lns3[b]["k"]
                        for s in range(2):
                            kr = "rt%d_%d" % (r3, s)
                            op("dve", lambda e: e.tensor_scalar(out=rt[r3][:, s, :], in0=rt[r3][:, s, :], scalar1=mv[:, s, 0:1],
                                                                scalar2=rs[:, s:s + 1], op0=ALU.subtract, op1=ALU.mult),
                               reads=[kk + "mv", kk + "rs"], writes=[kr])
                            op("dve", lambda e: e.tensor_tensor(out=rt[r3][:, s, :], in0=rt[r3][:, s, :], in1=lng[:], op=ALU.mult), reads=["ln1g"], writes=[kr])
                            op("pool", lambda e: e.tensor_tensor(out=rt[r3][:, s, :], in0=rt[r3][:, s, :], in1=lnb[:], op=ALU.add), reads=["ln1b"], writes=[kr])

                    def store3(ti):
                        col0, is_ctx = m3_tiles[ti]
                        r3 = ti % 3
                        dma("sp", XB[col0:col0 + TL, :].rearrange("(s p) d -> p s d", p=128), rt[r3][:], reads=["rt%d_0" % r3, "rt%d_1" % r3], writes=["XB_%d" % col0])

                    for i_ in range(min(3, n3)):
                        loadY(i_)
                    for i_ in range(min(2, n3)):
                        loadX(i_)
                    mm3(0)
                    res3(0)
                    stats3(0)
                    for ti in range(n3):
                        if ti + 3 < n3:
                            loadY(ti + 3)
                        if ti + 2 < n3:
                            loadX(ti + 2)
                        if ti >= 2:
                            store3(ti - 2)
                        if ti + 1 < n3:
                            mm3(ti + 1)
                            res3(ti + 1)
                            stats3(ti + 1)
                        norm3(ti)
                    if n3 >= 2:
                        store3(n3 - 2)
                    store3(n3 - 1)
                    fw.barrier()
                if stop_after == "M3a" and l == 0:
                    return nc

                with ExitStack() as e5:
                    e5.enter_context(nc.named_scope("M3b_l%d" % l))
                    wup = sb(e5, "wup", [128, 8, 2 * DFF], BF16)
                    for k in range(8):
                        for c0 in range(0, 2 * DFF, 2048):
                            c1 = min(c0 + 2048, 2 * DFF)
                            dma("pool", wup[:, k, c0:c1], wup_d[l, k * 128:(k + 1) * 128, c0:c1], writes=["wup"])
                    gate = sb(e5, "gate2", [128, 2, D])
                    lng = sb(e5, "ln2g", [128, D])
                    lnb = sb(e5, "ln2b", [128, D])
                    for s in range(2):
                        dma("sp", gate[:, s, :], MODS[s:s + 1, 5 * D:6 * D].partition_broadcast(128), reads=["MODS"], writes=["gate2"])
                    dma("sp", lng[:], ln2g_d[l:l + 1, :].partition_broadcast(128), writes=["ln2g"])
                    dma("sp", lnb[:], ln2b_d[l:l + 1, :].partition_broadcast(128), writes=["ln2b"])
                    op("dve", lambda e: e.tensor_scalar(out=gate[:], in0=gate[:], scalar1=1.0 / ALPHA, scalar2=None, op0=ALU.mult), writes=["gate2"])
                    xt = [sb(e5, "xu%d" % i, [128, 2, D]) for i in range(2)]
                    xh = sb(e5, "xhu", [128, 2, D])
                    h2T = [sb(e5, "h2T%d" % i, [128, 8, TL], BF16) for i in range(2)]
                    actT = sb(e5, "actT", [128, NF, TL], BF16)
                    sg = [sb(e5, "sg%d" % i, [128, TL]) for i in range(2)]
                    lns = ln_scr(e5, "l5")
                    lns2 = ln_scr(e5, "l6")
                    tp0 = ps(e5, "tq0", [128, 2, TL]); tp1 = ps(e5, "tq1", [128, 2, TL])
                    tps = [(tp0, "tq0"), (tp1, "tq1")]
                    pu = [ps(e5, "pu%d" % i, [128, 2, TL]) for i in range(2)]
                    pd = [ps(e5, "pd%d" % i, [128, 512]) for i in range(4)]

                    def load5(ti, b):
                        col0, is_ctx = m3_tiles[ti]
                        dma("sp", xt[b][:], XB[col0:col0 + TL, :].rearrange("(s p) d -> p s d", p=128), reads=["XB_%d" % col0], writes=["xu%d" % b])

                    su = [sb(e5, "su%d" % i, [128, TL]) for i in range(2)]
                    n5 = len(m3_tiles)
                    folded = [False]

                    def fold_gate():
                        for f in range(NF):
                            op("dve" if f % 2 == 0 else "pool",
                               lambda e: e.tensor_tensor(out=wdn[:, f, :], in0=wdn[:, f, :], in1=gate[:, 0, :], op=ALU.mult), reads=["gate2"], writes=["wdn"])
                        folded[0] = True

                    def stA(ti):
                        b = ti % 2
                        ln_tile(xt[b], "xu%d" % b, xh, "xhu", lns, EPS, eng="dve", iters=2)

                    def stT(ti):
                        col0, is_ctx = m3_tiles[ti]
                        b = ti % 2
                        transpose_mod(xh, "xhu", h2T[b], "h2T%d" % b, tps, modc, 1 if is_ctx else 0, 4, 3)

                    def stU(ti):
                        b = ti % 2
                        for f in range(NF):
                            p_, kp_ = pu[f % 2], "pu%d" % (f % 2)
                            for j in range(2):
                                c0 = j * DFF + f * 128
                                for k in range(8):
                                    op("pe", lambda e: e.matmul(p_[:, j, :], lhsT=wup[:, k, c0:c0 + 128], rhs=h2T[b][:, k, :], start=(k == 0), stop=(k == 7)),
                                       reads=["wup", "h2T%d" % b], writes=[kp_])
                            op("act", lambda e: e.activation(out=sg[f % 2][:], in_=p_[:, 0, :], func=AF.Silu), writes=[kp_, "sg%d" % (f % 2)])
                            op("act", lambda e: e.activation(out=su[f % 2][:], in_=p_[:, 1, :], func=AF.Identity), writes=[kp_, "su%d" % (f % 2)])
                            op("pool", lambda e: e.tensor_tensor(out=actT[:, f, :], in0=su[f % 2][:], in1=sg[f % 2][:], op=ALU.mult),
                               reads=["sg%d" % (f % 2), "su%d" % (f % 2)], writes=["actT"])

                    def stD(ti):
                        for s in range(2):
                            for hf_ in range(2):
                                p_, kp_ = pd[2 * s + hf_], "pd%d" % (2 * s + hf_)
                                for f in range(NF):
                                    op("pe", lambda e: e.matmul(p_[:], lhsT=actT[:, f, s * 128:(s + 1) * 128], rhs=wdn[:, f, hf_ * 512:(hf_ + 1) * 512],
                                                                start=(f == 0), stop=(f == NF - 1)), reads=["wdn", "actT"], writes=[kp_])

                    def stE(ti):
                        col0, is_ctx = m3_tiles[ti]
                        b = ti % 2
                        kx = "xu%d" % b
                        for s in range(2):
                            for hf_ in range(2):
                                p_, kp_ = pd[2 * s + hf_], "pd%d" % (2 * s + hf_)
                                cs = slice(hf_ * 512, (hf_ + 1) * 512)
                                if folded[0]:
                                    op("dve", lambda e: e.tensor_tensor(out=xt[b][:, s, cs], in0=p_[:], in1=xt[b][:, s, cs], op=ALU.add), writes=[kp_, kx])
                                else:
                                    op("dve", lambda e: e.tensor_tensor(out=xh[:, s, cs], in0=p_[:], in1=gate[:, 1 if is_ctx else 0, cs], op=ALU.mult),
                                       reads=["gate2"], writes=[kp_, "xhu"])
                        if not folded[0]:
                            op("pool", lambda e: e.tensor_tensor(out=xt[b][:], in0=xt[b][:], in1=xh[:], op=ALU.add), reads=["xhu"], writes=[kx])
                        ln_tile(xt[b], kx, xt[b], kx, lns2, EPS_POST, eng="dve", iters=3)
                        for s in range(2):
                            op("dve", lambda e: e.tensor_tensor(out=xt[b][:, s, :], in0=xt[b][:, s, :], in1=lng[:], op=ALU.mult), reads=["ln2g"], writes=[kx])
                            op("dve", lambda e: e.tensor_tensor(out=xt[b][:, s, :], in0=xt[b][:, s, :], in1=lnb[:], op=ALU.add), reads=["ln2b"], writes=[kx])
                        dst = out_d[col0:col0 + TL, :] if last else XB[col0:col0 + TL, :]
                        dma("sp", dst.rearrange("(s p) d -> p s d", p=128), xt[b][:], reads=[kx], writes=["XB_%d" % col0])

                    load5(0, 0)
                    if n5 > 1:
                        load5(1, 1)
                    if not m3_tiles[0][1]:
                        fold_gate()
                    stA(0)
                    stT(0)
                    for ti in range(n5):
                        stU(ti)
                        if ti + 1 < n5:
                            stA(ti + 1)
                        stD(ti)
                        if ti + 1 < n5:
                            stT(ti + 1)
                        stE(ti)
                        if m3_tiles[ti][1]:
                            fold_gate()
                        if ti + 2 < n5:
                            load5(ti + 2, ti % 2)
                    fw.barrier()
                if stop_after == "M3b" and l == 0:
                    return nc
        fw.wait_all("sp")
    return nc


def host_consts():
    c = {}
    c["ident"] = np.eye(128, dtype=np.float32)
    l1 = np.arange(128)[:, None].astype(np.float64)
    k1 = np.arange(128)[None, :].astype(np.float64)
    a = 2 * np.pi * l1 * k1 / 128.0
    c["t1"] = np.stack([np.cos(a), -np.sin(a), -np.cos(a)], 1).astype(np.float32).astype(ml_dtypes.bfloat16)
    l2 = np.arange(64)[:, None, None].astype(np.float64)
    kk1 = np.arange(128)[None, :, None].astype(np.float64)
    kk2 = np.arange(64)[None, None, :].astype(np.float64)
    ang = 2 * np.pi * ((kk1 + 128 * kk2) * l2 % T) / T
    c["tw"] = np.concatenate([np.cos(ang), np.sin(ang)], 0).reshape(128, T).astype(np.float32).astype(ml_dtypes.bfloat16)
    p = np.arange(128)[:, None, None].astype(np.float64)
    t = np.arange(2)[None, :, None].astype(np.float64)
    k = np.arange(256)[None, None, :].astype(np.float64)
    a2 = 2 * np.pi * (((128 * t + p) * k) % 256) / 256.0
    c["c256"] = np.stack([np.cos(a2), -np.sin(a2)], 2).astype(np.float32).astype(ml_dtypes.bfloat16)
    j = np.arange(64)[:, None].astype(np.float64)
    ch = np.arange(64)[None, :].astype(np.float64)
    a3 = 2 * np.pi * j * ch / 64.0
    cs = np.zeros((64, 2, 2, 128), np.float64)
    for gl in range(2):
        cs[:, 0, gl, gl * 64:(gl + 1) * 64] = np.cos(a3)
        cs[:, 1, gl, gl * 64:(gl + 1) * 64] = np.sin(a3)
    c["cspad"] = cs.astype(np.float32)
    quarter = D // 4
    freqs = 10000.0 ** (-np.arange(quarter, dtype=np.float32) / np.float32(quarter))
    r = np.repeat(np.arange(T // 64, dtype=np.float32), 64)
    col = np.tile(np.arange(64, dtype=np.float32), T // 64)

    def enc(pv):
        an = pv[:, None].astype(np.float32) * freqs[None, :].astype(np.float32)
        return np.concatenate([np.sin(an), np.cos(an)], -1)

    c["pos"] = np.concatenate([enc(r), enc(col)], -1).astype(np.float32)
    return c


_WNAMES = ["w_mod", "b_mod", "w_in", "conv_w", "conv_b", "lru_wa", "lru_ba", "lru_wx", "lru_bx", "lru_lam", "sg_ws", "sg_b",
           "fourier_w", "g_mix", "w_out", "ln1_g", "ln1_b", "w_up", "w_down", "ln2_g", "ln2_b"]


def make_in_maps(inputs, n_cores=N_CORES):
    consts = host_consts()
    pos = consts.pop("pos")
    shared = {n: np.ascontiguousarray(np.asarray(inputs[n], dtype=np.float32)) for n in _WNAMES}
    shared.update(consts)
    x = np.asarray(inputs["x"], dtype=np.float32)
    c = np.asarray(inputs["c"], dtype=np.float32)
    ctx = np.asarray(inputs["ctx"], dtype=np.float32)
    c_ctx = np.asarray(inputs["c_ctx"], dtype=np.float32)
    maps = []
    for core in range(n_cores):
        b, h = core // 2, core % 2
        m = dict(shared)
        m["x"] = np.ascontiguousarray(x[b, h * TLOC:(h + 1) * TLOC])
        m["pos"] = np.ascontiguousarray(pos[h * TLOC:(h + 1) * TLOC])
        m["ctx"] = np.ascontiguousarray(ctx[b])
        m["cc"] = np.ascontiguousarray(np.stack([c[b], c_ctx], 0))
        rm = np.zeros((128, 2), np.float32)
        rm[:, h] = 1.0
        m["rmask"] = rm
        maps.append(m)
    return maps


def kernel(**inputs):
    nc = build_program()
    maps = make_in_maps(inputs)
    res = run_bass_kernel_spmd(nc, maps, core_ids=list(range(N_CORES)))
    outs = [np.asarray(r["out"], dtype=np.float32) for r in res.results]
    return np.stack([np.concatenate([outs[2 * b], outs[2 * b + 1]], 0) for b in range(N_CORES // 2)], 0)
```

```python
import math
from contextlib import ExitStack

import numpy as np
import ml_dtypes
import concourse.bass as bass
import concourse.mybir as mybir
from concourse.bass_utils import run_bass_kernel_spmd

F32 = mybir.dt.float32
BF16 = mybir.dt.bfloat16
I32 = mybir.dt.int32
ALU = mybir.AluOpType
AF = mybir.ActivationFunctionType

D = 1024
T = 8192
TC = 256
TT = T + TC
TL = 256
NT = T // TL
DEPTH = 2
DA, DB, DC = 512, 256, 256
DIN = 1792
DFF = 2816
NF = DFF // 128
EPS = 1e-6
ALPHA = (2 * DEPTH) ** 0.25
EPS_POST = EPS / (ALPHA * ALPHA)
N_CORES = 8
TLOC = T // 2
NTL = TLOC // TL

SAME_ENG_SYNC = True
N_DMA_SEMS = 40


class FW:
    def __init__(self, nc, es):
        self.nc = nc
        self.es = es
        self.engs = {"pe": nc.tensor, "act": nc.scalar, "dve": nc.vector, "pool": nc.gpsimd, "sp": nc.sync}
        self.sems = {}
        self.cnt = {}
        for e in self.engs:
            self.sems[e] = es.enter_context(nc.semaphore("s_" + e))
            self.cnt[e] = 0
        self.dsem = {}
        for q in ("sp", "pool"):
            lst = []
            for i in range(N_DMA_SEMS):
                key = "d_%s_%d" % (q, i)
                self.sems[key] = es.enter_context(nc.semaphore(key))
                self.cnt[key] = 0
                lst.append(key)
            self.dsem[q] = [lst, 0]
        self.seen = {e: {} for e in self.engs}
        self.lastw = {}
        self.readers = {}
        self.ninst = 0

    def _wait(self, eng, ev):
        sk, v, prod = ev
        if prod == "pe" and eng == "pe":
            return
        if prod == eng and not SAME_ENG_SYNC:
            return
        if self.seen[eng].get(sk, 0) >= v:
            return
        self.engs[eng].wait_ge(self.sems[sk], v)
        self.seen[eng][sk] = v

    def _deps(self, eng, reads, writes):
        for k in reads:
            ev = self.lastw.get(k)
            if ev is not None:
                self._wait(eng, ev)
        for k in writes:
            ev = self.lastw.get(k)
            if ev is not None:
                self._wait(eng, ev)
            for ev in list(self.readers.get(k, {}).values()):
                self._wait(eng, ev)

    def _record(self, ev, reads, writes):
        for k in writes:
            self.lastw[k] = ev
            self.readers[k] = {}
        for k in reads:
            if k in writes:
                continue
            self.readers.setdefault(k, {})[ev[0]] = ev

    def op(self, eng, fn, reads=(), writes=()):
        self._deps(eng, reads, writes)
        inst = fn(self.engs[eng])
        self.cnt[eng] += 1
        inst.then_inc(self.sems[eng], 1)
        ev = (eng, self.cnt[eng], eng)
        self._record(ev, reads, writes)
        self.ninst += 1
        return ev

    def dma(self, q, out, in_, reads=(), writes=(), **kw):
        self._deps(q, reads, writes)
        lst, idx = self.dsem[q]
        sk = lst[idx % len(lst)]
        self.dsem[q][1] = idx + 1
        if self.cnt[sk] > 0:
            self._wait(q, (sk, self.cnt[sk], "dma"))
        inst = self.engs[q].dma_start(out=out, in_=in_, **kw)
        self.cnt[sk] += 16
        inst.then_inc(self.sems[sk], 16)
        ev = (sk, self.cnt[sk], "dma")
        self._record(ev, reads, writes)
        self.ninst += 1
        return ev

    def barrier(self):
        for e in self.engs:
            for e2 in ("pe", "act", "dve", "pool"):
                if e2 != e and self.cnt[e2] > 0:
                    self._wait(e, (e2, self.cnt[e2], e2))
            if self.cnt.get("cc", 0) > 0:
                self._wait(e, ("cc", self.cnt["cc"], "dma"))
            for q in self.dsem:
                for sk in self.dsem[q][0]:
                    if self.cnt[sk] > 0:
                        self._wait(e, (sk, self.cnt[sk], "dma"))
        self.lastw.clear()
        self.readers.clear()

    def collective(self, kind, in_ap, out_ap, groups, reads, writes):
        self._deps("pool", reads, writes)
        if "cc" not in self.sems:
            self.sems["cc"] = self.es.enter_context(self.nc.semaphore("s_cc"))
            self.cnt["cc"] = 0
        inst = self.nc.gpsimd.collective_compute(kind, ALU.bypass, replica_groups=groups, ins=[in_ap], outs=[out_ap])
        self.cnt["cc"] += 1
        inst.then_inc(self.sems["cc"], 1)
        ev = ("cc", self.cnt["cc"], "dma")
        self._record(ev, reads, writes)
        return ev

    def wait_all(self, eng="sp"):
        for ev in list(self.lastw.values()):
            self._wait(eng, ev)
        for d in list(self.readers.values()):
            for ev in list(d.values()):
                self._wait(eng, ev)
        for q in self.dsem:
            for sk in self.dsem[q][0]:
                if self.cnt[sk] > 0:
                    self._wait(eng, (sk, self.cnt[sk], "dma"))
        for e in ("pe", "act", "dve", "pool"):
            if self.cnt[e] > 0:
                self._wait(eng, (e, self.cnt[e], e))


def build_program(stop_after=None):
    nc = bass.Bass("TRN2", target_bir_lowering=False, num_devices=8)

    def din(name, shape, dt=F32):
        return nc.dram_tensor(name, list(shape), dt, kind="ExternalInput").ap()

    dbg = stop_after is not None

    CC_BUFS = ("XAL", "GGL", "ZL", "XAF", "GGF", "ZF")

    def dscr(name, shape, dt=F32):
        ext = dbg and name not in CC_BUFS
        return nc.dram_tensor(name, list(shape), dt, kind="ExternalOutput" if ext else "Internal").ap()

    x_d = din("x", [TLOC, D])
    ctx_d = din("ctx", [TC, D])
    cc_d = din("cc", [2, D])
    pos_d = din("pos", [TLOC, D])
    rmask_d = din("rmask", [128, 2])
    wmod_d = din("w_mod", [DEPTH, D, 6 * D])
    bmod_d = din("b_mod", [DEPTH, 6 * D])
    win_d = din("w_in", [DEPTH, D, DIN])
    convw_d = din("conv_w", [DEPTH, 4, DA])
    convb_d = din("conv_b", [DEPTH, DA])
    wa_d = din("lru_wa", [DEPTH, 2, 8, 64, 64])
    ba_d = din("lru_ba", [DEPTH, 2, DA])
    wx_d = din("lru_wx", [DEPTH, 2, 8, 64, 64])
    bx_d = din("lru_bx", [DEPTH, 2, DA])
    lam_d = din("lru_lam", [DEPTH, 2, DA])
    ws_d = din("sg_ws", [DEPTH, 4, 128, 128])
    sgb_d = din("sg_b", [DEPTH, 4, 128])
    wf_d = din("fourier_w", [DEPTH, 4, 64, 64])
    gmix_d = din("g_mix", [DEPTH, D])
    wout_d = din("w_out", [DEPTH, D, D])
    ln1g_d = din("ln1_g", [DEPTH, D])
    ln1b_d = din("ln1_b", [DEPTH, D])
    wup_d = din("w_up", [DEPTH, D, 2 * DFF])
    wdn_d = din("w_down", [DEPTH, DFF, D])
    ln2g_d = din("ln2_g", [DEPTH, D])
    ln2b_d = din("ln2_b", [DEPTH, D])
    ident_d = din("ident", [128, 128])
    t1_d = din("t1", [128, 3, 128], BF16)
    tw_d = din("tw", [128, T], BF16)
    c256_d = din("c256", [128, 2, 2, 256], BF16)
    cspad_d = din("cspad", [64, 2, 2, 128])
    out_d = nc.dram_tensor("out", [TLOC, D], F32, kind="ExternalOutput").ap()

    XB = dscr("XB", [TLOC + TC, D])
    XAL = dscr("XAL", [DA, TLOC])
    GGL = dscr("GGL", [DA, TLOC])
    ZL = dscr("ZL", [DC, TLOC], BF16)
    XAF = dscr("XAF", [2 * DA, TLOC])
    GGF = dscr("GGF", [2 * DA, TLOC])
    ZF = dscr("ZF", [2 * DC, TLOC], BF16)
    XAC = dscr("XAC", [DA, TC])
    GGC = dscr("GGC", [DA, TC])
    YBL = dscr("YBL", [DB, TLOC + TC], BF16)
    XC = dscr("XC", [DA, TT])
    HF = dscr("HF", [DA, TT])
    YT = dscr("YT", [D, TT], BF16)
    GD = dscr("GD", [128, 2, 64, 256], BF16)
    MODS = dscr("MODS", [2, 6 * D])

    XCv = XC.rearrange("(c p) t -> p c t", p=128)
    XALv = XAL.rearrange("(c p) t -> p c t", p=128)
    GGLv = GGL.rearrange("(c p) t -> p c t", p=128)
    ZLv = ZL.rearrange("(c p) t -> p c t", p=128)
    XACv = XAC.rearrange("(c p) t -> p c t", p=128)
    GGCv = GGC.rearrange("(c p) t -> p c t", p=128)
    YBLv = YBL.rearrange("(c p) t -> p c t", p=128)
    XAFv = XAF.rearrange("(c r p) t -> r p c t", r=2, p=128)
    GGFv = GGF.rearrange("(c r p) t -> r p c t", r=2, p=128)
    ZFv = ZF.rearrange("(r c p) t -> r p c t", r=2, p=128)
    PAIRS = [[0, 1], [2, 3], [4, 5], [6, 7]]
    HFv = HF.rearrange("(c p) t -> p c t", p=128)
    YTv = YT.rearrange("(c p) t -> p c t", p=128)

    TILES = [(T, True)] + [(TL * i, False) for i in range(NT)]
    LTILES = [(TLOC, True)] + [(TL * i, False) for i in range(NTL)]
    import os as _os
    if _os.environ.get("DBG_NT"):
        LTILES = LTILES[:int(_os.environ["DBG_NT"])]

    with ExitStack() as es:
        fw = FW(nc, es)
        op = fw.op
        dma = fw.dma

        uid = [0]

        def sb(es_, name, shape, dt=F32):
            uid[0] += 1
            return es_.enter_context(nc.sbuf_tensor("%s_s%d" % (name, uid[0]), list(shape), dt))

        def ps(es_, name, shape, dt=F32):
            uid[0] += 1
            return es_.enter_context(nc.psum_tensor("%s_p%d" % (name, uid[0]), list(shape), dt))

        es.enter_context(nc.allow_non_contiguous_dma(reason="small strided parameter loads"))

        ident = sb(es, "ident", [128, 128])
        ones_f = sb(es, "ones_f", [128, 128])
        dma("sp", ident[:], ident_d[:, :], writes=["ident"])
        op("pool", lambda e: e.memset(ones_f[:], 1.0), writes=["ones_f"])

        def rsqrt(out, x, tmp, kout, kx, ktmp, eng="pool", iters=3):
            xi = x.bitcast(I32)
            oi = out.bitcast(I32)
            op("dve", lambda e: e.tensor_scalar(out=oi, in0=xi, scalar1=1, scalar2=None,
                                                op0=ALU.arith_shift_right), reads=[kx], writes=[kout])
            op("dve", lambda e: e.tensor_scalar(out=oi, in0=oi, scalar1=-1.0, scalar2=float(0x5F3759DF),
                                                op0=ALU.mult, op1=ALU.add), writes=[kout])
            for _ in range(iters):
                op(eng, lambda e: e.tensor_tensor(out=tmp, in0=x, in1=out, op=ALU.mult), reads=[kx, kout], writes=[ktmp])
                op(eng, lambda e: e.tensor_tensor(out=tmp, in0=tmp, in1=out, op=ALU.mult), reads=[kout], writes=[ktmp])
                op(eng, lambda e: e.tensor_scalar(out=tmp, in0=tmp, scalar1=-0.5, scalar2=1.5,
                                                  op0=ALU.mult, op1=ALU.add), writes=[ktmp])
                op(eng, lambda e: e.tensor_tensor(out=out, in0=out, in1=tmp, op=ALU.mult), reads=[ktmp], writes=[kout])

        def resid_rows(l, col0, is_ctx):
            if l == 0:
                return (ctx_d[0:TL, :] if is_ctx else x_d[col0:col0 + TL, :])
            return XB[col0:col0 + TL, :]

        def load_resid(l, col0, is_ctx, xt, kx, pt=None, kp=None, from_xb=False, save_xb=False):
            if from_xb and not is_ctx:
                src = XB[col0:col0 + TL, :]
            else:
                src = resid_rows(l, col0, is_ctx)
            dma("sp", xt[:], src.rearrange("(s p) d -> p s d", p=128), reads=(["XB_%d" % col0] if (from_xb or l > 0) else []), writes=[kx])
            if l == 0 and not is_ctx and not from_xb:
                dma("sp", pt[:], pos_d[col0:col0 + TL, :].rearrange("(s p) d -> p s d", p=128), writes=[kp])
                op("pool", lambda e: e.tensor_tensor(out=xt[:], in0=xt[:], in1=pt[:], op=ALU.add),
                   reads=[kp], writes=[kx])
                if save_xb:
                    dma("sp", XB[col0:col0 + TL, :].rearrange("(s p) d -> p s d", p=128), xt[:], reads=[kx], writes=["XB_%d" % col0])

        def ln_tile(xt, kx, xh, kxh, scr, eps, eng="pool", iters=3):
            st, mv, ve, rs, tmp = scr["st"], scr["mv"], scr["ve"], scr["rs"], scr["tmp"]
            kk = scr["k"]
            for s in range(2):
                for h in range(2):
                    op("dve", lambda e: e.bn_stats(out=st[:, s, h, :], in_=xt[:, s, h * 512:(h + 1) * 512]),
                       reads=[kx], writes=[kk + "st"])
                op("dve", lambda e: e.bn_aggr(out=mv[:, s, :], in_=st[:, s, :, :].rearrange("p a b -> p (a b)")),
                   reads=[kk + "st"], writes=[kk + "mv"])
            op("dve", lambda e: e.tensor_scalar(out=ve[:], in0=mv[:, :, 1], scalar1=float(eps), scalar2=None,
                                                op0=ALU.add), reads=[kk + "mv"], writes=[kk + "ve"])
            rsqrt(rs[:], ve[:], tmp[:], kk + "rs", kk + "ve", kk + "tmp", eng=eng, iters=iters)
            for s in range(2):
                op("dve", lambda e: e.tensor_scalar(out=xh[:, s, :], in0=xt[:, s, :], scalar1=mv[:, s, 0:1],
                                                    scalar2=rs[:, s:s + 1], op0=ALU.subtract, op1=ALU.mult),
                   reads=[kx, kk + "mv", kk + "rs"], writes=[kxh])

        def ln_stats(xt, kx, scr, eps, iters=3):
            st, mv, ve, rs, tmp = scr["st"], scr["mv"], scr["ve"], scr["rs"], scr["tmp"]
            kk = scr["k"]
            for s in range(2):
                for h in range(2):
                    op("dve", lambda e: e.bn_stats(out=st[:, s, h, :], in_=xt[:, s, h * 512:(h + 1) * 512]),
                       reads=(kx if isinstance(kx, list) else [kx]), writes=[kk + "st%d%d" % (s, h)])
                op("dve", lambda e: e.bn_aggr(out=mv[:, s, :], in_=st[:, s, :, :].rearrange("p a b -> p (a b)")),
                   reads=[kk + "st%d0" % s, kk + "st%d1" % s], writes=[kk + "mv"])
            op("dve", lambda e: e.tensor_scalar(out=ve[:], in0=mv[:, :, 1], scalar1=float(eps), scalar2=None,
                                                op0=ALU.add), reads=[kk + "mv"], writes=[kk + "ve"])
            rsqrt(rs[:], ve[:], tmp[:], kk + "rs", kk + "ve", kk + "tmp", iters=iters)

        def ln_apply(xt, kx, xh, kxh, scr):
            mv, rs = scr["mv"], scr["rs"]
            kk = scr["k"]
            for s in range(2):
                op("dve", lambda e: e.tensor_scalar(out=xh[:, s, :], in0=xt[:, s, :], scalar1=mv[:, s, 0:1],
                                                    scalar2=rs[:, s:s + 1], op0=ALU.subtract, op1=ALU.mult),
                   reads=[kx, kk + "mv", kk + "rs"], writes=[kxh])

        def ln_scr(es_, name):
            return dict(st=sb(es_, name + "st", [128, 2, 2, 6]), mv=sb(es_, name + "mv", [128, 2, 2]),
                        ve=sb(es_, name + "ve", [128, 2]), rs=sb(es_, name + "rs", [128, 2]),
                        tmp=sb(es_, name + "tmp", [128, 2]), k=name)

        def transpose_mod(xh, kxh, hT, khT, tps, modc, strm, jsc, jsh):
            for kp in range(4):
                tp, ktp = tps[kp % 2]
                for j in range(2):
                    k = 2 * kp + j
                    for s in range(2):
                        op("pe", lambda e: e.transpose(out=tp[:, j, s * 128:(s + 1) * 128],
                                                       in_=xh[:, s, k * 128:(k + 1) * 128], identity=ident[:]),
                           reads=[kxh, "ident"], writes=[ktp])
                for j in range(2):
                    k = 2 * kp + j
                    op("act", lambda e: e.activation(out=hT[:, k, :], in_=tp[:, j, :], func=AF.Identity,
                                                     scale=modc[:, strm, jsc, k:k + 1], bias=modc[:, strm, jsh, k:k + 1]),
                       reads=["modc"], writes=[ktp, khT])

        for l in range(DEPTH):
            last = (l == DEPTH - 1)
            with ExitStack() as el:
                with ExitStack() as ep:
                    ep.enter_context(nc.named_scope("P0_l%d" % l))
                    cct = sb(ep, "cct", [128, 8, 2])
                    sct = sb(ep, "sct", [128, 8, 2])
                    scbc = sb(ep, "scbc", [128, 8, 128])
                    bmbc = sb(ep, "bmbc", [128, 6 * D])
                    modbc = sb(ep, "modbc", [128, 6 * D])
                    wm = [sb(ep, "wm%d" % i, [128, 3072]) for i in range(3)]
                    pmod = [ps(ep, "pmod%d" % i, [128, 512]) for i in range(6)]
                    for s in range(2):
                        dma("sp", cct[:, :, s], cc_d[s, :].rearrange("(k p) -> p k", p=128), writes=["cct"])
                    dma("sp", bmbc[:], bmod_d[l:l + 1, :].partition_broadcast(128), writes=["bmbc"])
                    op("act", lambda e: e.activation(out=sct[:], in_=cct[:], func=AF.Tanh, scale=0.5), reads=["cct"], writes=["sct"])
                    op("dve", lambda e: e.tensor_scalar(out=sct[:], in0=sct[:], scalar1=0.5, scalar2=0.5, op0=ALU.mult, op1=ALU.add), writes=["sct"])
                    op("dve", lambda e: e.tensor_tensor(out=sct[:], in0=sct[:], in1=cct[:], op=ALU.mult), reads=["cct"], writes=["sct"])
                    for k in range(8):
                        for s in range(2):
                            op("dve", lambda e: e.tensor_scalar(out=scbc[:, k, 64 * s:64 * s + 64], in0=ones_f[:, 0:64],
                                                                scalar1=sct[:, k, s:s + 1], scalar2=None, op0=ALU.mult),
                               reads=["sct", "ones_f"], writes=["scbc"])
                    ld = 0
                    for half in range(2):
                        for k in range(8):
                            w = wm[ld % 3]
                            kw = "wm%d" % (ld % 3)
                            ld += 1
                            dma("sp", w[:], wmod_d[l, k * 128:(k + 1) * 128, half * 3072:(half + 1) * 3072], writes=[kw])
                            for n in range(6):
                                op("pe", lambda e: e.matmul(pmod[n][:], lhsT=scbc[:, k, :], rhs=w[:, n * 512:(n + 1) * 512],
                                                            start=(k == 0), stop=(k == 7)),
                                   reads=["scbc", kw], writes=["pmod%d" % n])
                        for n in range(6):
                            c0 = half * 3072 + n * 512
                            op("dve", lambda e: e.tensor_tensor(out=modbc[:, c0:c0 + 512], in0=pmod[n][:], in1=bmbc[:, c0:c0 + 512],
                                                                op=ALU.add), reads=["bmbc"], writes=["pmod%d" % n, "modbc"])
                    dma("sp", MODS[0:1, :], modbc[0:1, :], reads=["modbc"], writes=["MODS"])
                    dma("sp", MODS[1:2, :], modbc[64:65, :], reads=["modbc"], writes=["MODS"])
                    fw.barrier()

                modc = sb(el, "modc", [128, 2, 6, 8])
                gmixc = sb(el, "gmixc", [128, 8])
                for s in range(2):
                    for j in range(6):
                        dma("sp", modc[:, s, j, :], MODS[s, j * D:(j + 1) * D].rearrange("(k p) -> p k", p=128), reads=["MODS"], writes=["modc"])
                dma("sp", gmixc[:], gmix_d[l, :].rearrange("(k p) -> p k", p=128), writes=["gmixc"])
                for j in (1, 4):
                    op("dve", lambda e: e.tensor_scalar(out=modc[:, :, j, :], in0=modc[:, :, j, :], scalar1=1.0, scalar2=None,
                                                        op0=ALU.add), writes=["modc"])

                with ExitStack() as ez:
                    zT = sb(ez, "zT", [128, 2, T], BF16)
                    zTc = sb(ez, "zTc", [128, 2, TC], BF16)
                    with ExitStack() as e1:
                        e1.enter_context(nc.named_scope("M1_l%d" % l))
                        win = sb(e1, "win", [128, 8, DIN], BF16)
                        for k in range(8):
                            dma("pool", win[:, k, :], win_d[l, k * 128:(k + 1) * 128, :], writes=["win"])
                        wsT = sb(e1, "wsT", [128, 4, 128], BF16)
                        wsr = sb(e1, "wsr", [128, 4, 128])
                        bsT = sb(e1, "bsT", [128, 4])
                        dma("sp", wsr[:], ws_d[l].rearrange("h p q -> p h q"), writes=["wsr"])
                        dma("sp", bsT[:], sgb_d[l].rearrange("h p -> p h"), writes=["bsT"])
                        xt = [sb(e1, "xt%d" % i, [128, 2, D]) for i in range(2)]
                        pt = [sb(e1, "pt%d" % i, [128, 2, D]) for i in range(2)] if l == 0 else [None, None]
                        xh = sb(e1, "xh", [128, 2, D])
                        hT = [sb(e1, "hT%d" % i, [128, 8, TL], BF16) for i in range(2)]
                        lns = ln_scr(e1, "l1")
                        xaS = [sb(e1, "xaS%d" % i, [128, 4, TL]) for i in range(2)]
                        ggS = [sb(e1, "ggS%d" % i, [128, 4, TL]) for i in range(2)]
                        yBT = [sb(e1, "yBT%d" % i, [128, 2, TL], BF16) for i in range(2)]
                        zS = [sb(e1, "zS%d" % i, [128, 2, TL], BF16) for i in range(2)]
                        zB = [sb(e1, "zB%d" % i, [128, 512]) for i in range(2)]
                        yb = [sb(e1, "yb%d" % i, [128, 256]) for i in range(2)]
                        vh = sb(e1, "vh", [128, 256], BF16)
                        bst = sb(e1, "bst", [128, 4, 6])
                        bmv = sb(e1, "bmv", [128, 4, 2])
                        bve = sb(e1, "bve", [128, 4])
                        brs = sb(e1, "brs", [128, 4])
                        btmp = sb(e1, "btmp", [128, 4])
                        ssB = sb(e1, "ssB", [128, 2])
                        rB = sb(e1, "rB", [128, 2])
                        rsB = sb(e1, "rsB", [128, 2])
                        rtmp = sb(e1, "rtmp", [128, 2])
                        junk = sb(e1, "junk", [128, 256])
                        tp0 = ps(e1, "tp0", [128, 2, TL]); tp1 = ps(e1, "tp1", [128, 2, TL])
                        tps = [(tp0, "tp0"), (tp1, "tp1")]
                        pa = [ps(e1, "pa%d" % i, [128, 2, TL]) for i in range(2)]
                        pb = [ps(e1, "pb%d" % i, [128, 512]) for i in range(2)]
                        pss = ps(e1, "pss", [128, 512])
                        pyt = ps(e1, "pyt", [128, 2, TL])
                        for h in range(4):
                            op("pe", lambda e: e.transpose(out=pb[0][:, h * 128:(h + 1) * 128], in_=wsr[:, h, :], identity=ident[:]),
                               reads=["wsr", "ident"], writes=["pb0"])
                        op("dve", lambda e: e.tensor_copy(out=wsT[:].rearrange("p h q -> p (h q)"), in_=pb[0][:]), writes=["pb0", "wsT"])

                        tl = LTILES
                        nM = len(tl)
                        zBt = [[sb(e1, "zBt%d_%d" % (i, s_), [128, 512]) for s_ in range(2)] for i in range(2)]

                        def m1_load(ti):
                            nb = ti % 2
                            load_resid(l, tl[ti][0], tl[ti][1], xt[nb], "xt%d" % nb, pt[nb], "pt%d" % nb, save_xb=True)

                        def m1_A(ti):
                            b = ti % 2
                            ln_tile(xt[b], "xt%d" % b, xh, "xh", lns, EPS, eng="dve", iters=2)

                        def m1_T(ti):
                            col0, is_ctx = tl[ti]
                            b = ti % 2
                            transpose_mod(xh, "xh", hT[b], "hT%d" % b, tps, modc, 1 if is_ctx else 0, 1, 0)

                        def m1_P(ti):
                            col0, is_ctx = tl[ti]
                            b = ti % 2
                            khT = "hT%d" % b
                            pai = 0
                            only_xa = last and is_ctx
                            for grp in range(2 if only_xa else 4):
                                p_, kp_ = pa[pai % 2], "pa%d" % (pai % 2)
                                pai += 1
                                for j in range(2):
                                    oc = grp * 2 + j
                                    for k in range(8):
                                        op("pe", lambda e: e.matmul(p_[:, j, :], lhsT=win[:, k, oc * 128:(oc + 1) * 128], rhs=hT[b][:, k, :],
                                                                    start=(k == 0), stop=(k == 7)), reads=["win", khT], writes=[kp_])
                                if grp < 2:
                                    op("act", lambda e: e.activation(out=xaS[b][:, 2 * grp:2 * grp + 2, :], in_=p_[:], func=AF.Identity),
                                       writes=[kp_, "xaS%d" % b])
                                else:
                                    g2 = grp - 2
                                    op("act", lambda e: e.activation(out=ggS[b][:, 2 * g2:2 * g2 + 2, :], in_=p_[:], func=AF.Gelu),
                                       writes=[kp_, "ggS%d" % b])
                            dma("sp", XACv[:, :, :] if is_ctx else XALv[:, :, col0:col0 + TL], xaS[b][:], reads=["xaS%d" % b], writes=["XA"])
                            if only_xa:
                                return
                            dma("sp", GGCv[:, :, :] if is_ctx else GGLv[:, :, col0:col0 + TL], ggS[b][:], reads=["ggS%d" % b], writes=["GG"])
                            p_, kp_ = pa[pai % 2], "pa%d" % (pai % 2)
                            for j in range(2):
                                for k in range(8):
                                    op("pe", lambda e: e.matmul(p_[:, j, :], lhsT=win[:, k, 1536 + j * 128:1536 + (j + 1) * 128], rhs=hT[b][:, k, :],
                                                                start=(k == 0), stop=(k == 7)), reads=["win", khT], writes=[kp_])
                            if is_ctx:
                                op("act", lambda e: e.activation(out=zTc[:, :, :], in_=p_[:], func=AF.Identity), writes=[kp_, "zTc"])
                            else:
                                op("act", lambda e: e.activation(out=zS[b][:], in_=p_[:], func=AF.Identity), writes=[kp_, "zS%d" % b])
                                dma("sp", ZLv[:, :, col0:col0 + TL], zS[b][:], reads=["zS%d" % b], writes=["ZL"])

                        def m1_Bproj(ti):
                            col0, is_ctx = tl[ti]
                            if last and is_ctx:
                                return
                            b = ti % 2
                            for s in range(2):
                                p_, kp_ = pb[s], "pb%d" % s
                                for k in range(8):
                                    op("pe", lambda e: e.matmul(p_[:], lhsT=hT[b][:, k, s * 128:(s + 1) * 128], rhs=win[:, k, 1024:1536],
                                                                start=(k == 0), stop=(k == 7)), reads=["win", "hT%d" % b], writes=[kp_])
                                op("act", lambda e: e.activation(out=zBt[b][s][:], in_=p_[:], func=AF.Gelu), writes=[kp_, "zBt%d_%d" % (b, s)])

                        bmv8 = sb(e1, "bmv8", [128, 8, 2])
                        bve8 = sb(e1, "bve8", [128, 8])
                        brs8 = sb(e1, "brs8", [128, 8])
                        btmp8 = sb(e1, "btmp8", [128, 8])
                        bst8 = sb(e1, "bst8", [128, 8, 6])
                        vh2 = sb(e1, "vh2", [128, 2, 256], BF16)
                        ybp = [[sb(e1, "ybp%d_%d" % (i, s_), [128, 256]) for s_ in range(2)] for i in range(2)]
                        ssBp = [sb(e1, "ssBp%d" % i, [128, 2]) for i in range(2)]

                        def skipB(ti):
                            return last and tl[ti][1]

                        def m1_B1a(ti):
                            if skipB(ti):
                                return
                            b = ti % 2
                            for s in range(2):
                                zb, kzb = zBt[b][s], "zBt%d_%d" % (b, s)
                                for h in range(4):
                                    op("dve", lambda e: e.bn_stats(out=bst8[:, 4 * s + h, :], in_=zb[:, 256 + 64 * h:256 + 64 * h + 64]),
                                       reads=[kzb], writes=["bst8_%d" % (4 * s + h)])
                                for h in range(4):
                                    op("dve", lambda e: e.bn_aggr(out=bmv8[:, 4 * s + h, :], in_=bst8[:, 4 * s + h, :]),
                                       reads=["bst8_%d" % (4 * s + h)], writes=["bmv8"])
                            op("dve", lambda e: e.tensor_scalar(out=bve8[:], in0=bmv8[:, :, 1], scalar1=EPS, scalar2=None, op0=ALU.add),
                               reads=["bmv8"], writes=["bve8"])
                            rsqrt(brs8[:], bve8[:], btmp8[:], "brs8", "bve8", "btmp8", iters=2)
                            for s in range(2):
                                zb, kzb = zBt[b][s], "zBt%d_%d" % (b, s)
                                for h in range(4):
                                    op("dve", lambda e: e.tensor_scalar(out=vh2[:, s, 64 * h:64 * h + 64], in0=zb[:, 256 + 64 * h:256 + 64 * h + 64],
                                                                        scalar1=bmv8[:, 4 * s + h, 0:1], scalar2=brs8[:, 4 * s + h:4 * s + h + 1],
                                                                        op0=ALU.subtract, op1=ALU.mult),
                                       reads=[kzb, "bmv8", "brs8"], writes=["vh2_%d" % (4 * s + h)])

                        def m1_smm(ti):
                            if skipB(ti):
                                return
                            for s in range(2):
                                for h in range(4):
                                    op("pe", lambda e: e.matmul(pss[:, 256 * s + 64 * h:256 * s + 64 * h + 64], lhsT=wsT[:, h, :], rhs=vh2[:, s, 64 * h:64 * h + 64],
                                                                start=True, stop=True), reads=["wsT", "vh2_%d" % (4 * s + h)], writes=["pss"])

                        def m1_B1b(ti):
                            if skipB(ti):
                                return
                            b = ti % 2
                            for s in range(2):
                                zb, kzb = zBt[b][s], "zBt%d_%d" % (b, s)
                                for h in range(4):
                                    op("dve", lambda e: e.scalar_tensor_tensor(out=ybp[b][s][:, 64 * h:64 * h + 64], in0=pss[:, 256 * s + 64 * h:256 * s + 64 * h + 64],
                                                                               scalar=bsT[:, h:h + 1], in1=zb[:, 64 * h:64 * h + 64],
                                                                               op0=ALU.add, op1=ALU.mult),
                                       reads=["bsT", kzb], writes=["pss", "ybp%d_%d_%d" % (b, s, h)])
                            for s in range(2):
                                op("act", lambda e: e.activation(out=junk[:], in_=ybp[b][s][:], func=AF.Square, accum_out=ssBp[b][:, s:s + 1]),
                                   reads=["ybp%d_%d_%d" % (b, s, h) for h in range(4)], writes=["junk", "ssBp%d" % b])

                        def m1_B2(ti):
                            if skipB(ti):
                                return
                            col0, is_ctx = tl[ti]
                            b = ti % 2
                            op("dve", lambda e: e.tensor_scalar(out=rB[:], in0=ssBp[b][:], scalar1=1.0 / DB, scalar2=EPS, op0=ALU.mult, op1=ALU.add),
                               reads=["ssBp%d" % b], writes=["rB"])
                            rsqrt(rsB[:], rB[:], rtmp[:], "rsB", "rB", "rtmp", iters=2)
                            for s in range(2):
                                kyb = ["ybp%d_%d_%d" % (b, s, h) for h in range(4)]
                                op("dve", lambda e: e.tensor_scalar(out=ybp[b][s][:], in0=ybp[b][s][:], scalar1=rsB[:, s:s + 1], scalar2=None, op0=ALU.mult),
                                   reads=["rsB"], writes=kyb)
                                for c in range(2):
                                    op("pe", lambda e: e.transpose(out=pyt[:, c, s * 128:(s + 1) * 128], in_=ybp[b][s][:, c * 128:(c + 1) * 128],
                                                                   identity=ident[:]), reads=kyb + ["ident"], writes=["pyt"])
                            for c in range(2):
                                op("act", lambda e: e.activation(out=yBT[b][:, c, :], in_=pyt[:, c, :], func=AF.Identity, scale=gmixc[:, 4 + c:5 + c]),
                                   reads=["gmixc"], writes=["pyt", "yBT%d" % b])
                            dma("sp", YBLv[:, :, col0:col0 + TL], yBT[b][:], reads=["yBT%d" % b], writes=["YBL"])

                        m1_load(0)
                        if nM > 1:
                            m1_load(1)
                        m1_A(0)
                        m1_T(0)
                        for ti in range(nM + 2):
                            if ti < nM:
                                m1_P(ti)
                            if ti + 1 < nM:
                                m1_A(ti + 1)
                            if 2 <= ti:
                                m1_B2(ti - 2)
                            if ti + 1 < nM:
                                m1_T(ti + 1)
                            if 1 <= ti <= nM:
                                m1_B1a(ti - 1)
                            if ti < nM:
                                m1_Bproj(ti)
                            if 1 <= ti <= nM:
                                m1_smm(ti - 1)
                                m1_B1b(ti - 1)
                            if ti + 2 < nM:
                                m1_load(ti + 2)
                        fw.barrier()
                    fw.collective("AllGather", ZL[:, :], ZF[:, :], PAIRS, ["ZL"], ["ZF"])
                    fw.barrier()
                    for c_ in range(4):
                        fw.collective("AllGather", XAL[c_ * 128:(c_ + 1) * 128, :], XAF[c_ * 256:(c_ + 1) * 256, :], PAIRS, ["XA"], ["XAF"])
                    for c_ in range(4):
                        fw.collective("AllGather", GGL[c_ * 128:(c_ + 1) * 128, :], GGF[c_ * 256:(c_ + 1) * 256, :], PAIRS, ["GG"], ["GGF"])
                    if stop_after == "M1" and l == 0:
                        xafd = nc.dram_tensor("XAFd", [2 * DA, TLOC], F32, kind="ExternalOutput").ap()
                        zfd = nc.dram_tensor("ZFd", [2 * DC, TLOC], BF16, kind="ExternalOutput").ap()
                        dma("sp", xafd[:, :], XAF[:, :], writes=["xafd"])
                        dma("sp", zfd[:, :], ZF[:, :], writes=["zfd"])
                        fw.wait_all("sp")
                        return nc

                    if True:
                        with ExitStack() as e2:
                            e2.enter_context(nc.named_scope("DFT_l%d" % l))
                            t1 = sb(e2, "t1", [128, 3, 128], BF16)
                            tw = sb(e2, "tw", [128, 128, 64], BF16)
                            c256 = sb(e2, "c256", [128, 2, 2, 256], BF16)
                            cspad = sb(e2, "cspad", [64, 2, 2, 128])
                            wft = sb(e2, "wft", [64, 4, 64])
                            abd = sb(e2, "abd", [128, 2, 256], BF16)
                            abdc = sb(e2, "abdc", [128, 2, 256], BF16)
                            XTs = sb(e2, "XTs", [128, 2, T])
                            XTc = sb(e2, "XTc", [128, 2, TC])
                            Yp = [sb(e2, "Yp%d" % i, [128, 512], BF16) for i in range(2)]
                            Gs = [sb(e2, "Gs%d" % i, [128, 2, 8, 256], BF16) for i in range(2)]
                            Rb = [sb(e2, "Rb%d" % i, [128, 8, 256], BF16) for i in range(2)]
                            sq = sb(e2, "sq", [128, 2, TL])
                            rr = sb(e2, "rr", [128, TL])
                            rrs = sb(e2, "rrs", [128, TL])
                            rrt = sb(e2, "rrt", [128, TL])
                            yCT = [sb(e2, "yCT%d" % i, [128, 2, TL], BF16) for i in range(2)]
                            pY = [ps(e2, "pY%d" % i, [128, 512]) for i in range(2)]
                            pG = [ps(e2, "pG%d" % i, [128, 2, 256]) for i in range(2)]
                            pX = [ps(e2, "pX%d" % i, [128, 8, 64]) for i in range(2)]
                            pS = ps(e2, "pS", [128, 512])
                            pS1 = ps(e2, "pS1", [128, 512])
                            for r_ in range(2):
                                dma("sp", zT[:, :, r_ * TLOC:(r_ + 1) * TLOC], ZFv[r_], writes=["zT"])
                            dma("sp", t1[:], t1_d[:, :, :], writes=["t1"])
                            dma("sp", tw[:].rearrange("p a b -> p (a b)"), tw_d[:, :], writes=["tw"])
                            dma("sp", c256[:], c256_d[:, :, :, :], writes=["c256"])
                            dma("sp", cspad[:], cspad_d[:, :, :, :], writes=["cspad"])
                            dma("sp", wft[:], wf_d[l].rearrange("g j e -> j g e"), writes=["wft"])
                            for cc in range(2):
                                for pq in range(2):
                                    for gl in range(2):
                                        op("pe", lambda e: e.matmul(pY[0][:, pq * 128 + gl * 64:pq * 128 + gl * 64 + 64],
                                                                    lhsT=cspad[:, pq, gl, :], rhs=wft[:, 2 * cc + gl, :], start=True, stop=True),
                                           reads=["cspad", "wft"], writes=["pY0"])
                                op("act", lambda e: e.activation(out=abd[:, cc, :], in_=pY[0][:, 0:256], func=AF.Identity,
                                                                 scale=1.0 / math.sqrt(T * 64.0)), writes=["pY0", "abd"])
                                op("act", lambda e: e.activation(out=abdc[:, cc, :], in_=pY[0][:, 0:256], func=AF.Identity,
                                                                 scale=1.0 / math.sqrt(TC * 64.0)), writes=["pY0", "abdc"])

                            rr2 = [sb(e2, "rr2_%d" % i, [128, TL]) for i in range(2)]
                            rrs2 = [sb(e2, "rrs2_%d" % i, [128, TL]) for i in range(2)]
                            rrt2 = [sb(e2, "rrt2_%d" % i, [128, TL]) for i in range(2)]

                            def rmsc1(src, b):
                                op("act", lambda e: e.activation(out=sq[:], in_=src, func=AF.Square), reads=["XT"], writes=["sq"])
                                pS_ = pS if b == 0 else pS1
                                for c in range(2):
                                    op("pe", lambda e: e.matmul(pS_[:, 0:TL], lhsT=ones_f[:], rhs=sq[:, c, :], start=(c == 0), stop=(c == 1)),
                                       reads=["ones_f", "sq"], writes=["pS%d" % b])

                            def rmsc2(b):
                                pS_ = pS if b == 0 else pS1
                                op("act", lambda e: e.activation(out=rr2[b][:], in_=pS_[:, 0:TL], func=AF.Ln, scale=1.0 / DC, bias=EPS),
                                   writes=["pS%d" % b, "rr2_%d" % b])
                                op("act", lambda e: e.activation(out=rrs2[b][:], in_=rr2[b][:], func=AF.Exp, scale=-0.5),
                                   reads=["rr2_%d" % b], writes=["rrs2_%d" % b])

                            def rmsc3(src, col0, b):
                                for c in range(2):
                                    op("dve", lambda e: e.scalar_tensor_tensor(out=yCT[b][:, c, :], in0=src[:, c, :], scalar=gmixc[:, 6 + c:7 + c],
                                                                               in1=rrs2[b][:], op0=ALU.mult, op1=ALU.mult),
                                       reads=["XT", "gmixc", "rrs2_%d" % b], writes=["yCT%d_%d" % (b, c)])
                                dma("sp", YTv[:, 6:8, col0:col0 + TL], yCT[b][:], reads=["yCT%d_0" % b, "yCT%d_1" % b], writes=["YT"])

                            def rms_store_c(src, col0, b):
                                rmsc1(src, b)
                                rmsc2(b)
                                rmsc3(src, col0, b)

                            if not last:
                                Ypc = [sb(e2, "Ypc%d" % i, [128, 512], BF16) for i in range(2)]
                                for t in range(2):
                                    for cc in range(2):
                                        op("pe", lambda e: e.matmul(pY[t][:, cc * 256:(cc + 1) * 256], lhsT=zTc[:, cc, t * 128:(t + 1) * 128],
                                                                    rhs=abdc[:, cc, :], start=True, stop=True), reads=["zTc", "abdc"], writes=["pY%d" % t])
                                    op("act", lambda e: e.activation(out=Ypc[t][:], in_=pY[t][:], func=AF.Identity), writes=["pY%d" % t, "Ypc%d" % t])
                                for cc in range(2):
                                    n = 0
                                    for t in range(2):
                                        for pq in range(2):
                                            op("pe", lambda e: e.matmul(pG[cc][:].rearrange("p a b -> p (a b)")[:, 0:256],
                                                                        lhsT=Ypc[t][:, cc * 256 + pq * 128:cc * 256 + (pq + 1) * 128],
                                                                        rhs=c256[:, t, pq, :], start=(n == 0), stop=(n == 3)),
                                               reads=["Ypc%d" % t, "c256"], writes=["pG%d" % cc])
                                            n += 1
                                    op("dve", lambda e: e.tensor_copy(out=XTc[:, cc, :], in_=pG[cc][:].rearrange("p a b -> p (a b)")[:, 0:256]),
                                       writes=["pG%d" % cc, "XT"])
                                rms_store_c(XTc[:, :, :], T, 0)

                            zTv = zT[:].rearrange("p c (a b) -> p c b a", b=64)
                            GDv = GD
                            for l2 in range(64):
                                b = l2 % 2
                                for cc in range(2):
                                    op("pe", lambda e: e.matmul(pY[b][:, cc * 256:(cc + 1) * 256], lhsT=zTv[:, cc, l2, :], rhs=abd[:, cc, :],
                                                                start=True, stop=True), reads=["zT", "abd"], writes=["pY%d" % b])
                                op("act", lambda e: e.activation(out=Yp[b][:], in_=pY[b][:], func=AF.Identity), writes=["pY%d" % b, "Yp%d" % b])
                                Ypv = Yp[b][:].rearrange("p (c q j) -> p q c j", c=2, q=2)
                                combos = [(0, 0, 0), (0, 1, 1), (1, 0, 1), (1, 1, 2)]
                                for (ri, pq, ti_) in combos:
                                    op("pe", lambda e: e.matmul(pG[b][:, ri, :].rearrange("p (c j) -> p c j", c=2), lhsT=t1[:, ti_, :], rhs=Ypv[:, pq, :, :],
                                                                start=(pq == 0), stop=(pq == 1)), reads=["t1", "Yp%d" % b], writes=["pG%d" % b])
                                gb = (l2 // 8) % 2
                                op("dve", lambda e: e.tensor_copy(out=Gs[gb][:, :, l2 % 8, :], in_=pG[b][:]), writes=["pG%d" % b, "Gs%d" % gb])
                                if l2 % 8 == 7:
                                    l0 = l2 - 7
                                    dma("sp", GDv[:, :, l0:l0 + 8, :], Gs[gb][:], reads=["Gs%d" % gb], writes=["GD"])
                            XTv = XTs[:].rearrange("p c (k2 k1) -> p c k1 k2", k1=128)
                            for kb in range(16):
                                b = kb % 2
                                dma("sp", Rb[b][:], GD[kb * 8:(kb + 1) * 8, :, :, :].rearrange("k r l c -> (r l) k c"), reads=["GD"], writes=["Rb%d" % b])
                                for cc in range(2):
                                    px, kpx = pX[cc], "pX%d" % cc
                                    for r in range(8):
                                        op("pe", lambda e: e.matmul(px[:, r, :], lhsT=Rb[b][:, r, cc * 128:(cc + 1) * 128], rhs=tw[:, kb * 8 + r, :],
                                                                    start=True, stop=True), reads=["Rb%d" % b, "tw"], writes=[kpx])
                                    op("act" if cc == 0 else "dve",
                                       (lambda e: e.activation(out=XTv[:, cc, kb * 8:(kb + 1) * 8, :], in_=px[:], func=AF.Identity)) if cc == 0 else
                                       (lambda e: e.tensor_copy(out=XTv[:, cc, kb * 8:(kb + 1) * 8, :], in_=px[:])),
                                       writes=[kpx, "XT"])
                            for ti in range(NT + 2):
                                if ti < NT:
                                    rmsc1(XTs[:, :, ti * TL:(ti + 1) * TL], ti % 2)
                                if 0 <= ti - 1 < NT:
                                    rmsc2((ti - 1) % 2)
                                if 0 <= ti - 2 < NT:
                                    t2 = ti - 2
                                    rmsc3(XTs[:, :, t2 * TL:(t2 + 1) * TL], t2 * TL, t2 % 2)
                            fw.barrier()
                if stop_after == "DFT" and l == 0:
                    return nc

                with ExitStack() as e3:
                    wg = sb(e3, "wg", [128, 2, 2, 4, 128], BF16)
                    cw = sb(e3, "cw", [128, 4, 4])
                    cb = sb(e3, "cb", [128, 4])
                    gb_ = sb(e3, "gbias", [128, 2, 2, 4])
                    lam = sb(e3, "lam", [128, 2, 4])
                    hnsp = sb(e3, "hnsp", [128, 2, 4])
                    nsp = sb(e3, "nsp", [128, 2, 4])
                    op("pool", lambda e: e.memset(wg[:], 0.0), writes=["wg"])
                    for ax, wd in enumerate((wa_d, wx_d)):
                        for h2 in range(2):
                            for d_ in range(2):
                                src = wd[l, d_].rearrange("(c h) i e -> h i c e", h=2)[h2]
                                dma("pool", wg[64 * h2:64 * h2 + 64, d_, ax, :, 64 * h2:64 * h2 + 64], src, writes=["wg"])
                    for k in range(4):
                        dma("sp", cw[:, :, k], convw_d[l, k, :].rearrange("(c p) -> p c", p=128), writes=["cw"])
                    dma("sp", cb[:], convb_d[l].rearrange("(c p) -> p c", p=128), writes=["cb"])
                    for d_ in range(2):
                        dma("sp", gb_[:, 0, d_, :], ba_d[l, d_, :].rearrange("(c p) -> p c", p=128), writes=["gbias"])
                        dma("sp", gb_[:, 1, d_, :], bx_d[l, d_, :].rearrange("(c p) -> p c", p=128), writes=["gbias"])
                        dma("sp", lam[:, d_, :], lam_d[l, d_, :].rearrange("(c p) -> p c", p=128), writes=["lam"])
                    op("dve", lambda e: e.tensor_scalar(out=gb_[:], in0=gb_[:], scalar1=0.5, scalar2=None, op0=ALU.mult), writes=["gbias"])
                    op("act", lambda e: e.activation(out=lam[:], in_=lam[:], func=AF.Exp, scale=-1.0), writes=["lam"])
                    op("act", lambda e: e.activation(out=lam[:], in_=lam[:], func=AF.Ln, bias=1.0), writes=["lam"])
                    op("dve", lambda e: e.tensor_scalar(out=hnsp[:], in0=lam[:], scalar1=-4.0, scalar2=None, op0=ALU.mult), reads=["lam"], writes=["hnsp"])
                    op("dve", lambda e: e.tensor_scalar(out=nsp[:], in0=lam[:], scalar1=-8.0, scalar2=None, op0=ALU.mult), reads=["lam"], writes=["nsp"])

                    xaH = [sb(e3, "xaH%d" % i, [128, 4, TL + 3]) for i in range(2)]
                    xc = [sb(e3, "xc%d" % i, [128, 4, TL]) for i in range(2)]
                    xcb = [sb(e3, "xcb%d" % i, [128, 4, TL], BF16) for i in range(2)]
                    hS = [sb(e3, "hS%d" % i, [128, 4, TL]) for i in range(2)]
                    hfL = [sb(e3, "hfL%d" % i, [128, 4, TL]) for i in range(2)]
                    ggL = [sb(e3, "ggL%d" % i, [128, 4, TL]) for i in range(2)]
                    tr = sb(e3, "tr", [128, 4, TL])
                    tiS = [sb(e3, "tiS%d" % i, [128, 4, TL]) for i in range(2)]
                    aS = [sb(e3, "aS%d" % i, [128, 4, TL]) for i in range(2)]
                    mS = [sb(e3, "mS%d" % i, [128, 4, TL]) for i in range(2)]
                    uS = sb(e3, "uS", [128, 4, TL])
                    ya = [sb(e3, "ya%d" % i, [128, 4, TL]) for i in range(3)]
                    sq = sb(e3, "sqa", [128, 4, TL])
                    rr = [sb(e3, "rra%d" % i, [128, TL]) for i in range(2)]
                    rrs = [sb(e3, "rrsa%d" % i, [128, TL]) for i in range(2)]
                    rrt = [sb(e3, "rrta%d" % i, [128, TL]) for i in range(2)]
                    yAT = [sb(e3, "yAT%d" % i, [128, 4, TL], BF16) for i in range(2)]
                    pg = [ps(e3, "pg%d" % i, [128, 2, TL]) for i in range(4)]
                    pSs = [ps(e3, "pSa%d" % i, [128, 512]) for i in range(2)]

                    def xck(b):
                        return ["xc%d_%d" % (b, c) for c in range(4)]

                    def gates(d, b):
                        for c in range(4):
                            for ax in range(2):
                                op("pe", lambda e: e.matmul(pg[c][:, ax, :], lhsT=wg[:, d, ax, c, :], rhs=xcb[b][:, c, :], start=True, stop=True),
                                   reads=["wg", "xcb%d" % b], writes=["pg%d" % c])
                        for c in range(4):
                            op("act", lambda e: e.activation(out=tr[:, c, :], in_=pg[c][:, 0, :], func=AF.Tanh, scale=0.5, bias=gb_[:, 0, d, c:c + 1]),
                               reads=["gbias"], writes=["pg%d" % c, "tr"])
                            op("act", lambda e: e.activation(out=tiS[b][:, c, :], in_=pg[c][:, 1, :], func=AF.Tanh, scale=0.5, bias=gb_[:, 1, d, c:c + 1]),
                               reads=["gbias"], writes=["pg%d" % c, "tiS%d" % b])
                        for c in range(4):
                            op("act", lambda e: e.activation(out=aS[b][:, c, :], in_=tr[:, c, :], func=AF.Exp, scale=hnsp[:, d, c:c + 1], bias=hnsp[:, d, c:c + 1]),
                               reads=["tr", "hnsp"], writes=["aS%d" % b])
                            op("act", lambda e: e.activation(out=mS[b][:, c, :], in_=tr[:, c, :], func=AF.Exp, scale=nsp[:, d, c:c + 1], bias=nsp[:, d, c:c + 1]),
                               reads=["tr", "nsp"], writes=["mS%d" % b])
                        op("act", lambda e: e.activation(out=mS[b][:], in_=mS[b][:], func=AF.Sqrt, scale=-0.25, bias=0.25), writes=["mS%d" % b])

                    def make_u(b):
                        op("dve", lambda e: e.scalar_tensor_tensor(out=uS[:], in0=tiS[b][:], scalar=1.0, in1=xc[b][:], op0=ALU.add, op1=ALU.mult),
                           reads=["tiS%d" % b] + xck(b), writes=["uS"])
                        op("dve", lambda e: e.tensor_tensor(out=uS[:], in0=uS[:], in1=mS[b][:], op=ALU.mult), reads=["mS%d" % b], writes=["uS"])

                    scope_ = nc.named_scope("M2a_l%d" % l)
                    scope_.__enter__()

                    def load_xa(ti, b):
                        col0, is_ctx = TILES[ti]
                        first = is_ctx or col0 == 0
                        lastt = is_ctx or col0 == T - TL
                        lo = 0 if first else 2
                        hi = 0 if lastt else 1
                        if first:
                            op("pool", lambda e: e.memset(xaH[b][:, :, 0:2], 0.0), writes=["xaH%d" % b])
                        if lastt:
                            op("pool", lambda e: e.memset(xaH[b][:, :, TL + 2:TL + 3], 0.0), writes=["xaH%d" % b])
                        if is_ctx:
                            dma("sp", xaH[b][:, :, 2:TL + 2], XACv[:, :, :], writes=["xaH%d" % b])
                        else:
                            g0, g1 = col0 - lo, col0 + TL + hi
                            d0 = 2 - lo
                            for r_ in range(2):
                                a0, a1 = max(g0, r_ * TLOC), min(g1, (r_ + 1) * TLOC)
                                if a1 > a0:
                                    dma("sp", xaH[b][:, :, d0 + a0 - g0:d0 + a1 - g0], XAFv[r_][:, :, a0 - r_ * TLOC:a1 - r_ * TLOC],
                                        reads=["XAF"], writes=["xaH%d" % b])

                    def f0(ti):
                        col0, is_ctx = TILES[ti]
                        b = ti % 2
                        for c in range(4):
                            op("dve", lambda e: e.tensor_scalar(out=xc[b][:, c, :], in0=xaH[b][:, c, 0:TL], scalar1=cw[:, c, 0:1], scalar2=cb[:, c:c + 1],
                                                                op0=ALU.mult, op1=ALU.add), reads=["xaH%d" % b, "cw", "cb"], writes=["xc%d_%d" % (b, c)])
                        for k in range(1, 4):
                            for c in range(4):
                                op("dve", lambda e: e.scalar_tensor_tensor(out=xc[b][:, c, :], in0=xaH[b][:, c, k:k + TL], scalar=cw[:, c, k:k + 1],
                                                                           in1=xc[b][:, c, :], op0=ALU.mult, op1=ALU.add),
                                   reads=["xaH%d" % b, "cw"], writes=["xc%d_%d" % (b, c)])
                        op("act", lambda e: e.activation(out=xcb[b][:], in_=xc[b][:], func=AF.Identity), reads=xck(b), writes=["xcb%d" % b])
                        dma("sp", XCv[:, :, col0:col0 + TL], xc[b][:], reads=xck(b), writes=["XC"])
                        gates(0, b)

                    def f1(ti):
                        col0, is_ctx = TILES[ti]
                        b = ti % 2
                        pb_ = (ti - 1) % 2
                        make_u(b)
                        for c in range(4):
                            init = 0.0 if ti == 0 else hS[pb_][:, c, TL - 1:TL]
                            op("dve", lambda e: e.tensor_tensor_scan(out=hS[b][:, c, :], data0=aS[b][:, c, :], data1=uS[:, c, :], initial=init,
                                                                     op0=ALU.mult, op1=ALU.add),
                               reads=["aS%d" % b, "uS"] + ([] if ti == 0 else ["hS%d" % pb_]), writes=["hS%d" % b])
                        if not (last and is_ctx):
                            dma("sp", HFv[:, :, col0:col0 + TL], hS[b][:], reads=["hS%d" % b], writes=["HF"])

                    nT = len(TILES)
                    load_xa(0, 0)
                    load_xa(1, 1)
                    f0(0)
                    for ti in range(nT):
                        if ti + 2 < nT:
                            load_xa(ti + 2, ti % 2)
                        if ti + 1 < nT:
                            f0(ti + 1)
                        f1(ti)
                    fw.barrier()
                    scope_.__exit__(None, None, None)
                    if stop_after == "M2a" and l == 0:
                        return nc
                    scope_ = nc.named_scope("M2b_l%d" % l)
                    scope_.__enter__()

                    order = [0] + list(range(NT, 0, -1))
                    nO = len(order)

                    def load_x(oi):
                        col0, is_ctx = TILES[order[oi]]
                        b = oi % 2
                        dma("sp", xc[b][:], XCv[:, :, col0:col0 + TL], reads=["XC"], writes=xck(b))

                    def load_hg(oi):
                        col0, is_ctx = TILES[order[oi]]
                        b = oi % 2
                        if not (last and is_ctx):
                            dma("sp", hfL[b][:], HFv[:, :, col0:col0 + TL], reads=["HF"], writes=["hfL%d" % b])
                            ggsrc = GGCv[:, :, :] if is_ctx else GGFv[col0 // TLOC][:, :, col0 % TLOC:col0 % TLOC + TL]
                            dma("sp", ggL[b][:], ggsrc, reads=["GGF"], writes=["ggL%d" % b])

                    def b0(oi):
                        b = oi % 2
                        op("act", lambda e: e.activation(out=xcb[b][:], in_=xc[b][:], func=AF.Identity), reads=xck(b), writes=["xcb%d" % b])
                        gates(1, b)

                    def b1(oi):
                        col0, is_ctx = TILES[order[oi]]
                        b = oi % 2
                        pb_ = (oi - 1) % 2
                        y3 = oi % 3
                        make_u(b)
                        for c in range(4):
                            init = 0.0 if oi == 0 else hS[pb_][:, c, 0:1]
                            op("dve", lambda e: e.tensor_tensor_scan(out=hS[b][:, c, ::-1], data0=aS[b][:, c, ::-1], data1=uS[:, c, ::-1], initial=init,
                                                                     op0=ALU.mult, op1=ALU.add),
                               reads=["aS%d" % b, "uS"] + ([] if oi == 0 else ["hS%d" % pb_]), writes=["hS%d" % b])
                        if last and is_ctx:
                            return
                        op("pool", lambda e: e.tensor_tensor(out=ya[y3][:], in0=hS[b][:], in1=hfL[b][:], op=ALU.add), reads=["hS%d" % b, "hfL%d" % b], writes=["ya%d" % y3])
                        op("pool", lambda e: e.tensor_tensor(out=ya[y3][:], in0=ya[y3][:], in1=ggL[b][:], op=ALU.mult), reads=["ggL%d" % b], writes=["ya%d" % y3])
                        op("act", lambda e: e.activation(out=sq[:], in_=ya[y3][:], func=AF.Square), reads=["ya%d" % y3], writes=["sqa"])
                        for c in range(4):
                            op("pe", lambda e: e.matmul(pSs[b][:, 0:TL], lhsT=ones_f[:], rhs=sq[:, c, :], start=(c == 0), stop=(c == 3)),
                               reads=["ones_f", "sqa"], writes=["pSa%d" % b])

                    def b2a(oi):
                        col0, is_ctx = TILES[order[oi]]
                        b = oi % 2
                        if last and is_ctx:
                            return
                        op("dve", lambda e: e.tensor_scalar(out=rr[b][:], in0=pSs[b][:, 0:TL], scalar1=1.0 / DA, scalar2=EPS, op0=ALU.mult, op1=ALU.add),
                           writes=["pSa%d" % b, "rra%d" % b])
                        rsqrt(rrs[b][:], rr[b][:], rrt[b][:], "rrsa%d" % b, "rra%d" % b, "rrta%d" % b, iters=2)

                    def b2b(oi):
                        col0, is_ctx = TILES[order[oi]]
                        b = oi % 2
                        y3 = oi % 3
                        if last and is_ctx:
                            return
                        for c in range(4):
                            op("dve", lambda e: e.scalar_tensor_tensor(out=yAT[b][:, c, :], in0=ya[y3][:, c, :], scalar=gmixc[:, c:c + 1], in1=rrs[b][:],
                                                                       op0=ALU.mult, op1=ALU.mult), reads=["ya%d" % y3, "gmixc", "rrsa%d" % b], writes=["yAT%d_%d" % (b, c)])
                        dma("sp", YTv[:, 0:4, col0:col0 + TL], yAT[b][:], reads=["yAT%d_%d" % (b, c) for c in range(4)], writes=["YT"])

                    load_x(0)
                    load_hg(0)
                    load_x(1)
                    b0(0)
                    for oi in range(nO + 2):
                        if oi + 1 < nO:
                            load_hg(oi + 1)
                            b0(oi + 1)
                        if oi < nO:
                            b1(oi)
                        if oi + 2 < nO:
                            load_x(oi + 2)
                        if 0 <= oi - 1 < nO:
                            b2a(oi - 1)
                        if 0 <= oi - 2 < nO:
                            b2b(oi - 2)
                    fw.barrier()
                    scope_.__exit__(None, None, None)
                if stop_after == "M2b" and l == 0:
                    return nc

                m3_tiles = LTILES[1:] if last else LTILES
                wdn = sb(el, "wdn", [128, NF, D], BF16)
                with ExitStack() as e4:
                    e4.enter_context(nc.named_scope("M3a_l%d" % l))
                    wout = sb(e4, "wout", [128, 8, D], BF16)
                    for k in range(8):
                        dma("pool", wout[:, k, :], wout_d[l, k * 128:(k + 1) * 128, :], writes=["wout"])
                    for f in range(NF):
                        dma("pool", wdn[:, f, :], wdn_d[l, f * 128:(f + 1) * 128, :], writes=["wdn"])
                    gate = sb(e4, "gate1", [128, 2, D])
                    lng = sb(e4, "ln1g", [128, D])
                    lnb = sb(e4, "ln1b", [128, D])
                    for s in range(2):
                        dma("sp", gate[:, s, :], MODS[s:s + 1, 2 * D:3 * D].partition_broadcast(128), reads=["MODS"], writes=["gate1"])
                    dma("sp", lng[:], ln1g_d[l:l + 1, :].partition_broadcast(128), writes=["ln1g"])
                    dma("sp", lnb[:], ln1b_d[l:l + 1, :].partition_broadcast(128), writes=["ln1b"])
                    op("dve", lambda e: e.tensor_scalar(out=gate[:], in0=gate[:], scalar1=1.0 / ALPHA, scalar2=None, op0=ALU.mult), writes=["gate1"])
                    rmask = sb(e4, "rmask", [128, 2])
                    dma("sp", rmask[:], rmask_d[:, :], writes=["rmask"])
                    woutL = sb(e4, "woutL", [128, 8, D], BF16)
                    wsel = [sb(e4, "wsel%d" % r_, [128, 6, D], BF16) for r_ in range(2)]
                    woutC = sb(e4, "woutC", [128, 8, D], BF16) if not last else None
                    AC = (0, 1, 2, 3, 6, 7)
                    for k in range(8):
                        op("dve", lambda e: e.tensor_tensor(out=woutL[:, k, :], in0=wout[:, k, :], in1=gate[:, 0, :], op=ALU.mult),
                           reads=["wout", "gate1"], writes=["woutL"])
                        if not last:
                            op("pool", lambda e: e.tensor_tensor(out=woutC[:, k, :], in0=wout[:, k, :], in1=gate[:, 1, :], op=ALU.mult),
                               reads=["wout", "gate1"], writes=["woutC"])
                    for r_ in range(2):
                        for j, k in enumerate(AC):
                            if r_ == 0:
                                op("dve", lambda e: e.tensor_scalar(out=wsel[r_][:, j, :], in0=woutL[:, k, :], scalar1=rmask[:, r_:r_ + 1], scalar2=None, op0=ALU.mult),
                                   reads=["woutL", "rmask"], writes=["wsel%d_%d" % (r_, j)])
                            else:
                                op("act", lambda e: e.activation(out=wsel[r_][:, j, :], in_=woutL[:, k, :], func=AF.Identity, scale=rmask[:, r_:r_ + 1]),
                                   reads=["woutL", "rmask"], writes=["wsel%d_%d" % (r_, j)])
                    yTb = [sb(e4, "yTb%d" % i, [128, 2, TL], BF16) for i in range(3)]
                    yC = [[sb(e4, "yC%d_%d" % (i, r_), [128, 6, TL], BF16) for r_ in range(2)] for i in range(3)]
                    xt = [sb(e4, "xt%d" % i, [128, 2, D]) for i in range(2)]
                    rt = [sb(e4, "rt%d" % i, [128, 2, D]) for i in range(3)]
                    lns3 = [ln_scr(e4, "l3a"), ln_scr(e4, "l3b")]
                    po = [ps(e4, "po%d" % i, [128, 512]) for i in range(8)]
                    n3 = len(m3_tiles)

                    def loadY(ti):
                        col0, is_ctx = m3_tiles[ti]
                        y = ti % 3
                        dma("sp", yTb[y][:], YBLv[:, :, col0:col0 + TL], reads=["YBL"], writes=["yTb%d" % y])
                        if is_ctx:
                            dma("sp", yC[y][0][:, 0:4, :], YTv[:, 0:4, T:T + TL], reads=["YT"], writes=["yC%d_0" % y])
                            dma("sp", yC[y][0][:, 4:6, :], YTv[:, 6:8, T:T + TL], reads=["YT"], writes=["yC%d_0" % y])
                        else:
                            for r_ in range(2):
                                g0 = r_ * TLOC + col0
                                dma("sp", yC[y][r_][:, 0:4, :], YTv[:, 0:4, g0:g0 + TL], reads=["YT"], writes=["yC%d_%d" % (y, r_)])
                                dma("sp", yC[y][r_][:, 4:6, :], YTv[:, 6:8, g0:g0 + TL], reads=["YT"], writes=["yC%d_%d" % (y, r_)])

                    def loadX(ti):
                        col0, is_ctx = m3_tiles[ti]
                        b = ti % 2
                        load_resid(l, col0, is_ctx, xt[b], "xt%d" % b, None, None, from_xb=True)

                    def mm3(ti):
                        col0, is_ctx = m3_tiles[ti]
                        b = ti % 2
                        y = ti % 3
                        for s in range(2):
                            for hf_ in range(2):
                                p_, kp_ = po[4 * b + 2 * s + hf_], "po%d" % (4 * b + 2 * s + hf_)
                                cs = slice(hf_ * 512, (hf_ + 1) * 512)
                                ts_ = slice(s * 128, (s + 1) * 128)
                                steps = []
                                wB = woutC if is_ctx else woutL
                                for j in range(2):
                                    steps.append((yTb[y][:, j, ts_], wB[:, 4 + j, cs], ["yTb%d" % y, "woutC" if is_ctx else "woutL"]))
                                if is_ctx:
                                    for j, k in enumerate(AC):
                                        steps.append((yC[y][0][:, j, ts_], woutC[:, k, cs], ["yC%d_0" % y, "woutC"]))
                                else:
                                    for r_ in range(2):
                                        for j in range(6):
                                            steps.append((yC[y][r_][:, j, ts_], wsel[r_][:, j, cs], ["yC%d_%d" % (y, r_), "wsel%d_%d" % (r_, j)]))
                                for n_, (lh, rh, rd) in enumerate(steps):
                                    op("pe", lambda e: e.matmul(p_[:], lhsT=lh, rhs=rh, start=(n_ == 0), stop=(n_ == len(steps) - 1)), reads=rd, writes=[kp_])

                    def res3(ti):
                        b = ti % 2
                        r3 = ti % 3
                        for s in range(2):
                            for hf_ in range(2):
                                p_, kp_ = po[4 * b + 2 * s + hf_], "po%d" % (4 * b + 2 * s + hf_)
                                op("dve", lambda e: e.tensor_tensor(out=rt[r3][:, s, hf_ * 512:(hf_ + 1) * 512], in0=p_[:],
                                                                    in1=xt[b][:, s, hf_ * 512:(hf_ + 1) * 512], op=ALU.add),
                                   reads=["xt%d" % b], writes=[kp_, "rt%d_%d" % (r3, s)])

                    def stats3(ti):
                        b = ti % 2
                        r3 = ti % 3
                        ln_stats(rt[r3], ["rt%d_0" % r3, "rt%d_1" % r3], lns3[b], EPS_POST, iters=2)

                    def norm3(ti):
                        col0, is_ctx = m3_tiles[ti]
                        b = ti % 2
                        r3 = ti % 3
                        mv, rs, kk = lns3[b]["mv"], lns3[b]["rs"], lns3[b]["k"]
                        for s in range(2):
                            kr = "rt%d_%d" % (r3, s)
                            op("dve", lambda e: e.tensor_scalar(out=rt[r3][:, s, :], in0=rt[r3][:, s, :], scalar1=mv[:, s, 0:1],
                                                                scalar2=rs[:, s:s + 1], op0=ALU.subtract, op1=ALU.mult),
                               reads=[kk + "mv", kk + "rs"], writes=[kr])
                            op("dve", lambda e: e.tensor_tensor(out=rt[r3][:, s, :], in0=rt[r3][:, s, :], in1=lng[:], op=ALU.mult), reads=["ln1g"], writes=[kr])
                            op("pool", lambda e: e.tensor_tensor(out=rt[r3][:, s, :], in0=rt[r3][:, s, :], in1=lnb[:], op=ALU.add), reads=["ln1b"], writes=[kr])

                    def store3(ti):
                        col0, is_ctx = m3_tiles[ti]
                        r3 = ti % 3
                        dma("sp", XB[col0:col0 + TL, :].rearrange("(s p) d -> p s d", p=128), rt[r3][:], reads=["rt%d_0" % r3, "rt%d_1" % r3], writes=["XB_%d" % col0])

                    for i_ in range(min(3, n3)):
                        loadY(i_)
                    for i_ in range(min(2, n3)):
                        loadX(i_)
                    mm3(0)
                    res3(0)
                    stats3(0)
                    for ti in range(n3):
                        if ti + 3 < n3:
                            loadY(ti + 3)
                        if ti + 2 < n3:
                            loadX(ti + 2)
                        if ti >= 2:
                            store3(ti - 2)
                        if ti + 1 < n3:
                            mm3(ti + 1)
                            res3(ti + 1)
                            stats3(ti + 1)
                        norm3(ti)
                    if n3 >= 2:
                        store3(n3 - 2)
                    store3(n3 - 1)
                    fw.barrier()
                if stop_after == "M3a" and l == 0:
                    return nc

                with ExitStack() as e5:
                    e5.enter_context(nc.named_scope("M3b_l%d" % l))
                    wup = sb(e5, "wup", [128, 8, 2 * DFF], BF16)
                    for c0 in range(0, 2 * DFF, 2048):
                        c1 = min(c0 + 2048, 2 * DFF)
                        for k in range(8):
                            dma("pool", wup[:, k, c0:c1], wup_d[l, k * 128:(k + 1) * 128, c0:c1], writes=["wup_p%d_%d" % (c0 // 2048, k)])
                    gate = sb(e5, "gate2", [128, 2, D])
                    lng = sb(e5, "ln2g", [128, D])
                    lnb = sb(e5, "ln2b", [128, D])
                    for s in range(2):
                        dma("sp", gate[:, s, :], MODS[s:s + 1, 5 * D:6 * D].partition_broadcast(128), reads=["MODS"], writes=["gate2"])
                    dma("sp", lng[:], ln2g_d[l:l + 1, :].partition_broadcast(128), writes=["ln2g"])
                    dma("sp", lnb[:], ln2b_d[l:l + 1, :].partition_broadcast(128), writes=["ln2b"])
                    op("dve", lambda e: e.tensor_scalar(out=gate[:], in0=gate[:], scalar1=1.0 / ALPHA, scalar2=None, op0=ALU.mult), writes=["gate2"])
                    xt = [sb(e5, "xu%d" % i, [128, 2, D]) for i in range(2)]
                    xh = sb(e5, "xhu", [128, 2, D])
                    h2T = [sb(e5, "h2T%d" % i, [128, 8, TL], BF16) for i in range(2)]
                    actT = sb(e5, "actT", [128, NF, TL], BF16)
                    sg = [sb(e5, "sg%d" % i, [128, TL]) for i in range(2)]
                    lns = ln_scr(e5, "l5")
                    lns2 = ln_scr(e5, "l6")
                    tp0 = ps(e5, "tq0", [128, 2, TL]); tp1 = ps(e5, "tq1", [128, 2, TL])
                    tps = [(tp0, "tq0"), (tp1, "tq1")]
                    pu = [ps(e5, "pu%d" % i, [128, 2, TL]) for i in range(2)]
                    pd = [ps(e5, "pd%d" % i, [128, 512]) for i in range(4)]

                    def load5(ti, b):
                        col0, is_ctx = m3_tiles[ti]
                        dma("sp", xt[b][:], XB[col0:col0 + TL, :].rearrange("(s p) d -> p s d", p=128), reads=["XB_%d" % col0], writes=["xu%d" % b])

                    su = [sb(e5, "su%d" % i, [128, TL]) for i in range(2)]
                    n5 = len(m3_tiles)
                    folded = [False]

                    def fold_gate():
                        for f in range(NF):
                            op("dve" if f % 2 == 0 else "pool",
                               lambda e: e.tensor_tensor(out=wdn[:, f, :], in0=wdn[:, f, :], in1=gate[:, 0, :], op=ALU.mult), reads=["gate2"], writes=["wdn"])
                        folded[0] = True

                    def stA(ti):
                        b = ti % 2
                        ln_tile(xt[b], "xu%d" % b, xh, "xhu", lns, EPS, eng="dve", iters=2)

                    def stT(ti):
                        col0, is_ctx = m3_tiles[ti]
                        b = ti % 2
                        transpose_mod(xh, "xhu", h2T[b], "h2T%d" % b, tps, modc, 1 if is_ctx else 0, 4, 3)

                    def stU(ti):
                        b = ti % 2
                        for f in range(NF):
                            p_, kp_ = pu[f % 2], "pu%d" % (f % 2)
                            for j in range(2):
                                c0 = j * DFF + f * 128
                                for k in range(8):
                                    op("pe", lambda e: e.matmul(p_[:, j, :], lhsT=wup[:, k, c0:c0 + 128], rhs=h2T[b][:, k, :], start=(k == 0), stop=(k == 7)),
                                       reads=["wup_p%d_%d" % (c0 // 2048, k), "h2T%d" % b], writes=[kp_])
                            op("act", lambda e: e.activation(out=sg[f % 2][:], in_=p_[:, 0, :], func=AF.Silu), writes=[kp_, "sg%d" % (f % 2)])
                            op("act", lambda e: e.activation(out=su[f % 2][:], in_=p_[:, 1, :], func=AF.Identity), writes=[kp_, "su%d" % (f % 2)])
                            op("pool", lambda e: e.tensor_tensor(out=actT[:, f, :], in0=su[f % 2][:], in1=sg[f % 2][:], op=ALU.mult),
                               reads=["sg%d" % (f % 2), "su%d" % (f % 2)], writes=["actT"])

                    def stD(ti):
                        for s in range(2):
                            for hf_ in range(2):
                                p_, kp_ = pd[2 * s + hf_], "pd%d" % (2 * s + hf_)
                                for f in range(NF):
                                    op("pe", lambda e: e.matmul(p_[:], lhsT=actT[:, f, s * 128:(s + 1) * 128], rhs=wdn[:, f, hf_ * 512:(hf_ + 1) * 512],
                                                                start=(f == 0), stop=(f == NF - 1)), reads=["wdn", "actT"], writes=[kp_])

                    def stE(ti):
                        col0, is_ctx = m3_tiles[ti]
                        b = ti % 2
                        kx = "xu%d" % b
                        for s in range(2):
                            for hf_ in range(2):
                                p_, kp_ = pd[2 * s + hf_], "pd%d" % (2 * s + hf_)
                                cs = slice(hf_ * 512, (hf_ + 1) * 512)
                                if folded[0]:
                                    op("dve", lambda e: e.tensor_tensor(out=xt[b][:, s, cs], in0=p_[:], in1=xt[b][:, s, cs], op=ALU.add), writes=[kp_, kx])
                                else:
                                    op("dve", lambda e: e.tensor_tensor(out=xh[:, s, cs], in0=p_[:], in1=gate[:, 1 if is_ctx else 0, cs], op=ALU.mult),
                                       reads=["gate2"], writes=[kp_, "xhu"])
                        if not folded[0]:
                            op("pool", lambda e: e.tensor_tensor(out=xt[b][:], in0=xt[b][:], in1=xh[:], op=ALU.add), reads=["xhu"], writes=[kx])
                        ln_tile(xt[b], kx, xt[b], kx, lns2, EPS_POST, eng="dve", iters=3)
                        for s in range(2):
                            op("dve", lambda e: e.tensor_tensor(out=xt[b][:, s, :], in0=xt[b][:, s, :], in1=lng[:], op=ALU.mult), reads=["ln2g"], writes=[kx])
                            op("dve", lambda e: e.tensor_tensor(out=xt[b][:, s, :], in0=xt[b][:, s, :], in1=lnb[:], op=ALU.add), reads=["ln2b"], writes=[kx])
                        dst = out_d[col0:col0 + TL, :] if last else XB[col0:col0 + TL, :]
                        dma("sp", dst.rearrange("(s p) d -> p s d", p=128), xt[b][:], reads=[kx], writes=["XB_%d" % col0])

                    load5(0, 0)
                    if n5 > 1:
                        load5(1, 1)
                    if not m3_tiles[0][1]:
                        fold_gate()
                    stA(0)
                    stT(0)
                    for ti in range(n5):
                        stU(ti)
                        if ti + 1 < n5:
                            stA(ti + 1)
                        stD(ti)
                        if ti + 1 < n5:
                            stT(ti + 1)
                        stE(ti)
                        if m3_tiles[ti][1]:
                            fold_gate()
                        if ti + 2 < n5:
                            load5(ti + 2, ti % 2)
                    fw.barrier()
                if stop_after == "M3b" and l == 0:
                    return nc
        fw.wait_all("sp")
    return nc


def host_consts():
    c = {}
    c["ident"] = np.eye(128, dtype=np.float32)
    l1 = np.arange(128)[:, None].astype(np.float64)
    k1 = np.arange(128)[None, :].astype(np.float64)
    a = 2 * np.pi * l1 * k1 / 128.0
    c["t1"] = np.stack([np.cos(a), -np.sin(a), -np.cos(a)], 1).astype(np.float32).astype(ml_dtypes.bfloat16)
    l2 = np.arange(64)[:, None, None].astype(np.float64)
    kk1 = np.arange(128)[None, :, None].astype(np.float64)
    kk2 = np.arange(64)[None, None, :].astype(np.float64)
    ang = 2 * np.pi * ((kk1 + 128 * kk2) * l2 % T) / T
    c["tw"] = np.concatenate([np.cos(ang), np.sin(ang)], 0).reshape(128, T).astype(np.float32).astype(ml_dtypes.bfloat16)
    p = np.arange(128)[:, None, None].astype(np.float64)
    t = np.arange(2)[None, :, None].astype(np.float64)
    k = np.arange(256)[None, None, :].astype(np.float64)
    a2 = 2 * np.pi * (((128 * t + p) * k) % 256) / 256.0
    c["c256"] = np.stack([np.cos(a2), -np.sin(a2)], 2).astype(np.float32).astype(ml_dtypes.bfloat16)
    j = np.arange(64)[:, None].astype(np.float64)
    ch = np.arange(64)[None, :].astype(np.float64)
    a3 = 2 * np.pi * j * ch / 64.0
    cs = np.zeros((64, 2, 2, 128), np.float64)
    for gl in range(2):
        cs[:, 0, gl, gl * 64:(gl + 1) * 64] = np.cos(a3)
        cs[:, 1, gl, gl * 64:(gl + 1) * 64] = np.sin(a3)
    c["cspad"] = cs.astype(np.float32)
    quarter = D // 4
    freqs = 10000.0 ** (-np.arange(quarter, dtype=np.float32) / np.float32(quarter))
    r = np.repeat(np.arange(T // 64, dtype=np.float32), 64)
    col = np.tile(np.arange(64, dtype=np.float32), T // 64)

    def enc(pv):
        an = pv[:, None].astype(np.float32) * freqs[None, :].astype(np.float32)
        return np.concatenate([np.sin(an), np.cos(an)], -1)

    c["pos"] = np.concatenate([enc(r), enc(col)], -1).astype(np.float32)
    return c


_WNAMES = ["w_mod", "b_mod", "w_in", "conv_w", "conv_b", "lru_wa", "lru_ba", "lru_wx", "lru_bx", "lru_lam", "sg_ws", "sg_b",
           "fourier_w", "g_mix", "w_out", "ln1_g", "ln1_b", "w_up", "w_down", "ln2_g", "ln2_b"]


def make_in_maps(inputs, n_cores=N_CORES):
    consts = host_consts()
    pos = consts.pop("pos")
    shared = {n: np.ascontiguousarray(np.asarray(inputs[n], dtype=np.float32)) for n in _WNAMES}
    shared.update(consts)
    x = np.asarray(inputs["x"], dtype=np.float32)
    c = np.asarray(inputs["c"], dtype=np.float32)
    ctx = np.asarray(inputs["ctx"], dtype=np.float32)
    c_ctx = np.asarray(inputs["c_ctx"], dtype=np.float32)
    maps = []
    for core in range(n_cores):
        b, h = core // 2, core % 2
        m = dict(shared)
        m["x"] = np.ascontiguousarray(x[b, h * TLOC:(h + 1) * TLOC])
        m["pos"] = np.ascontiguousarray(pos[h * TLOC:(h + 1) * TLOC])
        m["ctx"] = np.ascontiguousarray(ctx[b])
        m["cc"] = np.ascontiguousarray(np.stack([c[b], c_ctx], 0))
        rm = np.zeros((128, 2), np.float32)
        rm[:, h] = 1.0
        m["rmask"] = rm
        maps.append(m)
    return maps


def kernel(**inputs):
    nc = build_program()
    maps = make_in_maps(inputs)
    res = run_bass_kernel_spmd(nc, maps, core_ids=list(range(N_CORES)))
    outs = [np.asarray(r["out"], dtype=np.float32) for r in res.results]
    return np.stack([np.concatenate([outs[2 * b], outs[2 * b + 1]], 0) for b in range(N_CORES // 2)], 0)
```

```python
import math
from contextlib import ExitStack

import numpy as np
import ml_dtypes
import concourse.bass as bass
import concourse.mybir as mybir
from concourse.bass_utils import run_bass_kernel_spmd

F32 = mybir.dt.float32
BF16 = mybir.dt.bfloat16
I32 = mybir.dt.int32
ALU = mybir.AluOpType
AF = mybir.ActivationFunctionType

D = 1024
T = 8192
TC = 256
TT = T + TC
TL = 256
NT = T // TL
DEPTH = 2
DA, DB, DC = 512, 256, 256
DIN = 1792
DFF = 2816
NF = DFF // 128
EPS = 1e-6
ALPHA = (2 * DEPTH) ** 0.25
EPS_POST = EPS / (ALPHA * ALPHA)
N_CORES = 8
TLOC = T // 2
NTL = TLOC // TL

SAME_ENG_SYNC = True
N_DMA_SEMS = 40


class FW:
    def __init__(self, nc, es):
        self.nc = nc
        self.es = es
        self.engs = {"pe": nc.tensor, "act": nc.scalar, "dve": nc.vector, "pool": nc.gpsimd, "sp": nc.sync}
        self.sems = {}
        self.cnt = {}
        for e in self.engs:
            self.sems[e] = es.enter_context(nc.semaphore("s_" + e))
            self.cnt[e] = 0
        self.dsem = {}
        for q in ("sp", "pool"):
            lst = []
            for i in range(N_DMA_SEMS):
                key = "d_%s_%d" % (q, i)
                self.sems[key] = es.enter_context(nc.semaphore(key))
                self.cnt[key] = 0
                lst.append(key)
            self.dsem[q] = [lst, 0]
        self.seen = {e: {} for e in self.engs}
        self.lastw = {}
        self.readers = {}
        self.ninst = 0

    def _wait(self, eng, ev):
        sk, v, prod = ev
        if prod == "pe" and eng == "pe":
            return
        if prod == eng and not SAME_ENG_SYNC:
            return
        if self.seen[eng].get(sk, 0) >= v:
            return
        self.engs[eng].wait_ge(self.sems[sk], v)
        self.seen[eng][sk] = v

    def _deps(self, eng, reads, writes):
        for k in reads:
            ev = self.lastw.get(k)
            if ev is not None:
                self._wait(eng, ev)
        for k in writes:
            ev = self.lastw.get(k)
            if ev is not None:
                self._wait(eng, ev)
            for ev in list(self.readers.get(k, {}).values()):
                self._wait(eng, ev)

    def _record(self, ev, reads, writes):
        for k in writes:
            self.lastw[k] = ev
            self.readers[k] = {}
        for k in reads:
            if k in writes:
                continue
            self.readers.setdefault(k, {})[ev[0]] = ev

    def op(self, eng, fn, reads=(), writes=()):
        self._deps(eng, reads, writes)
        inst = fn(self.engs[eng])
        self.cnt[eng] += 1
        inst.then_inc(self.sems[eng], 1)
        ev = (eng, self.cnt[eng], eng)
        self._record(ev, reads, writes)
        self.ninst += 1
        return ev

    def dma(self, q, out, in_, reads=(), writes=(), **kw):
        self._deps(q, reads, writes)
        lst, idx = self.dsem[q]
        sk = lst[idx % len(lst)]
        self.dsem[q][1] = idx + 1
        if self.cnt[sk] > 0:
            self._wait(q, (sk, self.cnt[sk], "dma"))
        inst = self.engs[q].dma_start(out=out, in_=in_, **kw)
        self.cnt[sk] += 16
        inst.then_inc(self.sems[sk], 16)
        ev = (sk, self.cnt[sk], "dma")
        self._record(ev, reads, writes)
        self.ninst += 1
        return ev

    def barrier(self):
        for e in self.engs:
            for e2 in ("pe", "act", "dve", "pool"):
                if e2 != e and self.cnt[e2] > 0:
                    self._wait(e, (e2, self.cnt[e2], e2))
            if self.cnt.get("cc", 0) > 0:
                self._wait(e, ("cc", self.cnt["cc"], "dma"))
            for q in self.dsem:
                for sk in self.dsem[q][0]:
                    if self.cnt[sk] > 0:
                        self._wait(e, (sk, self.cnt[sk], "dma"))
        self.lastw.clear()
        self.readers.clear()

    def collective(self, kind, in_ap, out_ap, groups, reads, writes):
        self._deps("pool", reads, writes)
        if "cc" not in self.sems:
            self.sems["cc"] = self.es.enter_context(self.nc.semaphore("s_cc"))
            self.cnt["cc"] = 0
        inst = self.nc.gpsimd.collective_compute(kind, ALU.bypass, replica_groups=groups, ins=[in_ap], outs=[out_ap])
        self.cnt["cc"] += 1
        inst.then_inc(self.sems["cc"], 1)
        ev = ("cc", self.cnt["cc"], "dma")
        self._record(ev, reads, writes)
        return ev

    def wait_all(self, eng="sp"):
        for ev in list(self.lastw.values()):
            self._wait(eng, ev)
        for d in list(self.readers.values()):
            for ev in list(d.values()):
                self._wait(eng, ev)
        for q in self.dsem:
            for sk in self.dsem[q][0]:
                if self.cnt[sk] > 0:
                    self._wait(eng, (sk, self.cnt[sk], "dma"))
        for e in ("pe", "act", "dve", "pool"):
            if self.cnt[e] > 0:
                self._wait(eng, (e, self.cnt[e], e))


def build_program(stop_after=None):
    nc = bass.Bass("TRN2", target_bir_lowering=False, num_devices=8)

    def din(name, shape, dt=F32):
        return nc.dram_tensor(name, list(shape), dt, kind="ExternalInput").ap()

    dbg = stop_after is not None

    CC_BUFS = ("XAL", "GGL", "ZL", "XAF", "GGF", "ZF")

    def dscr(name, shape, dt=F32):
        ext = dbg and name not in CC_BUFS
        return nc.dram_tensor(name, list(shape), dt, kind="ExternalOutput" if ext else "Internal").ap()

    x_d = din("x", [TLOC, D])
    ctx_d = din("ctx", [TC, D])
    cc_d = din("cc", [2, D])
    pos_d = din("pos", [TLOC, D])
    rmask_d = din("rmask", [128, 2])
    wmod_d = din("w_mod", [DEPTH, D, 6 * D])
    bmod_d = din("b_mod", [DEPTH, 6 * D])
    win_d = din("w_in", [DEPTH, D, DIN])
    convw_d = din("conv_w", [DEPTH, 4, DA])
    convb_d = din("conv_b", [DEPTH, DA])
    wa_d = din("lru_wa", [DEPTH, 2, 8, 64, 64])
    ba_d = din("lru_ba", [DEPTH, 2, DA])
    wx_d = din("lru_wx", [DEPTH, 2, 8, 64, 64])
    bx_d = din("lru_bx", [DEPTH, 2, DA])
    lam_d = din("lru_lam", [DEPTH, 2, DA])
    ws_d = din("sg_ws", [DEPTH, 4, 128, 128])
    sgb_d = din("sg_b", [DEPTH, 4, 128])
    wf_d = din("fourier_w", [DEPTH, 4, 64, 64])
    gmix_d = din("g_mix", [DEPTH, D])
    wout_d = din("w_out", [DEPTH, D, D])
    ln1g_d = din("ln1_g", [DEPTH, D])
    ln1b_d = din("ln1_b", [DEPTH, D])
    wup_d = din("w_up", [DEPTH, D, 2 * DFF])
    wdn_d = din("w_down", [DEPTH, DFF, D])
    ln2g_d = din("ln2_g", [DEPTH, D])
    ln2b_d = din("ln2_b", [DEPTH, D])
    ident_d = din("ident", [128, 128])
    t1_d = din("t1", [128, 3, 128], BF16)
    tw_d = din("tw", [128, T], BF16)
    c256_d = din("c256", [128, 2, 2, 256], BF16)
    cspad_d = din("cspad", [64, 2, 2, 128])
    out_d = nc.dram_tensor("out", [TLOC, D], F32, kind="ExternalOutput").ap()

    XB = dscr("XB", [TLOC + TC, D])
    XAL = dscr("XAL", [DA, TLOC])
    GGL = dscr("GGL", [DA, TLOC])
    ZL = dscr("ZL", [DC, TLOC], BF16)
    XAF = dscr("XAF", [2 * DA, TLOC])
    GGF = dscr("GGF", [2 * DA, TLOC])
    ZF = dscr("ZF", [2 * DC, TLOC], BF16)
    XAC = dscr("XAC", [DA, TC])
    GGC = dscr("GGC", [DA, TC])
    YBL = dscr("YBL", [DB, TLOC + TC], BF16)
    XC = dscr("XC", [DA, TT])
    HF = dscr("HF", [DA, TT])
    YT = dscr("YT", [D, TT], BF16)
    GD = dscr("GD", [128, 2, 64, 256], BF16)
    MODS = dscr("MODS", [2, 6 * D])

    XCv = XC.rearrange("(c p) t -> p c t", p=128)
    XALv = XAL.rearrange("(c p) t -> p c t", p=128)
    GGLv = GGL.rearrange("(c p) t -> p c t", p=128)
    ZLv = ZL.rearrange("(c p) t -> p c t", p=128)
    XACv = XAC.rearrange("(c p) t -> p c t", p=128)
    GGCv = GGC.rearrange("(c p) t -> p c t", p=128)
    YBLv = YBL.rearrange("(c p) t -> p c t", p=128)
    XAFv = XAF.rearrange("(c r p) t -> r p c t", r=2, p=128)
    GGFv = GGF.rearrange("(c r p) t -> r p c t", r=2, p=128)
    ZFv = ZF.rearrange("(r c p) t -> r p c t", r=2, p=128)
    PAIRS = [[0, 1], [2, 3], [4, 5], [6, 7]]
    HFv = HF.rearrange("(c p) t -> p c t", p=128)
    YTv = YT.rearrange("(c p) t -> p c t", p=128)

    TILES = [(T, True)] + [(TL * i, False) for i in range(NT)]
    LTILES = [(TLOC, True)] + [(TL * i, False) for i in range(NTL)]
    import os as _os
    if _os.environ.get("DBG_NT"):
        LTILES = LTILES[:int(_os.environ["DBG_NT"])]

    with ExitStack() as es:
        fw = FW(nc, es)
        op = fw.op
        dma = fw.dma

        uid = [0]

        def sb(es_, name, shape, dt=F32):
            uid[0] += 1
            return es_.enter_context(nc.sbuf_tensor("%s_s%d" % (name, uid[0]), list(shape), dt))

        def ps(es_, name, shape, dt=F32):
            uid[0] += 1
            return es_.enter_context(nc.psum_tensor("%s_p%d" % (name, uid[0]), list(shape), dt))

        es.enter_context(nc.allow_non_contiguous_dma(reason="small strided parameter loads"))

        ident = sb(es, "ident", [128, 128])
        ones_f = sb(es, "ones_f", [128, 128])
        dma("sp", ident[:], ident_d[:, :], writes=["ident"])
        op("pool", lambda e: e.memset(ones_f[:], 1.0), writes=["ones_f"])

        def rsqrt(out, x, tmp, kout, kx, ktmp, eng="pool", iters=3):
            xi = x.bitcast(I32)
            oi = out.bitcast(I32)
            op("dve", lambda e: e.tensor_scalar(out=oi, in0=xi, scalar1=1, scalar2=None,
                                                op0=ALU.arith_shift_right), reads=[kx], writes=[kout])
            op("dve", lambda e: e.tensor_scalar(out=oi, in0=oi, scalar1=-1.0, scalar2=float(0x5F3759DF),
                                                op0=ALU.mult, op1=ALU.add), writes=[kout])
            for _ in range(iters):
                op(eng, lambda e: e.tensor_tensor(out=tmp, in0=x, in1=out, op=ALU.mult), reads=[kx, kout], writes=[ktmp])
                op(eng, lambda e: e.tensor_tensor(out=tmp, in0=tmp, in1=out, op=ALU.mult), reads=[kout], writes=[ktmp])
                op(eng, lambda e: e.tensor_scalar(out=tmp, in0=tmp, scalar1=-0.5, scalar2=1.5,
                                                  op0=ALU.mult, op1=ALU.add), writes=[ktmp])
                op(eng, lambda e: e.tensor_tensor(out=out, in0=out, in1=tmp, op=ALU.mult), reads=[ktmp], writes=[kout])

        def resid_rows(l, col0, is_ctx):
            if l == 0:
                return (ctx_d[0:TL, :] if is_ctx else x_d[col0:col0 + TL, :])
            return XB[col0:col0 + TL, :]

        def load_resid(l, col0, is_ctx, xt, kx, pt=None, kp=None, from_xb=False, save_xb=False):
            if from_xb and not is_ctx:
                src = XB[col0:col0 + TL, :]
            else:
                src = resid_rows(l, col0, is_ctx)
            dma("sp", xt[:], src.rearrange("(s p) d -> p s d", p=128), reads=(["XB_%d" % col0] if (from_xb or l > 0) else []), writes=[kx])
            if l == 0 and not is_ctx and not from_xb:
                dma("sp", pt[:], pos_d[col0:col0 + TL, :].rearrange("(s p) d -> p s d", p=128), writes=[kp])
                op("pool", lambda e: e.tensor_tensor(out=xt[:], in0=xt[:], in1=pt[:], op=ALU.add),
                   reads=[kp], writes=[kx])
                if save_xb:
                    dma("sp", XB[col0:col0 + TL, :].rearrange("(s p) d -> p s d", p=128), xt[:], reads=[kx], writes=["XB_%d" % col0])

        def ln_tile(xt, kx, xh, kxh, scr, eps, eng="pool", iters=3):
            st, mv, ve, rs, tmp = scr["st"], scr["mv"], scr["ve"], scr["rs"], scr["tmp"]
            kk = scr["k"]
            for s in range(2):
                for h in range(2):
                    op("dve", lambda e: e.bn_stats(out=st[:, s, h, :], in_=xt[:, s, h * 512:(h + 1) * 512]),
                       reads=[kx], writes=[kk + "st"])
                op("dve", lambda e: e.bn_aggr(out=mv[:, s, :], in_=st[:, s, :, :].rearrange("p a b -> p (a b)")),
                   reads=[kk + "st"], writes=[kk + "mv"])
            op("dve", lambda e: e.tensor_scalar(out=ve[:], in0=mv[:, :, 1], scalar1=float(eps), scalar2=None,
                                                op0=ALU.add), reads=[kk + "mv"], writes=[kk + "ve"])
            rsqrt(rs[:], ve[:], tmp[:], kk + "rs", kk + "ve", kk + "tmp", eng=eng, iters=iters)
            for s in range(2):
                op("dve", lambda e: e.tensor_scalar(out=xh[:, s, :], in0=xt[:, s, :], scalar1=mv[:, s, 0:1],
                                                    scalar2=rs[:, s:s + 1], op0=ALU.subtract, op1=ALU.mult),
                   reads=[kx, kk + "mv", kk + "rs"], writes=[kxh])

        def ln_stats(xt, kx, scr, eps, iters=3):
            st, mv, ve, rs, tmp = scr["st"], scr["mv"], scr["ve"], scr["rs"], scr["tmp"]
            kk = scr["k"]
            for s in range(2):
                for h in range(2):
                    op("dve", lambda e: e.bn_stats(out=st[:, s, h, :], in_=xt[:, s, h * 512:(h + 1) * 512]),
                       reads=(kx if isinstance(kx, list) else [kx]), writes=[kk + "st%d%d" % (s, h)])
                op("dve", lambda e: e.bn_aggr(out=mv[:, s, :], in_=st[:, s, :, :].rearrange("p a b -> p (a b)")),
                   reads=[kk + "st%d0" % s, kk + "st%d1" % s], writes=[kk + "mv"])
            op("dve", lambda e: e.tensor_scalar(out=ve[:], in0=mv[:, :, 1], scalar1=float(eps), scalar2=None,
                                                op0=ALU.add), reads=[kk + "mv"], writes=[kk + "ve"])
            rsqrt(rs[:], ve[:], tmp[:], kk + "rs", kk + "ve", kk + "tmp", iters=iters)

        def ln_apply(xt, kx, xh, kxh, scr):
            mv, rs = scr["mv"], scr["rs"]
            kk = scr["k"]
            for s in range(2):
                op("dve", lambda e: e.tensor_scalar(out=xh[:, s, :], in0=xt[:, s, :], scalar1=mv[:, s, 0:1],
                                                    scalar2=rs[:, s:s + 1], op0=ALU.subtract, op1=ALU.mult),
                   reads=[kx, kk + "mv", kk + "rs"], writes=[kxh])

        def ln_scr(es_, name):
            return dict(st=sb(es_, name + "st", [128, 2, 2, 6]), mv=sb(es_, name + "mv", [128, 2, 2]),
                        ve=sb(es_, name + "ve", [128, 2]), rs=sb(es_, name + "rs", [128, 2]),
                        tmp=sb(es_, name + "tmp", [128, 2]), k=name)

        def transpose_mod(xh, kxh, hT, khT, tps, modc, strm, jsc, jsh):
            for kp in range(4):
                tp, ktp = tps[kp % 2]
                for j in range(2):
                    k = 2 * kp + j
                    for s in range(2):
                        op("pe", lambda e: e.transpose(out=tp[:, j, s * 128:(s + 1) * 128],
                                                       in_=xh[:, s, k * 128:(k + 1) * 128], identity=ident[:]),
                           reads=[kxh, "ident"], writes=[ktp])
                for j in range(2):
                    k = 2 * kp + j
                    op("act", lambda e: e.activation(out=hT[:, k, :], in_=tp[:, j, :], func=AF.Identity,
                                                     scale=modc[:, strm, jsc, k:k + 1], bias=modc[:, strm, jsh, k:k + 1]),
                       reads=["modc"], writes=[ktp, khT])

        for l in range(DEPTH):
            last = (l == DEPTH - 1)
            with ExitStack() as el:
                with ExitStack() as ep:
                    ep.enter_context(nc.named_scope("P0_l%d" % l))
                    cct = sb(ep, "cct", [128, 8, 2])
                    sct = sb(ep, "sct", [128, 8, 2])
                    scbc = sb(ep, "scbc", [128, 8, 128])
                    bmbc = sb(ep, "bmbc", [128, 6 * D])
                    modbc = sb(ep, "modbc", [128, 6 * D])
                    wm = [sb(ep, "wm%d" % i, [128, 3072]) for i in range(3)]
                    pmod = [ps(ep, "pmod%d" % i, [128, 512]) for i in range(6)]
                    for s in range(2):
                        dma("sp", cct[:, :, s], cc_d[s, :].rearrange("(k p) -> p k", p=128), writes=["cct"])
                    dma("sp", bmbc[:], bmod_d[l:l + 1, :].partition_broadcast(128), writes=["bmbc"])
                    op("act", lambda e: e.activation(out=sct[:], in_=cct[:], func=AF.Tanh, scale=0.5), reads=["cct"], writes=["sct"])
                    op("dve", lambda e: e.tensor_scalar(out=sct[:], in0=sct[:], scalar1=0.5, scalar2=0.5, op0=ALU.mult, op1=ALU.add), writes=["sct"])
                    op("dve", lambda e: e.tensor_tensor(out=sct[:], in0=sct[:], in1=cct[:], op=ALU.mult), reads=["cct"], writes=["sct"])
                    for k in range(8):
                        for s in range(2):
                            op("dve", lambda e: e.tensor_scalar(out=scbc[:, k, 64 * s:64 * s + 64], in0=ones_f[:, 0:64],
                                                                scalar1=sct[:, k, s:s + 1], scalar2=None, op0=ALU.mult),
                               reads=["sct", "ones_f"], writes=["scbc"])
                    ld = 0
                    for half in range(2):
                        for k in range(8):
                            w = wm[ld % 3]
                            kw = "wm%d" % (ld % 3)
                            ld += 1
                            dma("sp", w[:], wmod_d[l, k * 128:(k + 1) * 128, half * 3072:(half + 1) * 3072], writes=[kw])
                            for n in range(6):
                                op("pe", lambda e: e.matmul(pmod[n][:], lhsT=scbc[:, k, :], rhs=w[:, n * 512:(n + 1) * 512],
                                                            start=(k == 0), stop=(k == 7)),
                                   reads=["scbc", kw], writes=["pmod%d" % n])
                        for n in range(6):
                            c0 = half * 3072 + n * 512
                            op("dve", lambda e: e.tensor_tensor(out=modbc[:, c0:c0 + 512], in0=pmod[n][:], in1=bmbc[:, c0:c0 + 512],
                                                                op=ALU.add), reads=["bmbc"], writes=["pmod%d" % n, "modbc"])
                    dma("sp", MODS[0:1, :], modbc[0:1, :], reads=["modbc"], writes=["MODS"])
                    dma("sp", MODS[1:2, :], modbc[64:65, :], reads=["modbc"], writes=["MODS"])
                    fw.barrier()

                modc = sb(el, "modc", [128, 2, 6, 8])
                gmixc = sb(el, "gmixc", [128, 8])
                for s in range(2):
                    for j in range(6):
                        dma("sp", modc[:, s, j, :], MODS[s, j * D:(j + 1) * D].rearrange("(k p) -> p k", p=128), reads=["MODS"], writes=["modc"])
                dma("sp", gmixc[:], gmix_d[l, :].rearrange("(k p) -> p k", p=128), writes=["gmixc"])
                for j in (1, 4):
                    op("dve", lambda e: e.tensor_scalar(out=modc[:, :, j, :], in0=modc[:, :, j, :], scalar1=1.0, scalar2=None,
                                                        op0=ALU.add), writes=["modc"])

                with ExitStack() as ez:
                    zT = sb(ez, "zT", [128, 2, T], BF16)
                    zTc = sb(ez, "zTc", [128, 2, TC], BF16)
                    with ExitStack() as e1:
                        e1.enter_context(nc.named_scope("M1_l%d" % l))
                        win = sb(e1, "win", [128, 8, DIN], BF16)
                        for pc_, (c0_, c1_) in enumerate(((0, 512), (512, 1024), (1024, 1536), (1536, DIN))):
                            for k in range(8):
                                dma("pool", win[:, k, c0_:c1_], win_d[l, k * 128:(k + 1) * 128, c0_:c1_], writes=["win_p%d" % pc_])
                        wsT = sb(e1, "wsT", [128, 4, 128], BF16)
                        wsr = sb(e1, "wsr", [128, 4, 128])
                        bsT = sb(e1, "bsT", [128, 4])
                        dma("sp", wsr[:], ws_d[l].rearrange("h p q -> p h q"), writes=["wsr"])
                        dma("sp", bsT[:], sgb_d[l].rearrange("h p -> p h"), writes=["bsT"])
                        xt = [sb(e1, "xt%d" % i, [128, 2, D]) for i in range(2)]
                        pt = [sb(e1, "pt%d" % i, [128, 2, D]) for i in range(2)] if l == 0 else [None, None]
                        xh = sb(e1, "xh", [128, 2, D])
                        hT = [sb(e1, "hT%d" % i, [128, 8, TL], BF16) for i in range(2)]
                        lns = ln_scr(e1, "l1")
                        xaS = [sb(e1, "xaS%d" % i, [128, 4, TL]) for i in range(2)]
                        ggS = [sb(e1, "ggS%d" % i, [128, 4, TL]) for i in range(2)]
                        yBT = [sb(e1, "yBT%d" % i, [128, 2, TL], BF16) for i in range(2)]
                        zS = [sb(e1, "zS%d" % i, [128, 2, TL], BF16) for i in range(2)]
                        zB = [sb(e1, "zB%d" % i, [128, 512]) for i in range(2)]
                        yb = [sb(e1, "yb%d" % i, [128, 256]) for i in range(2)]
                        vh = sb(e1, "vh", [128, 256], BF16)
                        bst = sb(e1, "bst", [128, 4, 6])
                        bmv = sb(e1, "bmv", [128, 4, 2])
                        bve = sb(e1, "bve", [128, 4])
                        brs = sb(e1, "brs", [128, 4])
                        btmp = sb(e1, "btmp", [128, 4])
                        ssB = sb(e1, "ssB", [128, 2])
                        rB = sb(e1, "rB", [128, 2])
                        rsB = sb(e1, "rsB", [128, 2])
                        rtmp = sb(e1, "rtmp", [128, 2])
                        junk = sb(e1, "junk", [128, 256])
                        tp0 = ps(e1, "tp0", [128, 2, TL]); tp1 = ps(e1, "tp1", [128, 2, TL])
                        tps = [(tp0, "tp0"), (tp1, "tp1")]
                        pa = [ps(e1, "pa%d" % i, [128, 2, TL]) for i in range(2)]
                        pb = [ps(e1, "pb%d" % i, [128, 512]) for i in range(2)]
                        pss = ps(e1, "pss", [128, 512])
                        pyt = ps(e1, "pyt", [128, 2, TL])
                        for h in range(4):
                            op("pe", lambda e: e.transpose(out=pb[0][:, h * 128:(h + 1) * 128], in_=wsr[:, h, :], identity=ident[:]),
                               reads=["wsr", "ident"], writes=["pb0"])
                        op("dve", lambda e: e.tensor_copy(out=wsT[:].rearrange("p h q -> p (h q)"), in_=pb[0][:]), writes=["pb0", "wsT"])

                        tl = LTILES
                        nM = len(tl)
                        zBt = [[sb(e1, "zBt%d_%d" % (i, s_), [128, 512]) for s_ in range(2)] for i in range(2)]

                        def m1_load(ti):
                            nb = ti % 2
                            load_resid(l, tl[ti][0], tl[ti][1], xt[nb], "xt%d" % nb, pt[nb], "pt%d" % nb, save_xb=True)

                        def m1_A(ti):
                            b = ti % 2
                            ln_tile(xt[b], "xt%d" % b, xh, "xh", lns, EPS, eng="dve", iters=2)

                        def m1_T(ti):
                            col0, is_ctx = tl[ti]
                            b = ti % 2
                            transpose_mod(xh, "xh", hT[b], "hT%d" % b, tps, modc, 1 if is_ctx else 0, 1, 0)

                        def m1_P(ti):
                            col0, is_ctx = tl[ti]
                            b = ti % 2
                            khT = "hT%d" % b
                            pai = 0
                            only_xa = last and is_ctx
                            for grp in range(2 if only_xa else 4):
                                p_, kp_ = pa[pai % 2], "pa%d" % (pai % 2)
                                pai += 1
                                for j in range(2):
                                    oc = grp * 2 + j
                                    for k in range(8):
                                        op("pe", lambda e: e.matmul(p_[:, j, :], lhsT=win[:, k, oc * 128:(oc + 1) * 128], rhs=hT[b][:, k, :],
                                                                    start=(k == 0), stop=(k == 7)), reads=["win_p%d" % (oc // 4), khT], writes=[kp_])
                                if grp < 2:
                                    op("act", lambda e: e.activation(out=xaS[b][:, 2 * grp:2 * grp + 2, :], in_=p_[:], func=AF.Identity),
                                       writes=[kp_, "xaS%d" % b])
                                else:
                                    g2 = grp - 2
                                    op("act", lambda e: e.activation(out=ggS[b][:, 2 * g2:2 * g2 + 2, :], in_=p_[:], func=AF.Gelu),
                                       writes=[kp_, "ggS%d" % b])
                            dma("sp", XACv[:, :, :] if is_ctx else XALv[:, :, col0:col0 + TL], xaS[b][:], reads=["xaS%d" % b], writes=["XA"])
                            if only_xa:
                                return
                            dma("sp", GGCv[:, :, :] if is_ctx else GGLv[:, :, col0:col0 + TL], ggS[b][:], reads=["ggS%d" % b], writes=["GG"])
                            p_, kp_ = pa[pai % 2], "pa%d" % (pai % 2)
                            for j in range(2):
                                for k in range(8):
                                    op("pe", lambda e: e.matmul(p_[:, j, :], lhsT=win[:, k, 1536 + j * 128:1536 + (j + 1) * 128], rhs=hT[b][:, k, :],
                                                                start=(k == 0), stop=(k == 7)), reads=["win_p3", khT], writes=[kp_])
                            if is_ctx:
                                op("act", lambda e: e.activation(out=zTc[:, :, :], in_=p_[:], func=AF.Identity), writes=[kp_, "zTc"])
                            else:
                                op("act", lambda e: e.activation(out=zS[b][:], in_=p_[:], func=AF.Identity), writes=[kp_, "zS%d" % b])
                                dma("sp", ZLv[:, :, col0:col0 + TL], zS[b][:], reads=["zS%d" % b], writes=["ZL"])

                        def m1_Bproj(ti):
                            col0, is_ctx = tl[ti]
                            if last and is_ctx:
                                return
                            b = ti % 2
                            for s in range(2):
                                p_, kp_ = pb[s], "pb%d" % s
                                for k in range(8):
                                    op("pe", lambda e: e.matmul(p_[:], lhsT=hT[b][:, k, s * 128:(s + 1) * 128], rhs=win[:, k, 1024:1536],
                                                                start=(k == 0), stop=(k == 7)), reads=["win_p2", "hT%d" % b], writes=[kp_])
                                op("act", lambda e: e.activation(out=zBt[b][s][:], in_=p_[:], func=AF.Gelu), writes=[kp_, "zBt%d_%d" % (b, s)])

                        bmv8 = sb(e1, "bmv8", [128, 8, 2])
                        bve8 = sb(e1, "bve8", [128, 8])
                        brs8 = sb(e1, "brs8", [128, 8])
                        btmp8 = sb(e1, "btmp8", [128, 8])
                        bst8 = sb(e1, "bst8", [128, 8, 6])
                        vh2 = sb(e1, "vh2", [128, 2, 256], BF16)
                        ybp = [[sb(e1, "ybp%d_%d" % (i, s_), [128, 256]) for s_ in range(2)] for i in range(2)]
                        ssBp = [sb(e1, "ssBp%d" % i, [128, 2]) for i in range(2)]

                        def skipB(ti):
                            return last and tl[ti][1]

                        def m1_B1a(ti):
                            if skipB(ti):
                                return
                            b = ti % 2
                            for s in range(2):
                                zb, kzb = zBt[b][s], "zBt%d_%d" % (b, s)
                                for h in range(4):
                                    op("dve", lambda e: e.bn_stats(out=bst8[:, 4 * s + h, :], in_=zb[:, 256 + 64 * h:256 + 64 * h + 64]),
                                       reads=[kzb], writes=["bst8_%d" % (4 * s + h)])
                                for h in range(4):
                                    op("dve", lambda e: e.bn_aggr(out=bmv8[:, 4 * s + h, :], in_=bst8[:, 4 * s + h, :]),
                                       reads=["bst8_%d" % (4 * s + h)], writes=["bmv8"])
                            op("dve", lambda e: e.tensor_scalar(out=bve8[:], in0=bmv8[:, :, 1], scalar1=EPS, scalar2=None, op0=ALU.add),
                               reads=["bmv8"], writes=["bve8"])
                            rsqrt(brs8[:], bve8[:], btmp8[:], "brs8", "bve8", "btmp8", iters=2)
                            for s in range(2):
                                zb, kzb = zBt[b][s], "zBt%d_%d" % (b, s)
                                for h in range(4):
                                    op("dve", lambda e: e.tensor_scalar(out=vh2[:, s, 64 * h:64 * h + 64], in0=zb[:, 256 + 64 * h:256 + 64 * h + 64],
                                                                        scalar1=bmv8[:, 4 * s + h, 0:1], scalar2=brs8[:, 4 * s + h:4 * s + h + 1],
                                                                        op0=ALU.subtract, op1=ALU.mult),
                                       reads=[kzb, "bmv8", "brs8"], writes=["vh2_%d" % (4 * s + h)])

                        def m1_smm(ti):
                            if skipB(ti):
                                return
                            for s in range(2):
                                for h in range(4):
                                    op("pe", lambda e: e.matmul(pss[:, 256 * s + 64 * h:256 * s + 64 * h + 64], lhsT=wsT[:, h, :], rhs=vh2[:, s, 64 * h:64 * h + 64],
                                                                start=True, stop=True), reads=["wsT", "vh2_%d" % (4 * s + h)], writes=["pss"])

                        def m1_B1b(ti):
                            if skipB(ti):
                                return
                            b = ti % 2
                            for s in range(2):
                                zb, kzb = zBt[b][s], "zBt%d_%d" % (b, s)
                                for h in range(4):
                                    op("dve", lambda e: e.scalar_tensor_tensor(out=ybp[b][s][:, 64 * h:64 * h + 64], in0=pss[:, 256 * s + 64 * h:256 * s + 64 * h + 64],
                                                                               scalar=bsT[:, h:h + 1], in1=zb[:, 64 * h:64 * h + 64],
                                                                               op0=ALU.add, op1=ALU.mult),
                                       reads=["bsT", kzb], writes=["pss", "ybp%d_%d_%d" % (b, s, h)])
                            for s in range(2):
                                op("act", lambda e: e.activation(out=junk[:], in_=ybp[b][s][:], func=AF.Square, accum_out=ssBp[b][:, s:s + 1]),
                                   reads=["ybp%d_%d_%d" % (b, s, h) for h in range(4)], writes=["junk", "ssBp%d" % b])

                        def m1_B2(ti):
                            if skipB(ti):
                                return
                            col0, is_ctx = tl[ti]
                            b = ti % 2
                            op("dve", lambda e: e.tensor_scalar(out=rB[:], in0=ssBp[b][:], scalar1=1.0 / DB, scalar2=EPS, op0=ALU.mult, op1=ALU.add),
                               reads=["ssBp%d" % b], writes=["rB"])
                            rsqrt(rsB[:], rB[:], rtmp[:], "rsB", "rB", "rtmp", iters=2)
                            for s in range(2):
                                kyb = ["ybp%d_%d_%d" % (b, s, h) for h in range(4)]
                                op("dve", lambda e: e.tensor_scalar(out=ybp[b][s][:], in0=ybp[b][s][:], scalar1=rsB[:, s:s + 1], scalar2=None, op0=ALU.mult),
                                   reads=["rsB"], writes=kyb)
                                for c in range(2):
                                    op("pe", lambda e: e.transpose(out=pyt[:, c, s * 128:(s + 1) * 128], in_=ybp[b][s][:, c * 128:(c + 1) * 128],
                                                                   identity=ident[:]), reads=kyb + ["ident"], writes=["pyt"])
                            for c in range(2):
                                op("act", lambda e: e.activation(out=yBT[b][:, c, :], in_=pyt[:, c, :], func=AF.Identity, scale=gmixc[:, 4 + c:5 + c]),
                                   reads=["gmixc"], writes=["pyt", "yBT%d" % b])
                            dma("sp", YBLv[:, :, col0:col0 + TL], yBT[b][:], reads=["yBT%d" % b], writes=["YBL"])

                        m1_load(0)
                        if nM > 1:
                            m1_load(1)
                        m1_A(0)
                        m1_T(0)
                        for ti in range(nM + 2):
                            if ti < nM:
                                m1_P(ti)
                            if ti + 1 < nM:
                                m1_A(ti + 1)
                            if 2 <= ti:
                                m1_B2(ti - 2)
                            if ti + 1 < nM:
                                m1_T(ti + 1)
                            if 1 <= ti <= nM:
                                m1_B1a(ti - 1)
                            if ti < nM:
                                m1_Bproj(ti)
                            if 1 <= ti <= nM:
                                m1_smm(ti - 1)
                                m1_B1b(ti - 1)
                            if ti + 2 < nM:
                                m1_load(ti + 2)
                        fw.barrier()
                    fw.collective("AllGather", ZL[:, :], ZF[:, :], PAIRS, ["ZL"], ["ZF"])
                    fw.barrier()
                    for c_ in range(4):
                        fw.collective("AllGather", XAL[c_ * 128:(c_ + 1) * 128, :], XAF[c_ * 256:(c_ + 1) * 256, :], PAIRS, ["XA"], ["XAF"])
                    for c_ in range(4):
                        fw.collective("AllGather", GGL[c_ * 128:(c_ + 1) * 128, :], GGF[c_ * 256:(c_ + 1) * 256, :], PAIRS, ["GG"], ["GGF"])
                    if stop_after == "M1" and l == 0:
                        xafd = nc.dram_tensor("XAFd", [2 * DA, TLOC], F32, kind="ExternalOutput").ap()
                        zfd = nc.dram_tensor("ZFd", [2 * DC, TLOC], BF16, kind="ExternalOutput").ap()
                        dma("sp", xafd[:, :], XAF[:, :], writes=["xafd"])
                        dma("sp", zfd[:, :], ZF[:, :], writes=["zfd"])
                        fw.wait_all("sp")
                        return nc

                    if True:
                        with ExitStack() as e2:
                            e2.enter_context(nc.named_scope("DFT_l%d" % l))
                            t1 = sb(e2, "t1", [128, 3, 128], BF16)
                            tw = sb(e2, "tw", [128, 128, 64], BF16)
                            c256 = sb(e2, "c256", [128, 2, 2, 256], BF16)
                            cspad = sb(e2, "cspad", [64, 2, 2, 128])
                            wft = sb(e2, "wft", [64, 4, 64])
                            abd = sb(e2, "abd", [128, 2, 256], BF16)
                            abdc = sb(e2, "abdc", [128, 2, 256], BF16)
                            XTs = sb(e2, "XTs", [128, 2, T])
                            XTc = sb(e2, "XTc", [128, 2, TC])
                            Yp = [sb(e2, "Yp%d" % i, [128, 512], BF16) for i in range(2)]
                            Gs = [sb(e2, "Gs%d" % i, [128, 2, 8, 256], BF16) for i in range(2)]
                            Rb = [sb(e2, "Rb%d" % i, [128, 8, 256], BF16) for i in range(2)]
                            sq = sb(e2, "sq", [128, 2, TL])
                            rr = sb(e2, "rr", [128, TL])
                            rrs = sb(e2, "rrs", [128, TL])
                            rrt = sb(e2, "rrt", [128, TL])
                            yCT = [sb(e2, "yCT%d" % i, [128, 2, TL], BF16) for i in range(2)]
                            pY = [ps(e2, "pY%d" % i, [128, 512]) for i in range(2)]
                            pG = [ps(e2, "pG%d" % i, [128, 2, 256]) for i in range(2)]
                            pX = [ps(e2, "pX%d" % i, [128, 8, 64]) for i in range(2)]
                            pS = ps(e2, "pS", [128, 512])
                            pS1 = ps(e2, "pS1", [128, 512])
                            for r_ in range(2):
                                dma("sp", zT[:, :, r_ * TLOC:(r_ + 1) * TLOC], ZFv[r_], writes=["zT"])
                            dma("sp", t1[:], t1_d[:, :, :], writes=["t1"])
                            dma("sp", tw[:].rearrange("p a b -> p (a b)"), tw_d[:, :], writes=["tw"])
                            dma("sp", c256[:], c256_d[:, :, :, :], writes=["c256"])
                            dma("sp", cspad[:], cspad_d[:, :, :, :], writes=["cspad"])
                            dma("sp", wft[:], wf_d[l].rearrange("g j e -> j g e"), writes=["wft"])
                            for cc in range(2):
                                for pq in range(2):
                                    for gl in range(2):
                                        op("pe", lambda e: e.matmul(pY[0][:, pq * 128 + gl * 64:pq * 128 + gl * 64 + 64],
                                                                    lhsT=cspad[:, pq, gl, :], rhs=wft[:, 2 * cc + gl, :], start=True, stop=True),
                                           reads=["cspad", "wft"], writes=["pY0"])
                                op("act", lambda e: e.activation(out=abd[:, cc, :], in_=pY[0][:, 0:256], func=AF.Identity,
                                                                 scale=1.0 / math.sqrt(T * 64.0)), writes=["pY0", "abd"])
                                op("act", lambda e: e.activation(out=abdc[:, cc, :], in_=pY[0][:, 0:256], func=AF.Identity,
                                                                 scale=1.0 / math.sqrt(TC * 64.0)), writes=["pY0", "abdc"])

                            rr2 = [sb(e2, "rr2_%d" % i, [128, TL]) for i in range(2)]
                            rrs2 = [sb(e2, "rrs2_%d" % i, [128, TL]) for i in range(2)]
                            rrt2 = [sb(e2, "rrt2_%d" % i, [128, TL]) for i in range(2)]

                            def rmsc1(src, b):
                                op("act", lambda e: e.activation(out=sq[:], in_=src, func=AF.Square), reads=["XT"], writes=["sq"])
                                pS_ = pS if b == 0 else pS1
                                for c in range(2):
                                    op("pe", lambda e: e.matmul(pS_[:, 0:TL], lhsT=ones_f[:], rhs=sq[:, c, :], start=(c == 0), stop=(c == 1)),
                                       reads=["ones_f", "sq"], writes=["pS%d" % b])

                            def rmsc2(b):
                                pS_ = pS if b == 0 else pS1
                                op("act", lambda e: e.activation(out=rr2[b][:], in_=pS_[:, 0:TL], func=AF.Ln, scale=1.0 / DC, bias=EPS),
                                   writes=["pS%d" % b, "rr2_%d" % b])
                                op("act", lambda e: e.activation(out=rrs2[b][:], in_=rr2[b][:], func=AF.Exp, scale=-0.5),
                                   reads=["rr2_%d" % b], writes=["rrs2_%d" % b])

                            def rmsc3(src, col0, b):
                                for c in range(2):
                                    op("dve", lambda e: e.scalar_tensor_tensor(out=yCT[b][:, c, :], in0=src[:, c, :], scalar=gmixc[:, 6 + c:7 + c],
                                                                               in1=rrs2[b][:], op0=ALU.mult, op1=ALU.mult),
                                       reads=["XT", "gmixc", "rrs2_%d" % b], writes=["yCT%d_%d" % (b, c)])
                                dma("sp", YTv[:, 6:8, col0:col0 + TL], yCT[b][:], reads=["yCT%d_0" % b, "yCT%d_1" % b], writes=["YT"])

                            def rms_store_c(src, col0, b):
                                rmsc1(src, b)
                                rmsc2(b)
                                rmsc3(src, col0, b)

                            if not last:
                                Ypc = [sb(e2, "Ypc%d" % i, [128, 512], BF16) for i in range(2)]
                                for t in range(2):
                                    for cc in range(2):
                                        op("pe", lambda e: e.matmul(pY[t][:, cc * 256:(cc + 1) * 256], lhsT=zTc[:, cc, t * 128:(t + 1) * 128],
                                                                    rhs=abdc[:, cc, :], start=True, stop=True), reads=["zTc", "abdc"], writes=["pY%d" % t])
                                    op("act", lambda e: e.activation(out=Ypc[t][:], in_=pY[t][:], func=AF.Identity), writes=["pY%d" % t, "Ypc%d" % t])
                                for cc in range(2):
                                    n = 0
                                    for t in range(2):
                                        for pq in range(2):
                                            op("pe", lambda e: e.matmul(pG[cc][:].rearrange("p a b -> p (a b)")[:, 0:256],
                                                                        lhsT=Ypc[t][:, cc * 256 + pq * 128:cc * 256 + (pq + 1) * 128],
                                                                        rhs=c256[:, t, pq, :], start=(n == 0), stop=(n == 3)),
                                               reads=["Ypc%d" % t, "c256"], writes=["pG%d" % cc])
                                            n += 1
                                    op("dve", lambda e: e.tensor_copy(out=XTc[:, cc, :], in_=pG[cc][:].rearrange("p a b -> p (a b)")[:, 0:256]),
                                       writes=["pG%d" % cc, "XT"])
                                rms_store_c(XTc[:, :, :], T, 0)

                            zTv = zT[:].rearrange("p c (a b) -> p c b a", b=64)
                            GDv = GD
                            for l2 in range(64):
                                b = l2 % 2
                                for cc in range(2):
                                    op("pe", lambda e: e.matmul(pY[b][:, cc * 256:(cc + 1) * 256], lhsT=zTv[:, cc, l2, :], rhs=abd[:, cc, :],
                                                                start=True, stop=True), reads=["zT", "abd"], writes=["pY%d" % b])
                                op("act", lambda e: e.activation(out=Yp[b][:], in_=pY[b][:], func=AF.Identity), writes=["pY%d" % b, "Yp%d" % b])
                                Ypv = Yp[b][:].rearrange("p (c q j) -> p q c j", c=2, q=2)
                                combos = [(0, 0, 0), (0, 1, 1), (1, 0, 1), (1, 1, 2)]
                                for (ri, pq, ti_) in combos:
                                    op("pe", lambda e: e.matmul(pG[b][:, ri, :].rearrange("p (c j) -> p c j", c=2), lhsT=t1[:, ti_, :], rhs=Ypv[:, pq, :, :],
                                                                start=(pq == 0), stop=(pq == 1)), reads=["t1", "Yp%d" % b], writes=["pG%d" % b])
                                gb = (l2 // 8) % 2
                                op("dve", lambda e: e.tensor_copy(out=Gs[gb][:, :, l2 % 8, :], in_=pG[b][:]), writes=["pG%d" % b, "Gs%d" % gb])
                                if l2 % 8 == 7:
                                    l0 = l2 - 7
                                    dma("sp", GDv[:, :, l0:l0 + 8, :], Gs[gb][:], reads=["Gs%d" % gb], writes=["GD"])
                            XTv = XTs[:].rearrange("p c (k2 k1) -> p c k1 k2", k1=128)
                            for kb in range(16):
                                b = kb % 2
                                dma("sp", Rb[b][:], GD[kb * 8:(kb + 1) * 8, :, :, :].rearrange("k r l c -> (r l) k c"), reads=["GD"], writes=["Rb%d" % b])
                                for cc in range(2):
                                    px, kpx = pX[cc], "pX%d" % cc
                                    for r in range(8):
                                        op("pe", lambda e: e.matmul(px[:, r, :], lhsT=Rb[b][:, r, cc * 128:(cc + 1) * 128], rhs=tw[:, kb * 8 + r, :],
                                                                    start=True, stop=True), reads=["Rb%d" % b, "tw"], writes=[kpx])
                                    op("act" if cc == 0 else "dve",
                                       (lambda e: e.activation(out=XTv[:, cc, kb * 8:(kb + 1) * 8, :], in_=px[:], func=AF.Identity)) if cc == 0 else
                                       (lambda e: e.tensor_copy(out=XTv[:, cc, kb * 8:(kb + 1) * 8, :], in_=px[:])),
                                       writes=[kpx, "XT"])
                            for ti in range(NT + 2):
                                if ti < NT:
                                    rmsc1(XTs[:, :, ti * TL:(ti + 1) * TL], ti % 2)
                                if 0 <= ti - 1 < NT:
                                    rmsc2((ti - 1) % 2)
                                if 0 <= ti - 2 < NT:
                                    t2 = ti - 2
                                    rmsc3(XTs[:, :, t2 * TL:(t2 + 1) * TL], t2 * TL, t2 % 2)
                            fw.barrier()
                if stop_after == "DFT" and l == 0:
                    return nc

                with ExitStack() as e3:
                    wg = sb(e3, "wg", [128, 2, 2, 4, 128], BF16)
                    cw = sb(e3, "cw", [128, 4, 4])
                    cb = sb(e3, "cb", [128, 4])
                    gb_ = sb(e3, "gbias", [128, 2, 2, 4])
                    lam = sb(e3, "lam", [128, 2, 4])
                    hnsp = sb(e3, "hnsp", [128, 2, 4])
                    nsp = sb(e3, "nsp", [128, 2, 4])
                    op("pool", lambda e: e.memset(wg[:], 0.0), writes=["wg"])
                    for ax, wd in enumerate((wa_d, wx_d)):
                        for h2 in range(2):
                            for d_ in range(2):
                                src = wd[l, d_].rearrange("(c h) i e -> h i c e", h=2)[h2]
                                dma("pool", wg[64 * h2:64 * h2 + 64, d_, ax, :, 64 * h2:64 * h2 + 64], src, writes=["wg"])
                    for k in range(4):
                        dma("sp", cw[:, :, k], convw_d[l, k, :].rearrange("(c p) -> p c", p=128), writes=["cw"])
                    dma("sp", cb[:], convb_d[l].rearrange("(c p) -> p c", p=128), writes=["cb"])
                    for d_ in range(2):
                        dma("sp", gb_[:, 0, d_, :], ba_d[l, d_, :].rearrange("(c p) -> p c", p=128), writes=["gbias"])
                        dma("sp", gb_[:, 1, d_, :], bx_d[l, d_, :].rearrange("(c p) -> p c", p=128), writes=["gbias"])
                        dma("sp", lam[:, d_, :], lam_d[l, d_, :].rearrange("(c p) -> p c", p=128), writes=["lam"])
                    op("dve", lambda e: e.tensor_scalar(out=gb_[:], in0=gb_[:], scalar1=0.5, scalar2=None, op0=ALU.mult), writes=["gbias"])
                    op("act", lambda e: e.activation(out=lam[:], in_=lam[:], func=AF.Exp, scale=-1.0), writes=["lam"])
                    op("act", lambda e: e.activation(out=lam[:], in_=lam[:], func=AF.Ln, bias=1.0), writes=["lam"])
                    op("dve", lambda e: e.tensor_scalar(out=hnsp[:], in0=lam[:], scalar1=-4.0, scalar2=None, op0=ALU.mult), reads=["lam"], writes=["hnsp"])
                    op("dve", lambda e: e.tensor_scalar(out=nsp[:], in0=lam[:], scalar1=-8.0, scalar2=None, op0=ALU.mult), reads=["lam"], writes=["nsp"])

                    xaH = [sb(e3, "xaH%d" % i, [128, 4, TL + 3]) for i in range(2)]
                    xc = [sb(e3, "xc%d" % i, [128, 4, TL]) for i in range(2)]
                    xcb = [sb(e3, "xcb%d" % i, [128, 4, TL], BF16) for i in range(2)]
                    hS = [sb(e3, "hS%d" % i, [128, 4, TL]) for i in range(2)]
                    hfL = [sb(e3, "hfL%d" % i, [128, 4, TL]) for i in range(2)]
                    ggL = [sb(e3, "ggL%d" % i, [128, 4, TL]) for i in range(2)]
                    tr = sb(e3, "tr", [128, 4, TL])
                    tiS = [sb(e3, "tiS%d" % i, [128, 4, TL]) for i in range(2)]
                    aS = [sb(e3, "aS%d" % i, [128, 4, TL]) for i in range(2)]
                    mS = [sb(e3, "mS%d" % i, [128, 4, TL]) for i in range(2)]
                    uS = sb(e3, "uS", [128, 4, TL])
                    ya = [sb(e3, "ya%d" % i, [128, 4, TL]) for i in range(3)]
                    sq = sb(e3, "sqa", [128, 4, TL])
                    rr = [sb(e3, "rra%d" % i, [128, TL]) for i in range(2)]
                    rrs = [sb(e3, "rrsa%d" % i, [128, TL]) for i in range(2)]
                    rrt = [sb(e3, "rrta%d" % i, [128, TL]) for i in range(2)]
                    yAT = [sb(e3, "yAT%d" % i, [128, 4, TL], BF16) for i in range(2)]
                    pg = [ps(e3, "pg%d" % i, [128, 2, TL]) for i in range(4)]
                    pSs = [ps(e3, "pSa%d" % i, [128, 512]) for i in range(2)]

                    def xck(b):
                        return ["xc%d_%d" % (b, c) for c in range(4)]

                    def gates(d, b):
                        for c in range(4):
                            for ax in range(2):
                                op("pe", lambda e: e.matmul(pg[c][:, ax, :], lhsT=wg[:, d, ax, c, :], rhs=xcb[b][:, c, :], start=True, stop=True),
                                   reads=["wg", "xcb%d" % b], writes=["pg%d" % c])
                        for c in range(4):
                            op("act", lambda e: e.activation(out=tr[:, c, :], in_=pg[c][:, 0, :], func=AF.Tanh, scale=0.5, bias=gb_[:, 0, d, c:c + 1]),
                               reads=["gbias"], writes=["pg%d" % c, "tr"])
                            op("act", lambda e: e.activation(out=tiS[b][:, c, :], in_=pg[c][:, 1, :], func=AF.Tanh, scale=0.5, bias=gb_[:, 1, d, c:c + 1]),
                               reads=["gbias"], writes=["pg%d" % c, "tiS%d" % b])
                        for c in range(4):
                            op("act", lambda e: e.activation(out=aS[b][:, c, :], in_=tr[:, c, :], func=AF.Exp, scale=hnsp[:, d, c:c + 1], bias=hnsp[:, d, c:c + 1]),
                               reads=["tr", "hnsp"], writes=["aS%d" % b])
                            op("act", lambda e: e.activation(out=mS[b][:, c, :], in_=tr[:, c, :], func=AF.Exp, scale=nsp[:, d, c:c + 1], bias=nsp[:, d, c:c + 1]),
                               reads=["tr", "nsp"], writes=["mS%d" % b])
                        op("act", lambda e: e.activation(out=mS[b][:], in_=mS[b][:], func=AF.Sqrt, scale=-0.25, bias=0.25), writes=["mS%d" % b])

                    def make_u(b):
                        op("dve", lambda e: e.scalar_tensor_tensor(out=uS[:], in0=tiS[b][:], scalar=1.0, in1=xc[b][:], op0=ALU.add, op1=ALU.mult),
                           reads=["tiS%d" % b] + xck(b), writes=["uS"])
                        op("dve", lambda e: e.tensor_tensor(out=uS[:], in0=uS[:], in1=mS[b][:], op=ALU.mult), reads=["mS%d" % b], writes=["uS"])

                    scope_ = nc.named_scope("M2a_l%d" % l)
                    scope_.__enter__()

                    def load_xa(ti, b):
                        col0, is_ctx = TILES[ti]
                        first = is_ctx or col0 == 0
                        lastt = is_ctx or col0 == T - TL
                        lo = 0 if first else 2
                        hi = 0 if lastt else 1
                        if first:
                            op("pool", lambda e: e.memset(xaH[b][:, :, 0:2], 0.0), writes=["xaH%d" % b])
                        if lastt:
                            op("pool", lambda e: e.memset(xaH[b][:, :, TL + 2:TL + 3], 0.0), writes=["xaH%d" % b])
                        if is_ctx:
                            dma("sp", xaH[b][:, :, 2:TL + 2], XACv[:, :, :], writes=["xaH%d" % b])
                        else:
                            g0, g1 = col0 - lo, col0 + TL + hi
                            d0 = 2 - lo
                            for r_ in range(2):
                                a0, a1 = max(g0, r_ * TLOC), min(g1, (r_ + 1) * TLOC)
                                if a1 > a0:
                                    dma("sp", xaH[b][:, :, d0 + a0 - g0:d0 + a1 - g0], XAFv[r_][:, :, a0 - r_ * TLOC:a1 - r_ * TLOC],
                                        reads=["XAF"], writes=["xaH%d" % b])

                    def f0(ti):
                        col0, is_ctx = TILES[ti]
                        b = ti % 2
                        for c in range(4):
                            op("dve", lambda e: e.tensor_scalar(out=xc[b][:, c, :], in0=xaH[b][:, c, 0:TL], scalar1=cw[:, c, 0:1], scalar2=cb[:, c:c + 1],
                                                                op0=ALU.mult, op1=ALU.add), reads=["xaH%d" % b, "cw", "cb"], writes=["xc%d_%d" % (b, c)])
                        for k in range(1, 4):
                            for c in range(4):
                                op("dve", lambda e: e.scalar_tensor_tensor(out=xc[b][:, c, :], in0=xaH[b][:, c, k:k + TL], scalar=cw[:, c, k:k + 1],
                                                                           in1=xc[b][:, c, :], op0=ALU.mult, op1=ALU.add),
                                   reads=["xaH%d" % b, "cw"], writes=["xc%d_%d" % (b, c)])
                        op("act", lambda e: e.activation(out=xcb[b][:], in_=xc[b][:], func=AF.Identity), reads=xck(b), writes=["xcb%d" % b])
                        dma("sp", XCv[:, :, col0:col0 + TL], xc[b][:], reads=xck(b), writes=["XC"])
                        gates(0, b)

                    def f1(ti):
                        col0, is_ctx = TILES[ti]
                        b = ti % 2
                        pb_ = (ti - 1) % 2
                        make_u(b)
                        for c in range(4):
                            init = 0.0 if ti == 0 else hS[pb_][:, c, TL - 1:TL]
                            op("dve", lambda e: e.tensor_tensor_scan(out=hS[b][:, c, :], data0=aS[b][:, c, :], data1=uS[:, c, :], initial=init,
                                                                     op0=ALU.mult, op1=ALU.add),
                               reads=["aS%d" % b, "uS"] + ([] if ti == 0 else ["hS%d" % pb_]), writes=["hS%d" % b])
                        if not (last and is_ctx):
                            dma("sp", HFv[:, :, col0:col0 + TL], hS[b][:], reads=["hS%d" % b], writes=["HF"])

                    nT = len(TILES)
                    load_xa(0, 0)
                    load_xa(1, 1)
                    f0(0)
                    for ti in range(nT):
                        if ti + 2 < nT:
                            load_xa(ti + 2, ti % 2)
                        if ti + 1 < nT:
                            f0(ti + 1)
                        f1(ti)
                    fw.barrier()
                    scope_.__exit__(None, None, None)
                    if stop_after == "M2a" and l == 0:
                        return nc
                    scope_ = nc.named_scope("M2b_l%d" % l)
                    scope_.__enter__()

                    order = [0] + list(range(NT, 0, -1))
                    nO = len(order)

                    def load_x(oi):
                        col0, is_ctx = TILES[order[oi]]
                        b = oi % 2
                        dma("sp", xc[b][:], XCv[:, :, col0:col0 + TL], reads=["XC"], writes=xck(b))

                    def load_hg(oi):
                        col0, is_ctx = TILES[order[oi]]
                        b = oi % 2
                        if not (last and is_ctx):
                            dma("sp", hfL[b][:], HFv[:, :, col0:col0 + TL], reads=["HF"], writes=["hfL%d" % b])
                            ggsrc = GGCv[:, :, :] if is_ctx else GGFv[col0 // TLOC][:, :, col0 % TLOC:col0 % TLOC + TL]
                            dma("sp", ggL[b][:], ggsrc, reads=["GGF"], writes=["ggL%d" % b])

                    def b0(oi):
                        b = oi % 2
                        op("act", lambda e: e.activation(out=xcb[b][:], in_=xc[b][:], func=AF.Identity), reads=xck(b), writes=["xcb%d" % b])
                        gates(1, b)

                    def b1(oi):
                        col0, is_ctx = TILES[order[oi]]
                        b = oi % 2
                        pb_ = (oi - 1) % 2
                        y3 = oi % 3
                        make_u(b)
                        for c in range(4):
                            init = 0.0 if oi == 0 else hS[pb_][:, c, 0:1]
                            op("dve", lambda e: e.tensor_tensor_scan(out=hS[b][:, c, ::-1], data0=aS[b][:, c, ::-1], data1=uS[:, c, ::-1], initial=init,
                                                                     op0=ALU.mult, op1=ALU.add),
                               reads=["aS%d" % b, "uS"] + ([] if oi == 0 else ["hS%d" % pb_]), writes=["hS%d" % b])
                        if last and is_ctx:
                            return
                        op("pool", lambda e: e.tensor_tensor(out=ya[y3][:], in0=hS[b][:], in1=hfL[b][:], op=ALU.add), reads=["hS%d" % b, "hfL%d" % b], writes=["ya%d" % y3])
                        op("pool", lambda e: e.tensor_tensor(out=ya[y3][:], in0=ya[y3][:], in1=ggL[b][:], op=ALU.mult), reads=["ggL%d" % b], writes=["ya%d" % y3])
                        op("act", lambda e: e.activation(out=sq[:], in_=ya[y3][:], func=AF.Square), reads=["ya%d" % y3], writes=["sqa"])
                        for c in range(4):
                            op("pe", lambda e: e.matmul(pSs[b][:, 0:TL], lhsT=ones_f[:], rhs=sq[:, c, :], start=(c == 0), stop=(c == 3)),
                               reads=["ones_f", "sqa"], writes=["pSa%d" % b])

                    def b2a(oi):
                        col0, is_ctx = TILES[order[oi]]
                        b = oi % 2
                        if last and is_ctx:
                            return
                        op("dve", lambda e: e.tensor_scalar(out=rr[b][:], in0=pSs[b][:, 0:TL], scalar1=1.0 / DA, scalar2=EPS, op0=ALU.mult, op1=ALU.add),
                           writes=["pSa%d" % b, "rra%d" % b])
                        rsqrt(rrs[b][:], rr[b][:], rrt[b][:], "rrsa%d" % b, "rra%d" % b, "rrta%d" % b, iters=2)

                    def b2b(oi):
                        col0, is_ctx = TILES[order[oi]]
                        b = oi % 2
                        y3 = oi % 3
                        if last and is_ctx:
                            return
                        for c in range(4):
                            op("dve", lambda e: e.scalar_tensor_tensor(out=yAT[b][:, c, :], in0=ya[y3][:, c, :], scalar=gmixc[:, c:c + 1], in1=rrs[b][:],
                                                                       op0=ALU.mult, op1=ALU.mult), reads=["ya%d" % y3, "gmixc", "rrsa%d" % b], writes=["yAT%d_%d" % (b, c)])
                        dma("sp", YTv[:, 0:4, col0:col0 + TL], yAT[b][:], reads=["yAT%d_%d" % (b, c) for c in range(4)], writes=["YT"])

                    load_x(0)
                    load_hg(0)
                    load_x(1)
                    b0(0)
                    for oi in range(nO + 2):
                        if oi + 1 < nO:
                            load_hg(oi + 1)
                            b0(oi + 1)
                        if oi < nO:
                            b1(oi)
                        if oi + 2 < nO:
                            load_x(oi + 2)
                        if 0 <= oi - 1 < nO:
                            b2a(oi - 1)
                        if 0 <= oi - 2 < nO:
                            b2b(oi - 2)
                    fw.barrier()
                    scope_.__exit__(None, None, None)
                if stop_after == "M2b" and l == 0:
                    return nc

                m3_tiles = LTILES[1:] if last else LTILES
                wdn = sb(el, "wdn", [128, NF, D], BF16)
                with ExitStack() as e4:
                    e4.enter_context(nc.named_scope("M3a_l%d" % l))
                    wout = sb(e4, "wout", [128, 8, D], BF16)
                    for k in range(8):
                        dma("pool", wout[:, k, :], wout_d[l, k * 128:(k + 1) * 128, :], writes=["wout"])
                    for f in range(NF):
                        dma("pool", wdn[:, f, :], wdn_d[l, f * 128:(f + 1) * 128, :], writes=["wdn"])
                    gate = sb(e4, "gate1", [128, 2, D])
                    lng = sb(e4, "ln1g", [128, D])
                    lnb = sb(e4, "ln1b", [128, D])
                    for s in range(2):
                        dma("sp", gate[:, s, :], MODS[s:s + 1, 2 * D:3 * D].partition_broadcast(128), reads=["MODS"], writes=["gate1"])
                    dma("sp", lng[:], ln1g_d[l:l + 1, :].partition_broadcast(128), writes=["ln1g"])
                    dma("sp", lnb[:], ln1b_d[l:l + 1, :].partition_broadcast(128), writes=["ln1b"])
                    op("dve", lambda e: e.tensor_scalar(out=gate[:], in0=gate[:], scalar1=1.0 / ALPHA, scalar2=None, op0=ALU.mult), writes=["gate1"])
                    rmask = sb(e4, "rmask", [128, 2])
                    dma("sp", rmask[:], rmask_d[:, :], writes=["rmask"])
                    woutL = sb(e4, "woutL", [128, 8, D], BF16)
                    wsel = [sb(e4, "wsel%d" % r_, [128, 6, D], BF16) for r_ in range(2)]
                    woutC = sb(e4, "woutC", [128, 8, D], BF16) if not last else None
                    AC = (0, 1, 2, 3, 6, 7)
                    for k in range(8):
                        op("dve", lambda e: e.tensor_tensor(out=woutL[:, k, :], in0=wout[:, k, :], in1=gate[:, 0, :], op=ALU.mult),
                           reads=["wout", "gate1"], writes=["woutL"])
                        if not last:
                            op("pool", lambda e: e.tensor_tensor(out=woutC[:, k, :], in0=wout[:, k, :], in1=gate[:, 1, :], op=ALU.mult),
                               reads=["wout", "gate1"], writes=["woutC"])
                    for r_ in range(2):
                        for j, k in enumerate(AC):
                            if r_ == 0:
                                op("dve", lambda e: e.tensor_scalar(out=wsel[r_][:, j, :], in0=woutL[:, k, :], scalar1=rmask[:, r_:r_ + 1], scalar2=None, op0=ALU.mult),
                                   reads=["woutL", "rmask"], writes=["wsel%d_%d" % (r_, j)])
                            else:
                                op("act", lambda e: e.activation(out=wsel[r_][:, j, :], in_=woutL[:, k, :], func=AF.Identity, scale=rmask[:, r_:r_ + 1]),
                                   reads=["woutL", "rmask"], writes=["wsel%d_%d" % (r_, j)])
                    yTb = [sb(e4, "yTb%d" % i, [128, 2, TL], BF16) for i in range(3)]
                    yC = [[sb(e4, "yC%d_%d" % (i, r_), [128, 6, TL], BF16) for r_ in range(2)] for i in range(3)]
                    xt = [sb(e4, "xt%d" % i, [128, 2, D]) for i in range(2)]
                    rt = [sb(e4, "rt%d" % i, [128, 2, D]) for i in range(3)]
                    lns3 = [ln_scr(e4, "l3a"), ln_scr(e4, "l3b")]
                    po = [ps(e4, "po%d" % i, [128, 512]) for i in range(8)]
                    n3 = len(m3_tiles)

                    def loadY(ti):
                        col0, is_ctx = m3_tiles[ti]
                        y = ti % 3
                        dma("sp", yTb[y][:], YBLv[:, :, col0:col0 + TL], reads=["YBL"], writes=["yTb%d" % y])
                        if is_ctx:
                            dma("sp", yC[y][0][:, 0:4, :], YTv[:, 0:4, T:T + TL], reads=["YT"], writes=["yC%d_0" % y])
                            dma("sp", yC[y][0][:, 4:6, :], YTv[:, 6:8, T:T + TL], reads=["YT"], writes=["yC%d_0" % y])
                        else:
                            for r_ in range(2):
                                g0 = r_ * TLOC + col0
                                dma("sp", yC[y][r_][:, 0:4, :], YTv[:, 0:4, g0:g0 + TL], reads=["YT"], writes=["yC%d_%d" % (y, r_)])
                                dma("sp", yC[y][r_][:, 4:6, :], YTv[:, 6:8, g0:g0 + TL], reads=["YT"], writes=["yC%d_%d" % (y, r_)])

                    def loadX(ti):
                        col0, is_ctx = m3_tiles[ti]
                        b = ti % 2
                        load_resid(l, col0, is_ctx, xt[b], "xt%d" % b, None, None, from_xb=True)

                    def mm3(ti):
                        col0, is_ctx = m3_tiles[ti]
                        b = ti % 2
                        y = ti % 3
                        for s in range(2):
                            for hf_ in range(2):
                                p_, kp_ = po[4 * b + 2 * s + hf_], "po%d" % (4 * b + 2 * s + hf_)
                                cs = slice(hf_ * 512, (hf_ + 1) * 512)
                                ts_ = slice(s * 128, (s + 1) * 128)
                                steps = []
                                wB = woutC if is_ctx else woutL
                                for j in range(2):
                                    steps.append((yTb[y][:, j, ts_], wB[:, 4 + j, cs], ["yTb%d" % y, "woutC" if is_ctx else "woutL"]))
                                if is_ctx:
                                    for j, k in enumerate(AC):
                                        steps.append((yC[y][0][:, j, ts_], woutC[:, k, cs], ["yC%d_0" % y, "woutC"]))
                                else:
                                    for r_ in range(2):
                                        for j in range(6):
                                            steps.append((yC[y][r_][:, j, ts_], wsel[r_][:, j, cs], ["yC%d_%d" % (y, r_), "wsel%d_%d" % (r_, j)]))
                                for n_, (lh, rh, rd) in enumerate(steps):
                                    op("pe", lambda e: e.matmul(p_[:], lhsT=lh, rhs=rh, start=(n_ == 0), stop=(n_ == len(steps) - 1)), reads=rd, writes=[kp_])

                    def res3(ti):
                        b = ti % 2
                        r3 = ti % 3
                        for s in range(2):
                            for hf_ in range(2):
                                p_, kp_ = po[4 * b + 2 * s + hf_], "po%d" % (4 * b + 2 * s + hf_)
                                op("dve", lambda e: e.tensor_tensor(out=rt[r3][:, s, hf_ * 512:(hf_ + 1) * 512], in0=p_[:],
                                                                    in1=xt[b][:, s, hf_ * 512:(hf_ + 1) * 512], op=ALU.add),
                                   reads=["xt%d" % b], writes=[kp_, "rt%d_%d" % (r3, s)])

                    def stats3(ti):
                        b = ti % 2
                        r3 = ti % 3
                        ln_stats(rt[r3], ["rt%d_0" % r3, "rt%d_1" % r3], lns3[b], EPS_POST, iters=2)

                    def norm3(ti):
                        col0, is_ctx = m3_tiles[ti]
                        b = ti % 2
                        r3 = ti % 3
                        mv, rs, kk = lns3[b]["mv"], lns3[b]["rs"], lns3[b]["k"]
                        for s in range(2):
                            kr = "rt%d_%d" % (r3, s)
                            op("dve", lambda e: e.tensor_scalar(out=rt[r3][:, s, :], in0=rt[r3][:, s, :], scalar1=mv[:, s, 0:1],
                                                                scalar2=rs[:, s:s + 1], op0=ALU.subtract, op1=ALU.mult),
                               reads=[kk + "mv", kk + "rs"], writes=[kr])
                            op("dve", lambda e: e.tensor_tensor(out=rt[r3][:, s, :], in0=rt[r3][:, s, :], in1=lng[:], op=ALU.mult), reads=["ln1g"], writes=[kr])
                            op("pool", lambda e: e.tensor_tensor(out=rt[r3][:, s, :], in0=rt[r3][:, s, :], in1=lnb[:], op=ALU.add), reads=["ln1b"], writes=[kr])

                    def store3(ti):
                        col0, is_ctx = m3_tiles[ti]
                        r3 = ti % 3
                        dma("sp", XB[col0:col0 + TL, :].rearrange("(s p) d -> p s d", p=128), rt[r3][:], reads=["rt%d_0" % r3, "rt%d_1" % r3], writes=["XB_%d" % col0])

                    for i_ in range(min(3, n3)):
                        loadY(i_)
                    for i_ in range(min(2, n3)):
                        loadX(i_)
                    mm3(0)
                    res3(0)
                    stats3(0)
                    for ti in range(n3):
                        if ti + 3 < n3:
                            loadY(ti + 3)
                        if ti + 2 < n3:
                            loadX(ti + 2)
                        if ti >= 2:
                            store3(ti - 2)
                        if ti + 1 < n3:
                            mm3(ti + 1)
                            res3(ti + 1)
                            stats3(ti + 1)
                        norm3(ti)
                    if n3 >= 2:
                        store3(n3 - 2)
                    store3(n3 - 1)
                    fw.barrier()
                if stop_after == "M3a" and l == 0:
                    return nc

                with ExitStack() as e5:
                    e5.enter_context(nc.named_scope("M3b_l%d" % l))
                    wup = sb(e5, "wup", [128, 8, 2 * DFF], BF16)
                    for c0 in range(0, 2 * DFF, 2048):
                        c1 = min(c0 + 2048, 2 * DFF)
                        for k in range(8):
                            dma("pool", wup[:, k, c0:c1], wup_d[l, k * 128:(k + 1) * 128, c0:c1], writes=["wup_p%d_%d" % (c0 // 2048, k)])
                    gate = sb(e5, "gate2", [128, 2, D])
                    lng = sb(e5, "ln2g", [128, D])
                    lnb = sb(e5, "ln2b", [128, D])
                    for s in range(2):
                        dma("sp", gate[:, s, :], MODS[s:s + 1, 5 * D:6 * D].partition_broadcast(128), reads=["MODS"], writes=["gate2"])
                    dma("sp", lng[:], ln2g_d[l:l + 1, :].partition_broadcast(128), writes=["ln2g"])
                    dma("sp", lnb[:], ln2b_d[l:l + 1, :].partition_broadcast(128), writes=["ln2b"])
                    op("dve", lambda e: e.tensor_scalar(out=gate[:], in0=gate[:], scalar1=1.0 / ALPHA, scalar2=None, op0=ALU.mult), writes=["gate2"])
                    xt = [sb(e5, "xu%d" % i, [128, 2, D]) for i in range(2)]
                    xh = sb(e5, "xhu", [128, 2, D])
                    h2T = [sb(e5, "h2T%d" % i, [128, 8, TL], BF16) for i in range(2)]
                    actT = sb(e5, "actT", [128, NF, TL], BF16)
                    sg = [sb(e5, "sg%d" % i, [128, TL]) for i in range(2)]
                    lns = ln_scr(e5, "l5")
                    lns2 = ln_scr(e5, "l6")
                    tp0 = ps(e5, "tq0", [128, 2, TL]); tp1 = ps(e5, "tq1", [128, 2, TL])
                    tps = [(tp0, "tq0"), (tp1, "tq1")]
                    pu = [ps(e5, "pu%d" % i, [128, 2, TL]) for i in range(2)]
                    pd = [ps(e5, "pd%d" % i, [128, 512]) for i in range(4)]

                    def load5(ti, b):
                        col0, is_ctx = m3_tiles[ti]
                        dma("sp", xt[b][:], XB[col0:col0 + TL, :].rearrange("(s p) d -> p s d", p=128), reads=["XB_%d" % col0], writes=["xu%d" % b])

                    su = [sb(e5, "su%d" % i, [128, TL]) for i in range(2)]
                    n5 = len(m3_tiles)
                    folded = [False]

                    def fold_gate():
                        for f in range(NF):
                            op("dve" if f % 2 == 0 else "pool",
                               lambda e: e.tensor_tensor(out=wdn[:, f, :], in0=wdn[:, f, :], in1=gate[:, 0, :], op=ALU.mult), reads=["gate2"], writes=["wdn"])
                        folded[0] = True

                    def stA(ti):
                        b = ti % 2
                        ln_tile(xt[b], "xu%d" % b, xh, "xhu", lns, EPS, eng="dve", iters=2)

                    def stT(ti):
                        col0, is_ctx = m3_tiles[ti]
                        b = ti % 2
                        transpose_mod(xh, "xhu", h2T[b], "h2T%d" % b, tps, modc, 1 if is_ctx else 0, 4, 3)

                    def stU(ti):
                        b = ti % 2
                        for f in range(NF):
                            p_, kp_ = pu[f % 2], "pu%d" % (f % 2)
                            for j in range(2):
                                c0 = j * DFF + f * 128
                                for k in range(8):
                                    op("pe", lambda e: e.matmul(p_[:, j, :], lhsT=wup[:, k, c0:c0 + 128], rhs=h2T[b][:, k, :], start=(k == 0), stop=(k == 7)),
                                       reads=["wup_p%d_%d" % (c0 // 2048, k), "h2T%d" % b], writes=[kp_])
                            op("act", lambda e: e.activation(out=sg[f % 2][:], in_=p_[:, 0, :], func=AF.Silu), writes=[kp_, "sg%d" % (f % 2)])
                            op("act", lambda e: e.activation(out=su[f % 2][:], in_=p_[:, 1, :], func=AF.Identity), writes=[kp_, "su%d" % (f % 2)])
                            op("pool", lambda e: e.tensor_tensor(out=actT[:, f, :], in0=su[f % 2][:], in1=sg[f % 2][:], op=ALU.mult),
                               reads=["sg%d" % (f % 2), "su%d" % (f % 2)], writes=["actT"])

                    def stD(ti):
                        for s in range(2):
                            for hf_ in range(2):
                                p_, kp_ = pd[2 * s + hf_], "pd%d" % (2 * s + hf_)
                                for f in range(NF):
                                    op("pe", lambda e: e.matmul(p_[:], lhsT=actT[:, f, s * 128:(s + 1) * 128], rhs=wdn[:, f, hf_ * 512:(hf_ + 1) * 512],
                                                                start=(f == 0), stop=(f == NF - 1)), reads=["wdn", "actT"], writes=[kp_])

                    def stE(ti):
                        col0, is_ctx = m3_tiles[ti]
                        b = ti % 2
                        kx = "xu%d" % b
                        for s in range(2):
                            for hf_ in range(2):
                                p_, kp_ = pd[2 * s + hf_], "pd%d" % (2 * s + hf_)
                                cs = slice(hf_ * 512, (hf_ + 1) * 512)
                                if folded[0]:
                                    op("dve", lambda e: e.tensor_tensor(out=xt[b][:, s, cs], in0=p_[:], in1=xt[b][:, s, cs], op=ALU.add), writes=[kp_, kx])
                                else:
                                    op("dve", lambda e: e.tensor_tensor(out=xh[:, s, cs], in0=p_[:], in1=gate[:, 1 if is_ctx else 0, cs], op=ALU.mult),
                                       reads=["gate2"], writes=[kp_, "xhu"])
                        if not folded[0]:
                            op("pool", lambda e: e.tensor_tensor(out=xt[b][:], in0=xt[b][:], in1=xh[:], op=ALU.add), reads=["xhu"], writes=[kx])
                        ln_tile(xt[b], kx, xt[b], kx, lns2, EPS_POST, eng="dve", iters=3)
                        for s in range(2):
                            op("dve", lambda e: e.tensor_tensor(out=xt[b][:, s, :], in0=xt[b][:, s, :], in1=lng[:], op=ALU.mult), reads=["ln2g"], writes=[kx])
                            op("dve", lambda e: e.tensor_tensor(out=xt[b][:, s, :], in0=xt[b][:, s, :], in1=lnb[:], op=ALU.add), reads=["ln2b"], writes=[kx])
                        dst = out_d[col0:col0 + TL, :] if last else XB[col0:col0 + TL, :]
                        dma("sp", dst.rearrange("(s p) d -> p s d", p=128), xt[b][:], reads=[kx], writes=["XB_%d" % col0])

                    load5(0, 0)
                    if n5 > 1:
                        load5(1, 1)
                    if not m3_tiles[0][1]:
                        fold_gate()
                    stA(0)
                    stT(0)
                    for ti in range(n5):
                        stU(ti)
                        if ti + 1 < n5:
                            stA(ti + 1)
                        stD(ti)
                        if ti + 1 < n5:
                            stT(ti + 1)
                        stE(ti)
                        if m3_tiles[ti][1]:
                            fold_gate()
                        if ti + 2 < n5:
                            load5(ti + 2, ti % 2)
                    fw.barrier()
                if stop_after == "M3b" and l == 0:
                    return nc
        fw.wait_all("sp")
    return nc


def host_consts():
    c = {}
    c["ident"] = np.eye(128, dtype=np.float32)
    l1 = np.arange(128)[:, None].astype(np.float64)
    k1 = np.arange(128)[None, :].astype(np.float64)
    a = 2 * np.pi * l1 * k1 / 128.0
    c["t1"] = np.stack([np.cos(a), -np.sin(a), -np.cos(a)], 1).astype(np.float32).astype(ml_dtypes.bfloat16)
    l2 = np.arange(64)[:, None, None].astype(np.float64)
    kk1 = np.arange(128)[None, :, None].astype(np.float64)
    kk2 = np.arange(64)[None, None, :].astype(np.float64)
    ang = 2 * np.pi * ((kk1 + 128 * kk2) * l2 % T) / T
    c["tw"] = np.concatenate([np.cos(ang), np.sin(ang)], 0).reshape(128, T).astype(np.float32).astype(ml_dtypes.bfloat16)
    p = np.arange(128)[:, None, None].astype(np.float64)
    t = np.arange(2)[None, :, None].astype(np.float64)
    k = np.arange(256)[None, None, :].astype(np.float64)
    a2 = 2 * np.pi * (((128 * t + p) * k) % 256) / 256.0
    c["c256"] = np.stack([np.cos(a2), -np.sin(a2)], 2).astype(np.float32).astype(ml_dtypes.bfloat16)
    j = np.arange(64)[:, None].astype(np.float64)
    ch = np.arange(64)[None, :].astype(np.float64)
    a3 = 2 * np.pi * j * ch / 64.0
    cs = np.zeros((64, 2, 2, 128), np.float64)
    for gl in range(2):
        cs[:, 0, gl, gl * 64:(gl + 1) * 64] = np.cos(a3)
        cs[:, 1, gl, gl * 64:(gl + 1) * 64] = np.sin(a3)
    c["cspad"] = cs.astype(np.float32)
    quarter = D // 4
    freqs = 10000.0 ** (-np.arange(quarter, dtype=np.float32) / np.float32(quarter))
    r = np.repeat(np.arange(T // 64, dtype=np.float32), 64)
    col = np.tile(np.arange(64, dtype=np.float32), T // 64)

    def enc(pv):
        an = pv[:, None].astype(np.float32) * freqs[None, :].astype(np.float32)
        return np.concatenate([np.sin(an), np.cos(an)], -1)

    c["pos"] = np.concatenate([enc(r), enc(col)], -1).astype(np.float32)
    return c


_WNAMES = ["w_mod", "b_mod", "w_in", "conv_w", "conv_b", "lru_wa", "lru_ba", "lru_wx", "lru_bx", "lru_lam", "sg_ws", "sg_b",
           "fourier_w", "g_mix", "w_out", "ln1_g", "ln1_b", "w_up", "w_down", "ln2_g", "ln2_b"]


def make_in_maps(inputs, n_cores=N_CORES):
    consts = host_consts()
    pos = consts.pop("pos")
    shared = {n: np.ascontiguousarray(np.asarray(inputs[n], dtype=np.float32)) for n in _WNAMES}
    shared.update(consts)
    x = np.asarray(inputs["x"], dtype=np.float32)
    c = np.asarray(inputs["c"], dtype=np.float32)
    ctx = np.asarray(inputs["ctx"], dtype=np.float32)
    c_ctx = np.asarray(inputs["c_ctx"], dtype=np.float32)
    maps = []
    for core in range(n_cores):
        b, h = core // 2, core % 2
        m = dict(shared)
        m["x"] = np.ascontiguousarray(x[b, h * TLOC:(h + 1) * TLOC])
        m["pos"] = np.ascontiguousarray(pos[h * TLOC:(h + 1) * TLOC])
        m["ctx"] = np.ascontiguousarray(ctx[b])
        m["cc"] = np.ascontiguousarray(np.stack([c[b], c_ctx], 0))
        rm = np.zeros((128, 2), np.float32)
        rm[:, h] = 1.0
        m["rmask"] = rm
        maps.append(m)
    return maps


def kernel(**inputs):
    nc = build_program()
    maps = make_in_maps(inputs)
    res = run_bass_kernel_spmd(nc, maps, core_ids=list(range(N_CORES)))
    outs = [np.asarray(r["out"], dtype=np.float32) for r in res.results]
    return np.stack([np.concatenate([outs[2 * b], outs[2 * b + 1]], 0) for b in range(N_CORES // 2)], 0)
```
